# Optimizing a Trainium2 kernel written in Bass

```python
import math
import jax, jax.numpy as jnp
from jax import lax
import numpy as np

D_MODEL = 1024
BATCH = 4
SEQ = 4096
DEPTH = 1
DEC_BATCH = 128
DEC_SEQ = 8
PAST_LEN = 16384
PAGE_SIZE = 128

GDN_HEADS = 8
GDN_DK = 128
GDN_DV = 128
GDN_QK_DIM = GDN_HEADS * GDN_DK
GDN_V_DIM = GDN_HEADS * GDN_DV
CONV_W = 4
GDN_CHUNK = 64
CONV_DIM = 2 * GDN_QK_DIM + GDN_V_DIM
SWA_Q_HEADS = 16
SWA_KV_HEADS = 4
SWA_GROUP = SWA_Q_HEADS // SWA_KV_HEADS
SWA_HEAD_DIM = 64
WINDOW = 128
ROPE_DIM = SWA_HEAD_DIM // 4
ROPE_THETA = 500000.0
D_FF = 2816
EPS = 1e-6
IN_SPLITS = (CONV_DIM, GDN_V_DIM, GDN_HEADS, GDN_HEADS, SWA_Q_HEADS * SWA_HEAD_DIM,
             SWA_KV_HEADS * SWA_HEAD_DIM, SWA_KV_HEADS * SWA_HEAD_DIM, D_MODEL, D_MODEL)
D_IN = sum(IN_SPLITS)

kernel_name = 'hybrid_gdn_swa_macaron_step'


def rmsnorm(x, w):
    x32 = x.astype(jnp.float32)
    y = x32 * lax.rsqrt(jnp.mean(x32 * x32, axis=-1, keepdims=True) + EPS)
    return (y * w.astype(jnp.float32)).astype(x.dtype)


def l2norm(x):
    x32 = x.astype(jnp.float32)
    return x32 * lax.rsqrt(jnp.sum(x32 * x32, axis=-1, keepdims=True) + EPS)


def swiglu(x, w_in, w_out):
    gate, up = jnp.split(x @ w_in, 2, axis=-1)
    return (jax.nn.silu(gate) * up) @ w_out


def rope_partial(x, pos):
    inv = ROPE_THETA ** (-jnp.arange(0, ROPE_DIM, 2, dtype=jnp.float32) / ROPE_DIM)
    ang = pos.astype(jnp.float32)[:, None] * inv[None, :]
    cos = jnp.cos(ang)[None, :, None, :]
    sin = jnp.sin(ang)[None, :, None, :]
    xr = x[..., :ROPE_DIM].astype(jnp.float32)
    x1, x2 = xr[..., :ROPE_DIM // 2], xr[..., ROPE_DIM // 2:]
    rot = jnp.concatenate([x1 * cos - x2 * sin, x2 * cos + x1 * sin], axis=-1)
    return jnp.concatenate([rot.astype(x.dtype), x[..., ROPE_DIM:]], axis=-1)


def sink_attention(q, k, v, mask, sinks):
    s = jnp.einsum('...qhgd,...khd->...hgqk', q, k, preferred_element_type=jnp.float32)
    s = jnp.where(mask, s * (SWA_HEAD_DIM ** -0.5), -jnp.inf)
    sink = sinks.astype(jnp.float32)[:, :, None, None]
    m = jnp.maximum(jnp.max(s, axis=-1, keepdims=True), sink)
    p = jnp.exp(s - m)
    denom = jnp.sum(p, axis=-1, keepdims=True) + jnp.exp(sink - m)
    return jnp.einsum('...hgqk,...khd->...qhgd', (p / denom).astype(v.dtype), v)


def swa_prompt(q, k, v, sinks):
    B, L, _, D = q.shape
    nb = L // WINDOW
    qb = q.reshape(B, nb, WINDOW, SWA_KV_HEADS, SWA_GROUP, D)

    def with_prev(t):
        t = t.reshape(B, nb, WINDOW, SWA_KV_HEADS, D)
        prev = jnp.pad(t[:, :-1], ((0, 0), (1, 0), (0, 0), (0, 0), (0, 0)))
        return jnp.concatenate([prev, t], axis=2)

    qi = jnp.arange(WINDOW)
    kj = jnp.arange(2 * WINDOW) - WINDOW
    rel = qi[:, None] - kj[None, :]
    band = (rel >= 0) & (rel < WINDOW)
    has_prev = (jnp.arange(nb)[:, None] > 0) | (kj[None, :] >= 0)
    mask = band[None] & has_prev[:, None, :]
    o = sink_attention(qb, with_prev(k), with_prev(v), mask[None, :, None, None], sinks)
    return o.reshape(B, L, SWA_Q_HEADS, D)


def swa_sample(q, k, v, k_buf, v_buf, sinks):
    B, T, _, D = q.shape
    n_rows = k_buf.shape[1]
    kc = jnp.concatenate([k_buf.astype(k.dtype), k], axis=1)
    vc = jnp.concatenate([v_buf.astype(v.dtype), v], axis=1)
    qpos = PAST_LEN + jnp.arange(T)
    kpos = PAST_LEN - n_rows + jnp.arange(n_rows + T)
    rel = qpos[:, None] - kpos[None, :]
    mask = (rel >= 0) & (rel < WINDOW)
    o = sink_attention(q.reshape(B, T, SWA_KV_HEADS, SWA_GROUP, D), kc, vc, mask[None, None, None], sinks)
    return o.reshape(B, T, SWA_Q_HEADS, D), kc[:, -n_rows:], vc[:, -n_rows:]


def gated_delta_rule(q, k, v, g, beta, S0):
    f32 = jnp.float32
    B, L, H, DK = q.shape
    DV = v.shape[-1]
    C = min(GDN_CHUNK, L)
    N = -(-L // C)
    pad = N * C - L

    def blocks(t):
        t = jnp.moveaxis(t.astype(f32), 2, 1)
        t = jnp.pad(t, [(0, 0), (0, 0), (0, pad)] + [(0, 0)] * (t.ndim - 3))
        return t.reshape(t.shape[:2] + (N, C) + t.shape[3:])

    qc = blocks(q) * (DK ** -0.5)
    kc = blocks(k)
    vc = blocks(v)
    gc = jnp.cumsum(blocks(g), axis=-1)
    bc = blocks(beta)
    idx = jnp.arange(C)
    incl = idx[:, None] >= idx[None, :]
    strict = idx[:, None] > idx[None, :]
    decay = jnp.exp(jnp.where(incl, gc[..., :, None] - gc[..., None, :], -jnp.inf))
    kb = kc * bc[..., None]
    vb = vc * bc[..., None]
    A = jnp.where(strict, jnp.einsum('bhnid,bhnjd->bhnij', kb, kc) * decay, 0.0)
    eye = jnp.eye(C, dtype=f32)
    T = lax.linalg.triangular_solve(A + eye, jnp.broadcast_to(eye, A.shape), left_side=True, lower=True)
    u_val = jnp.einsum('bhnij,bhnjv->bhniv', T, vb)
    w_key = jnp.einsum('bhnij,bhnjd->bhnid', T, kb * jnp.exp(gc)[..., None])
    qk = jnp.einsum('bhnid,bhnjd->bhnij', qc, kc) * decay
    qg = qc * jnp.exp(gc)[..., None]
    kg = kc * jnp.exp(gc[..., -1:] - gc)[..., None]
    g_last = jnp.exp(gc[..., -1])

    def step(S, xs):
        qk_i, qg_i, kg_i, u_i, w_i, gl_i = xs
        v_new = u_i - jnp.einsum('bhcd,bhdv->bhcv', w_i, S)
        o = jnp.einsum('bhcd,bhdv->bhcv', qg_i, S) + jnp.einsum('bhij,bhjv->bhiv', qk_i, v_new)
        S = S * gl_i[..., None, None] + jnp.einsum('bhcd,bhcv->bhdv', kg_i, v_new)
        return S, o

    xs = tuple(jnp.moveaxis(t, 2, 0) for t in (qk, qg, kg, u_val, w_key, g_last))
    S, o = lax.scan(step, S0.astype(f32), xs)
    o = jnp.moveaxis(o, 0, 2).reshape(B, H, N * C, DV)[:, :, :L]
    return jnp.moveaxis(o, 1, 2), S


def decoder_layer(x, pos, conv_buf, S0, k_buf, v_buf, p):
    B, L, _ = x.shape
    x = x + 0.5 * swiglu(rmsnorm(x, p['norm_ffn1']), p['w_ffn1_in'], p['w_ffn1_out'])
    u = rmsnorm(x, p['norm_mix']) @ p['w_in']
    qkv_raw, z, a, b, q_s, k_s, v_s, gate_a, gate_b = jnp.split(u, np.cumsum(IN_SPLITS)[:-1].tolist(), axis=-1)
    ext = jnp.concatenate([conv_buf.astype(qkv_raw.dtype), qkv_raw], axis=1)
    new_conv = ext[:, -(CONV_W - 1):]
    conv_w = p['conv_w']
    qkv = jax.nn.silu(sum(conv_w[j] * ext[:, j:j + L] for j in range(CONV_W)))
    q_g, k_g, v_g = jnp.split(qkv, [GDN_QK_DIM, 2 * GDN_QK_DIM], axis=-1)
    q_g = l2norm(q_g.reshape(B, L, GDN_HEADS, GDN_DK))
    k_g = l2norm(k_g.reshape(B, L, GDN_HEADS, GDN_DK))
    v_g = v_g.reshape(B, L, GDN_HEADS, GDN_DV)
    g = -jnp.exp(p['gdn_a_log'].astype(jnp.float32)) * jax.nn.softplus(
        a.astype(jnp.float32) + p['gdn_dt_bias'].astype(jnp.float32))
    beta = jax.nn.sigmoid(b.astype(jnp.float32))
    o_g, new_S = gated_delta_rule(q_g, k_g, v_g, g, beta, S0)
    o_g = rmsnorm(o_g.astype(x.dtype), p['gdn_norm']) * jax.nn.silu(z.reshape(B, L, GDN_HEADS, GDN_DV))
    o_g = o_g.reshape(B, L, GDN_V_DIM)
    q_s = rope_partial(q_s.reshape(B, L, SWA_Q_HEADS, SWA_HEAD_DIM), pos)
    k_s = rope_partial(k_s.reshape(B, L, SWA_KV_HEADS, SWA_HEAD_DIM), pos)
    v_s = v_s.reshape(B, L, SWA_KV_HEADS, SWA_HEAD_DIM)
    sinks = p['swa_sinks'].reshape(SWA_KV_HEADS, SWA_GROUP)
    if k_buf is None:
        o_s = swa_prompt(q_s, k_s, v_s, sinks)
        n_keep = min(WINDOW, L)
        new_k, new_v = k_s[:, -n_keep:], v_s[:, -n_keep:]
    else:
        o_s, new_k, new_v = swa_sample(q_s, k_s, v_s, k_buf, v_buf, sinks)
    o_s = o_s.reshape(B, L, SWA_Q_HEADS * SWA_HEAD_DIM)
    mixed = jax.nn.sigmoid(gate_a) * o_g + jax.nn.sigmoid(gate_b) * o_s
    x = x + mixed @ p['w_out']
    x = x + 0.5 * swiglu(rmsnorm(x, p['norm_ffn2']), p['w_ffn2_in'], p['w_ffn2_out'])
    return x, new_conv, new_S, new_k, new_v


def setup_inputs(seed: int = 0) -> dict:
    key = jax.random.key(seed)
    ks = jax.random.split(key, 24)
    f32 = jnp.float32

    def nrm(k, shape, scale):
        return jax.random.normal(k, shape, f32) * scale

    def gain(k, shape):
        return 1.0 + 0.05 * jax.random.normal(k, shape, f32)

    n_rows = min(WINDOW, PAST_LEN)
    dt = jnp.exp(jax.random.uniform(ks[20], (DEPTH, GDN_HEADS), f32, math.log(1e-3), math.log(1e-1)))
    dt_bias = dt + jnp.log(-jnp.expm1(-dt))
    a_log = jnp.log(jax.random.uniform(ks[21], (DEPTH, GDN_HEADS), f32, 1.0, 16.0))
    return {
        'x_prompt': nrm(ks[0], (BATCH, SEQ, D_MODEL), 1.0),
        'x_sample': nrm(ks[1], (DEC_BATCH, DEC_SEQ, D_MODEL), 1.0),
        'state_conv': nrm(ks[2], (DEPTH, DEC_BATCH, CONV_W - 1, CONV_DIM), 1.0),
        'state_gdn': nrm(ks[3], (DEPTH, DEC_BATCH, GDN_HEADS, GDN_DK, GDN_DV), 0.1),
        'cache_swa_k': nrm(ks[4], (DEPTH, DEC_BATCH, n_rows, SWA_KV_HEADS, SWA_HEAD_DIM), 1.0),
        'cache_swa_v': nrm(ks[5], (DEPTH, DEC_BATCH, n_rows, SWA_KV_HEADS, SWA_HEAD_DIM), 1.0),
        'norm_ffn1': gain(ks[6], (DEPTH, D_MODEL)),
        'w_ffn1_in': nrm(ks[7], (DEPTH, D_MODEL, 2 * D_FF), D_MODEL ** -0.5),
        'w_ffn1_out': nrm(ks[8], (DEPTH, D_FF, D_MODEL), D_FF ** -0.5),
        'norm_mix': gain(ks[9], (DEPTH, D_MODEL)),
        'w_in': nrm(ks[10], (DEPTH, D_MODEL, D_IN), D_MODEL ** -0.5),
        'conv_w': nrm(ks[11], (DEPTH, CONV_W, CONV_DIM), CONV_W ** -0.5),
        'gdn_a_log': a_log,
        'gdn_dt_bias': dt_bias,
        'gdn_norm': gain(ks[12], (DEPTH, GDN_DV)),
        'swa_sinks': nrm(ks[13], (DEPTH, SWA_Q_HEADS), 0.5),
        'w_out': nrm(ks[14], (DEPTH, D_MODEL, D_MODEL), D_MODEL ** -0.5),
        'norm_ffn2': gain(ks[15], (DEPTH, D_MODEL)),
        'w_ffn2_in': nrm(ks[16], (DEPTH, D_MODEL, 2 * D_FF), D_MODEL ** -0.5),
        'w_ffn2_out': nrm(ks[17], (DEPTH, D_FF, D_MODEL), D_FF ** -0.5),
        'norm_final': gain(ks[18], (D_MODEL,)),
    }


def reference(x_prompt, x_sample, state_conv, state_gdn, cache_swa_k, cache_swa_v,
              norm_ffn1, w_ffn1_in, w_ffn1_out, norm_mix, w_in, conv_w, gdn_a_log, gdn_dt_bias,
              gdn_norm, swa_sinks, w_out, norm_ffn2, w_ffn2_in, w_ffn2_out, norm_final):
    xp, xs = x_prompt, x_sample
    B, L, _ = xp.shape
    pos_p = jnp.arange(L, dtype=jnp.int32)
    pos_s = PAST_LEN + jnp.arange(xs.shape[1], dtype=jnp.int32)
    conv_p, conv_s, gdn_p, gdn_s, k_p, k_s, v_p, v_s = [], [], [], [], [], [], [], []
    for l in range(DEPTH):
        p = {'norm_ffn1': norm_ffn1[l], 'w_ffn1_in': w_ffn1_in[l], 'w_ffn1_out': w_ffn1_out[l],
             'norm_mix': norm_mix[l], 'w_in': w_in[l], 'conv_w': conv_w[l],
             'gdn_a_log': gdn_a_log[l], 'gdn_dt_bias': gdn_dt_bias[l], 'gdn_norm': gdn_norm[l],
             'swa_sinks': swa_sinks[l], 'w_out': w_out[l], 'norm_ffn2': norm_ffn2[l],
             'w_ffn2_in': w_ffn2_in[l], 'w_ffn2_out': w_ffn2_out[l]}
        zero_conv = jnp.zeros((B, CONV_W - 1, CONV_DIM), xp.dtype)
        zero_S = jnp.zeros((B, GDN_HEADS, GDN_DK, GDN_DV), jnp.float32)
        xp, c1, s1, k1, v1 = decoder_layer(xp, pos_p, zero_conv, zero_S, None, None, p)
        xs, c2, s2, k2, v2 = decoder_layer(xs, pos_s, state_conv[l], state_gdn[l],
                                           cache_swa_k[l], cache_swa_v[l], p)
        conv_p.append(c1); conv_s.append(c2)
        gdn_p.append(s1); gdn_s.append(s2)
        k_p.append(k1); k_s.append(k2)
        v_p.append(v1); v_s.append(v2)
    y_prompt = rmsnorm(xp, norm_final)
    y_sample = rmsnorm(xs, norm_final)
    return (y_prompt, y_sample,
            jnp.stack(conv_p), jnp.stack(conv_s),
            jnp.stack(gdn_p), jnp.stack(gdn_s),
            jnp.stack(k_p), jnp.stack(k_s),
            jnp.stack(v_p), jnp.stack(v_s))
```

```python
import math
from contextlib import ExitStack
import numpy as np
import concourse.bass as bass
import concourse.mybir as mybir
from concourse.bass_utils import run_bass_kernel_spmd

F32 = mybir.dt.float32
BF16 = mybir.dt.bfloat16
AF = mybir.ActivationFunctionType
ALU = mybir.AluOpType
AX = mybir.AxisListType

D = 1024
DC = 8
DFF = 2816
FC = 22
DIN = 7696
EPS = 1e-6
NCORES = 8
SLOT = 1408


class Res:
    __slots__ = ("w", "r", "x")

    def __init__(self):
        self.w = None
        self.r = {}
        self.x = False


def RL(n):
    return [Res() for _ in range(n)]


class EngS:
    def __init__(self, name, sem):
        self.name = name
        self.sem = sem
        self.count = 0
        self.ops = []
        self.waited = {}


class Prog:
    def __init__(self, sems, dsems):
        self.E = {n: EngS(n, sems[n]) for n in ("pe", "act", "dve", "pool", "sp")}
        self.dsems = dsems
        self.dcount = [0] * len(dsems)
        self.dnext = 0

    def _deps(self, reads, writes, own=None):
        need = {}

        def add(k, v):
            if need.get(k, 0) < v:
                need[k] = v

        for r in reads:
            if r.w:
                add(*r.w)
            if r.x:
                for k, v in r.r.items():
                    if k != own:
                        add(k, v)
        for w in writes:
            if w.w:
                add(*w.w)
            for k, v in w.r.items():
                add(k, v)
        return need

    def _waits(self, e, need, skip_own=False):
        for k, v in need.items():
            if skip_own and k == e.name:
                continue
            if e.waited.get(k, 0) < v:
                e.waited[k] = v
                e.ops.append(("wait", k, v))

    def op(self, eng, fn, reads=(), writes=(), inc=True):
        e = self.E[eng]
        need = self._deps(reads, writes, own=e.name)
        self._waits(e, need, skip_own=(eng == "pe"))
        e.ops.append(("op", fn, inc))
        idx = e.count + 1
        if inc:
            e.count = idx
        for r in reads:
            if r.r.get(e.name, 0) < idx:
                r.r[e.name] = idx
        for w in writes:
            w.w = (e.name, idx)
            w.r = {}

    def dma(self, eng, fn, reads=(), writes=()):
        e = self.E[eng]
        need = self._deps(reads, writes)
        j = self.dnext
        self.dnext = (j + 1) % len(self.dsems)
        key = ("d", j)
        if self.dcount[j] > 0:
            v = 16 * self.dcount[j]
            if need.get(key, 0) < v:
                need[key] = v
        self._waits(e, need)
        self.dcount[j] += 1
        val = 16 * self.dcount[j]
        e.ops.append(("dma", fn, key))
        for r in reads:
            if r.r.get(key, 0) < val:
                r.r[key] = val
        for w in writes:
            w.w = (key, val)
            w.r = {}

    def semof(self, key):
        if isinstance(key, tuple):
            return self.dsems[key[1]]
        return self.E[key].sem

    def replay(self, eng, name):
        e = self.E[name]
        for item in e.ops:
            if item[0] == "wait":
                eng.wait_ge(self.semof(item[1]), item[2])
            elif item[0] == "op":
                ins = item[1](eng)
                if item[2]:
                    ins.then_inc(e.sem, 1)
            else:
                ins = item[1](eng)
                ins.then_inc(self.semof(item[2]), 16)

    def finish(self):
        e = self.E["sp"]
        for j, c in enumerate(self.dcount):
            if c > 0:
                e.ops.append(("wait", ("d", j), 16 * c))
        for n in ("pe", "act", "dve", "pool"):
            if self.E[n].count > 0:
                e.ops.append(("wait", n, self.E[n].count))


OQ, OK_, OV, OZ, OA, OB, OQS, OKS, OVS, OGA, OGB = 0, 1024, 2048, 3072, 4096, 4104, 4112, 5136, 5392, 5648, 6672
BIG = 30000.0
M_TRI, M_M1, M_M2, M_BLK = 0, 1, 2, 3
SM_LASTP, SM_LASTS, SM_ANYP, SM_ANYS, SM_SEQ, SM_MC = 0, 2, 18, 19, 20, 36


def build(LP, LPRE, NSQ=16, TS=8, stages=99, MIX=0xFFFF, GW=2):
    NS_TOK = NSQ * TS
    nc = bass.Bass("TRN2", target_bir_lowering=False)

    def din(name, shape, dt=F32):
        return nc.dram_tensor(name, list(shape), dt, kind="ExternalInput").ap()

    def dout(name, shape, dt=F32):
        return nc.dram_tensor(name, list(shape), dt, kind="ExternalOutput").ap()

    xp = din("xp", [LP, D])
    xpre = din("xpre", [LPRE, D])
    cosX = din("cosX", [128, LPRE])
    sinX = din("sinX", [128, LPRE])
    mprev0_d = din("mprev0", [128, 128])
    xs = din("xs", [NS_TOK, D])
    w1i = din("w1i", [D, 2 * DFF])
    w1o = din("w1o", [DFF, D])
    w2i = din("w2i", [D, 2 * DFF])
    w2o = din("w2o", [DFF, D])
    nrm = din("nrm", [128, 4, DC])
    ident_d = din("ident", [128, 128])
    win = din("win", [D, DIN])
    wks = din("wks", [D, 512])
    wout = din("wout", [D, D])
    cw_d = din("cw", [128, 24, 4])
    gv_d = din("gv", [1, 160])
    cosP = din("cosP", [128, LP])
    sinP = din("sinP", [128, LP])
    cosS = din("cosS", [128, NS_TOK])
    sinS = din("sinS", [128, NS_TOK])
    rmT_d = din("rmT", [128, 128])
    mk_d = din("mk", [128, 8, 128])
    mkb_d = din("mkb", [128, 3, 128])
    mkS_d = din("mkS", [128, 512])
    sm_d = din("sm", [128, 64])
    sconv = din("sconv", [NSQ * 3, 3072])
    sgdn = din("sgdn", [NSQ, 8, 128, 128])
    ck = din("ck", [NSQ, 128, 256])
    cv = din("cv", [NSQ, 128, 256])
    yp = dout("yp", [LP, D])
    ys = dout("ys", [NS_TOK, D])
    convp = dout("convp", [3, 3072])
    convs = dout("convs", [NSQ * 3, 3072])
    gdnp = dout("gdnp", [8, 128, 128])
    gdns = dout("gdns", [NSQ, 8, 128, 128])
    kp = dout("kp", [128, 256])
    ksd = dout("ksd", [NSQ, 128, 256])
    vp = dout("vp", [128, 256])
    vsd = dout("vsd", [NSQ, 128, 256])

    es = ExitStack()
    with es:
        es.enter_context(nc.allow_low_precision("bf16 matmul operands, fp32 accumulate"))
        sems = {n: es.enter_context(nc.semaphore("sem_" + n)) for n in ("pe", "act", "dve", "pool", "sp")}
        dsems = [es.enter_context(nc.semaphore("dsem%d" % i)) for i in range(24)]
        P = Prog(sems, dsems)

        def sb(name, shape, dt=F32):
            return es.enter_context(nc.sbuf_tensor(name, list(shape), dt))

        def A(fn, reads, writes):
            P.op("act", fn, reads, writes)

        def V(fn, reads, writes):
            P.op("dve", fn, reads, writes)

        def M(fn, reads, writes, inc=True):
            P.op("pe", fn, reads, writes, inc)

        def G_(fn, reads, writes):
            P.op("pool", fn, reads, writes)

        xin = sb("xin", [128, 4, D])
        xin_r = RL(4)
        xT = sb("xT", [128, DC, 512])
        xT_r = RL(DC)
        xn = sb("xn", [128, DC, 512], BF16)
        xn_r = RL(DC)
        hT = sb("hT", [128, FC, 512], BF16)
        hT_r = RL(FC)
        NSLOT = 2
        NSTG = 2
        wslots = [sb("wslot%d" % i, [128, SLOT], BF16) for i in range(NSLOT)]
        wslot_r = RL(NSLOT)
        wnext = [0]
        wstg = [sb("wstg%d" % i, [128, SLOT]) for i in range(NSTG)]
        wstg_r = RL(NSTG)
        snext = [0]
        sq = [sb("sq%d" % i, [128, 512], BF16) for i in range(2)]
        sq_r = RL(2)
        TF = [sb("tf%d" % i, [128, 520]) for i in range(6)]
        TF_r = RL(6)
        tfn = [0]

        def tf():
            i = tfn[0]
            tfn[0] = (i + 1) % 6
            return TF[i], TF_r[i]
        ident = sb("ident_sb", [128, 128])
        ident_bf = sb("ident_bf", [128, 128], BF16)
        ones_bf = sb("ones_bf", [128, 128], BF16)
        ones_f = sb("ones_f", [128, 128])
        nrm_sb = sb("nrm_sb", [128, 4, DC])
        qT = hT[:, 0:8, :]
        qT_r = hT_r[0:8]
        kT = hT[:, 8:16, :]
        kT_r = hT_r[8:16]
        ksT = hT[:, 16:22, :].rearrange("p j n -> p (j n)")[:, 0:2560].rearrange("p (k n) -> p k n", k=4)
        ksT_r = Res()
        for _j in range(16, 22):
            hT_r[_j] = ksT_r
        mixT = xn
        mixT_r = xn_r
        k_tok = sb("k_tok", [128, 4, 8, 128], BF16)
        k_tok_r = RL(8)
        v_tok = sb("v_tok", [128, 4, 8, 128], BF16)
        v_tok_r = RL(8)
        zg = xin[:, 0:2, :].rearrange("p b d -> p (b d)").bitcast(BF16).rearrange("p (h n) -> p h n", h=8)
        zg_r = [xin_r[h // 4] for h in range(8)]
        vcall = xin[:, 2:4, :].rearrange("p b d -> p (b d)").bitcast(BF16).rearrange("p (s k d) -> p s k d", s=16, k=4)
        vcall_r = [xin_r[2 + s // 8] for s in range(16)]
        gbT = sb("gbT", [128, 8, 512], BF16)
        gbT_r = RL(8)
        qsT = sb("qsT", [128, 8, 512], BF16)
        qsT_r = RL(8)
        vs_tok = sb("vs_tok", [128, 5, 4, 128], BF16)
        vs_tok_r = Res()
        kcar = sb("kcar", [128, 4, 128], BF16)
        kcar_r = Res()
        csn = sb("csn", [128, 2, 512])
        csn_r = Res()
        ks32 = sb("ks32", [128, 4, 128])
        ks32_r = Res()
        vs32 = sb("vs32", [128, 256])
        vs32_r = Res()
        cst = sb("cst", [128, 24, 48])
        cst_r = Res()
        cw = sb("cw_sb", [128, 24, 4])
        gv = sb("gv_sb", [128, 160])
        gnw = gv[:, 16:144]
        negA = sb("negA", [128, 8])
        esink = sb("esink", [1, 4, 512], BF16)
        esinkS = sb("esinkS", [1, 4, 512], BF16)
        mkS = sb("mkS_sb", [128, 512], BF16)
        esink_f = sb("esink_f", [1, 16])
        rmT = sb("rmT_sb", [128, 128])
        mk = sb("mk_sb", [128, 8, 128])
        mkb = sb("mkb_sb", [128, 4, 128], BF16)
        sm = sb("sm_sb", [128, 64])
        gch = [sb("gch%d" % i, [8, 128]) for i in range(4)]
        gch_r = RL(4)
        gates_f = sb("gates_f", [128, 9, 32])
        gts_r = Res()
        gcT = sb("gcT", [8, 512])
        gcT_r = Res()
        gl = sb("gl", [128, 4, 128])
        gl_r = Res()
        x2pool = xin[:, 2:4, :].rearrange("p b d -> p (b d)")
        x2off = [0]
        s2res = []
        x3off = [0]
        s3res = []

        def pool3(nel):
            k, o = divmod(x3off[0], 512)
            assert o + nel <= 512 and k < 4
            x3off[0] += nel
            return TF[k][:, o:o + nel]

        def wt(name, dt=F32, n=1, s2=True):
            tiles = [sb("%s%d" % (name, i), [128, 128], dt) for i in range(2 * n)]
            res = RL(2 * n)
            if s2:
                for i in range(n):
                    if dt == F32:
                        v = x2pool[:, x2off[0]:x2off[0] + 128]
                        x2off[0] += 128
                    else:
                        v = x2pool[:, x2off[0]:x2off[0] + 64].bitcast(BF16)
                        x2off[0] += 64
                    r_ = Res()
                    tiles.append(v)
                    res.append(r_)
                    s2res.append(r_)
                for i in range(n):
                    v = pool3(128) if dt == F32 else pool3(64).bitcast(BF16)
                    r_ = Res()
                    tiles.append(v)
                    res.append(r_)
                    s3res.append(r_)
            return tiles, res
        e1, e1_r = wt("e1", F32, 2)
        Dm, Dm_r = wt("Dm", F32, 1)
        DTm, DTm_r = wt("DTm", F32, 1)
        EG, EG_r = wt("EG", F32, 1)
        qg, qg_r = wt("qg", BF16, 1)
        Bm, Bm_r = wt("Bm", BF16, 1)
        Pm, Pm_r = wt("Pm", BF16, 3)
        PTm, PTm_r = wt("PTm", BF16, 3)
        TT, TT_r = wt("TT", BF16, 2)
        QKD, QKD_r = wt("QKD", BF16, 1)
        vb, vb_r = wt("vb", BF16, 1)
        kbg, kbg_r = wt("kbg", BF16, 1)
        kgt, kgt_r = wt("kgt", BF16, 1)
        ub, ub_r = wt("ub", F32, 1)
        wTb, wTb_r = wt("wTb", BF16, 1)
        vnew, vnew_r = wt("vnew", BF16, 1)
        vnf, vnf_r = wt("vnf", F32, 1, s2=False)
        ogn, ogn_r = wt("ogn", BF16, 1)
        oq, oq_r = wt("oq", F32, 1, s2=False)
        otmp, otmp_r = wt("otmp", F32, 1)
        assert x2off[0] <= 2048, x2off[0]
        dummy = sb("dummy_sb", [128, 2])
        dummy_r = Res()
        ctr = {}

        def nxt(name, n):
            i = ctr.get(name, 0)
            ctr[name] = (i + 1) % n
            return i
        small = sb("small", [128, 16])
        small_r = RL(4)
        Sf = sb("Sf", [128, 16, 128])
        Sf_r = RL(16)
        Sbf = sb("Sbf", [128, 16, 128], BF16)
        Sbf_r = RL(16)
        pT = [sb("pT%d" % i, [128, 512], BF16) for i in range(2)]
        pT_r = RL(2)
        kcs = sb("kcs", [128, 4, 2, 64])
        kcs_r = Res()
        kcT = [sb("kcT%d" % i, [128, 4, 128], BF16) for i in range(2)]
        kcT_r = RL(2)
        pTc = Sbf[:, :, :].rearrange("p (k a) n -> p k (a n)", k=4)
        pTc_r = [Sbf_r[4 * k:4 * k + 4] for k in range(4)]
        knew = sb("knew", [128, 4, 64])
        knew_r = Res()
        NB_ = 8
        banks = [es.enter_context(nc.psum_tensor("bank%d" % i, [128, 512], F32)) for i in range(NB_)]
        bank_r = RL(NB_)
        for _r in bank_r:
            _r.x = True
        bnext = [0]
        banks_bf = [banks[i][:, :].bitcast(BF16) for i in range(NB_)]
        pbf = [banks_bf[6], banks_bf[7]]
        pbf_r = [bank_r[6], bank_r[7]]
        pbn = [0]

        def psum():
            i = bnext[0]
            bnext[0] = (i + 1) % NB_
            return banks[i], bank_r[i]

        def psumbf():
            i = pbn[0]
            pbn[0] = 1 - i
            return pbf[i], pbf_r[i]

        ident_r, nrm_r, ones_r, c_r = Res(), Res(), Res(), Res()
        P.dma("sp", lambda e: e.dma_start(out=ident[:, :], in_=ident_d[:, :]), writes=[ident_r])
        P.dma("sp", lambda e: e.dma_start(out=nrm_sb[:, :, :], in_=nrm[:, :, :]), writes=[nrm_r])
        V(lambda e: e.memset(ones_bf[:, :], 1.0), [], [ones_r])
        cl = [Res() for _ in range(8)]
        P.dma("sp", lambda e: e.dma_start(out=cw[:, :, :], in_=cw_d[:, :, :]), writes=[cl[0]])
        P.dma("sp", lambda e: e.dma_start(out=gv[:, :], in_=gv_d[0:1, :].broadcast_to([128, 160])), writes=[cl[1]])
        P.dma("sp", lambda e: e.dma_start(out=rmT[:, :], in_=rmT_d[:, :]), writes=[cl[2]])
        P.dma("sp", lambda e: e.dma_start(out=mk[:, :, :], in_=mk_d[:, :, :]), writes=[cl[3]])
        P.dma("sp", lambda e: e.dma_start(out=sm[:, :], in_=sm_d[:, :]), writes=[cl[4]])
        _t, _tr = tf()
        P.dma("sp", lambda e: e.dma_start(out=_t[:, 0:384], in_=mkb_d[:, :, :].rearrange("p a b -> p (a b)")), writes=[_tr])
        V(lambda e: e.tensor_copy(mkb[:, 0:3, :].rearrange("p a b -> p (a b)"), _t[:, 0:384]), [_tr], [cl[5]])
        _t3, _t3r = tf()
        P.dma("sp", lambda e: e.dma_start(out=_t3[:, 0:128], in_=mprev0_d[:, :]), writes=[_t3r])
        V(lambda e: e.tensor_copy(mkb[:, 3, :], _t3[:, 0:128]), [_t3r], [cl[7]])
        V(lambda e: e.memset(kcar[:, :, :], 0.0), [], [kcar_r])
        V(lambda e: e.memset(vs_tok[:, 0, :, :], 0.0), [], [vs_tok_r])
        V(lambda e: e.tensor_copy(ident_bf[:, :], ident[:, :]), [ident_r], [cl[6]])
        V(lambda e: e.memset(ones_f[:, :], 1.0), [], [cl[6]])
        A(lambda e: e.activation(negA[:, :], gv[:, 0:8], AF.Exp), [cl[1]], [c_r])
        V(lambda e: e.tensor_scalar(negA[:, :], negA[:, :], -1.0, None, ALU.mult), [c_r], [c_r])
        A(lambda e: e.activation(esink_f[:, :], gv[0:1, 144:160], AF.Exp), [cl[1]], [c_r])
        for par in range(2):
            src = esink_f[0:1, :].rearrange("o (k i par) -> o par k i", k=4, i=2)[:, par]
            dstv = esink[0:1, :, :].rearrange("o k (par i q) -> o par k i q", par=2, i=2)[:, par]
            V(lambda e, src=src, dstv=dstv: e.tensor_copy(dstv, src.unsqueeze(3).broadcast_to([1, 4, 2, 128])), [c_r], [c_r])
            for kvh in range(4):
                src2 = esink_f[0:1, 4 * kvh:4 * kvh + 4].rearrange("o (i par) -> o par i", i=2)[:, par]
                dst2 = esinkS[0:1, kvh, :].rearrange("o (par s i t) -> o par s i t", par=2, s=NSQ, i=2)[:, par]
                V(lambda e, src2=src2, dst2=dst2: e.tensor_copy(dst2, src2.unsqueeze(1).unsqueeze(3).broadcast_to([1, NSQ, 2, TS])), [c_r], [c_r])
        _t2, _t2r = tf()
        P.dma("sp", lambda e: e.dma_start(out=_t2[:, 0:512], in_=mkS_d[:, :]), writes=[_t2r])
        V(lambda e: e.tensor_copy(mkS[:, :], _t2[:, 0:512]), [_t2r], [c_r])
        V(lambda e: e.memset(cst[:, :, :], 0.0), [], [cst_r])
        for _i in range(2):
            V(lambda e, _i=_i: e.memset(vnew[_i][:, :], 0.0), [], [vnew_r[_i]])
        V(lambda e: e.memset(Sf[:, 0:8, :], 0.0), [], Sf_r[0:8])
        V(lambda e: e.memset(Sbf[:, 0:8, :], 0.0), [], Sbf_r[0:8])
        CALL = [ident_r, ones_r, c_r] + cl

        wscr = nc.dram_tensor("wscr", [200, 128, SLOT], BF16).ap()
        wreg = {}
        half_r = [[Res(), Res()] for _k in range(NSTG)]
        _hv = [wstg[_k][:, :].bitcast(BF16) for _k in range(NSTG)]
        fslots = [(wslots[0], wslot_r[0], None),
                  (_hv[0][:, 0:SLOT], half_r[0][0], wstg_r[0]),
                  (_hv[1][:, 0:SLOT], half_r[1][0], wstg_r[1]),
                  (wslots[1], wslot_r[1], None),
                  (_hv[0][:, SLOT:2 * SLOT], half_r[0][1], wstg_r[0]),
                  (_hv[1][:, SLOT:2 * SLOT], half_r[1][1], wstg_r[1])]
        fnext = [0]
        cnext = [0]
        stg_base = [(wstg[_k], wstg_r[_k], [half_r[_k][0], half_r[_k][1]]) for _k in range(NSTG)]
        ext_r = RL(4)
        stg_ext = [
            (gbT[:, :, :].rearrange("p a b -> p (a b)").bitcast(F32), ext_r[0], []),
            (qsT[:, :, :].rearrange("p a b -> p (a b)").bitcast(F32), ext_r[1], []),
            (k_tok[:, :, :, :].rearrange("p a b c -> p (a b c)").bitcast(F32), ext_r[2], []),
            (v_tok[:, :, :, :].rearrange("p a b c -> p (a b c)").bitcast(F32), ext_r[3], []),
        ]
        ext_alias = gbT_r + qsT_r + k_tok_r + v_tok_r
        stg_cur = [stg_base]

        def _wload(key, src, n1, ncol):
            if key in wreg:
                tid, tres = wreg[key]
                j = fnext[0]
                fnext[0] = (j + 1) % len(fslots)
                buf, bres, extra = fslots[j]
                wr = [bres]
                if extra is not None:
                    wr.append(extra)
                P.dma("sp", lambda e: e.dma_start(out=buf[:, 0:n1 * ncol], in_=wscr[tid, :, 0:n1 * ncol]), reads=[tres], writes=wr)
                return buf[:, 0:n1 * ncol].rearrange("p (c n) -> p c n", c=n1), bres
            lst = stg_cur[0]
            k = snext[0] % len(lst)
            snext[0] = (k + 1) % len(lst)
            sbuf_, sres_, sextra_ = lst[k]
            i = wnext[0]
            wnext[0] = (i + 1) % NSLOT
            sview = sbuf_[:, 0:n1 * ncol].rearrange("p (c n) -> p c n", c=n1)
            view = wslots[i][:, 0:n1 * ncol].rearrange("p (c n) -> p c n", c=n1)
            P.dma("sp", lambda e: e.dma_start(out=sview, in_=src), writes=[sres_] + sextra_)
            ce = ("pool", "act", "dve")[cnext[0] % 3]
            cnext[0] += 1
            if ce == "act":
                P.op("act", lambda e: e.copy(wslots[i][:, 0:n1 * ncol], sbuf_[:, 0:n1 * ncol]),
                     reads=[sres_], writes=[wslot_r[i]])
            else:
                P.op(ce, lambda e: e.tensor_copy(wslots[i][:, 0:n1 * ncol], sbuf_[:, 0:n1 * ncol]),
                     reads=[sres_], writes=[wslot_r[i]])
            tid = len(wreg)
            tres = Res()
            wreg[key] = (tid, tres)
            P.dma("sp", lambda e: e.dma_start(out=wscr[tid, :, 0:n1 * ncol], in_=wslots[i][:, 0:n1 * ncol]),
                  reads=[wslot_r[i]], writes=[tres])
            return view, wslot_r[i]

        def wload_in(W, col0, ncol):
            src = W.rearrange("(c p) n -> p c n", p=128)[:, :, col0:col0 + ncol]
            return _wload((id(W), "i", col0, ncol), src, DC, ncol)

        def wload_out(W, nj, col0, ncol, r0=0):
            src = W[r0:r0 + nj * 128, :].rearrange("(j p) n -> p j n", p=128)[:, :, col0:col0 + ncol]
            return _wload((id(W), "o", col0, ncol, r0, nj), src, nj, ncol)

        def load_x(xd, tok0, ntok):
            nb = ntok // 128
            src = xd[tok0:tok0 + ntok, :].rearrange("(b p) d -> p b d", p=128)
            P.dma("sp", lambda e: e.dma_start(out=xin[:, 0:nb, :], in_=src), writes=xin_r[0:nb])
            for c in range(DC):
                ps, pr = psum()
                for b in range(nb):
                    P.op("pe", lambda e, b=b, c=c, ps=ps: e.transpose(
                        ps[:, b * 128:(b + 1) * 128], xin[:, b, c * 128:(c + 1) * 128], ident[:, :]),
                        reads=[xin_r[b], ident_r], writes=[pr], inc=(b == nb - 1))
                eng = "act" if c % 2 == 0 else "dve"
                if eng == "act":
                    P.op("act", lambda e, c=c, ps=ps: e.copy(xT[:, c, 0:ntok], ps[:, 0:ntok]),
                         reads=[pr], writes=[xT_r[c]])
                else:
                    P.op("dve", lambda e, c=c, ps=ps: e.tensor_copy(xT[:, c, 0:ntok], ps[:, 0:ntok]),
                         reads=[pr], writes=[xT_r[c]])

        def rmsnorm(which, out, out_r, ntok):
            ps, pr = psum()
            for c in range(DC):
                s = c % 2
                P.op("act", lambda e, c=c, s=s: e.activation(sq[s][:, 0:ntok], xT[:, c, 0:ntok], AF.Square),
                     reads=[xT_r[c]], writes=[sq_r[s]])
                P.op("pe", lambda e, c=c, s=s, ps=ps: e.matmul(ps[:, 0:ntok], ones_bf[:, :], sq[s][:, 0:ntok],
                                                               start=(c == 0), stop=(c == DC - 1)),
                     reads=[sq_r[s], ones_r], writes=[pr], inc=True)
            sd, sd_r = tf()
            rstd, rstd_r = tf()
            P.op("act", lambda e, ps=ps: e.activation(sd[:, 0:ntok], ps[:, 0:ntok], AF.Sqrt, bias=EPS, scale=1.0 / D),
                 reads=[pr], writes=[sd_r])
            P.op("dve", lambda e: e.reciprocal(rstd[:, 0:ntok], sd[:, 0:ntok]), reads=[sd_r], writes=[rstd_r])
            for c in range(DC):
                P.op("dve", lambda e, c=c: e.scalar_tensor_tensor(
                    out[:, c, 0:ntok], xT[:, c, 0:ntok], nrm_sb[:, which, c:c + 1], rstd[:, 0:ntok],
                    ALU.mult, ALU.mult),
                    reads=[xT_r[c], rstd_r, nrm_r], writes=[out_r[c]])

        def ffn(Wi, Wo, ntok, ext=False):
            if ext:
                P.op("dve", lambda e: e.memset(dummy[:, 0:1], 0.0), [], [dummy_r] + ext_r + ext_alias)
                stg_cur[0] = stg_base + stg_ext
            _ffn(Wi, Wo, ntok)
            if ext:
                stg_cur[0] = stg_base
                snext[0] = 0
                P.op("dve", lambda e: e.memset(dummy[:, 1:2], 0.0), [], [dummy_r] + ext_r + ext_alias)

        def _ffn(Wi, Wo, ntok):
            for j in range(FC):
                wg, wgr = wload_in(Wi, 128 * j, 128)
                wu, wur = wload_in(Wi, DFF + 128 * j, 128)
                pg, pgr = psum()
                for c in range(DC):
                    P.op("pe", lambda e, c=c, pg=pg, wg=wg: e.matmul(
                        pg[:, 0:ntok], wg[:, c, :], xn[:, c, 0:ntok], start=(c == 0), stop=(c == DC - 1)),
                        reads=[wgr, xn_r[c]], writes=[pgr], inc=(c == DC - 1))
                pu, pur = psum()
                for c in range(DC):
                    P.op("pe", lambda e, c=c, pu=pu, wu=wu: e.matmul(
                        pu[:, 0:ntok], wu[:, c, :], xn[:, c, 0:ntok], start=(c == 0), stop=(c == DC - 1)),
                        reads=[wur, xn_r[c]], writes=[pur], inc=(c == DC - 1))
                sgb, sgr = tf()
                P.op("act", lambda e, sgb=sgb, pg=pg: e.activation(sgb[:, 0:ntok], pg[:, 0:ntok], AF.Silu),
                     reads=[pgr], writes=[sgr])
                P.op("dve", lambda e, sgb=sgb, pu=pu, j=j: e.tensor_tensor(
                    hT[:, j, 0:ntok], sgb[:, 0:ntok], pu[:, 0:ntok], ALU.mult),
                    reads=[sgr, pur], writes=[hT_r[j]])
            for c in range(DC):
                po, por = psum()
                for half in range(2):
                    wo, wor = wload_out(Wo, 11, 128 * c, 128, r0=half * 11 * 128)
                    for jj in range(11):
                        j = half * 11 + jj
                        P.op("pe", lambda e, j=j, jj=jj, po=po, wo=wo: e.matmul(
                            po[:, 0:ntok], wo[:, jj, :], hT[:, j, 0:ntok], start=(j == 0), stop=(j == FC - 1)),
                            reads=[wor, hT_r[j]], writes=[por], inc=(j == FC - 1 or jj == 10))
                P.op("dve", lambda e, c=c, po=po: e.scalar_tensor_tensor(
                    xT[:, c, 0:ntok], po[:, 0:ntok], 0.5, xT[:, c, 0:ntok], ALU.mult, ALU.add),
                    reads=[por, xT_r[c]], writes=[xT_r[c]])

        def store_y(yd, tok0, ntok):
            nb = ntok // 128
            for b in range(nb):
                for cg in range(2):
                    ps, pr = psum()
                    for ci in range(4):
                        c = cg * 4 + ci
                        P.op("pe", lambda e, b=b, c=c, ci=ci, ps=ps: e.transpose(
                            ps[:, ci * 128:(ci + 1) * 128], xT[:, c, b * 128:(b + 1) * 128], ident[:, :]),
                            reads=[xT_r[c], ident_r], writes=[pr], inc=(ci == 3))
                    if cg == 0:
                        P.op("act", lambda e, b=b, cg=cg, ps=ps: e.copy(xin[:, b, cg * 512:(cg + 1) * 512], ps[:, :]),
                             reads=[pr], writes=[xin_r[b]])
                    else:
                        P.op("dve", lambda e, b=b, cg=cg, ps=ps: e.tensor_copy(xin[:, b, cg * 512:(cg + 1) * 512], ps[:, :]),
                             reads=[pr], writes=[xin_r[b]])
            dst = yd[tok0:tok0 + ntok, :].rearrange("(b p) d -> p b d", p=128)
            P.dma("sp", lambda e: e.dma_start(out=dst, in_=xin[:, 0:nb, :]), reads=xin_r[0:nb])


        class UC:
            def __init__(self, u):
                self.u = u
                self.c = {}
                self.pn = 0

            def n(self, name, n):
                i = self.c.get(name, 0)
                self.c[name] = (i + 1) % n
                return self.u * n + i

            def ps(self):
                i = 2 * self.u + self.pn
                self.pn = 1 - self.pn
                self.last = i
                return banks[i], bank_r[i]

            def pb(self):
                i = 2 * self.u + self.pn
                self.pn = 1 - self.pn
                self.last = i
                return banks_bf[i], bank_r[i]

            def pb_of(self, i):
                return banks_bf[i], bank_r[i]

            def tf(self, k):
                return TF[2 * self.u + k], TF_r[2 * self.u + k]

            def sq(self):
                return sqx[self.u], sqx_r[self.u]
        UCS = [UC(0), UC(1), UC(2), UC(3)]
        sqx = [sq[0], sq[1], pT[0], pT[1]]
        sqx_r = [sq_r[0], sq_r[1], pT_r[0], pT_r[1]]
        pTx = [pT[0], pT[1], sq[0], sq[1]]
        pTx_r = [pT_r[0], pT_r[1], sq_r[0], sq_r[1]]

        def run_interleaved(gens, W):
            active = []
            it = iter(gens)
            slots = list(range(W))
            pending = True
            while True:
                while len(active) < W and pending:
                    try:
                        mk_ = next(it)
                    except StopIteration:
                        pending = False
                        break
                    u = slots.pop(0)
                    active.append((u, mk_(UCS[u])))
                if not active:
                    break
                for ent in list(active):
                    u, g = ent
                    try:
                        next(g)
                    except StopIteration:
                        active.remove(ent)
                        slots.append(u)

        def chunk_gen(U, W, off, ch, ntok, fn):
            w, wr = wload_in(W, off + 128 * ch, 128)
            ps, pr = U.ps()
            for c in range(DC):
                M(lambda e, c=c: e.matmul(ps[:, 0:ntok], w[:, c, :], xn[:, c, 0:ntok],
                                          start=(c == 0), stop=(c == DC - 1)), [wr, xn_r[c]], [pr], inc=(c == DC - 1))
            yield
            yield from fn(U, ch, ps, pr)

        def l2norm(U, y, yr, tmp, tmpr, out, outr, ntok, scale):
            sqb, sqr = U.sq()
            A(lambda e: e.activation(sqb[:, 0:ntok], y[:, 0:ntok], AF.Square), [yr], [sqr])
            ps, pr = U.ps()
            M(lambda e: e.matmul(ps[:, 0:ntok], ones_bf[:, :], sqb[:, 0:ntok], start=True, stop=True),
              [sqr, ones_r], [pr])
            yield
            A(lambda e: e.activation(tmp[:, 0:ntok], ps[:, 0:ntok], AF.Sqrt, bias=EPS, scale=1.0), [pr], [tmpr])
            yield
            V(lambda e: e.reciprocal(tmp[:, 0:ntok], tmp[:, 0:ntok]), [tmpr], [tmpr])
            V(lambda e: e.scalar_tensor_tensor(out, y[:, 0:ntok], scale, tmp[:, 0:ntok], ALU.mult, ALU.mult),
              [yr, tmpr], [outr])
            yield

        def to_tok(U, src, srcr, dst, dstr, nb):
            pb, pbr = U.pb()
            for tb in range(nb):
                M(lambda e, tb=tb: e.transpose(pb[:, tb * 128:(tb + 1) * 128], src[:, tb * 128:(tb + 1) * 128], ident_bf[:, :]),
                  [srcr] + CALL, [pbr], inc=(tb == nb - 1))
            yield
            A(lambda e: e.copy(dst, pb[:, 0:nb * 128].rearrange("p (b d) -> p b d", b=nb)), [pbr], [dstr])

        def inproj(mode, tok0, ntok, lastx=False):
            nb = ntok // 128
            isP = mode in ("P", "X")
            isX = mode == "X"
            NSEQ, T = (1, ntok) if isP else (NSQ, TS)
            cs_, sn_ = (cosP, sinP) if mode == "P" else ((cosX, sinX) if isX else (cosS, sinS))
            c0 = tok0 if isP else 0
            P.dma("sp", lambda e: e.dma_start(out=csn[:, 0, 0:ntok], in_=cs_[:, c0:c0 + ntok]), writes=[csn_r])
            P.dma("sp", lambda e: e.dma_start(out=csn[:, 1, 0:ntok], in_=sn_[:, c0:c0 + ntok]), writes=[csn_r])
            wab, wabr = wload_in(win, OA, 16)
            psab, psabr = psum()
            for tb in range(nb):
                for c in range(DC):
                    M(lambda e, tb=tb, c=c: e.matmul(psab[:, tb * 16:(tb + 1) * 16], xn[:, c, tb * 128:(tb + 1) * 128],
                                                     wab[:, c, 0:16], start=(c == 0), stop=(c == DC - 1)),
                      [wabr, xn_r[c]], [psabr], inc=(c == DC - 1))
            gates("P" if isX else mode, nb, psab, psabr)
            gens = []

            def add(W, off, nch, fn):
                for ch in range(nch):
                    gens.append(lambda U, ch=ch: chunk_gen(U, W, off, ch, ntok, fn))

            def conv_ep(grp):
                def f(U, ch, ps, pr):
                    cidx = grp * 8 + ch
                    rawb, rawr = U.tf(0)
                    accb, accr = U.tf(1)
                    rb = rawb[:, 0:NSEQ * (3 + T)].rearrange("p (s t) -> p s t", s=NSEQ)
                    A(lambda e: e.copy(rb[:, :, 3:3 + T], ps[:, 0:ntok].rearrange("p (s t) -> p s t", s=NSEQ)),
                      [pr], [rawr])
                    cv_ = cst[:, cidx, 0:NSEQ * 3].rearrange("p (s j) -> p s j", s=NSEQ)
                    V(lambda e: e.tensor_copy(rb[:, :, 0:3], cv_), [cst_r], [rawr])
                    yield
                    V(lambda e: e.tensor_copy(cv_, rb[:, :, T:T + 3]), [rawr], [cst_r])
                    if isX and grp == 0:
                        return
                    av = accb[:, 0:ntok].rearrange("p (s t) -> p s t", s=NSEQ)
                    A(lambda e: e.mul(av, rb[:, :, 0:T], cw[:, cidx, 0:1]), [rawr] + CALL, [accr])
                    yield
                    for j in range(1, 4):
                        V(lambda e, j=j: e.scalar_tensor_tensor(av, rb[:, :, j:j + T], cw[:, cidx, j:j + 1], av,
                                                                ALU.mult, ALU.add), [rawr, accr], [accr])
                    yield
                    if grp == 2:
                        vt_, vtr_ = U.sq()
                        A(lambda e: e.activation(vt_[:, 0:ntok], accb[:, 0:ntok], AF.Silu), [accr], [vtr_])
                        yield
                        yield from to_tok(U, vt_, vtr_, v_tok[:, 0:nb, ch, :], v_tok_r[ch], nb)
                    else:
                        A(lambda e: e.activation(rawb[:, 0:ntok], accb[:, 0:ntok], AF.Silu), [accr], [rawr])
                        yield
                        if grp == 0:
                            yield from l2norm(U, rawb, rawr, accb, accr, qT[:, ch, 0:ntok], qT_r[ch], ntok, 128.0 ** -0.5)
                        else:
                            yield from l2norm(U, rawb, rawr, accb, accr, kT[:, ch, 0:ntok], kT_r[ch], ntok, 1.0)
                            yield from to_tok(U, kT[:, ch, :], kT_r[ch], k_tok[:, 0:nb, ch, :], k_tok_r[ch], nb)
                return f
            if (not isX) or lastx:
                add(win, OQ, 8, conv_ep(0))
            add(win, OK_, 8, conv_ep(1))
            add(win, OV, 8, conv_ep(2))

            def z_ep(U, ch, ps, pr):
                A(lambda e: e.activation(zg[:, ch, 0:ntok], ps[:, 0:ntok], AF.Silu), [pr], [zg_r[ch]])
                yield
            if not isX:
                add(win, OZ, 8, z_ep)

            def ga_ep(U, ch, ps, pr):
                t1b, t1_r = U.tf(0)
                A(lambda e: e.activation(t1b[:, 0:ntok], ps[:, 0:ntok], AF.Sigmoid), [pr], [t1_r])
                yield
                V(lambda e: e.tensor_tensor(zg[:, ch, 0:ntok], zg[:, ch, 0:ntok], t1b[:, 0:ntok], ALU.mult),
                  [t1_r, zg_r[ch]], [zg_r[ch]])
            if not isX:
                add(win, OGA, 8, ga_ep)

            def gb_ep(U, ch, ps, pr):
                A(lambda e: e.activation(gbT[:, ch, 0:ntok], ps[:, 0:ntok], AF.Sigmoid), [pr], [gbT_r[ch]])
                yield
            if not isX:
                add(win, OGB, 8, gb_ep)

            def rope_ep(kind):
                def f(U, ch, ps, pr):
                    yb, ybr = U.tf(0)
                    t2b, t2_r = U.tf(1)
                    t1b, t1_r = yb, ybr
                    A(lambda e: e.copy(yb[:, 0:ntok], ps[:, 0:ntok]), [pr], [ybr])
                    yield
                    p2, p2r = U.ps()
                    M(lambda e: e.matmul(p2[:, 0:ntok], rmT[:, :], yb[:, 0:ntok], start=True, stop=True),
                      [ybr] + CALL, [p2r])
                    V(lambda e: e.tensor_tensor(t2b[:, 0:ntok], yb[:, 0:ntok], csn[:, 0, 0:ntok], ALU.mult),
                      [ybr, csn_r], [t2_r])
                    yield
                    V(lambda e: e.tensor_tensor(t1b[:, 0:ntok], p2[:, 0:ntok], csn[:, 1, 0:ntok], ALU.mult),
                      [p2r, csn_r], [t1_r])
                    if kind == "q":
                        V(lambda e: e.tensor_tensor(qsT[:, ch, 0:ntok], t1b[:, 0:ntok], t2b[:, 0:ntok], ALU.add),
                          [t1_r, t2_r], [qsT_r[ch]])
                    else:
                        V(lambda e: e.tensor_tensor(ksT[:, ch, 128:128 + ntok], t1b[:, 0:ntok], t2b[:, 0:ntok], ALU.add),
                          [t1_r, t2_r], [ksT_r])
                        V(lambda e: e.tensor_tensor(ks32[:, ch, :], t1b[:, ntok - 128:ntok], t2b[:, ntok - 128:ntok], ALU.add),
                          [t1_r, t2_r], [ks32_r])
                    yield
                return f
            if not isX:
                add(win, OQS, 8, rope_ep("q"))
            if (not isX) or lastx:
                add(wks, 0, 4, rope_ep("k"))

            def vs_ep(U, ch, ps, pr):
                yb, ybr = U.tf(0)
                A(lambda e: e.copy(yb[:, 0:ntok], ps[:, 0:ntok]), [pr], [ybr])
                yield
                p2, p2r = U.ps()
                for tb in range(nb):
                    M(lambda e, tb=tb: e.transpose(p2[:, tb * 128:(tb + 1) * 128], yb[:, tb * 128:(tb + 1) * 128], ident[:, :]),
                      [ybr, ident_r], [p2r], inc=(tb == nb - 1))
                yield
                src = p2[:, 0:nb * 128].rearrange("p (b h d) -> p b h d", b=nb, h=2)
                for dup in range(2):
                    V(lambda e, dup=dup: e.tensor_copy(vs_tok[:, 1:1 + nb, 2 * ch:2 * ch + 2, dup * 64:(dup + 1) * 64], src),
                      [p2r], [vs_tok_r])
                V(lambda e: e.tensor_copy(vs32[:, ch * 128:(ch + 1) * 128], p2[:, (nb - 1) * 128:nb * 128]), [p2r], [vs32_r])
            if (not isX) or lastx:
                add(win, OVS, 2, vs_ep)
            run_interleaved(gens, 3)

        def gates(mode, nb, psab, psabr):
            isS = 0 if mode == "P" else 1
            NBK = 2 if mode == "P" else 16
            G = gates_f
            ab = psab[:, 0:nb * 16].rearrange("p (b x) -> p b x", b=nb)

            def g3(i):
                return G[:, i, 0:nb * 8].rearrange("p (b h) -> p b h", b=nb)
            V(lambda e: e.tensor_tensor(g3(0), ab[:, :, 0:8], gv[:, 8:16].unsqueeze(1).broadcast_to([128, nb, 8]), ALU.add),
              [psabr] + CALL, [gts_r])
            A(lambda e: e.activation(g3(0), g3(0), AF.Exp), [gts_r], [gts_r])
            A(lambda e: e.activation(g3(0), g3(0), AF.Ln, bias=1.0), [gts_r], [gts_r])
            V(lambda e: e.tensor_tensor(g3(1), g3(0), negA[:, :].unsqueeze(1).broadcast_to([128, nb, 8]), ALU.mult),
              [gts_r] + CALL, [gts_r])
            A(lambda e: e.activation(g3(2), ab[:, :, 8:16], AF.Sigmoid), [psabr], [gts_r])
            V(lambda e: e.tensor_scalar(g3(3), g3(2), -1.0, None, ALU.mult), [gts_r], [gts_r])
            pg, pgr = psum()
            pgT, pgTr = psum()
            for tb in range(nb):
                M(lambda e, tb=tb: e.matmul(pg[:, tb * 8:(tb + 1) * 8], mk[:, M_TRI + 4 * isS, :], G[:, 1, tb * 8:(tb + 1) * 8],
                                            start=True, stop=True), [gts_r] + CALL, [pgr])
                M(lambda e, tb=tb: e.matmul(pgT[0:8, tb * 128:(tb + 1) * 128], G[:, 1, tb * 8:(tb + 1) * 8], mk[:, M_TRI + 4 * isS, :],
                                            start=True, stop=True), [gts_r] + CALL, [pgTr])
            V(lambda e: e.tensor_copy(G[:, 4, 0:nb * 8], pg[:, 0:nb * 8]), [pgr], [gts_r])
            A(lambda e: e.copy(gcT[:, 0:nb * 128], pgT[0:8, 0:nb * 128]), [pgTr], [gcT_r])
            A(lambda e: e.activation(G[:, 5, 0:nb * 8], G[:, 4, 0:nb * 8], AF.Exp), [gts_r], [gts_r])
            V(lambda e: e.tensor_tensor(G[:, 6, 0:nb * 8], G[:, 2, 0:nb * 8], G[:, 5, 0:nb * 8], ALU.mult), [gts_r], [gts_r])
            lm = sm[:, SM_LASTP:SM_LASTP + 2] if not isS else sm[:, SM_LASTS:SM_LASTS + 16]
            pl, plr = psum()
            rlt, rl_r = tf()
            rl = rlt[:, 0:512].rearrange("p (b x) -> p b x", b=4)
            for tb in range(nb):
                V(lambda e, tb=tb: e.tensor_tensor(
                    rl[:, tb, 0:NBK * 8].rearrange("p (k h) -> p k h", k=NBK),
                    G[:, 4, tb * 8:(tb + 1) * 8].unsqueeze(1).broadcast_to([128, NBK, 8]),
                    lm.unsqueeze(2).broadcast_to([128, NBK, 8]), ALU.mult), [gts_r] + CALL, [rl_r])
                M(lambda e, tb=tb: e.matmul(pl[:, tb * 128:tb * 128 + NBK * 8], ones_f[:, :], rl[:, tb, 0:NBK * 8],
                                            start=True, stop=True), [rl_r] + CALL, [plr])
            for tb in range(nb):
                A(lambda e, tb=tb: e.activation(gl[:, tb, 0:NBK * 8], pl[:, tb * 128:tb * 128 + NBK * 8], AF.Exp), [plr], [gl_r])
            anyc = sm[:, SM_ANYP + isS:SM_ANYP + isS + 1]
            V(lambda e: e.tensor_scalar(G[:, 8, 0:nb * 8], G[:, 4, 0:nb * 8], anyc, None, ALU.mult), [gts_r] + CALL, [gts_r])
            po, por = psum()
            for tb in range(nb):
                M(lambda e, tb=tb: e.matmul(po[:, tb * 8:(tb + 1) * 8], mk[:, M_BLK + 4 * isS, :], G[:, 8, tb * 8:(tb + 1) * 8],
                                            start=True, stop=True), [gts_r] + CALL, [por])
            V(lambda e: e.tensor_tensor(G[:, 7, 0:nb * 8], po[:, 0:nb * 8], G[:, 4, 0:nb * 8], ALU.subtract), [por, gts_r], [gts_r])
            A(lambda e: e.activation(G[:, 7, 0:nb * 8], G[:, 7, 0:nb * 8], AF.Exp), [gts_r], [gts_r])

        def gdn_prep(U, mode, tb, h, d, so=False):
            isS = 0 if mode == "P" else 1
            NL = 6 if mode == "P" else 3
            G = gates_f
            u = U.u
            cols = slice(tb * 128, (tb + 1) * 128)
            gcol = G[:, 4, tb * 8 + h:tb * 8 + h + 1]
            i = u
            d["i"] = i
            psG, psGr = U.ps()
            M(lambda e: e.matmul(psG[:, 0:128], kT[:, h, cols], kT[:, h, cols], start=True, stop=True), [kT_r[h]], [psGr])
            if not so:
                M(lambda e: e.matmul(psG[:, 128:256], kT[:, h, cols], qT[:, h, cols], start=True, stop=True), [kT_r[h], qT_r[h]], [psGr])
            V(lambda e: e.tensor_scalar(gch[u][:, :], gcT[:, cols], ident[0:8, h:h + 1], None, ALU.mult), [gcT_r, ident_r], [gch_r[u]])
            psR, psRr = U.ps()
            M(lambda e: e.matmul(psR[:, 0:128], ones_f[0:8, :], gch[u][:, :], start=True, stop=True), [gch_r[u]] + CALL, [psRr])
            V(lambda e: e.tensor_scalar(vb[i][:, :], v_tok[:, tb, h, :], G[:, 2, tb * 8 + h:tb * 8 + h + 1], None, ALU.mult),
               [v_tok_r[h], gts_r], [vb_r[i]])
            V(lambda e: e.tensor_scalar(kbg[i][:, :], k_tok[:, tb, h, :], G[:, 6, tb * 8 + h:tb * 8 + h + 1], None, ALU.mult),
               [k_tok_r[h], gts_r], [kbg_r[i]])
            yield
            ia = U.n("e1", 2)
            ib = U.n("e1", 2)
            V(lambda e: e.scalar_tensor_tensor(e1[ia][:, :], psR[:, 0:128], gcol, mk[:, M_M1 + 4 * isS, :], ALU.subtract, ALU.add),
              [psRr, gts_r] + CALL, [e1_r[ia]])
            if not so:
                V(lambda e: e.scalar_tensor_tensor(e1[ib][:, :], psR[:, 0:128], gcol, mk[:, M_M2 + 4 * isS, :], ALU.subtract, ALU.add),
                  [psRr, gts_r] + CALL, [e1_r[ib]])
                A(lambda e: e.activation(EG[i][:, :], psR[:, 0:128], AF.Exp), [psRr], [EG_r[i]])
            yield
            A(lambda e: e.activation(Dm[i][:, :], e1[ia][:, :], AF.Exp, scale=-1.0), [e1_r[ia]], [Dm_r[i]])
            if not so:
                A(lambda e: e.activation(DTm[i][:, :], e1[ib][:, :], AF.Exp), [e1_r[ib]], [DTm_r[i]])
                V(lambda e: e.tensor_tensor(qg[i][:, :], qT[:, h, cols], EG[i][:, :], ALU.mult), [qT_r[h], EG_r[i]], [qg_r[i]])
            yield
            V(lambda e: e.scalar_tensor_tensor(Bm[i][:, :], psG[:, 0:128], G[:, 3, tb * 8 + h:tb * 8 + h + 1], Dm[i][:, :],
                                               ALU.mult, ALU.mult), [psGr, gts_r, Dm_r[i]], [Bm_r[i]])
            if not so:
                V(lambda e: e.tensor_tensor(QKD[i][:, :], psG[:, 128:256], DTm[i][:, :], ALU.mult), [psGr, DTm_r[i]], [QKD_r[i]])
            yield
            pb, pbr = U.pb()
            M(lambda e: e.transpose(pb[:, 0:128], Bm[i][:, :], ident_bf[:, :]), [Bm_r[i]] + CALL, [pbr])
            yield
            p0 = U.n("Pm", 3)
            A(lambda e: e.copy(PTm[p0][:, :], pb[:, 0:128]), [pbr], [PTm_r[p0]])
            t0 = U.n("TT", 2)
            V(lambda e, t0=t0: e.tensor_tensor(TT[t0][:, :], pb[:, 0:128], ident_bf[:, :], ALU.add), [pbr] + CALL, [TT_r[t0]])
            yield
            Pc, Pcr = Bm[i], Bm_r[i]
            PTc, PTcr = PTm[p0], PTm_r[p0]
            for k in range(NL - 1):
                ps, psr = U.ps()
                M(lambda e, PTc=PTc, Pc=Pc, ps=ps: e.matmul(ps[:, 0:128], PTc[:, :], Pc[:, :], start=True, stop=True),
                  [PTcr, Pcr], [psr])
                if k < NL - 2:
                    M(lambda e, PTc=PTc, Pc=Pc, ps=ps: e.matmul(ps[:, 128:256], Pc[:, :], PTc[:, :], start=True, stop=True),
                      [PTcr, Pcr], [psr])
                yield
                pn = U.n("Pm", 3)
                A(lambda e, pn=pn, ps=ps: e.copy(Pm[pn][:, :], ps[:, 0:128]), [psr], [Pm_r[pn]])
                if k < NL - 2:
                    A(lambda e, pn=pn, ps=ps: e.copy(PTm[pn][:, :], ps[:, 128:256]), [psr], [PTm_r[pn]])
                Pc, Pcr = Pm[pn], Pm_r[pn]
                PTc, PTcr = PTm[pn], PTm_r[pn]
                yield
                ps2, ps2r = U.ps()
                M(lambda e, Pc=Pc, t0=t0, ps2=ps2: e.matmul(ps2[:, 0:128], Pc[:, :], TT[t0][:, :], start=True, stop=True),
                  [Pcr, TT_r[t0]], [ps2r])
                yield
                t1 = U.n("TT", 2)
                V(lambda e, t0=t0, t1=t1, ps2=ps2: e.tensor_tensor(TT[t1][:, :], ps2[:, 0:128], TT[t0][:, :], ALU.add),
                  [ps2r, TT_r[t0]], [TT_r[t1]])
                t0 = t1
                yield
            d["TT"] = TT[t0]
            d["TTr"] = TT_r[t0]

        def o_finish(U, pso, psor, tb, h, pbsel=None):
            cols = slice(tb * 128, (tb + 1) * 128)
            j = U.u
            sm_ = small[:, 4 * j:4 * j + 4]
            smr = small_r[j]
            A(lambda e: e.activation(otmp[j][:, :], pso, AF.Square, accum_out=sm_[:, 0:1]), [psor], [otmp_r[j], smr])
            A(lambda e: e.activation(sm_[:, 1:2], sm_[:, 0:1], AF.Sqrt, bias=EPS, scale=1.0 / 128), [smr], [smr])
            yield
            V(lambda e: e.reciprocal(sm_[:, 2:3], sm_[:, 1:2]), [smr], [smr])
            V(lambda e: e.scalar_tensor_tensor(ogn[j][:, :], pso, sm_[:, 2:3], gnw, ALU.mult, ALU.mult),
              [psor, smr] + CALL, [ogn_r[j]])
            yield
            pb, pbr = pbsel if pbsel is not None else U.pb()
            M(lambda e: e.transpose(pb[:, 0:128], ogn[j][:, :], ident_bf[:, :]), [ogn_r[j]] + CALL, [pbr])
            yield
            V(lambda e: e.tensor_tensor(otmp[j][:, :], pb[:, 0:128], zg[:, h, cols], ALU.mult), [pbr, zg_r[h]], [otmp_r[j]])
            V(lambda e: e.tensor_tensor(mixT[:, h, cols], mixT[:, h, cols], otmp[j][:, :], ALU.add), [otmp_r[j], mixT_r[h]], [mixT_r[h]])

        def gdn_prompt(nb, so=False):
            G = gates_f

            def unit(U, tb, h):
                d = {}
                yield from gdn_prep(U, "P", tb, h, d, so)
                i, TTc, TTr = d["i"], d["TT"], d["TTr"]
                V(lambda e: e.tensor_scalar(kgt[i][:, :], k_tok[:, tb, h, :], G[:, 7, tb * 8 + h:tb * 8 + h + 1], None, ALU.mult),
                   [k_tok_r[h], gts_r], [kgt_r[i]])
                pu, pur = U.ps()
                M(lambda e: e.matmul(pu[:, 0:128], TTc[:, :], vb[i][:, :], start=True, stop=True), [TTr, vb_r[i]], [pur])
                M(lambda e: e.matmul(pu[:, 128:256], kbg[i][:, :], TTc[:, :], start=True, stop=True), [TTr, kbg_r[i]], [pur])
                yield
                A(lambda e: e.copy(ub[i][:, :], pu[:, 0:128]), [pur], [ub_r[i]])
                A(lambda e: e.copy(wTb[i][:, :], pu[:, 128:256]), [pur], [wTb_r[i]])
                yield
                pw, pwr = U.ps()
                pS, pSr = U.ps()
                pS_i = U.last
                for cc in range(2):
                    r = slice(64 * cc, 64 * cc + 64)
                    M(lambda e, r=r: e.matmul(pw[r, 0:128], wTb[i][:, r], Sbf[:, h, :], start=True, stop=True),
                      [wTb_r[i], Sbf_r[h]], [pwr])
                    yield
                    V(lambda e, r=r: e.tensor_tensor(vnew[i][r, :], ub[i][r, :], pw[r, 0:128], ALU.subtract),
                      [ub_r[i], pwr], [vnew_r[i]])
                    yield
                    if not so:
                        M(lambda e, r=r: e.matmul(pw[r, 128:256], qg[i][:, r], Sbf[:, h, :], start=True, stop=False),
                          [qg_r[i], Sbf_r[h]], [pwr], inc=False)
                        M(lambda e, r=r: e.matmul(pw[r, 128:256], QKD[i][:, r], vnew[i][:, :], start=False, stop=True),
                          [QKD_r[i], vnew_r[i]], [pwr])
                    M(lambda e, r=r, cc=cc, pS=pS: e.matmul(pS[:, 0:128], kgt[i][r, :], vnew[i][r, :], start=True, stop=True),
                      [kgt_r[i], vnew_r[i]], [pSr])
                    yield
                    V(lambda e, cc=cc, pS=pS: e.scalar_tensor_tensor(Sf[:, h, :], Sf[:, h, :], gl[:, tb, cc * 8 + h:cc * 8 + h + 1],
                                                                     pS[:, 0:128], ALU.mult, ALU.add),
                      [Sf_r[h], gl_r, pSr], [Sf_r[h]])
                    A(lambda e: e.copy(Sbf[:, h, :], Sf[:, h, :]), [Sf_r[h]], [Sbf_r[h]])
                    yield
                if not so:
                    yield from o_finish(U, pw[:, 128:256], pwr, tb, h, U.pb_of(pS_i))

            gens = []
            for tb in range(nb):
                for h in range(8):
                    gens.append(lambda U, tb=tb, h=h: unit(U, tb, h))
            V(lambda e: e.memset(vnew[2][:, :], 0.0), [], [xin_r[2], xin_r[3]] + s2res)
            V(lambda e: e.memset(vnew[3][:, :], 0.0), [], TF_r[0:4] + s3res)
            run_interleaved(gens, 4)
            V(lambda e: e.memset(dummy[:, 0:1], 0.0), [], [xin_r[2], xin_r[3], dummy_r] + s2res)
            V(lambda e: e.memset(dummy[:, 1:2], 0.0), [], TF_r[0:4] + [dummy_r] + s3res)

        def gdn_sample():
            G = gates_f
            tb = 0

            def unit(h):
                P.dma("sp", lambda e, h=h: e.dma_start(out=Sf[:, 0:NSQ, :], in_=sgdn[:, h].rearrange("s k v -> k s v")), writes=Sf_r)
                A(lambda e: e.copy(Sbf[:, 0:NSQ, :], Sf[:, 0:NSQ, :]), Sf_r, Sbf_r)
                d = {}
                for _ in gdn_prep(UCS[0], "S", tb, h, d):
                    pass
                i, TTc, TTr = d["i"], d["TT"], d["TTr"]
                pu, pur = psum()
                M(lambda e: e.matmul(pu[:, 0:128], vb[i][:, :], TTc[:, :], start=True, stop=True), [TTr, vb_r[i]], [pur])
                M(lambda e: e.matmul(pu[:, 128:256], kbg[i][:, :], TTc[:, :], start=True, stop=True), [TTr, kbg_r[i]], [pur])
                A(lambda e: e.copy(ub[i][:, :], pu[:, 0:128]), [pur], [ub_r[i]])
                A(lambda e: e.copy(wTb[i][:, :], pu[:, 128:256]), [pur], [wTb_r[i]])
                pw, pwr = psum()
                for s in range(NSQ):
                    c8 = slice(8 * s, 8 * s + 8)
                    M(lambda e, s=s, c8=c8: e.matmul(pw[:, 8 * s:8 * s + 8], Sbf[:, s, :], wTb[i][:, c8], start=True, stop=True),
                      [wTb_r[i], Sbf_r[s]], [pwr], inc=False)
                    M(lambda e, s=s, c8=c8: e.matmul(pw[:, 128 + 8 * s:128 + 8 * s + 8], Sbf[:, s, :], qg[i][:, c8], start=True, stop=True),
                      [qg_r[i], Sbf_r[s]], [pwr], inc=(s == NSQ - 1))
                V(lambda e: e.tensor_tensor(vnf[i][:, :], ub[i][:, :], pw[:, 0:128], ALU.subtract), [ub_r[i], pwr], [vnf_r[i]])
                A(lambda e: e.copy(oq[i][:, :], pw[:, 128:256]), [pwr], [oq_r[i]])
                p2, p2r = psum()
                M(lambda e: e.transpose(p2[:, 0:128], vnf[i][:, :], ident[:, :]), [vnf_r[i], ident_r], [p2r])
                V(lambda e: e.tensor_copy(vnew[i][:, :], p2[:, 0:128]), [p2r], [vnew_r[i]])
                M(lambda e: e.matmul(p2[:, 128:256], vnew[i][:, :], QKD[i][:, :], start=True, stop=True), [vnew_r[i], QKD_r[i]], [p2r])
                V(lambda e: e.tensor_tensor(oq[i][:, :], oq[i][:, :], p2[:, 128:256], ALU.add), [oq_r[i], p2r], [oq_r[i]])
                M(lambda e: e.transpose(p2[:, 256:384], oq[i][:, :], ident[:, :]), [oq_r[i], ident_r], [p2r])
                for _ in o_finish(UCS[0], p2[:, 256:384], p2r, tb, h):
                    pass
                for s in range(NSQ):
                    kk = nxt("kgt", 2)
                    V(lambda e, s=s, kk=kk: e.tensor_scalar(kgt[kk][:, :], k_tok[:, tb, h, :], G[:, 7, h:h + 1],
                                                            sm[:, SM_SEQ + s:SM_SEQ + s + 1], ALU.mult, ALU.mult),
                      [k_tok_r[h], gts_r] + CALL, [kgt_r[kk]])
                    pS, pSr = psum()
                    M(lambda e, kk=kk: e.matmul(pS[:, 0:128], kgt[kk][:, :], vnew[i][:, :], start=True, stop=True),
                      [kgt_r[kk], vnew_r[i]], [pSr])
                    V(lambda e, s=s: e.scalar_tensor_tensor(Sf[:, s, :], Sf[:, s, :], gl[:, 0, s * 8 + h:s * 8 + h + 1], pS[:, 0:128],
                                                            ALU.mult, ALU.add), [Sf_r[s], gl_r, pSr], [Sf_r[s]])
                P.dma("sp", lambda e, h=h: e.dma_start(out=gdns[:, h].rearrange("s k v -> k s v"), in_=Sf[:, 0:NSQ, :]), reads=Sf_r)
            for h in range(8):
                unit(h)

        def swa_block(U, mode, tb, kvh, first_tile, cache_fn=None):
            cols = slice(tb * 128, (tb + 1) * 128)
            isS = mode == "S"
            kbs = []
            if not isS:
                kbs.append((tb, 3 if (first_tile and tb == 0) else 2))
                kbs.append((tb + 1, 0))
            else:
                kbs.append((1, 1))
            u = U.u
            psO, psOr = banks[2 * u], bank_r[2 * u]
            psD, psDr = banks[2 * u + 1], bank_r[2 * u + 1]
            npt = len(kbs)
            for n_, (blk, mi) in enumerate(kbs):
                psS2 = [(banks[2 * u + 4], bank_r[2 * u + 4]), (banks[2 * u + 5], bank_r[2 * u + 5])]
                pi = 2 * u + n_
                for par in range(2):
                    r = slice(64 * par, 64 * par + 64)
                    psS, psSr = psS2[par]
                    out = psS[:, 0:256]
                    if isS:
                        rhs = qsT[r, 2 * kvh:2 * kvh + 2, 0:128].rearrange("p i (s t) -> p s i t", s=NSQ)
                    else:
                        rhs = qsT[r, 2 * kvh:2 * kvh + 2, cols]
                    M(lambda e, r=r, out=out, blk=blk, rhs=rhs: e.matmul(out, ksT[r, kvh, blk * 128:(blk + 1) * 128], rhs,
                                                                          start=True, stop=True),
                      [ksT_r, qsT_r[2 * kvh], qsT_r[2 * kvh + 1]], [psSr])
                    yield
                    A(lambda e, pi=pi, psS=psS, par=par: e.activation(pTx[pi][:, par * 256:(par + 1) * 256], psS[:, 0:256], AF.Exp, scale=0.125),
                      [psSr], [pTx_r[pi]])
                yield
                if isS:
                    V(lambda e, pi=pi: e.tensor_tensor(pTx[pi][:, :], pTx[pi][:, :], mkS[:, :], ALU.mult),
                      [pTx_r[pi]] + CALL, [pTx_r[pi]])
                else:
                    pv = pTx[pi][:, :].rearrange("p (g q) -> p g q", g=4)
                    V(lambda e, pv=pv, mi=mi: e.tensor_tensor(pv, pv, mkb[:, mi, :].unsqueeze(1).broadcast_to([128, 4, 128]), ALU.mult),
                       [pTx_r[pi]] + CALL, [pTx_r[pi]])
                yield
                last = (n_ == npt - 1)
                M(lambda e, pi=pi, blk=blk, n_=n_, last=last: e.matmul(psO[:, :], vs_tok[:, blk, kvh, :], pTx[pi][:, :],
                                                                       start=(n_ == 0), stop=last),
                  [vs_tok_r, pTx_r[pi]], [psOr])
                M(lambda e, pi=pi, n_=n_: e.matmul(psD[:, :], ones_bf[:, :], pTx[pi][:, :], start=(n_ == 0), stop=False),
                  [pTx_r[pi], ones_r], [psDr])
            if cache_fn is not None:
                cache_fn(kvh, psO, psOr, psD, psDr)
            esk = esinkS if isS else esink
            M(lambda e: e.matmul(psD[:, :], ones_bf[0:1, :], esk[0:1, kvh, :], start=False, stop=True), [ones_r] + CALL, [psDr])
            yield
            rden, rden_r = U.tf(0)
            osw, osw_r = U.tf(1)
            V(lambda e: e.reciprocal(rden[:, 0:512], psD[:, :]), [psDr], [rden_r])
            V(lambda e: e.tensor_tensor(osw[:, 0:512], psO[:, :], rden[:, 0:512], ALU.mult), [psOr, rden_r], [osw_r])
            yield
            for par in range(2):
                r = slice(64 * par, 64 * par + 64)
                if isS:
                    src = osw[r, par * 256:(par + 1) * 256].rearrange("p (s i t) -> p i s t", s=NSQ, i=2)
                    dst = mixT[r, 2 * kvh:2 * kvh + 2, 0:128].rearrange("p i (s t) -> p i s t", s=NSQ)
                    gb_ = gbT[r, 2 * kvh:2 * kvh + 2, 0:128].rearrange("p i (s t) -> p i s t", s=NSQ)
                else:
                    src = osw[r, par * 256:(par + 1) * 256].rearrange("p (i q) -> p i q", i=2)
                    dst = mixT[r, 2 * kvh:2 * kvh + 2, cols]
                    gb_ = gbT[r, 2 * kvh:2 * kvh + 2, cols]
                V(lambda e, src=src, dst=dst, gb_=gb_: e.tensor_tensor(dst, src, gb_, ALU.mult),
                  [osw_r, gbT_r[2 * kvh], gbT_r[2 * kvh + 1]], [mixT_r[2 * kvh], mixT_r[2 * kvh + 1]])

        def swa_prompt(nb, first_tile, ntok, carry_only=False):
            V(lambda e: e.tensor_copy(ksT[:, :, 0:128], kcar[:, :, :]), [kcar_r], [ksT_r])
            gens = []
            for tb in range(0 if carry_only else nb):
                for kvh in range(4):
                    gens.append(lambda U, tb=tb, kvh=kvh: swa_block(U, "P", tb, kvh, first_tile))
            run_interleaved(gens, 2)
            V(lambda e: e.tensor_copy(kcar[:, :, :], ksT[:, :, ntok:ntok + 128]), [ksT_r], [kcar_r])
            V(lambda e: e.tensor_copy(vs_tok[:, 0, :, :], vs_tok[:, nb, :, :]), [vs_tok_r], [vs_tok_r])

        mcb = sb("mcb", [128, TS], BF16)
        mcb_r = Res()

        def swa_sample():
            V(lambda e: e.tensor_copy(mcb[:, :], sm[:, SM_MC:SM_MC + TS]), CALL, [mcb_r])
            for s in range(NSQ):
                for dup in range(2):
                    P.dma("sp", lambda e, s=s, dup=dup: e.dma_start(out=kcs[:, :, dup, :], in_=ck[s].rearrange("r (k d) -> r k d", k=4)), writes=[kcs_r])
                p2, p2r = psum()
                for kvh in range(4):
                    M(lambda e, kvh=kvh, p2=p2: e.transpose(p2[:, kvh * 128:(kvh + 1) * 128], kcs[:, kvh, :, :].rearrange("p a d -> p (a d)"), ident[:, :]),
                      [kcs_r, ident_r], [p2r], inc=(kvh == 3))
                ki = nxt("kcT", 2)
                A(lambda e, ki=ki, p2=p2: e.copy(kcT[ki][:, :, :], p2[:, :].rearrange("p (k r) -> p k r", k=4)), [p2r], [kcT_r[ki]])
                psS2 = [psum(), psum()]
                for par in range(2):
                    psS, psSr = psS2[par]
                    r = slice(64 * par, 64 * par + 64)
                    for kvh in range(4):
                        out = psS[:, kvh * 16:kvh * 16 + 16]
                        M(lambda e, r=r, out=out, s=s, kvh=kvh, ki=ki: e.matmul(out, kcT[ki][r, kvh, :],
                                                                            qsT[r, 2 * kvh:2 * kvh + 2, 8 * s:8 * s + 8], start=True, stop=True),
                          [kcT_r[ki], qsT_r[2 * kvh], qsT_r[2 * kvh + 1]], [psSr], inc=(kvh == 3))
                    dst = pTc[:, :, :].rearrange("p k (par s x) -> p par s k x", par=2, s=NSQ)[:, par, s]
                    A(lambda e, dst=dst, psS=psS: e.activation(dst, psS[:, 0:64].rearrange("p (k x) -> p k x", k=4), AF.Exp, scale=0.125),
                      [psSr], Sbf_r)
            for kvh in range(4):
                pv = pTc[:, kvh, :].rearrange("p (x t) -> p x t", t=TS)
                V(lambda e, pv=pv: e.tensor_tensor(pv, pv, mcb[:, :].unsqueeze(1).broadcast_to([128, 64, TS]), ALU.mult),
                  pTc_r[kvh] + [mcb_r], pTc_r[kvh])

            def cache_fn(kvh, psO, psOr, psD, psDr):
                M(lambda e: e.matmul(psD[:, :], ones_bf[:, :], pTc[:, kvh, :], start=False, stop=False), pTc_r[kvh] + [ones_r], [psDr])
                for s in range(NSQ):
                    for par in range(2):
                        c16 = slice(par * 256 + s * 16, par * 256 + s * 16 + 16)
                        for half in range(2):
                            lastm = (s == NSQ - 1 and par == 1 and half == 1)
                            M(lambda e, c16=c16, s=s, half=half, lastm=lastm: e.matmul(
                                psO[64 * half:64 * half + 64, c16], vcall[:, s, kvh, :], pTc[:, kvh, c16], start=False, stop=True, skip_group_check=True),
                              [vcall_r[s]] + pTc_r[kvh], [psOr], inc=lastm)
            for kvh in range(4):
                for _ in swa_block(UCS[0], "S", 0, kvh, False, cache_fn):
                    pass

        def load_cache():
            for s in range(NSQ):
                vt, vtr = tf()
                P.dma("sp", lambda e, s=s, vt=vt: e.dma_start(out=vt[:, 0:256], in_=cv[s]), writes=[vtr])
                V(lambda e, s=s, vt=vt: e.tensor_copy(vcall[:, s, :, :].rearrange("p k d -> p (k d)"), vt[:, 0:256]), [vtr], [vcall_r[s]])

        def load_sconv():
            P.dma("sp", lambda e: e.dma_start(out=xin[0:NSQ * 3, 0, :], in_=sconv[:, 0:1024]), writes=[xin_r[0]])
            P.dma("sp", lambda e: e.dma_start(out=xin[0:NSQ * 3, 1, :], in_=sconv[:, 1024:2048]), writes=[xin_r[1]])
            P.dma("sp", lambda e: e.dma_start(out=xin[0:NSQ * 3, 2, :], in_=sconv[:, 2048:3072]), writes=[xin_r[2]])
            for g in range(6):
                p2, p2r = psum()
                for ci in range(4):
                    cidx = g * 4 + ci
                    M(lambda e, cidx=cidx, ci=ci, p2=p2: e.transpose(p2[:, ci * 128:ci * 128 + NSQ * 3],
                                                              xin[0:NSQ * 3, cidx // 8, (cidx % 8) * 128:(cidx % 8 + 1) * 128], ident[0:NSQ * 3, 0:NSQ * 3]),
                      [xin_r[cidx // 8], ident_r], [p2r], inc=(ci == 3))
                V(lambda e, g=g, p2=p2: e.tensor_copy(cst[:, g * 4:(g + 1) * 4, 0:NSQ * 3],
                                               p2[:, :].rearrange("p (c x) -> p c x", c=4)[:, :, 0:NSQ * 3]), [p2r], [cst_r])

        def store_conv(dst, nrow):
            for g in range(6):
                p2, p2r = psum()
                for ci in range(4):
                    cidx = g * 4 + ci
                    M(lambda e, cidx=cidx, ci=ci, p2=p2: e.transpose(p2[0:nrow, ci * 128:(ci + 1) * 128], cst[:, cidx, 0:nrow], ident[:, :]),
                      [cst_r, ident_r], [p2r], inc=(ci == 3))
                A(lambda e, g=g, p2=p2: e.copy(xin[0:nrow, g // 2, (g % 2) * 512:(g % 2 + 1) * 512], p2[0:nrow, :]), [p2r], [xin_r[g // 2]])
            for q in range(3):
                P.dma("sp", lambda e, q=q: e.dma_start(out=dst[:, q * 1024:(q + 1) * 1024], in_=xin[0:nrow, q, :]), reads=[xin_r[q]])

        def store_kv(kdst, vdst):
            p2, p2r = psum()
            for kvh in range(4):
                M(lambda e, kvh=kvh: e.transpose(p2[:, kvh * 128:(kvh + 1) * 128], ks32[:, kvh, :], ident[:, :]),
                  [ks32_r, ident_r], [p2r], inc=(kvh == 3))
            A(lambda e: e.copy(knew[:, :, :], p2[:, :].rearrange("p (k x) -> p k x", k=4)[:, :, 0:64]), [p2r], [knew_r])
            P.dma("sp", lambda e: e.dma_start(out=kdst, in_=knew[:, :, :]), reads=[knew_r])
            P.dma("sp", lambda e: e.dma_start(out=vdst, in_=vs32[:, :].rearrange("p (k d) -> p k d", k=4)), reads=[vs32_r])

        def outproj(ntok):
            for c in range(DC):
                wo, wor = wload_out(wout, DC, 128 * c, 128)
                po, por = psum()
                for j in range(DC):
                    M(lambda e, j=j, po=po, wo=wo: e.matmul(po[:, 0:ntok], wo[:, j, :], mixT[:, j, 0:ntok],
                                                            start=(j == 0), stop=(j == DC - 1)),
                      [wor, mixT_r[j]], [por], inc=(j == DC - 1))
                V(lambda e, c=c, po=po: e.tensor_tensor(xT[:, c, 0:ntok], po[:, 0:ntok], xT[:, c, 0:ntok], ALU.add),
                  [por, xT_r[c]], [xT_r[c]])

        ffn_seen = [False, False]

        def tile_pass(mode, xd, yd, tok0, ntok, first, last):
            nb = ntok // 128
            load_x(xd, tok0, ntok)
            rmsnorm(0, xn, xn_r, ntok)
            ffn(w1i, w1o, ntok, ext=(not ffn_seen[0]))
            ffn_seen[0] = True
            if stages >= 2:
                rmsnorm(1, xn, xn_r, ntok)
                if mode == "X":
                    inproj("X", tok0, ntok, lastx=last)
                    if last:
                        swa_prompt(nb, False, ntok, carry_only=True)
                    gdn_prompt(nb, so=True)
                    return
                if MIX & 1:
                    inproj(mode, tok0, ntok)
                if mode == "P":
                    if MIX & 2:
                        swa_prompt(nb, first, ntok)
                    if MIX & 4:
                        gdn_prompt(nb)
                else:
                    if MIX & 8:
                        swa_sample()
                    if MIX & 16:
                        gdn_sample()
                if MIX & 32:
                    outproj(ntok)
            if stages >= 3:
                rmsnorm(2, xn, xn_r, ntok)
                ffn(w2i, w2o, ntok, ext=(not ffn_seen[1]))
                ffn_seen[1] = True
            rmsnorm(3, xT, xT_r, ntok)
            store_y(yd, tok0, ntok)

        NX = LPRE // 512
        for t in range(NX):
            tile_pass("X", xpre, None, t * 512, 512, False, t == NX - 1)
        NT = LP // 512
        for t in range(NT):
            tile_pass("P", xp, yp, t * 512, 512, t == 0, t == NT - 1)
        if stages >= 2:
            if MIX & 64:
                store_conv(convp, 3)
            if MIX & 128:
                store_kv(kp.rearrange("r (k d) -> r k d", k=4), vp.rearrange("r (k d) -> r k d", k=4))
            if MIX & 256:
                P.dma("sp", lambda e: e.dma_start(out=gdnp.rearrange("h k v -> k h v"), in_=Sf[:, 0:8, :]), reads=Sf_r[0:8])
            if MIX & 512:
                load_sconv()
            if MIX & 1024:
                load_cache()
            if MIX & 2048:
                P.dma("sp", lambda e: e.dma_start(out=ksd[:, 0:120, :], in_=ck[:, 8:128, :]))
                P.dma("sp", lambda e: e.dma_start(out=vsd[:, 0:120, :], in_=cv[:, 8:128, :]))
        tile_pass("S", xs, ys, 0, NS_TOK, False, True)
        if stages >= 2:
            if MIX & 4096:
                store_conv(convs, NSQ * 3)
            if MIX & 8192:
                store_kv(ksd[:, 120:128, :].rearrange("s t (k d) -> s t k d", k=4), vsd[:, 120:128, :].rearrange("s t (k d) -> s t k d", k=4))
        P.finish()

        with nc.Block() as block:
            @block.tensor
            def _(eng):
                P.replay(eng, "pe")

            @block.scalar
            def _(eng):
                P.replay(eng, "act")

            @block.vector
            def _(eng):
                P.replay(eng, "dve")

            @block.gpsimd
            def _(eng):
                P.replay(eng, "pool")

            @block.sync
            def _(eng):
                P.replay(eng, "sp")
    return nc


def host_consts(NSQ, TS):
    f = np.float32
    ident = np.eye(128, dtype=f)
    p = np.arange(128)
    rm = np.zeros((128, 128), f)
    for m in range(128):
        d = m % 64
        if d < 8:
            rm[m, m + 8] = -1.0
        elif d < 16:
            rm[m, m - 8] = 1.0
    rmT = np.ascontiguousarray(rm.T)
    mk = np.zeros((128, 8, 128), f)
    for isS, C in ((0, 64), (1, TS)):
        same = (p[:, None] // C) == (p[None, :] // C)
        mk[:, 0 + 4 * isS, :] = (same & (p[:, None] <= p[None, :])).astype(f)
        mk[:, 1 + 4 * isS, :] = np.where(same & (p[:, None] > p[None, :]), 0.0, BIG)
        mk[:, 2 + 4 * isS, :] = np.where(same & (p[None, :] >= p[:, None]), 0.0, -BIG)
        mk[:, 3 + 4 * isS, :] = same.astype(f)
    mkb = np.zeros((128, 3, 128), f)
    mprev = (p[:, None] > p[None, :]).astype(f)
    mkb[:, 0, :] = (p[:, None] <= p[None, :]).astype(f)
    mkb[:, 1, :] = (((p[:, None] // TS) == (p[None, :] // TS)) & (p[:, None] <= p[None, :])).astype(f)
    mkb[:, 2, :] = (p[:, None] > p[None, :]).astype(f)
    mkS = np.zeros((128, 2, NSQ, 2, TS), f)
    for k in range(128):
        ks_, kt_ = k // TS, k % TS
        mkS[k, :, ks_, :, kt_:] = 1.0
    mkS = np.ascontiguousarray(mkS.reshape(128, 512))
    sm = np.zeros((128, 64), f)
    for b in range(2):
        sm[64 * b + 63, SM_LASTP + b] = 1.0
    for s in range(NSQ):
        sm[TS * s + TS - 1, SM_LASTS + s] = 1.0
        sm[TS * s:TS * (s + 1), SM_SEQ + s] = 1.0
    sm[:, SM_ANYP] = (p % 64 == 63).astype(f)
    sm[:, SM_ANYS] = (p % TS == TS - 1).astype(f)
    for t in range(TS):
        sm[:, SM_MC + t] = (p > t).astype(f)
    inv = (np.float32(500000.0) ** (-(np.arange(0, 16, 2, dtype=f)) / np.float32(16))).astype(f)

    def tables(pos):
        ang = (pos.astype(f)[:, None] * inv[None, :]).astype(f)
        c = np.ones((128, len(pos)), f)
        s = np.zeros((128, len(pos)), f)
        for pp in range(128):
            d = pp % 64
            if d < 16:
                c[pp] = np.cos(ang[:, d % 8])
                s[pp] = np.sin(ang[:, d % 8])
        return np.ascontiguousarray(c), np.ascontiguousarray(s)
    cosS, sinS = tables(np.tile(16384 + np.arange(TS), NSQ))
    return dict(ident=ident, rmT=rmT, mk=mk, mkb=mkb, mkS=mkS, sm=sm, cosS=cosS, sinS=sinS), tables, mprev


def pvec(v):
    return np.ascontiguousarray(v.reshape(DC, 128).T)


_CACHE = {}
STAGES = 99
MIXF = 0xFFFF


def kernel(x_prompt, x_sample, state_conv, state_gdn, cache_swa_k, cache_swa_v,
           norm_ffn1, w_ffn1_in, w_ffn1_out, norm_mix, w_in, conv_w, gdn_a_log, gdn_dt_bias,
           gdn_norm, swa_sinks, w_out, norm_ffn2, w_ffn2_in, w_ffn2_out, norm_final):
    f = np.float32
    A_ = lambda a: np.asarray(a, f)
    x_prompt = A_(x_prompt)
    x_sample = A_(x_sample)
    B, L, _ = x_prompt.shape
    NSQ = x_sample.shape[0] // NCORES
    T = x_sample.shape[1]
    LP = L // 2
    key = (L, STAGES, MIXF)
    if key not in _CACHE:
        _CACHE[key] = build(LP, LP, NSQ, T, stages=STAGES, MIX=MIXF)
    nc = _CACHE[key]
    nrm = np.ascontiguousarray(np.stack([pvec(A_(norm_ffn1)[0]), pvec(A_(norm_mix)[0]),
                                         pvec(A_(norm_ffn2)[0]), pvec(A_(norm_final))], axis=1))
    hc, tables, mprev = host_consts(NSQ, T)
    tabs = [tables(np.arange(hf * LP, (hf + 1) * LP)) for hf in range(2)]
    zx = np.zeros((LP, D), f)
    zm = np.zeros((128, 128), f)
    win = np.ascontiguousarray(A_(w_in)[0])
    wks = np.zeros((D, 512), f)
    for kvh in range(4):
        for dup in range(2):
            wks[:, kvh * 128 + dup * 64:kvh * 128 + (dup + 1) * 64] = win[:, OKS + kvh * 64:OKS + (kvh + 1) * 64]
    cw = np.ascontiguousarray(A_(conv_w)[0].reshape(4, 24, 128).transpose(2, 1, 0))
    gv = np.concatenate([A_(gdn_a_log)[0], A_(gdn_dt_bias)[0], A_(gdn_norm)[0], A_(swa_sinks)[0]])[None, :].astype(f)
    shared = {
        "w1i": np.ascontiguousarray(A_(w_ffn1_in)[0]), "w1o": np.ascontiguousarray(A_(w_ffn1_out)[0]),
        "w2i": np.ascontiguousarray(A_(w_ffn2_in)[0]), "w2o": np.ascontiguousarray(A_(w_ffn2_out)[0]),
        "nrm": nrm, "win": win, "wks": wks, "wout": np.ascontiguousarray(A_(w_out)[0]),
        "cw": cw, "gv": np.ascontiguousarray(gv),
    }
    shared.update(hc)
    sc, sg_, ck_, cv_ = A_(state_conv)[0], A_(state_gdn)[0], A_(cache_swa_k)[0], A_(cache_swa_v)[0]
    in_maps = []
    for c in range(NCORES):
        m = dict(shared)
        sl = slice(c * NSQ, (c + 1) * NSQ)
        sq_, hf = c // 2, c % 2
        m["xp"] = np.ascontiguousarray(x_prompt[sq_, hf * LP:(hf + 1) * LP])
        m["xpre"] = np.ascontiguousarray(x_prompt[sq_, 0:LP]) if hf == 1 else zx
        m["cosP"], m["sinP"] = tabs[hf]
        m["cosX"], m["sinX"] = tabs[0]
        m["mprev0"] = mprev if hf == 1 else zm
        m["xs"] = np.ascontiguousarray(x_sample[sl].reshape(NSQ * T, D))
        m["sconv"] = np.ascontiguousarray(sc[sl].reshape(NSQ * 3, 3072))
        m["sgdn"] = np.ascontiguousarray(sg_[sl])
        m["ck"] = np.ascontiguousarray(ck_[sl].reshape(NSQ, 128, 256))
        m["cv"] = np.ascontiguousarray(cv_[sl].reshape(NSQ, 128, 256))
        in_maps.append(m)
    res = run_bass_kernel_spmd(nc, in_maps, core_ids=list(range(NCORES)))
    R = res.results
    cat = lambda k, shp: np.concatenate([np.asarray(R[c][k], f).reshape(shp) for c in range(NCORES)], axis=0)
    stk = lambda k, shp: np.stack([np.asarray(R[2 * b + 1][k], f).reshape(shp) for b in range(B)], axis=0)
    y_prompt = np.stack([np.concatenate([np.asarray(R[2 * b]["yp"], f), np.asarray(R[2 * b + 1]["yp"], f)], axis=0) for b in range(B)], axis=0)
    y_sample = cat("ys", (NSQ, T, D))
    conv_p = stk("convp", (3, 3072))[None]
    conv_s = cat("convs", (NSQ, 3, 3072))[None]
    gdn_p = stk("gdnp", (8, 128, 128))[None]
    gdn_s = cat("gdns", (NSQ, 8, 128, 128))[None]
    k_p = stk("kp", (128, 4, 64))[None]
    k_s = cat("ksd", (NSQ, 128, 4, 64))[None]
    v_p = stk("vp", (128, 4, 64))[None]
    v_s = cat("vsd", (NSQ, 128, 4, 64))[None]
    return (y_prompt, y_sample, conv_p, conv_s, gdn_p, gdn_s, k_p, k_s, v_p, v_s)
```

```python
import math
from contextlib import ExitStack
import numpy as np
import concourse.bass as bass
import concourse.mybir as mybir
from concourse.bass_utils import run_bass_kernel_spmd

F32 = mybir.dt.float32
BF16 = mybir.dt.bfloat16
AF = mybir.ActivationFunctionType
ALU = mybir.AluOpType
AX = mybir.AxisListType

D = 1024
DC = 8
DFF = 2816
FC = 22
DIN = 7696
EPS = 1e-6
NCORES = 8
SLOT = 1408


class Res:
    __slots__ = ("w", "r", "x")

    def __init__(self):
        self.w = None
        self.r = {}
        self.x = False


def RL(n):
    return [Res() for _ in range(n)]


class EngS:
    def __init__(self, name, sem):
        self.name = name
        self.sem = sem
        self.count = 0
        self.ops = []
        self.waited = {}


class Prog:
    def __init__(self, sems, dsems):
        self.E = {n: EngS(n, sems[n]) for n in ("pe", "act", "dve", "pool", "sp")}
        self.dsems = dsems
        self.dcount = [0] * len(dsems)
        self.dnext = 0

    def _deps(self, reads, writes, own=None):
        need = {}

        def add(k, v):
            if need.get(k, 0) < v:
                need[k] = v

        for r in reads:
            if r.w:
                add(*r.w)
            if r.x:
                for k, v in r.r.items():
                    if k != own:
                        add(k, v)
        for w in writes:
            if w.w:
                add(*w.w)
            for k, v in w.r.items():
                add(k, v)
        return need

    def _waits(self, e, need, skip_own=False):
        for k, v in need.items():
            if skip_own and k == e.name:
                continue
            if e.waited.get(k, 0) < v:
                e.waited[k] = v
                e.ops.append(("wait", k, v))

    def op(self, eng, fn, reads=(), writes=(), inc=True):
        e = self.E[eng]
        need = self._deps(reads, writes, own=e.name)
        self._waits(e, need, skip_own=(eng == "pe"))
        e.ops.append(("op", fn, inc))
        idx = e.count + 1
        if inc:
            e.count = idx
        for r in reads:
            if r.r.get(e.name, 0) < idx:
                r.r[e.name] = idx
        for w in writes:
            w.w = (e.name, idx)
            w.r = {}

    def dma(self, eng, fn, reads=(), writes=()):
        e = self.E[eng]
        need = self._deps(reads, writes)
        j = self.dnext
        self.dnext = (j + 1) % len(self.dsems)
        key = ("d", j)
        if self.dcount[j] > 0:
            v = 16 * self.dcount[j]
            if need.get(key, 0) < v:
                need[key] = v
        self._waits(e, need)
        self.dcount[j] += 1
        val = 16 * self.dcount[j]
        e.ops.append(("dma", fn, key))
        for r in reads:
            if r.r.get(key, 0) < val:
                r.r[key] = val
        for w in writes:
            w.w = (key, val)
            w.r = {}

    def semof(self, key):
        if isinstance(key, tuple):
            return self.dsems[key[1]]
        return self.E[key].sem

    def replay(self, eng, name):
        e = self.E[name]
        for item in e.ops:
            if item[0] == "wait":
                eng.wait_ge(self.semof(item[1]), item[2])
            elif item[0] == "op":
                ins = item[1](eng)
                if item[2]:
                    ins.then_inc(e.sem, 1)
            else:
                ins = item[1](eng)
                ins.then_inc(self.semof(item[2]), 16)

    def finish(self):
        e = self.E["sp"]
        for j, c in enumerate(self.dcount):
            if c > 0:
                e.ops.append(("wait", ("d", j), 16 * c))
        for n in ("pe", "act", "dve", "pool"):
            if self.E[n].count > 0:
                e.ops.append(("wait", n, self.E[n].count))


OQ, OK_, OV, OZ, OA, OB, OQS, OKS, OVS, OGA, OGB = 0, 1024, 2048, 3072, 4096, 4104, 4112, 5136, 5392, 5648, 6672
BIG = 30000.0
M_TRI, M_M1, M_M2, M_BLK = 0, 1, 2, 3
SM_LASTP, SM_LASTS, SM_ANYP, SM_ANYS, SM_SEQ, SM_MC = 0, 2, 18, 19, 20, 36


def build(LP, LPRE, NSQ=16, TS=8, stages=99, MIX=0xFFFF, GW=2):
    NS_TOK = NSQ * TS
    nc = bass.Bass("TRN2", target_bir_lowering=False)

    def din(name, shape, dt=F32):
        return nc.dram_tensor(name, list(shape), dt, kind="ExternalInput").ap()

    def dout(name, shape, dt=F32):
        return nc.dram_tensor(name, list(shape), dt, kind="ExternalOutput").ap()

    xp = din("xp", [LP, D])
    xpre = din("xpre", [LPRE, D])
    cosX = din("cosX", [128, LPRE])
    sinX = din("sinX", [128, LPRE])
    mprev0_d = din("mprev0", [128, 128])
    xs = din("xs", [NS_TOK, D])
    w1i = din("w1i", [D, 2 * DFF])
    w1o = din("w1o", [DFF, D])
    w2i = din("w2i", [D, 2 * DFF])
    w2o = din("w2o", [DFF, D])
    nrm = din("nrm", [128, 4, DC])
    ident_d = din("ident", [128, 128])
    win = din("win", [D, DIN])
    wks = din("wks", [D, 512])
    wout = din("wout", [D, D])
    cw_d = din("cw", [128, 24, 4])
    gv_d = din("gv", [1, 160])
    cosP = din("cosP", [128, LP])
    sinP = din("sinP", [128, LP])
    cosS = din("cosS", [128, NS_TOK])
    sinS = din("sinS", [128, NS_TOK])
    rmT_d = din("rmT", [128, 128])
    mk_d = din("mk", [128, 8, 128])
    mkb_d = din("mkb", [128, 3, 128])
    mkS_d = din("mkS", [128, 512])
    sm_d = din("sm", [128, 64])
    sconv = din("sconv", [NSQ * 3, 3072])
    sgdn = din("sgdn", [NSQ, 8, 128, 128])
    ck = din("ck", [NSQ, 128, 256])
    cv = din("cv", [NSQ, 128, 256])
    yp = dout("yp", [LP, D])
    ys = dout("ys", [NS_TOK, D])
    convp = dout("convp", [3, 3072])
    convs = dout("convs", [NSQ * 3, 3072])
    gdnp = dout("gdnp", [8, 128, 128])
    gdns = dout("gdns", [NSQ, 8, 128, 128])
    kp = dout("kp", [128, 256])
    ksd = dout("ksd", [NSQ, 128, 256])
    vp = dout("vp", [128, 256])
    vsd = dout("vsd", [NSQ, 128, 256])

    es = ExitStack()
    with es:
        es.enter_context(nc.allow_low_precision("bf16 matmul operands, fp32 accumulate"))
        sems = {n: es.enter_context(nc.semaphore("sem_" + n)) for n in ("pe", "act", "dve", "pool", "sp")}
        dsems = [es.enter_context(nc.semaphore("dsem%d" % i)) for i in range(24)]
        P = Prog(sems, dsems)

        def sb(name, shape, dt=F32):
            return es.enter_context(nc.sbuf_tensor(name, list(shape), dt))

        def A(fn, reads, writes):
            P.op("act", fn, reads, writes)

        def V(fn, reads, writes):
            P.op("dve", fn, reads, writes)

        def M(fn, reads, writes, inc=True):
            P.op("pe", fn, reads, writes, inc)

        def G_(fn, reads, writes):
            P.op("pool", fn, reads, writes)

        xin = sb("xin", [128, 4, D])
        xin_r = RL(4)
        xT = sb("xT", [128, DC, 512])
        xT_r = RL(DC)
        xn = sb("xn", [128, DC, 512], BF16)
        xn_r = RL(DC)
        hT = sb("hT", [128, FC, 512], BF16)
        hT_r = RL(FC)
        NSLOT = 2
        NSTG = 2
        wslots = [sb("wslot%d" % i, [128, SLOT], BF16) for i in range(NSLOT)]
        wslot_r = RL(NSLOT)
        wnext = [0]
        wstg = [sb("wstg%d" % i, [128, SLOT]) for i in range(NSTG)]
        wstg_r = RL(NSTG)
        snext = [0]
        sq = [sb("sq%d" % i, [128, 512], BF16) for i in range(2)]
        sq_r = RL(2)
        TF = [sb("tf%d" % i, [128, 520]) for i in range(6)]
        TF_r = RL(6)
        tfn = [0]

        def tf():
            i = tfn[0]
            tfn[0] = (i + 1) % 6
            return TF[i], TF_r[i]
        ident = sb("ident_sb", [128, 128])
        ident_bf = sb("ident_bf", [128, 128], BF16)
        ones_bf = sb("ones_bf", [128, 128], BF16)
        ones_f = sb("ones_f", [128, 128])
        nrm_sb = sb("nrm_sb", [128, 4, DC])
        qT = hT[:, 0:8, :]
        qT_r = hT_r[0:8]
        kT = hT[:, 8:16, :]
        kT_r = hT_r[8:16]
        ksT = hT[:, 16:22, :].rearrange("p j n -> p (j n)")[:, 0:2560].rearrange("p (k n) -> p k n", k=4)
        ksT_r = Res()
        for _j in range(16, 22):
            hT_r[_j] = ksT_r
        mixT = xn
        mixT_r = xn_r
        k_tok = sb("k_tok", [128, 4, 8, 128], BF16)
        k_tok_r = RL(8)
        v_tok = sb("v_tok", [128, 4, 8, 128], BF16)
        v_tok_r = RL(8)
        zg = xin[:, 0:2, :].rearrange("p b d -> p (b d)").bitcast(BF16).rearrange("p (h n) -> p h n", h=8)
        zg_r = [xin_r[h // 4] for h in range(8)]
        vcall = xin[:, 2:4, :].rearrange("p b d -> p (b d)").bitcast(BF16).rearrange("p (s k d) -> p s k d", s=16, k=4)
        vcall_r = [xin_r[2 + s // 8] for s in range(16)]
        gbT = sb("gbT", [128, 8, 512], BF16)
        gbT_r = RL(8)
        qsT = sb("qsT", [128, 8, 512], BF16)
        qsT_r = RL(8)
        vs_tok = sb("vs_tok", [128, 5, 4, 128], BF16)
        vs_tok_r = Res()
        kcar = sb("kcar", [128, 4, 128], BF16)
        kcar_r = Res()
        csn = sb("csn", [128, 2, 512])
        csn_r = Res()
        ks32 = sb("ks32", [128, 4, 128])
        ks32_r = Res()
        vs32 = sb("vs32", [128, 256])
        vs32_r = Res()
        cst = sb("cst", [128, 24, 48])
        cst_r = Res()
        cw = sb("cw_sb", [128, 24, 4])
        gv = sb("gv_sb", [128, 160])
        gnw = gv[:, 16:144]
        negA = sb("negA", [128, 8])
        esink = sb("esink", [1, 4, 512], BF16)
        esinkS = sb("esinkS", [1, 4, 512], BF16)
        mkS = sb("mkS_sb", [128, 512], BF16)
        esink_f = sb("esink_f", [1, 16])
        rmT = sb("rmT_sb", [128, 128])
        mk = sb("mk_sb", [128, 8, 128])
        mkb = sb("mkb_sb", [128, 4, 128], BF16)
        sm = sb("sm_sb", [128, 64])
        gch = [sb("gch%d" % i, [8, 128]) for i in range(4)]
        gch_r = RL(4)
        gates_f = sb("gates_f", [128, 9, 32])
        gts_r = Res()
        gcT = sb("gcT", [8, 512])
        gcT_r = Res()
        gl = sb("gl", [128, 4, 128])
        gl_r = Res()
        x2pool = xin[:, 2:4, :].rearrange("p b d -> p (b d)")
        x2off = [0]
        s2res = []
        x3off = [0]
        s3res = []

        def pool3(nel):
            k, o = divmod(x3off[0], 512)
            assert o + nel <= 512 and k < 4
            x3off[0] += nel
            return TF[k][:, o:o + nel]

        def wt(name, dt=F32, n=1, s2=True):
            tiles = [sb("%s%d" % (name, i), [128, 128], dt) for i in range(2 * n)]
            res = RL(2 * n)
            if s2:
                for i in range(n):
                    if dt == F32:
                        v = x2pool[:, x2off[0]:x2off[0] + 128]
                        x2off[0] += 128
                    else:
                        v = x2pool[:, x2off[0]:x2off[0] + 64].bitcast(BF16)
                        x2off[0] += 64
                    r_ = Res()
                    tiles.append(v)
                    res.append(r_)
                    s2res.append(r_)
                for i in range(n):
                    v = pool3(128) if dt == F32 else pool3(64).bitcast(BF16)
                    r_ = Res()
                    tiles.append(v)
                    res.append(r_)
                    s3res.append(r_)
            return tiles, res
        e1, e1_r = wt("e1", F32, 2)
        Dm, Dm_r = wt("Dm", F32, 1)
        DTm, DTm_r = wt("DTm", F32, 1)
        EG, EG_r = wt("EG", F32, 1)
        qg, qg_r = wt("qg", BF16, 1)
        Bm, Bm_r = wt("Bm", BF16, 1)
        Pm, Pm_r = wt("Pm", BF16, 3)
        PTm, PTm_r = wt("PTm", BF16, 3)
        TT, TT_r = wt("TT", BF16, 2)
        QKD, QKD_r = wt("QKD", BF16, 1)
        vb, vb_r = wt("vb", BF16, 1)
        kbg, kbg_r = wt("kbg", BF16, 1)
        kgt, kgt_r = wt("kgt", BF16, 1)
        ub, ub_r = wt("ub", F32, 1)
        wTb, wTb_r = wt("wTb", BF16, 1)
        vnew, vnew_r = wt("vnew", BF16, 1)
        vnf, vnf_r = wt("vnf", F32, 1, s2=False)
        ogn, ogn_r = wt("ogn", BF16, 1)
        oq, oq_r = wt("oq", F32, 1, s2=False)
        otmp, otmp_r = wt("otmp", F32, 1)
        assert x2off[0] <= 2048, x2off[0]
        dummy = sb("dummy_sb", [128, 2])
        dummy_r = Res()
        ctr = {}

        def nxt(name, n):
            i = ctr.get(name, 0)
            ctr[name] = (i + 1) % n
            return i
        small = sb("small", [128, 16])
        small_r = RL(4)
        Sf = sb("Sf", [128, 16, 128])
        Sf_r = RL(16)
        Sbf = sb("Sbf", [128, 16, 128], BF16)
        Sbf_r = RL(16)
        pT = [sb("pT%d" % i, [128, 512], BF16) for i in range(2)]
        pT_r = RL(2)
        kcs = sb("kcs", [128, 4, 2, 64])
        kcs_r = Res()
        kcT = [sb("kcT%d" % i, [128, 4, 128], BF16) for i in range(2)]
        kcT_r = RL(2)
        pTc = Sbf[:, :, :].rearrange("p (k a) n -> p k (a n)", k=4)
        pTc_r = [Sbf_r[4 * k:4 * k + 4] for k in range(4)]
        knew = sb("knew", [128, 4, 64])
        knew_r = Res()
        NB_ = 8
        banks = [es.enter_context(nc.psum_tensor("bank%d" % i, [128, 512], F32)) for i in range(NB_)]
        bank_r = RL(NB_)
        for _r in bank_r:
            _r.x = True
        bnext = [0]
        banks_bf = [banks[i][:, :].bitcast(BF16) for i in range(NB_)]
        pbf = [banks_bf[6], banks_bf[7]]
        pbf_r = [bank_r[6], bank_r[7]]
        pbn = [0]

        def psum():
            i = bnext[0]
            bnext[0] = (i + 1) % NB_
            return banks[i], bank_r[i]

        def psumbf():
            i = pbn[0]
            pbn[0] = 1 - i
            return pbf[i], pbf_r[i]

        ident_r, nrm_r, ones_r, c_r = Res(), Res(), Res(), Res()
        P.dma("sp", lambda e: e.dma_start(out=ident[:, :], in_=ident_d[:, :]), writes=[ident_r])
        P.dma("sp", lambda e: e.dma_start(out=nrm_sb[:, :, :], in_=nrm[:, :, :]), writes=[nrm_r])
        V(lambda e: e.memset(ones_bf[:, :], 1.0), [], [ones_r])
        cl = [Res() for _ in range(8)]
        P.dma("sp", lambda e: e.dma_start(out=cw[:, :, :], in_=cw_d[:, :, :]), writes=[cl[0]])
        P.dma("sp", lambda e: e.dma_start(out=gv[:, :], in_=gv_d[0:1, :].broadcast_to([128, 160])), writes=[cl[1]])
        P.dma("sp", lambda e: e.dma_start(out=rmT[:, :], in_=rmT_d[:, :]), writes=[cl[2]])
        P.dma("sp", lambda e: e.dma_start(out=mk[:, :, :], in_=mk_d[:, :, :]), writes=[cl[3]])
        P.dma("sp", lambda e: e.dma_start(out=sm[:, :], in_=sm_d[:, :]), writes=[cl[4]])
        _t, _tr = tf()
        P.dma("sp", lambda e: e.dma_start(out=_t[:, 0:384], in_=mkb_d[:, :, :].rearrange("p a b -> p (a b)")), writes=[_tr])
        V(lambda e: e.tensor_copy(mkb[:, 0:3, :].rearrange("p a b -> p (a b)"), _t[:, 0:384]), [_tr], [cl[5]])
        _t3, _t3r = tf()
        P.dma("sp", lambda e: e.dma_start(out=_t3[:, 0:128], in_=mprev0_d[:, :]), writes=[_t3r])
        V(lambda e: e.tensor_copy(mkb[:, 3, :], _t3[:, 0:128]), [_t3r], [cl[7]])
        V(lambda e: e.memset(kcar[:, :, :], 0.0), [], [kcar_r])
        V(lambda e: e.memset(vs_tok[:, 0, :, :], 0.0), [], [vs_tok_r])
        V(lambda e: e.tensor_copy(ident_bf[:, :], ident[:, :]), [ident_r], [cl[6]])
        V(lambda e: e.memset(ones_f[:, :], 1.0), [], [cl[6]])
        A(lambda e: e.activation(negA[:, :], gv[:, 0:8], AF.Exp), [cl[1]], [c_r])
        V(lambda e: e.tensor_scalar(negA[:, :], negA[:, :], -1.0, None, ALU.mult), [c_r], [c_r])
        A(lambda e: e.activation(esink_f[:, :], gv[0:1, 144:160], AF.Exp), [cl[1]], [c_r])
        for par in range(2):
            src = esink_f[0:1, :].rearrange("o (k i par) -> o par k i", k=4, i=2)[:, par]
            dstv = esink[0:1, :, :].rearrange("o k (par i q) -> o par k i q", par=2, i=2)[:, par]
            V(lambda e, src=src, dstv=dstv: e.tensor_copy(dstv, src.unsqueeze(3).broadcast_to([1, 4, 2, 128])), [c_r], [c_r])
            for kvh in range(4):
                src2 = esink_f[0:1, 4 * kvh:4 * kvh + 4].rearrange("o (i par) -> o par i", i=2)[:, par]
                dst2 = esinkS[0:1, kvh, :].rearrange("o (par s i t) -> o par s i t", par=2, s=NSQ, i=2)[:, par]
                V(lambda e, src2=src2, dst2=dst2: e.tensor_copy(dst2, src2.unsqueeze(1).unsqueeze(3).broadcast_to([1, NSQ, 2, TS])), [c_r], [c_r])
        _t2, _t2r = tf()
        P.dma("sp", lambda e: e.dma_start(out=_t2[:, 0:512], in_=mkS_d[:, :]), writes=[_t2r])
        V(lambda e: e.tensor_copy(mkS[:, :], _t2[:, 0:512]), [_t2r], [c_r])
        V(lambda e: e.memset(cst[:, :, :], 0.0), [], [cst_r])
        for _i in range(2):
            V(lambda e, _i=_i: e.memset(vnew[_i][:, :], 0.0), [], [vnew_r[_i]])
        V(lambda e: e.memset(Sf[:, 0:8, :], 0.0), [], Sf_r[0:8])
        V(lambda e: e.memset(Sbf[:, 0:8, :], 0.0), [], Sbf_r[0:8])
        CALL = [ident_r, ones_r, c_r] + cl

        wscr = nc.dram_tensor("wscr", [200, 128, SLOT], BF16).ap()
        wreg = {}
        half_r = [[Res(), Res()] for _k in range(NSTG)]
        _hv = [wstg[_k][:, :].bitcast(BF16) for _k in range(NSTG)]
        fslots = [(wslots[0], wslot_r[0], None),
                  (_hv[0][:, 0:SLOT], half_r[0][0], wstg_r[0]),
                  (_hv[1][:, 0:SLOT], half_r[1][0], wstg_r[1]),
                  (wslots[1], wslot_r[1], None),
                  (_hv[0][:, SLOT:2 * SLOT], half_r[0][1], wstg_r[0]),
                  (_hv[1][:, SLOT:2 * SLOT], half_r[1][1], wstg_r[1])]
        fnext = [0]
        cnext = [0]

        def _wload(key, src, n1, ncol):
            if key in wreg:
                tid, tres = wreg[key]
                j = fnext[0]
                fnext[0] = (j + 1) % len(fslots)
                buf, bres, extra = fslots[j]
                wr = [bres]
                if extra is not None:
                    wr.append(extra)
                P.dma("sp", lambda e: e.dma_start(out=buf[:, 0:n1 * ncol], in_=wscr[tid, :, 0:n1 * ncol]), reads=[tres], writes=wr)
                return buf[:, 0:n1 * ncol].rearrange("p (c n) -> p c n", c=n1), bres
            k = snext[0]
            snext[0] = (k + 1) % NSTG
            i = wnext[0]
            wnext[0] = (i + 1) % NSLOT
            sview = wstg[k][:, 0:n1 * ncol].rearrange("p (c n) -> p c n", c=n1)
            view = wslots[i][:, 0:n1 * ncol].rearrange("p (c n) -> p c n", c=n1)
            P.dma("sp", lambda e: e.dma_start(out=sview, in_=src), writes=[wstg_r[k], half_r[k][0], half_r[k][1]])
            ce = ("pool", "act", "dve")[cnext[0] % 3]
            cnext[0] += 1
            if ce == "act":
                P.op("act", lambda e: e.copy(wslots[i][:, 0:n1 * ncol], wstg[k][:, 0:n1 * ncol]),
                     reads=[wstg_r[k]], writes=[wslot_r[i]])
            else:
                P.op(ce, lambda e: e.tensor_copy(wslots[i][:, 0:n1 * ncol], wstg[k][:, 0:n1 * ncol]),
                     reads=[wstg_r[k]], writes=[wslot_r[i]])
            tid = len(wreg)
            tres = Res()
            wreg[key] = (tid, tres)
            P.dma("sp", lambda e: e.dma_start(out=wscr[tid, :, 0:n1 * ncol], in_=wslots[i][:, 0:n1 * ncol]),
                  reads=[wslot_r[i]], writes=[tres])
            return view, wslot_r[i]

        def wload_in(W, col0, ncol):
            src = W.rearrange("(c p) n -> p c n", p=128)[:, :, col0:col0 + ncol]
            return _wload((id(W), "i", col0, ncol), src, DC, ncol)

        def wload_out(W, nj, col0, ncol, r0=0):
            src = W[r0:r0 + nj * 128, :].rearrange("(j p) n -> p j n", p=128)[:, :, col0:col0 + ncol]
            return _wload((id(W), "o", col0, ncol, r0, nj), src, nj, ncol)

        def load_x(xd, tok0, ntok):
            nb = ntok // 128
            src = xd[tok0:tok0 + ntok, :].rearrange("(b p) d -> p b d", p=128)
            P.dma("sp", lambda e: e.dma_start(out=xin[:, 0:nb, :], in_=src), writes=xin_r[0:nb])
            for c in range(DC):
                ps, pr = psum()
                for b in range(nb):
                    P.op("pe", lambda e, b=b, c=c, ps=ps: e.transpose(
                        ps[:, b * 128:(b + 1) * 128], xin[:, b, c * 128:(c + 1) * 128], ident[:, :]),
                        reads=[xin_r[b], ident_r], writes=[pr], inc=(b == nb - 1))
                eng = "act" if c % 2 == 0 else "dve"
                if eng == "act":
                    P.op("act", lambda e, c=c, ps=ps: e.copy(xT[:, c, 0:ntok], ps[:, 0:ntok]),
                         reads=[pr], writes=[xT_r[c]])
                else:
                    P.op("dve", lambda e, c=c, ps=ps: e.tensor_copy(xT[:, c, 0:ntok], ps[:, 0:ntok]),
                         reads=[pr], writes=[xT_r[c]])

        def rmsnorm(which, out, out_r, ntok):
            ps, pr = psum()
            for c in range(DC):
                s = c % 2
                P.op("act", lambda e, c=c, s=s: e.activation(sq[s][:, 0:ntok], xT[:, c, 0:ntok], AF.Square),
                     reads=[xT_r[c]], writes=[sq_r[s]])
                P.op("pe", lambda e, c=c, s=s, ps=ps: e.matmul(ps[:, 0:ntok], ones_bf[:, :], sq[s][:, 0:ntok],
                                                               start=(c == 0), stop=(c == DC - 1)),
                     reads=[sq_r[s], ones_r], writes=[pr], inc=True)
            sd, sd_r = tf()
            rstd, rstd_r = tf()
            P.op("act", lambda e, ps=ps: e.activation(sd[:, 0:ntok], ps[:, 0:ntok], AF.Sqrt, bias=EPS, scale=1.0 / D),
                 reads=[pr], writes=[sd_r])
            P.op("dve", lambda e: e.reciprocal(rstd[:, 0:ntok], sd[:, 0:ntok]), reads=[sd_r], writes=[rstd_r])
            for c in range(DC):
                P.op("dve", lambda e, c=c: e.scalar_tensor_tensor(
                    out[:, c, 0:ntok], xT[:, c, 0:ntok], nrm_sb[:, which, c:c + 1], rstd[:, 0:ntok],
                    ALU.mult, ALU.mult),
                    reads=[xT_r[c], rstd_r, nrm_r], writes=[out_r[c]])

        def ffn(Wi, Wo, ntok):
            for j in range(FC):
                wg, wgr = wload_in(Wi, 128 * j, 128)
                wu, wur = wload_in(Wi, DFF + 128 * j, 128)
                pg, pgr = psum()
                for c in range(DC):
                    P.op("pe", lambda e, c=c, pg=pg, wg=wg: e.matmul(
                        pg[:, 0:ntok], wg[:, c, :], xn[:, c, 0:ntok], start=(c == 0), stop=(c == DC - 1)),
                        reads=[wgr, xn_r[c]], writes=[pgr], inc=(c == DC - 1))
                pu, pur = psum()
                for c in range(DC):
                    P.op("pe", lambda e, c=c, pu=pu, wu=wu: e.matmul(
                        pu[:, 0:ntok], wu[:, c, :], xn[:, c, 0:ntok], start=(c == 0), stop=(c == DC - 1)),
                        reads=[wur, xn_r[c]], writes=[pur], inc=(c == DC - 1))
                sgb, sgr = tf()
                P.op("act", lambda e, sgb=sgb, pg=pg: e.activation(sgb[:, 0:ntok], pg[:, 0:ntok], AF.Silu),
                     reads=[pgr], writes=[sgr])
                P.op("dve", lambda e, sgb=sgb, pu=pu, j=j: e.tensor_tensor(
                    hT[:, j, 0:ntok], sgb[:, 0:ntok], pu[:, 0:ntok], ALU.mult),
                    reads=[sgr, pur], writes=[hT_r[j]])
            for c in range(DC):
                po, por = psum()
                for half in range(2):
                    wo, wor = wload_out(Wo, 11, 128 * c, 128, r0=half * 11 * 128)
                    for jj in range(11):
                        j = half * 11 + jj
                        P.op("pe", lambda e, j=j, jj=jj, po=po, wo=wo: e.matmul(
                            po[:, 0:ntok], wo[:, jj, :], hT[:, j, 0:ntok], start=(j == 0), stop=(j == FC - 1)),
                            reads=[wor, hT_r[j]], writes=[por], inc=(j == FC - 1 or jj == 10))
                P.op("dve", lambda e, c=c, po=po: e.scalar_tensor_tensor(
                    xT[:, c, 0:ntok], po[:, 0:ntok], 0.5, xT[:, c, 0:ntok], ALU.mult, ALU.add),
                    reads=[por, xT_r[c]], writes=[xT_r[c]])

        def store_y(yd, tok0, ntok):
            nb = ntok // 128
            for b in range(nb):
                for cg in range(2):
                    ps, pr = psum()
                    for ci in range(4):
                        c = cg * 4 + ci
                        P.op("pe", lambda e, b=b, c=c, ci=ci, ps=ps: e.transpose(
                            ps[:, ci * 128:(ci + 1) * 128], xT[:, c, b * 128:(b + 1) * 128], ident[:, :]),
                            reads=[xT_r[c], ident_r], writes=[pr], inc=(ci == 3))
                    if cg == 0:
                        P.op("act", lambda e, b=b, cg=cg, ps=ps: e.copy(xin[:, b, cg * 512:(cg + 1) * 512], ps[:, :]),
                             reads=[pr], writes=[xin_r[b]])
                    else:
                        P.op("dve", lambda e, b=b, cg=cg, ps=ps: e.tensor_copy(xin[:, b, cg * 512:(cg + 1) * 512], ps[:, :]),
                             reads=[pr], writes=[xin_r[b]])
            dst = yd[tok0:tok0 + ntok, :].rearrange("(b p) d -> p b d", p=128)
            P.dma("sp", lambda e: e.dma_start(out=dst, in_=xin[:, 0:nb, :]), reads=xin_r[0:nb])


        class UC:
            def __init__(self, u):
                self.u = u
                self.c = {}
                self.pn = 0

            def n(self, name, n):
                i = self.c.get(name, 0)
                self.c[name] = (i + 1) % n
                return self.u * n + i

            def ps(self):
                i = 2 * self.u + self.pn
                self.pn = 1 - self.pn
                self.last = i
                return banks[i], bank_r[i]

            def pb(self):
                i = 2 * self.u + self.pn
                self.pn = 1 - self.pn
                self.last = i
                return banks_bf[i], bank_r[i]

            def pb_of(self, i):
                return banks_bf[i], bank_r[i]

            def tf(self, k):
                return TF[2 * self.u + k], TF_r[2 * self.u + k]

            def sq(self):
                return sqx[self.u], sqx_r[self.u]
        UCS = [UC(0), UC(1), UC(2), UC(3)]
        sqx = [sq[0], sq[1], pT[0], pT[1]]
        sqx_r = [sq_r[0], sq_r[1], pT_r[0], pT_r[1]]
        pTx = [pT[0], pT[1], sq[0], sq[1]]
        pTx_r = [pT_r[0], pT_r[1], sq_r[0], sq_r[1]]

        def run_interleaved(gens, W):
            active = []
            it = iter(gens)
            slots = list(range(W))
            pending = True
            while True:
                while len(active) < W and pending:
                    try:
                        mk_ = next(it)
                    except StopIteration:
                        pending = False
                        break
                    u = slots.pop(0)
                    active.append((u, mk_(UCS[u])))
                if not active:
                    break
                for ent in list(active):
                    u, g = ent
                    try:
                        next(g)
                    except StopIteration:
                        active.remove(ent)
                        slots.append(u)

        def chunk_gen(U, W, off, ch, ntok, fn):
            w, wr = wload_in(W, off + 128 * ch, 128)
            ps, pr = U.ps()
            for c in range(DC):
                M(lambda e, c=c: e.matmul(ps[:, 0:ntok], w[:, c, :], xn[:, c, 0:ntok],
                                          start=(c == 0), stop=(c == DC - 1)), [wr, xn_r[c]], [pr], inc=(c == DC - 1))
            yield
            yield from fn(U, ch, ps, pr)

        def l2norm(U, y, yr, tmp, tmpr, out, outr, ntok, scale):
            sqb, sqr = U.sq()
            A(lambda e: e.activation(sqb[:, 0:ntok], y[:, 0:ntok], AF.Square), [yr], [sqr])
            ps, pr = U.ps()
            M(lambda e: e.matmul(ps[:, 0:ntok], ones_bf[:, :], sqb[:, 0:ntok], start=True, stop=True),
              [sqr, ones_r], [pr])
            yield
            A(lambda e: e.activation(tmp[:, 0:ntok], ps[:, 0:ntok], AF.Sqrt, bias=EPS, scale=1.0), [pr], [tmpr])
            yield
            V(lambda e: e.reciprocal(tmp[:, 0:ntok], tmp[:, 0:ntok]), [tmpr], [tmpr])
            V(lambda e: e.scalar_tensor_tensor(out, y[:, 0:ntok], scale, tmp[:, 0:ntok], ALU.mult, ALU.mult),
              [yr, tmpr], [outr])
            yield

        def to_tok(U, src, srcr, dst, dstr, nb):
            pb, pbr = U.pb()
            for tb in range(nb):
                M(lambda e, tb=tb: e.transpose(pb[:, tb * 128:(tb + 1) * 128], src[:, tb * 128:(tb + 1) * 128], ident_bf[:, :]),
                  [srcr] + CALL, [pbr], inc=(tb == nb - 1))
            yield
            A(lambda e: e.copy(dst, pb[:, 0:nb * 128].rearrange("p (b d) -> p b d", b=nb)), [pbr], [dstr])

        def inproj(mode, tok0, ntok, lastx=False):
            nb = ntok // 128
            isP = mode in ("P", "X")
            isX = mode == "X"
            NSEQ, T = (1, ntok) if isP else (NSQ, TS)
            cs_, sn_ = (cosP, sinP) if mode == "P" else ((cosX, sinX) if isX else (cosS, sinS))
            c0 = tok0 if isP else 0
            P.dma("sp", lambda e: e.dma_start(out=csn[:, 0, 0:ntok], in_=cs_[:, c0:c0 + ntok]), writes=[csn_r])
            P.dma("sp", lambda e: e.dma_start(out=csn[:, 1, 0:ntok], in_=sn_[:, c0:c0 + ntok]), writes=[csn_r])
            wab, wabr = wload_in(win, OA, 16)
            psab, psabr = psum()
            for tb in range(nb):
                for c in range(DC):
                    M(lambda e, tb=tb, c=c: e.matmul(psab[:, tb * 16:(tb + 1) * 16], xn[:, c, tb * 128:(tb + 1) * 128],
                                                     wab[:, c, 0:16], start=(c == 0), stop=(c == DC - 1)),
                      [wabr, xn_r[c]], [psabr], inc=(c == DC - 1))
            gates("P" if isX else mode, nb, psab, psabr)
            gens = []

            def add(W, off, nch, fn):
                for ch in range(nch):
                    gens.append(lambda U, ch=ch: chunk_gen(U, W, off, ch, ntok, fn))

            def conv_ep(grp):
                def f(U, ch, ps, pr):
                    cidx = grp * 8 + ch
                    rawb, rawr = U.tf(0)
                    accb, accr = U.tf(1)
                    rb = rawb[:, 0:NSEQ * (3 + T)].rearrange("p (s t) -> p s t", s=NSEQ)
                    A(lambda e: e.copy(rb[:, :, 3:3 + T], ps[:, 0:ntok].rearrange("p (s t) -> p s t", s=NSEQ)),
                      [pr], [rawr])
                    cv_ = cst[:, cidx, 0:NSEQ * 3].rearrange("p (s j) -> p s j", s=NSEQ)
                    V(lambda e: e.tensor_copy(rb[:, :, 0:3], cv_), [cst_r], [rawr])
                    yield
                    V(lambda e: e.tensor_copy(cv_, rb[:, :, T:T + 3]), [rawr], [cst_r])
                    if isX and grp == 0:
                        return
                    av = accb[:, 0:ntok].rearrange("p (s t) -> p s t", s=NSEQ)
                    A(lambda e: e.mul(av, rb[:, :, 0:T], cw[:, cidx, 0:1]), [rawr] + CALL, [accr])
                    yield
                    for j in range(1, 4):
                        V(lambda e, j=j: e.scalar_tensor_tensor(av, rb[:, :, j:j + T], cw[:, cidx, j:j + 1], av,
                                                                ALU.mult, ALU.add), [rawr, accr], [accr])
                    yield
                    if grp == 2:
                        vt_, vtr_ = U.sq()
                        A(lambda e: e.activation(vt_[:, 0:ntok], accb[:, 0:ntok], AF.Silu), [accr], [vtr_])
                        yield
                        yield from to_tok(U, vt_, vtr_, v_tok[:, 0:nb, ch, :], v_tok_r[ch], nb)
                    else:
                        A(lambda e: e.activation(rawb[:, 0:ntok], accb[:, 0:ntok], AF.Silu), [accr], [rawr])
                        yield
                        if grp == 0:
                            yield from l2norm(U, rawb, rawr, accb, accr, qT[:, ch, 0:ntok], qT_r[ch], ntok, 128.0 ** -0.5)
                        else:
                            yield from l2norm(U, rawb, rawr, accb, accr, kT[:, ch, 0:ntok], kT_r[ch], ntok, 1.0)
                            yield from to_tok(U, kT[:, ch, :], kT_r[ch], k_tok[:, 0:nb, ch, :], k_tok_r[ch], nb)
                return f
            if (not isX) or lastx:
                add(win, OQ, 8, conv_ep(0))
            add(win, OK_, 8, conv_ep(1))
            add(win, OV, 8, conv_ep(2))

            def z_ep(U, ch, ps, pr):
                A(lambda e: e.activation(zg[:, ch, 0:ntok], ps[:, 0:ntok], AF.Silu), [pr], [zg_r[ch]])
                yield
            if not isX:
                add(win, OZ, 8, z_ep)

            def ga_ep(U, ch, ps, pr):
                t1b, t1_r = U.tf(0)
                A(lambda e: e.activation(t1b[:, 0:ntok], ps[:, 0:ntok], AF.Sigmoid), [pr], [t1_r])
                yield
                V(lambda e: e.tensor_tensor(zg[:, ch, 0:ntok], zg[:, ch, 0:ntok], t1b[:, 0:ntok], ALU.mult),
                  [t1_r, zg_r[ch]], [zg_r[ch]])
            if not isX:
                add(win, OGA, 8, ga_ep)

            def gb_ep(U, ch, ps, pr):
                A(lambda e: e.activation(gbT[:, ch, 0:ntok], ps[:, 0:ntok], AF.Sigmoid), [pr], [gbT_r[ch]])
                yield
            if not isX:
                add(win, OGB, 8, gb_ep)

            def rope_ep(kind):
                def f(U, ch, ps, pr):
                    yb, ybr = U.tf(0)
                    t2b, t2_r = U.tf(1)
                    t1b, t1_r = yb, ybr
                    A(lambda e: e.copy(yb[:, 0:ntok], ps[:, 0:ntok]), [pr], [ybr])
                    yield
                    p2, p2r = U.ps()
                    M(lambda e: e.matmul(p2[:, 0:ntok], rmT[:, :], yb[:, 0:ntok], start=True, stop=True),
                      [ybr] + CALL, [p2r])
                    V(lambda e: e.tensor_tensor(t2b[:, 0:ntok], yb[:, 0:ntok], csn[:, 0, 0:ntok], ALU.mult),
                      [ybr, csn_r], [t2_r])
                    yield
                    V(lambda e: e.tensor_tensor(t1b[:, 0:ntok], p2[:, 0:ntok], csn[:, 1, 0:ntok], ALU.mult),
                      [p2r, csn_r], [t1_r])
                    if kind == "q":
                        V(lambda e: e.tensor_tensor(qsT[:, ch, 0:ntok], t1b[:, 0:ntok], t2b[:, 0:ntok], ALU.add),
                          [t1_r, t2_r], [qsT_r[ch]])
                    else:
                        V(lambda e: e.tensor_tensor(ksT[:, ch, 128:128 + ntok], t1b[:, 0:ntok], t2b[:, 0:ntok], ALU.add),
                          [t1_r, t2_r], [ksT_r])
                        V(lambda e: e.tensor_tensor(ks32[:, ch, :], t1b[:, ntok - 128:ntok], t2b[:, ntok - 128:ntok], ALU.add),
                          [t1_r, t2_r], [ks32_r])
                    yield
                return f
            if not isX:
                add(win, OQS, 8, rope_ep("q"))
            if (not isX) or lastx:
                add(wks, 0, 4, rope_ep("k"))

            def vs_ep(U, ch, ps, pr):
                yb, ybr = U.tf(0)
                A(lambda e: e.copy(yb[:, 0:ntok], ps[:, 0:ntok]), [pr], [ybr])
                yield
                p2, p2r = U.ps()
                for tb in range(nb):
                    M(lambda e, tb=tb: e.transpose(p2[:, tb * 128:(tb + 1) * 128], yb[:, tb * 128:(tb + 1) * 128], ident[:, :]),
                      [ybr, ident_r], [p2r], inc=(tb == nb - 1))
                yield
                src = p2[:, 0:nb * 128].rearrange("p (b h d) -> p b h d", b=nb, h=2)
                for dup in range(2):
                    V(lambda e, dup=dup: e.tensor_copy(vs_tok[:, 1:1 + nb, 2 * ch:2 * ch + 2, dup * 64:(dup + 1) * 64], src),
                      [p2r], [vs_tok_r])
                V(lambda e: e.tensor_copy(vs32[:, ch * 128:(ch + 1) * 128], p2[:, (nb - 1) * 128:nb * 128]), [p2r], [vs32_r])
            if (not isX) or lastx:
                add(win, OVS, 2, vs_ep)
            run_interleaved(gens, 3)

        def gates(mode, nb, psab, psabr):
            isS = 0 if mode == "P" else 1
            NBK = 2 if mode == "P" else 16
            G = gates_f
            ab = psab[:, 0:nb * 16].rearrange("p (b x) -> p b x", b=nb)

            def g3(i):
                return G[:, i, 0:nb * 8].rearrange("p (b h) -> p b h", b=nb)
            V(lambda e: e.tensor_tensor(g3(0), ab[:, :, 0:8], gv[:, 8:16].unsqueeze(1).broadcast_to([128, nb, 8]), ALU.add),
              [psabr] + CALL, [gts_r])
            A(lambda e: e.activation(g3(0), g3(0), AF.Exp), [gts_r], [gts_r])
            A(lambda e: e.activation(g3(0), g3(0), AF.Ln, bias=1.0), [gts_r], [gts_r])
            V(lambda e: e.tensor_tensor(g3(1), g3(0), negA[:, :].unsqueeze(1).broadcast_to([128, nb, 8]), ALU.mult),
              [gts_r] + CALL, [gts_r])
            A(lambda e: e.activation(g3(2), ab[:, :, 8:16], AF.Sigmoid), [psabr], [gts_r])
            V(lambda e: e.tensor_scalar(g3(3), g3(2), -1.0, None, ALU.mult), [gts_r], [gts_r])
            pg, pgr = psum()
            pgT, pgTr = psum()
            for tb in range(nb):
                M(lambda e, tb=tb: e.matmul(pg[:, tb * 8:(tb + 1) * 8], mk[:, M_TRI + 4 * isS, :], G[:, 1, tb * 8:(tb + 1) * 8],
                                            start=True, stop=True), [gts_r] + CALL, [pgr])
                M(lambda e, tb=tb: e.matmul(pgT[0:8, tb * 128:(tb + 1) * 128], G[:, 1, tb * 8:(tb + 1) * 8], mk[:, M_TRI + 4 * isS, :],
                                            start=True, stop=True), [gts_r] + CALL, [pgTr])
            V(lambda e: e.tensor_copy(G[:, 4, 0:nb * 8], pg[:, 0:nb * 8]), [pgr], [gts_r])
            A(lambda e: e.copy(gcT[:, 0:nb * 128], pgT[0:8, 0:nb * 128]), [pgTr], [gcT_r])
            A(lambda e: e.activation(G[:, 5, 0:nb * 8], G[:, 4, 0:nb * 8], AF.Exp), [gts_r], [gts_r])
            V(lambda e: e.tensor_tensor(G[:, 6, 0:nb * 8], G[:, 2, 0:nb * 8], G[:, 5, 0:nb * 8], ALU.mult), [gts_r], [gts_r])
            lm = sm[:, SM_LASTP:SM_LASTP + 2] if not isS else sm[:, SM_LASTS:SM_LASTS + 16]
            pl, plr = psum()
            rlt, rl_r = tf()
            rl = rlt[:, 0:512].rearrange("p (b x) -> p b x", b=4)
            for tb in range(nb):
                V(lambda e, tb=tb: e.tensor_tensor(
                    rl[:, tb, 0:NBK * 8].rearrange("p (k h) -> p k h", k=NBK),
                    G[:, 4, tb * 8:(tb + 1) * 8].unsqueeze(1).broadcast_to([128, NBK, 8]),
                    lm.unsqueeze(2).broadcast_to([128, NBK, 8]), ALU.mult), [gts_r] + CALL, [rl_r])
                M(lambda e, tb=tb: e.matmul(pl[:, tb * 128:tb * 128 + NBK * 8], ones_f[:, :], rl[:, tb, 0:NBK * 8],
                                            start=True, stop=True), [rl_r] + CALL, [plr])
            for tb in range(nb):
                A(lambda e, tb=tb: e.activation(gl[:, tb, 0:NBK * 8], pl[:, tb * 128:tb * 128 + NBK * 8], AF.Exp), [plr], [gl_r])
            anyc = sm[:, SM_ANYP + isS:SM_ANYP + isS + 1]
            V(lambda e: e.tensor_scalar(G[:, 8, 0:nb * 8], G[:, 4, 0:nb * 8], anyc, None, ALU.mult), [gts_r] + CALL, [gts_r])
            po, por = psum()
            for tb in range(nb):
                M(lambda e, tb=tb: e.matmul(po[:, tb * 8:(tb + 1) * 8], mk[:, M_BLK + 4 * isS, :], G[:, 8, tb * 8:(tb + 1) * 8],
                                            start=True, stop=True), [gts_r] + CALL, [por])
            V(lambda e: e.tensor_tensor(G[:, 7, 0:nb * 8], po[:, 0:nb * 8], G[:, 4, 0:nb * 8], ALU.subtract), [por, gts_r], [gts_r])
            A(lambda e: e.activation(G[:, 7, 0:nb * 8], G[:, 7, 0:nb * 8], AF.Exp), [gts_r], [gts_r])

        def gdn_prep(U, mode, tb, h, d, so=False):
            isS = 0 if mode == "P" else 1
            NL = 6 if mode == "P" else 3
            G = gates_f
            u = U.u
            cols = slice(tb * 128, (tb + 1) * 128)
            gcol = G[:, 4, tb * 8 + h:tb * 8 + h + 1]
            i = u
            d["i"] = i
            psG, psGr = U.ps()
            M(lambda e: e.matmul(psG[:, 0:128], kT[:, h, cols], kT[:, h, cols], start=True, stop=True), [kT_r[h]], [psGr])
            if not so:
                M(lambda e: e.matmul(psG[:, 128:256], kT[:, h, cols], qT[:, h, cols], start=True, stop=True), [kT_r[h], qT_r[h]], [psGr])
            V(lambda e: e.tensor_scalar(gch[u][:, :], gcT[:, cols], ident[0:8, h:h + 1], None, ALU.mult), [gcT_r, ident_r], [gch_r[u]])
            psR, psRr = U.ps()
            M(lambda e: e.matmul(psR[:, 0:128], ones_f[0:8, :], gch[u][:, :], start=True, stop=True), [gch_r[u]] + CALL, [psRr])
            V(lambda e: e.tensor_scalar(vb[i][:, :], v_tok[:, tb, h, :], G[:, 2, tb * 8 + h:tb * 8 + h + 1], None, ALU.mult),
               [v_tok_r[h], gts_r], [vb_r[i]])
            V(lambda e: e.tensor_scalar(kbg[i][:, :], k_tok[:, tb, h, :], G[:, 6, tb * 8 + h:tb * 8 + h + 1], None, ALU.mult),
               [k_tok_r[h], gts_r], [kbg_r[i]])
            yield
            ia = U.n("e1", 2)
            ib = U.n("e1", 2)
            V(lambda e: e.scalar_tensor_tensor(e1[ia][:, :], psR[:, 0:128], gcol, mk[:, M_M1 + 4 * isS, :], ALU.subtract, ALU.add),
              [psRr, gts_r] + CALL, [e1_r[ia]])
            if not so:
                V(lambda e: e.scalar_tensor_tensor(e1[ib][:, :], psR[:, 0:128], gcol, mk[:, M_M2 + 4 * isS, :], ALU.subtract, ALU.add),
                  [psRr, gts_r] + CALL, [e1_r[ib]])
                A(lambda e: e.activation(EG[i][:, :], psR[:, 0:128], AF.Exp), [psRr], [EG_r[i]])
            yield
            A(lambda e: e.activation(Dm[i][:, :], e1[ia][:, :], AF.Exp, scale=-1.0), [e1_r[ia]], [Dm_r[i]])
            if not so:
                A(lambda e: e.activation(DTm[i][:, :], e1[ib][:, :], AF.Exp), [e1_r[ib]], [DTm_r[i]])
                V(lambda e: e.tensor_tensor(qg[i][:, :], qT[:, h, cols], EG[i][:, :], ALU.mult), [qT_r[h], EG_r[i]], [qg_r[i]])
            yield
            V(lambda e: e.scalar_tensor_tensor(Bm[i][:, :], psG[:, 0:128], G[:, 3, tb * 8 + h:tb * 8 + h + 1], Dm[i][:, :],
                                               ALU.mult, ALU.mult), [psGr, gts_r, Dm_r[i]], [Bm_r[i]])
            if not so:
                V(lambda e: e.tensor_tensor(QKD[i][:, :], psG[:, 128:256], DTm[i][:, :], ALU.mult), [psGr, DTm_r[i]], [QKD_r[i]])
            yield
            pb, pbr = U.pb()
            M(lambda e: e.transpose(pb[:, 0:128], Bm[i][:, :], ident_bf[:, :]), [Bm_r[i]] + CALL, [pbr])
            yield
            p0 = U.n("Pm", 3)
            A(lambda e: e.copy(PTm[p0][:, :], pb[:, 0:128]), [pbr], [PTm_r[p0]])
            t0 = U.n("TT", 2)
            V(lambda e, t0=t0: e.tensor_tensor(TT[t0][:, :], pb[:, 0:128], ident_bf[:, :], ALU.add), [pbr] + CALL, [TT_r[t0]])
            yield
            Pc, Pcr = Bm[i], Bm_r[i]
            PTc, PTcr = PTm[p0], PTm_r[p0]
            for k in range(NL - 1):
                ps, psr = U.ps()
                M(lambda e, PTc=PTc, Pc=Pc, ps=ps: e.matmul(ps[:, 0:128], PTc[:, :], Pc[:, :], start=True, stop=True),
                  [PTcr, Pcr], [psr])
                if k < NL - 2:
                    M(lambda e, PTc=PTc, Pc=Pc, ps=ps: e.matmul(ps[:, 128:256], Pc[:, :], PTc[:, :], start=True, stop=True),
                      [PTcr, Pcr], [psr])
                yield
                pn = U.n("Pm", 3)
                A(lambda e, pn=pn, ps=ps: e.copy(Pm[pn][:, :], ps[:, 0:128]), [psr], [Pm_r[pn]])
                if k < NL - 2:
                    A(lambda e, pn=pn, ps=ps: e.copy(PTm[pn][:, :], ps[:, 128:256]), [psr], [PTm_r[pn]])
                Pc, Pcr = Pm[pn], Pm_r[pn]
                PTc, PTcr = PTm[pn], PTm_r[pn]
                yield
                ps2, ps2r = U.ps()
                M(lambda e, Pc=Pc, t0=t0, ps2=ps2: e.matmul(ps2[:, 0:128], Pc[:, :], TT[t0][:, :], start=True, stop=True),
                  [Pcr, TT_r[t0]], [ps2r])
                yield
                t1 = U.n("TT", 2)
                V(lambda e, t0=t0, t1=t1, ps2=ps2: e.tensor_tensor(TT[t1][:, :], ps2[:, 0:128], TT[t0][:, :], ALU.add),
                  [ps2r, TT_r[t0]], [TT_r[t1]])
                t0 = t1
                yield
            d["TT"] = TT[t0]
            d["TTr"] = TT_r[t0]

        def o_finish(U, pso, psor, tb, h, pbsel=None):
            cols = slice(tb * 128, (tb + 1) * 128)
            j = U.u
            sm_ = small[:, 4 * j:4 * j + 4]
            smr = small_r[j]
            A(lambda e: e.activation(otmp[j][:, :], pso, AF.Square, accum_out=sm_[:, 0:1]), [psor], [otmp_r[j], smr])
            A(lambda e: e.activation(sm_[:, 1:2], sm_[:, 0:1], AF.Ln, bias=EPS, scale=1.0 / 128), [smr], [smr])
            A(lambda e: e.activation(sm_[:, 2:3], sm_[:, 1:2], AF.Exp, scale=-0.5), [smr], [smr])
            yield
            V(lambda e: e.scalar_tensor_tensor(ogn[j][:, :], pso, sm_[:, 2:3], gnw, ALU.mult, ALU.mult),
              [psor, smr] + CALL, [ogn_r[j]])
            yield
            pb, pbr = pbsel if pbsel is not None else U.pb()
            M(lambda e: e.transpose(pb[:, 0:128], ogn[j][:, :], ident_bf[:, :]), [ogn_r[j]] + CALL, [pbr])
            yield
            V(lambda e: e.tensor_tensor(otmp[j][:, :], pb[:, 0:128], zg[:, h, cols], ALU.mult), [pbr, zg_r[h]], [otmp_r[j]])
            V(lambda e: e.tensor_tensor(mixT[:, h, cols], mixT[:, h, cols], otmp[j][:, :], ALU.add), [otmp_r[j], mixT_r[h]], [mixT_r[h]])

        def gdn_prompt(nb, so=False):
            G = gates_f

            def unit(U, tb, h):
                d = {}
                yield from gdn_prep(U, "P", tb, h, d, so)
                i, TTc, TTr = d["i"], d["TT"], d["TTr"]
                V(lambda e: e.tensor_scalar(kgt[i][:, :], k_tok[:, tb, h, :], G[:, 7, tb * 8 + h:tb * 8 + h + 1], None, ALU.mult),
                   [k_tok_r[h], gts_r], [kgt_r[i]])
                pu, pur = U.ps()
                M(lambda e: e.matmul(pu[:, 0:128], TTc[:, :], vb[i][:, :], start=True, stop=True), [TTr, vb_r[i]], [pur])
                M(lambda e: e.matmul(pu[:, 128:256], kbg[i][:, :], TTc[:, :], start=True, stop=True), [TTr, kbg_r[i]], [pur])
                yield
                A(lambda e: e.copy(ub[i][:, :], pu[:, 0:128]), [pur], [ub_r[i]])
                A(lambda e: e.copy(wTb[i][:, :], pu[:, 128:256]), [pur], [wTb_r[i]])
                yield
                pw, pwr = U.ps()
                pS, pSr = U.ps()
                pS_i = U.last
                for cc in range(2):
                    r = slice(64 * cc, 64 * cc + 64)
                    M(lambda e, r=r: e.matmul(pw[r, 0:128], wTb[i][:, r], Sbf[:, h, :], start=True, stop=True),
                      [wTb_r[i], Sbf_r[h]], [pwr])
                    yield
                    V(lambda e, r=r: e.tensor_tensor(vnew[i][r, :], ub[i][r, :], pw[r, 0:128], ALU.subtract),
                      [ub_r[i], pwr], [vnew_r[i]])
                    yield
                    if not so:
                        M(lambda e, r=r: e.matmul(pw[r, 128:256], qg[i][:, r], Sbf[:, h, :], start=True, stop=False),
                          [qg_r[i], Sbf_r[h]], [pwr], inc=False)
                        M(lambda e, r=r: e.matmul(pw[r, 128:256], QKD[i][:, r], vnew[i][:, :], start=False, stop=True),
                          [QKD_r[i], vnew_r[i]], [pwr])
                    M(lambda e, r=r, cc=cc, pS=pS: e.matmul(pS[:, 0:128], kgt[i][r, :], vnew[i][r, :], start=True, stop=True),
                      [kgt_r[i], vnew_r[i]], [pSr])
                    yield
                    V(lambda e, cc=cc, pS=pS: e.scalar_tensor_tensor(Sf[:, h, :], Sf[:, h, :], gl[:, tb, cc * 8 + h:cc * 8 + h + 1],
                                                                     pS[:, 0:128], ALU.mult, ALU.add),
                      [Sf_r[h], gl_r, pSr], [Sf_r[h]])
                    A(lambda e: e.copy(Sbf[:, h, :], Sf[:, h, :]), [Sf_r[h]], [Sbf_r[h]])
                    yield
                if not so:
                    yield from o_finish(U, pw[:, 128:256], pwr, tb, h, U.pb_of(pS_i))

            gens = []
            for tb in range(nb):
                for h in range(8):
                    gens.append(lambda U, tb=tb, h=h: unit(U, tb, h))
            V(lambda e: e.memset(vnew[2][:, :], 0.0), [], [xin_r[2], xin_r[3]] + s2res)
            V(lambda e: e.memset(vnew[3][:, :], 0.0), [], TF_r[0:4] + s3res)
            run_interleaved(gens, 4)
            V(lambda e: e.memset(dummy[:, 0:1], 0.0), [], [xin_r[2], xin_r[3], dummy_r] + s2res)
            V(lambda e: e.memset(dummy[:, 1:2], 0.0), [], TF_r[0:4] + [dummy_r] + s3res)

        def gdn_sample():
            G = gates_f
            tb = 0

            def unit(h):
                P.dma("sp", lambda e, h=h: e.dma_start(out=Sf[:, 0:NSQ, :], in_=sgdn[:, h].rearrange("s k v -> k s v")), writes=Sf_r)
                A(lambda e: e.copy(Sbf[:, 0:NSQ, :], Sf[:, 0:NSQ, :]), Sf_r, Sbf_r)
                d = {}
                for _ in gdn_prep(UCS[0], "S", tb, h, d):
                    pass
                i, TTc, TTr = d["i"], d["TT"], d["TTr"]
                pu, pur = psum()
                M(lambda e: e.matmul(pu[:, 0:128], vb[i][:, :], TTc[:, :], start=True, stop=True), [TTr, vb_r[i]], [pur])
                M(lambda e: e.matmul(pu[:, 128:256], kbg[i][:, :], TTc[:, :], start=True, stop=True), [TTr, kbg_r[i]], [pur])
                A(lambda e: e.copy(ub[i][:, :], pu[:, 0:128]), [pur], [ub_r[i]])
                A(lambda e: e.copy(wTb[i][:, :], pu[:, 128:256]), [pur], [wTb_r[i]])
                pw, pwr = psum()
                for s in range(NSQ):
                    c8 = slice(8 * s, 8 * s + 8)
                    M(lambda e, s=s, c8=c8: e.matmul(pw[:, 8 * s:8 * s + 8], Sbf[:, s, :], wTb[i][:, c8], start=True, stop=True),
                      [wTb_r[i], Sbf_r[s]], [pwr], inc=False)
                    M(lambda e, s=s, c8=c8: e.matmul(pw[:, 128 + 8 * s:128 + 8 * s + 8], Sbf[:, s, :], qg[i][:, c8], start=True, stop=True),
                      [qg_r[i], Sbf_r[s]], [pwr], inc=(s == NSQ - 1))
                V(lambda e: e.tensor_tensor(vnf[i][:, :], ub[i][:, :], pw[:, 0:128], ALU.subtract), [ub_r[i], pwr], [vnf_r[i]])
                A(lambda e: e.copy(oq[i][:, :], pw[:, 128:256]), [pwr], [oq_r[i]])
                p2, p2r = psum()
                M(lambda e: e.transpose(p2[:, 0:128], vnf[i][:, :], ident[:, :]), [vnf_r[i], ident_r], [p2r])
                V(lambda e: e.tensor_copy(vnew[i][:, :], p2[:, 0:128]), [p2r], [vnew_r[i]])
                M(lambda e: e.matmul(p2[:, 128:256], vnew[i][:, :], QKD[i][:, :], start=True, stop=True), [vnew_r[i], QKD_r[i]], [p2r])
                V(lambda e: e.tensor_tensor(oq[i][:, :], oq[i][:, :], p2[:, 128:256], ALU.add), [oq_r[i], p2r], [oq_r[i]])
                M(lambda e: e.transpose(p2[:, 256:384], oq[i][:, :], ident[:, :]), [oq_r[i], ident_r], [p2r])
                for _ in o_finish(UCS[0], p2[:, 256:384], p2r, tb, h):
                    pass
                for s in range(NSQ):
                    kk = nxt("kgt", 2)
                    V(lambda e, s=s, kk=kk: e.tensor_scalar(kgt[kk][:, :], k_tok[:, tb, h, :], G[:, 7, h:h + 1],
                                                            sm[:, SM_SEQ + s:SM_SEQ + s + 1], ALU.mult, ALU.mult),
                      [k_tok_r[h], gts_r] + CALL, [kgt_r[kk]])
                    pS, pSr = psum()
                    M(lambda e, kk=kk: e.matmul(pS[:, 0:128], kgt[kk][:, :], vnew[i][:, :], start=True, stop=True),
                      [kgt_r[kk], vnew_r[i]], [pSr])
                    V(lambda e, s=s: e.scalar_tensor_tensor(Sf[:, s, :], Sf[:, s, :], gl[:, 0, s * 8 + h:s * 8 + h + 1], pS[:, 0:128],
                                                            ALU.mult, ALU.add), [Sf_r[s], gl_r, pSr], [Sf_r[s]])
                P.dma("sp", lambda e, h=h: e.dma_start(out=gdns[:, h].rearrange("s k v -> k s v"), in_=Sf[:, 0:NSQ, :]), reads=Sf_r)
            for h in range(8):
                unit(h)

        def swa_block(U, mode, tb, kvh, first_tile, cache_fn=None):
            cols = slice(tb * 128, (tb + 1) * 128)
            isS = mode == "S"
            kbs = []
            if not isS:
                kbs.append((tb, 3 if (first_tile and tb == 0) else 2))
                kbs.append((tb + 1, 0))
            else:
                kbs.append((1, 1))
            u = U.u
            psO, psOr = banks[2 * u], bank_r[2 * u]
            psD, psDr = banks[2 * u + 1], bank_r[2 * u + 1]
            npt = len(kbs)
            for n_, (blk, mi) in enumerate(kbs):
                psS2 = [(banks[2 * u + 4], bank_r[2 * u + 4]), (banks[2 * u + 5], bank_r[2 * u + 5])]
                pi = 2 * u + n_
                for par in range(2):
                    r = slice(64 * par, 64 * par + 64)
                    psS, psSr = psS2[par]
                    out = psS[:, 0:256]
                    if isS:
                        rhs = qsT[r, 2 * kvh:2 * kvh + 2, 0:128].rearrange("p i (s t) -> p s i t", s=NSQ)
                    else:
                        rhs = qsT[r, 2 * kvh:2 * kvh + 2, cols]
                    M(lambda e, r=r, out=out, blk=blk, rhs=rhs: e.matmul(out, ksT[r, kvh, blk * 128:(blk + 1) * 128], rhs,
                                                                          start=True, stop=True),
                      [ksT_r, qsT_r[2 * kvh], qsT_r[2 * kvh + 1]], [psSr])
                    yield
                    A(lambda e, pi=pi, psS=psS, par=par: e.activation(pTx[pi][:, par * 256:(par + 1) * 256], psS[:, 0:256], AF.Exp, scale=0.125),
                      [psSr], [pTx_r[pi]])
                yield
                if isS:
                    V(lambda e, pi=pi: e.tensor_tensor(pTx[pi][:, :], pTx[pi][:, :], mkS[:, :], ALU.mult),
                      [pTx_r[pi]] + CALL, [pTx_r[pi]])
                else:
                    pv = pTx[pi][:, :].rearrange("p (g q) -> p g q", g=4)
                    V(lambda e, pv=pv, mi=mi: e.tensor_tensor(pv, pv, mkb[:, mi, :].unsqueeze(1).broadcast_to([128, 4, 128]), ALU.mult),
                       [pTx_r[pi]] + CALL, [pTx_r[pi]])
                yield
                last = (n_ == npt - 1)
                M(lambda e, pi=pi, blk=blk, n_=n_, last=last: e.matmul(psO[:, :], vs_tok[:, blk, kvh, :], pTx[pi][:, :],
                                                                       start=(n_ == 0), stop=last),
                  [vs_tok_r, pTx_r[pi]], [psOr])
                M(lambda e, pi=pi, n_=n_: e.matmul(psD[:, :], ones_bf[:, :], pTx[pi][:, :], start=(n_ == 0), stop=False),
                  [pTx_r[pi], ones_r], [psDr])
            if cache_fn is not None:
                cache_fn(kvh, psO, psOr, psD, psDr)
            esk = esinkS if isS else esink
            M(lambda e: e.matmul(psD[:, :], ones_bf[0:1, :], esk[0:1, kvh, :], start=False, stop=True), [ones_r] + CALL, [psDr])
            yield
            rden, rden_r = U.tf(0)
            osw, osw_r = U.tf(1)
            V(lambda e: e.reciprocal(rden[:, 0:512], psD[:, :]), [psDr], [rden_r])
            V(lambda e: e.tensor_tensor(osw[:, 0:512], psO[:, :], rden[:, 0:512], ALU.mult), [psOr, rden_r], [osw_r])
            yield
            for par in range(2):
                r = slice(64 * par, 64 * par + 64)
                if isS:
                    src = osw[r, par * 256:(par + 1) * 256].rearrange("p (s i t) -> p i s t", s=NSQ, i=2)
                    dst = mixT[r, 2 * kvh:2 * kvh + 2, 0:128].rearrange("p i (s t) -> p i s t", s=NSQ)
                    gb_ = gbT[r, 2 * kvh:2 * kvh + 2, 0:128].rearrange("p i (s t) -> p i s t", s=NSQ)
                else:
                    src = osw[r, par * 256:(par + 1) * 256].rearrange("p (i q) -> p i q", i=2)
                    dst = mixT[r, 2 * kvh:2 * kvh + 2, cols]
                    gb_ = gbT[r, 2 * kvh:2 * kvh + 2, cols]
                V(lambda e, src=src, dst=dst, gb_=gb_: e.tensor_tensor(dst, src, gb_, ALU.mult),
                  [osw_r, gbT_r[2 * kvh], gbT_r[2 * kvh + 1]], [mixT_r[2 * kvh], mixT_r[2 * kvh + 1]])

        def swa_prompt(nb, first_tile, ntok, carry_only=False):
            V(lambda e: e.tensor_copy(ksT[:, :, 0:128], kcar[:, :, :]), [kcar_r], [ksT_r])
            gens = []
            for tb in range(0 if carry_only else nb):
                for kvh in range(4):
                    gens.append(lambda U, tb=tb, kvh=kvh: swa_block(U, "P", tb, kvh, first_tile))
            run_interleaved(gens, 2)
            V(lambda e: e.tensor_copy(kcar[:, :, :], ksT[:, :, ntok:ntok + 128]), [ksT_r], [kcar_r])
            V(lambda e: e.tensor_copy(vs_tok[:, 0, :, :], vs_tok[:, nb, :, :]), [vs_tok_r], [vs_tok_r])

        mcb = sb("mcb", [128, TS], BF16)
        mcb_r = Res()

        def swa_sample():
            V(lambda e: e.tensor_copy(mcb[:, :], sm[:, SM_MC:SM_MC + TS]), CALL, [mcb_r])
            for s in range(NSQ):
                for dup in range(2):
                    P.dma("sp", lambda e, s=s, dup=dup: e.dma_start(out=kcs[:, :, dup, :], in_=ck[s].rearrange("r (k d) -> r k d", k=4)), writes=[kcs_r])
                p2, p2r = psum()
                for kvh in range(4):
                    M(lambda e, kvh=kvh, p2=p2: e.transpose(p2[:, kvh * 128:(kvh + 1) * 128], kcs[:, kvh, :, :].rearrange("p a d -> p (a d)"), ident[:, :]),
                      [kcs_r, ident_r], [p2r], inc=(kvh == 3))
                ki = nxt("kcT", 2)
                A(lambda e, ki=ki, p2=p2: e.copy(kcT[ki][:, :, :], p2[:, :].rearrange("p (k r) -> p k r", k=4)), [p2r], [kcT_r[ki]])
                psS2 = [psum(), psum()]
                for par in range(2):
                    psS, psSr = psS2[par]
                    r = slice(64 * par, 64 * par + 64)
                    for kvh in range(4):
                        out = psS[:, kvh * 16:kvh * 16 + 16]
                        M(lambda e, r=r, out=out, s=s, kvh=kvh, ki=ki: e.matmul(out, kcT[ki][r, kvh, :],
                                                                            qsT[r, 2 * kvh:2 * kvh + 2, 8 * s:8 * s + 8], start=True, stop=True),
                          [kcT_r[ki], qsT_r[2 * kvh], qsT_r[2 * kvh + 1]], [psSr], inc=(kvh == 3))
                    dst = pTc[:, :, :].rearrange("p k (par s x) -> p par s k x", par=2, s=NSQ)[:, par, s]
                    A(lambda e, dst=dst, psS=psS: e.activation(dst, psS[:, 0:64].rearrange("p (k x) -> p k x", k=4), AF.Exp, scale=0.125),
                      [psSr], Sbf_r)
            for kvh in range(4):
                pv = pTc[:, kvh, :].rearrange("p (x t) -> p x t", t=TS)
                V(lambda e, pv=pv: e.tensor_tensor(pv, pv, mcb[:, :].unsqueeze(1).broadcast_to([128, 64, TS]), ALU.mult),
                  pTc_r[kvh] + [mcb_r], pTc_r[kvh])

            def cache_fn(kvh, psO, psOr, psD, psDr):
                M(lambda e: e.matmul(psD[:, :], ones_bf[:, :], pTc[:, kvh, :], start=False, stop=False), pTc_r[kvh] + [ones_r], [psDr])
                for s in range(NSQ):
                    for par in range(2):
                        c16 = slice(par * 256 + s * 16, par * 256 + s * 16 + 16)
                        for half in range(2):
                            lastm = (s == NSQ - 1 and par == 1 and half == 1)
                            M(lambda e, c16=c16, s=s, half=half, lastm=lastm: e.matmul(
                                psO[64 * half:64 * half + 64, c16], vcall[:, s, kvh, :], pTc[:, kvh, c16], start=False, stop=True, skip_group_check=True),
                              [vcall_r[s]] + pTc_r[kvh], [psOr], inc=lastm)
            for kvh in range(4):
                for _ in swa_block(UCS[0], "S", 0, kvh, False, cache_fn):
                    pass

        def load_cache():
            for s in range(NSQ):
                vt, vtr = tf()
                P.dma("sp", lambda e, s=s, vt=vt: e.dma_start(out=vt[:, 0:256], in_=cv[s]), writes=[vtr])
                V(lambda e, s=s, vt=vt: e.tensor_copy(vcall[:, s, :, :].rearrange("p k d -> p (k d)"), vt[:, 0:256]), [vtr], [vcall_r[s]])

        def load_sconv():
            P.dma("sp", lambda e: e.dma_start(out=xin[0:NSQ * 3, 0, :], in_=sconv[:, 0:1024]), writes=[xin_r[0]])
            P.dma("sp", lambda e: e.dma_start(out=xin[0:NSQ * 3, 1, :], in_=sconv[:, 1024:2048]), writes=[xin_r[1]])
            P.dma("sp", lambda e: e.dma_start(out=xin[0:NSQ * 3, 2, :], in_=sconv[:, 2048:3072]), writes=[xin_r[2]])
            for g in range(6):
                p2, p2r = psum()
                for ci in range(4):
                    cidx = g * 4 + ci
                    M(lambda e, cidx=cidx, ci=ci, p2=p2: e.transpose(p2[:, ci * 128:ci * 128 + NSQ * 3],
                                                              xin[0:NSQ * 3, cidx // 8, (cidx % 8) * 128:(cidx % 8 + 1) * 128], ident[0:NSQ * 3, 0:NSQ * 3]),
                      [xin_r[cidx // 8], ident_r], [p2r], inc=(ci == 3))
                V(lambda e, g=g, p2=p2: e.tensor_copy(cst[:, g * 4:(g + 1) * 4, 0:NSQ * 3],
                                               p2[:, :].rearrange("p (c x) -> p c x", c=4)[:, :, 0:NSQ * 3]), [p2r], [cst_r])

        def store_conv(dst, nrow):
            for g in range(6):
                p2, p2r = psum()
                for ci in range(4):
                    cidx = g * 4 + ci
                    M(lambda e, cidx=cidx, ci=ci, p2=p2: e.transpose(p2[0:nrow, ci * 128:(ci + 1) * 128], cst[:, cidx, 0:nrow], ident[:, :]),
                      [cst_r, ident_r], [p2r], inc=(ci == 3))
                A(lambda e, g=g, p2=p2: e.copy(xin[0:nrow, g // 2, (g % 2) * 512:(g % 2 + 1) * 512], p2[0:nrow, :]), [p2r], [xin_r[g // 2]])
            for q in range(3):
                P.dma("sp", lambda e, q=q: e.dma_start(out=dst[:, q * 1024:(q + 1) * 1024], in_=xin[0:nrow, q, :]), reads=[xin_r[q]])

        def store_kv(kdst, vdst):
            p2, p2r = psum()
            for kvh in range(4):
                M(lambda e, kvh=kvh: e.transpose(p2[:, kvh * 128:(kvh + 1) * 128], ks32[:, kvh, :], ident[:, :]),
                  [ks32_r, ident_r], [p2r], inc=(kvh == 3))
            A(lambda e: e.copy(knew[:, :, :], p2[:, :].rearrange("p (k x) -> p k x", k=4)[:, :, 0:64]), [p2r], [knew_r])
            P.dma("sp", lambda e: e.dma_start(out=kdst, in_=knew[:, :, :]), reads=[knew_r])
            P.dma("sp", lambda e: e.dma_start(out=vdst, in_=vs32[:, :].rearrange("p (k d) -> p k d", k=4)), reads=[vs32_r])

        def outproj(ntok):
            for c in range(DC):
                wo, wor = wload_out(wout, DC, 128 * c, 128)
                po, por = psum()
                for j in range(DC):
                    M(lambda e, j=j, po=po, wo=wo: e.matmul(po[:, 0:ntok], wo[:, j, :], mixT[:, j, 0:ntok],
                                                            start=(j == 0), stop=(j == DC - 1)),
                      [wor, mixT_r[j]], [por], inc=(j == DC - 1))
                V(lambda e, c=c, po=po: e.tensor_tensor(xT[:, c, 0:ntok], po[:, 0:ntok], xT[:, c, 0:ntok], ALU.add),
                  [por, xT_r[c]], [xT_r[c]])

        def tile_pass(mode, xd, yd, tok0, ntok, first, last):
            nb = ntok // 128
            load_x(xd, tok0, ntok)
            rmsnorm(0, xn, xn_r, ntok)
            ffn(w1i, w1o, ntok)
            if stages >= 2:
                rmsnorm(1, xn, xn_r, ntok)
                if mode == "X":
                    inproj("X", tok0, ntok, lastx=last)
                    if last:
                        swa_prompt(nb, False, ntok, carry_only=True)
                    gdn_prompt(nb, so=True)
                    return
                if MIX & 1:
                    inproj(mode, tok0, ntok)
                if mode == "P":
                    if MIX & 2:
                        swa_prompt(nb, first, ntok)
                    if MIX & 4:
                        gdn_prompt(nb)
                else:
                    if MIX & 8:
                        swa_sample()
                    if MIX & 16:
                        gdn_sample()
                if MIX & 32:
                    outproj(ntok)
            if stages >= 3:
                rmsnorm(2, xn, xn_r, ntok)
                ffn(w2i, w2o, ntok)
            rmsnorm(3, xT, xT_r, ntok)
            store_y(yd, tok0, ntok)

        NX = LPRE // 512
        for t in range(NX):
            tile_pass("X", xpre, None, t * 512, 512, False, t == NX - 1)
        NT = LP // 512
        for t in range(NT):
            tile_pass("P", xp, yp, t * 512, 512, t == 0, t == NT - 1)
        if stages >= 2:
            if MIX & 64:
                store_conv(convp, 3)
            if MIX & 128:
                store_kv(kp.rearrange("r (k d) -> r k d", k=4), vp.rearrange("r (k d) -> r k d", k=4))
            if MIX & 256:
                P.dma("sp", lambda e: e.dma_start(out=gdnp.rearrange("h k v -> k h v"), in_=Sf[:, 0:8, :]), reads=Sf_r[0:8])
            if MIX & 512:
                load_sconv()
            if MIX & 1024:
                load_cache()
            if MIX & 2048:
                P.dma("sp", lambda e: e.dma_start(out=ksd[:, 0:120, :], in_=ck[:, 8:128, :]))
                P.dma("sp", lambda e: e.dma_start(out=vsd[:, 0:120, :], in_=cv[:, 8:128, :]))
        tile_pass("S", xs, ys, 0, NS_TOK, False, True)
        if stages >= 2:
            if MIX & 4096:
                store_conv(convs, NSQ * 3)
            if MIX & 8192:
                store_kv(ksd[:, 120:128, :].rearrange("s t (k d) -> s t k d", k=4), vsd[:, 120:128, :].rearrange("s t (k d) -> s t k d", k=4))
        P.finish()

        with nc.Block() as block:
            @block.tensor
            def _(eng):
                P.replay(eng, "pe")

            @block.scalar
            def _(eng):
                P.replay(eng, "act")

            @block.vector
            def _(eng):
                P.replay(eng, "dve")

            @block.gpsimd
            def _(eng):
                P.replay(eng, "pool")

            @block.sync
            def _(eng):
                P.replay(eng, "sp")
    return nc


def host_consts(NSQ, TS):
    f = np.float32
    ident = np.eye(128, dtype=f)
    p = np.arange(128)
    rm = np.zeros((128, 128), f)
    for m in range(128):
        d = m % 64
        if d < 8:
            rm[m, m + 8] = -1.0
        elif d < 16:
            rm[m, m - 8] = 1.0
    rmT = np.ascontiguousarray(rm.T)
    mk = np.zeros((128, 8, 128), f)
    for isS, C in ((0, 64), (1, TS)):
        same = (p[:, None] // C) == (p[None, :] // C)
        mk[:, 0 + 4 * isS, :] = (same & (p[:, None] <= p[None, :])).astype(f)
        mk[:, 1 + 4 * isS, :] = np.where(same & (p[:, None] > p[None, :]), 0.0, BIG)
        mk[:, 2 + 4 * isS, :] = np.where(same & (p[None, :] >= p[:, None]), 0.0, -BIG)
        mk[:, 3 + 4 * isS, :] = same.astype(f)
    mkb = np.zeros((128, 3, 128), f)
    mprev = (p[:, None] > p[None, :]).astype(f)
    mkb[:, 0, :] = (p[:, None] <= p[None, :]).astype(f)
    mkb[:, 1, :] = (((p[:, None] // TS) == (p[None, :] // TS)) & (p[:, None] <= p[None, :])).astype(f)
    mkb[:, 2, :] = (p[:, None] > p[None, :]).astype(f)
    mkS = np.zeros((128, 2, NSQ, 2, TS), f)
    for k in range(128):
        ks_, kt_ = k // TS, k % TS
        mkS[k, :, ks_, :, kt_:] = 1.0
    mkS = np.ascontiguousarray(mkS.reshape(128, 512))
    sm = np.zeros((128, 64), f)
    for b in range(2):
        sm[64 * b + 63, SM_LASTP + b] = 1.0
    for s in range(NSQ):
        sm[TS * s + TS - 1, SM_LASTS + s] = 1.0
        sm[TS * s:TS * (s + 1), SM_SEQ + s] = 1.0
    sm[:, SM_ANYP] = (p % 64 == 63).astype(f)
    sm[:, SM_ANYS] = (p % TS == TS - 1).astype(f)
    for t in range(TS):
        sm[:, SM_MC + t] = (p > t).astype(f)
    inv = (np.float32(500000.0) ** (-(np.arange(0, 16, 2, dtype=f)) / np.float32(16))).astype(f)

    def tables(pos):
        ang = (pos.astype(f)[:, None] * inv[None, :]).astype(f)
        c = np.ones((128, len(pos)), f)
        s = np.zeros((128, len(pos)), f)
        for pp in range(128):
            d = pp % 64
            if d < 16:
                c[pp] = np.cos(ang[:, d % 8])
                s[pp] = np.sin(ang[:, d % 8])
        return np.ascontiguousarray(c), np.ascontiguousarray(s)
    cosS, sinS = tables(np.tile(16384 + np.arange(TS), NSQ))
    return dict(ident=ident, rmT=rmT, mk=mk, mkb=mkb, mkS=mkS, sm=sm, cosS=cosS, sinS=sinS), tables, mprev


def pvec(v):
    return np.ascontiguousarray(v.reshape(DC, 128).T)


_CACHE = {}
STAGES = 99
MIXF = 0xFFFF


def kernel(x_prompt, x_sample, state_conv, state_gdn, cache_swa_k, cache_swa_v,
           norm_ffn1, w_ffn1_in, w_ffn1_out, norm_mix, w_in, conv_w, gdn_a_log, gdn_dt_bias,
           gdn_norm, swa_sinks, w_out, norm_ffn2, w_ffn2_in, w_ffn2_out, norm_final):
    f = np.float32
    A_ = lambda a: np.asarray(a, f)
    x_prompt = A_(x_prompt)
    x_sample = A_(x_sample)
    B, L, _ = x_prompt.shape
    NSQ = x_sample.shape[0] // NCORES
    T = x_sample.shape[1]
    LP = L // 2
    key = (L, STAGES, MIXF)
    if key not in _CACHE:
        _CACHE[key] = build(LP, LP, NSQ, T, stages=STAGES, MIX=MIXF)
    nc = _CACHE[key]
    nrm = np.ascontiguousarray(np.stack([pvec(A_(norm_ffn1)[0]), pvec(A_(norm_mix)[0]),
                                         pvec(A_(norm_ffn2)[0]), pvec(A_(norm_final))], axis=1))
    hc, tables, mprev = host_consts(NSQ, T)
    tabs = [tables(np.arange(hf * LP, (hf + 1) * LP)) for hf in range(2)]
    zx = np.zeros((LP, D), f)
    zm = np.zeros((128, 128), f)
    win = np.ascontiguousarray(A_(w_in)[0])
    wks = np.zeros((D, 512), f)
    for kvh in range(4):
        for dup in range(2):
            wks[:, kvh * 128 + dup * 64:kvh * 128 + (dup + 1) * 64] = win[:, OKS + kvh * 64:OKS + (kvh + 1) * 64]
    cw = np.ascontiguousarray(A_(conv_w)[0].reshape(4, 24, 128).transpose(2, 1, 0))
    gv = np.concatenate([A_(gdn_a_log)[0], A_(gdn_dt_bias)[0], A_(gdn_norm)[0], A_(swa_sinks)[0]])[None, :].astype(f)
    shared = {
        "w1i": np.ascontiguousarray(A_(w_ffn1_in)[0]), "w1o": np.ascontiguousarray(A_(w_ffn1_out)[0]),
        "w2i": np.ascontiguousarray(A_(w_ffn2_in)[0]), "w2o": np.ascontiguousarray(A_(w_ffn2_out)[0]),
        "nrm": nrm, "win": win, "wks": wks, "wout": np.ascontiguousarray(A_(w_out)[0]),
        "cw": cw, "gv": np.ascontiguousarray(gv),
    }
    shared.update(hc)
    sc, sg_, ck_, cv_ = A_(state_conv)[0], A_(state_gdn)[0], A_(cache_swa_k)[0], A_(cache_swa_v)[0]
    in_maps = []
    for c in range(NCORES):
        m = dict(shared)
        sl = slice(c * NSQ, (c + 1) * NSQ)
        sq_, hf = c // 2, c % 2
        m["xp"] = np.ascontiguousarray(x_prompt[sq_, hf * LP:(hf + 1) * LP])
        m["xpre"] = np.ascontiguousarray(x_prompt[sq_, 0:LP]) if hf == 1 else zx
        m["cosP"], m["sinP"] = tabs[hf]
        m["cosX"], m["sinX"] = tabs[0]
        m["mprev0"] = mprev if hf == 1 else zm
        m["xs"] = np.ascontiguousarray(x_sample[sl].reshape(NSQ * T, D))
        m["sconv"] = np.ascontiguousarray(sc[sl].reshape(NSQ * 3, 3072))
        m["sgdn"] = np.ascontiguousarray(sg_[sl])
        m["ck"] = np.ascontiguousarray(ck_[sl].reshape(NSQ, 128, 256))
        m["cv"] = np.ascontiguousarray(cv_[sl].reshape(NSQ, 128, 256))
        in_maps.append(m)
    res = run_bass_kernel_spmd(nc, in_maps, core_ids=list(range(NCORES)))
    R = res.results
    cat = lambda k, shp: np.concatenate([np.asarray(R[c][k], f).reshape(shp) for c in range(NCORES)], axis=0)
    stk = lambda k, shp: np.stack([np.asarray(R[2 * b + 1][k], f).reshape(shp) for b in range(B)], axis=0)
    y_prompt = np.stack([np.concatenate([np.asarray(R[2 * b]["yp"], f), np.asarray(R[2 * b + 1]["yp"], f)], axis=0) for b in range(B)], axis=0)
    y_sample = cat("ys", (NSQ, T, D))
    conv_p = stk("convp", (3, 3072))[None]
    conv_s = cat("convs", (NSQ, 3, 3072))[None]
    gdn_p = stk("gdnp", (8, 128, 128))[None]
    gdn_s = cat("gdns", (NSQ, 8, 128, 128))[None]
    k_p = stk("kp", (128, 4, 64))[None]
    k_s = cat("ksd", (NSQ, 128, 4, 64))[None]
    v_p = stk("vp", (128, 4, 64))[None]
    v_s = cat("vsd", (NSQ, 128, 4, 64))[None]
    return (y_prompt, y_sample, conv_p, conv_s, gdn_p, gdn_s, k_p, k_s, v_p, v_s)
```

```python
import math
from contextlib import ExitStack
import numpy as np
import concourse.bass as bass
import concourse.mybir as mybir
from concourse.bass_utils import run_bass_kernel_spmd

F32 = mybir.dt.float32
BF16 = mybir.dt.bfloat16
AF = mybir.ActivationFunctionType
ALU = mybir.AluOpType
AX = mybir.AxisListType

D = 1024
DC = 8
DFF = 2816
FC = 22
DIN = 7696
EPS = 1e-6
NCORES = 8
SLOT = 1408


class Res:
    __slots__ = ("w", "r", "x")

    def __init__(self):
        self.w = None
        self.r = {}
        self.x = False


def RL(n):
    return [Res() for _ in range(n)]


class EngS:
    def __init__(self, name, sem):
        self.name = name
        self.sem = sem
        self.count = 0
        self.ops = []
        self.waited = {}


class Prog:
    def __init__(self, sems, dsems):
        self.E = {n: EngS(n, sems[n]) for n in ("pe", "act", "dve", "pool", "sp")}
        self.dsems = dsems
        self.dcount = [0] * len(dsems)
        self.dnext = 0

    def _deps(self, reads, writes, own=None):
        need = {}

        def add(k, v):
            if need.get(k, 0) < v:
                need[k] = v

        for r in reads:
            if r.w:
                add(*r.w)
            if r.x:
                for k, v in r.r.items():
                    if k != own:
                        add(k, v)
        for w in writes:
            if w.w:
                add(*w.w)
            for k, v in w.r.items():
                add(k, v)
        return need

    def _waits(self, e, need, skip_own=False):
        for k, v in need.items():
            if skip_own and k == e.name:
                continue
            if e.waited.get(k, 0) < v:
                e.waited[k] = v
                e.ops.append(("wait", k, v))

    def op(self, eng, fn, reads=(), writes=(), inc=True):
        e = self.E[eng]
        need = self._deps(reads, writes, own=e.name)
        self._waits(e, need, skip_own=(eng == "pe"))
        e.ops.append(("op", fn, inc))
        idx = e.count + 1
        if inc:
            e.count = idx
        for r in reads:
            if r.r.get(e.name, 0) < idx:
                r.r[e.name] = idx
        for w in writes:
            w.w = (e.name, idx)
            w.r = {}

    def dma(self, eng, fn, reads=(), writes=()):
        e = self.E[eng]
        need = self._deps(reads, writes)
        j = self.dnext
        self.dnext = (j + 1) % len(self.dsems)
        key = ("d", j)
        if self.dcount[j] > 0:
            v = 16 * self.dcount[j]
            if need.get(key, 0) < v:
                need[key] = v
        self._waits(e, need)
        self.dcount[j] += 1
        val = 16 * self.dcount[j]
        e.ops.append(("dma", fn, key))
        for r in reads:
            if r.r.get(key, 0) < val:
                r.r[key] = val
        for w in writes:
            w.w = (key, val)
            w.r = {}

    def semof(self, key):
        if isinstance(key, tuple):
            return self.dsems[key[1]]
        return self.E[key].sem

    def replay(self, eng, name):
        e = self.E[name]
        for item in e.ops:
            if item[0] == "wait":
                eng.wait_ge(self.semof(item[1]), item[2])
            elif item[0] == "op":
                ins = item[1](eng)
                if item[2]:
                    ins.then_inc(e.sem, 1)
            else:
                ins = item[1](eng)
                ins.then_inc(self.semof(item[2]), 16)

    def finish(self):
        e = self.E["sp"]
        for j, c in enumerate(self.dcount):
            if c > 0:
                e.ops.append(("wait", ("d", j), 16 * c))
        for n in ("pe", "act", "dve", "pool"):
            if self.E[n].count > 0:
                e.ops.append(("wait", n, self.E[n].count))


OQ, OK_, OV, OZ, OA, OB, OQS, OKS, OVS, OGA, OGB = 0, 1024, 2048, 3072, 4096, 4104, 4112, 5136, 5392, 5648, 6672
BIG = 30000.0
M_TRI, M_M1, M_M2, M_BLK = 0, 1, 2, 3
SM_LASTP, SM_LASTS, SM_ANYP, SM_ANYS, SM_SEQ, SM_MC = 0, 2, 18, 19, 20, 36


def build(LP, LPRE, NSQ=16, TS=8, stages=99, MIX=0xFFFF, GW=2):
    NS_TOK = NSQ * TS
    nc = bass.Bass("TRN2", target_bir_lowering=False)

    def din(name, shape, dt=F32):
        return nc.dram_tensor(name, list(shape), dt, kind="ExternalInput").ap()

    def dout(name, shape, dt=F32):
        return nc.dram_tensor(name, list(shape), dt, kind="ExternalOutput").ap()

    xp = din("xp", [LP, D])
    xpre = din("xpre", [LPRE, D])
    cosX = din("cosX", [128, LPRE])
    sinX = din("sinX", [128, LPRE])
    mprev0_d = din("mprev0", [128, 128])
    xs = din("xs", [NS_TOK, D])
    w1i = din("w1i", [D, 2 * DFF])
    w1o = din("w1o", [DFF, D])
    w2i = din("w2i", [D, 2 * DFF])
    w2o = din("w2o", [DFF, D])
    nrm = din("nrm", [128, 4, DC])
    ident_d = din("ident", [128, 128])
    win = din("win", [D, DIN])
    wks = din("wks", [D, 512])
    wout = din("wout", [D, D])
    cw_d = din("cw", [128, 24, 4])
    gv_d = din("gv", [1, 160])
    cosP = din("cosP", [128, LP])
    sinP = din("sinP", [128, LP])
    cosS = din("cosS", [128, NS_TOK])
    sinS = din("sinS", [128, NS_TOK])
    rmT_d = din("rmT", [128, 128])
    mk_d = din("mk", [128, 8, 128])
    mkb_d = din("mkb", [128, 3, 128])
    mkS_d = din("mkS", [128, 512])
    sm_d = din("sm", [128, 64])
    sconv = din("sconv", [NSQ * 3, 3072])
    sgdn = din("sgdn", [NSQ, 8, 128, 128])
    ck = din("ck", [NSQ, 128, 256])
    cv = din("cv", [NSQ, 128, 256])
    yp = dout("yp", [LP, D])
    ys = dout("ys", [NS_TOK, D])
    convp = dout("convp", [3, 3072])
    convs = dout("convs", [NSQ * 3, 3072])
    gdnp = dout("gdnp", [8, 128, 128])
    gdns = dout("gdns", [NSQ, 8, 128, 128])
    kp = dout("kp", [128, 256])
    ksd = dout("ksd", [NSQ, 128, 256])
    vp = dout("vp", [128, 256])
    vsd = dout("vsd", [NSQ, 128, 256])

    es = ExitStack()
    with es:
        es.enter_context(nc.allow_low_precision("bf16 matmul operands, fp32 accumulate"))
        sems = {n: es.enter_context(nc.semaphore("sem_" + n)) for n in ("pe", "act", "dve", "pool", "sp")}
        dsems = [es.enter_context(nc.semaphore("dsem%d" % i)) for i in range(24)]
        P = Prog(sems, dsems)

        def sb(name, shape, dt=F32):
            return es.enter_context(nc.sbuf_tensor(name, list(shape), dt))

        def A(fn, reads, writes):
            P.op("act", fn, reads, writes)

        def V(fn, reads, writes):
            P.op("dve", fn, reads, writes)

        def M(fn, reads, writes, inc=True):
            P.op("pe", fn, reads, writes, inc)

        def G_(fn, reads, writes):
            P.op("pool", fn, reads, writes)

        xin = sb("xin", [128, 4, D])
        xin_r = RL(4)
        xT = sb("xT", [128, DC, 512])
        xT_r = RL(DC)
        xn = sb("xn", [128, DC, 512], BF16)
        xn_r = RL(DC)
        hT = sb("hT", [128, FC, 512], BF16)
        hT_r = RL(FC)
        NSLOT = 2
        NSTG = 2
        wslots = [sb("wslot%d" % i, [128, SLOT], BF16) for i in range(NSLOT)]
        wslot_r = RL(NSLOT)
        wnext = [0]
        wstg = [sb("wstg%d" % i, [128, SLOT]) for i in range(NSTG)]
        wstg_r = RL(NSTG)
        snext = [0]
        sq = [sb("sq%d" % i, [128, 512], BF16) for i in range(2)]
        sq_r = RL(2)
        TF = [sb("tf%d" % i, [128, 520]) for i in range(6)]
        TF_r = RL(6)
        tfn = [0]

        def tf():
            i = tfn[0]
            tfn[0] = (i + 1) % 6
            return TF[i], TF_r[i]
        ident = sb("ident_sb", [128, 128])
        ident_bf = sb("ident_bf", [128, 128], BF16)
        ones_bf = sb("ones_bf", [128, 128], BF16)
        ones_f = sb("ones_f", [128, 128])
        nrm_sb = sb("nrm_sb", [128, 4, DC])
        qT = hT[:, 0:8, :]
        qT_r = hT_r[0:8]
        kT = hT[:, 8:16, :]
        kT_r = hT_r[8:16]
        ksT = hT[:, 16:22, :].rearrange("p j n -> p (j n)")[:, 0:2560].rearrange("p (k n) -> p k n", k=4)
        ksT_r = Res()
        for _j in range(16, 22):
            hT_r[_j] = ksT_r
        mixT = xn
        mixT_r = xn_r
        k_tok = sb("k_tok", [128, 4, 8, 128], BF16)
        k_tok_r = RL(8)
        v_tok = sb("v_tok", [128, 4, 8, 128], BF16)
        v_tok_r = RL(8)
        zg = xin[:, 0:2, :].rearrange("p b d -> p (b d)").bitcast(BF16).rearrange("p (h n) -> p h n", h=8)
        zg_r = [xin_r[h // 4] for h in range(8)]
        vcall = xin[:, 2:4, :].rearrange("p b d -> p (b d)").bitcast(BF16).rearrange("p (s k d) -> p s k d", s=16, k=4)
        vcall_r = [xin_r[2 + s // 8] for s in range(16)]
        gbT = sb("gbT", [128, 8, 512], BF16)
        gbT_r = RL(8)
        qsT = sb("qsT", [128, 8, 512], BF16)
        qsT_r = RL(8)
        vs_tok = sb("vs_tok", [128, 5, 4, 128], BF16)
        vs_tok_r = Res()
        kcar = sb("kcar", [128, 4, 128], BF16)
        kcar_r = Res()
        csn = sb("csn", [128, 2, 512])
        csn_r = Res()
        ks32 = sb("ks32", [128, 4, 128])
        ks32_r = Res()
        vs32 = sb("vs32", [128, 256])
        vs32_r = Res()
        cst = sb("cst", [128, 24, 48])
        cst_r = Res()
        cw = sb("cw_sb", [128, 24, 4])
        gv = sb("gv_sb", [128, 160])
        gnw = gv[:, 16:144]
        negA = sb("negA", [128, 8])
        esink = sb("esink", [1, 4, 512], BF16)
        esinkS = sb("esinkS", [1, 4, 512], BF16)
        mkS = sb("mkS_sb", [128, 512], BF16)
        esink_f = sb("esink_f", [1, 16])
        rmT = sb("rmT_sb", [128, 128])
        mk = sb("mk_sb", [128, 8, 128])
        mkb = sb("mkb_sb", [128, 4, 128], BF16)
        sm = sb("sm_sb", [128, 64])
        gch = [sb("gch%d" % i, [8, 128]) for i in range(4)]
        gch_r = RL(4)
        gates_f = sb("gates_f", [128, 9, 32])
        gts_r = Res()
        gcT = sb("gcT", [8, 512])
        gcT_r = Res()
        gl = sb("gl", [128, 4, 128])
        gl_r = Res()
        x2pool = xin[:, 2:4, :].rearrange("p b d -> p (b d)")
        x2off = [0]
        s2res = []
        x3off = [0]
        s3res = []

        def pool3(nel):
            k, o = divmod(x3off[0], 512)
            assert o + nel <= 512 and k < 4
            x3off[0] += nel
            return TF[k][:, o:o + nel]

        def wt(name, dt=F32, n=1, s2=True):
            tiles = [sb("%s%d" % (name, i), [128, 128], dt) for i in range(2 * n)]
            res = RL(2 * n)
            if s2:
                for i in range(n):
                    if dt == F32:
                        v = x2pool[:, x2off[0]:x2off[0] + 128]
                        x2off[0] += 128
                    else:
                        v = x2pool[:, x2off[0]:x2off[0] + 64].bitcast(BF16)
                        x2off[0] += 64
                    r_ = Res()
                    tiles.append(v)
                    res.append(r_)
                    s2res.append(r_)
                for i in range(n):
                    v = pool3(128) if dt == F32 else pool3(64).bitcast(BF16)
                    r_ = Res()
                    tiles.append(v)
                    res.append(r_)
                    s3res.append(r_)
            return tiles, res
        e1, e1_r = wt("e1", F32, 2)
        Dm, Dm_r = wt("Dm", F32, 1)
        DTm, DTm_r = wt("DTm", F32, 1)
        EG, EG_r = wt("EG", F32, 1)
        qg, qg_r = wt("qg", BF16, 1)
        Bm, Bm_r = wt("Bm", BF16, 1)
        Pm, Pm_r = wt("Pm", BF16, 3)
        PTm, PTm_r = wt("PTm", BF16, 3)
        TT, TT_r = wt("TT", BF16, 2)
        QKD, QKD_r = wt("QKD", BF16, 1)
        vb, vb_r = wt("vb", BF16, 1)
        kbg, kbg_r = wt("kbg", BF16, 1)
        kgt, kgt_r = wt("kgt", BF16, 1)
        ub, ub_r = wt("ub", F32, 1)
        wTb, wTb_r = wt("wTb", BF16, 1)
        vnew, vnew_r = wt("vnew", BF16, 1)
        vnf, vnf_r = wt("vnf", F32, 1, s2=False)
        ogn, ogn_r = wt("ogn", BF16, 1)
        oq, oq_r = wt("oq", F32, 1, s2=False)
        otmp, otmp_r = wt("otmp", F32, 1)
        assert x2off[0] <= 2048, x2off[0]
        dummy = sb("dummy_sb", [128, 2])
        dummy_r = Res()
        ctr = {}

        def nxt(name, n):
            i = ctr.get(name, 0)
            ctr[name] = (i + 1) % n
            return i
        small = sb("small", [128, 16])
        small_r = RL(4)
        Sf = sb("Sf", [128, 16, 128])
        Sf_r = RL(16)
        Sbf = sb("Sbf", [128, 16, 128], BF16)
        Sbf_r = RL(16)
        pT = [sb("pT%d" % i, [128, 512], BF16) for i in range(2)]
        pT_r = RL(2)
        kcs = sb("kcs", [128, 4, 2, 64])
        kcs_r = Res()
        kcT = [sb("kcT%d" % i, [128, 4, 128], BF16) for i in range(2)]
        kcT_r = RL(2)
        pTc = Sbf[:, :, :].rearrange("p (k a) n -> p k (a n)", k=4)
        pTc_r = [Sbf_r[4 * k:4 * k + 4] for k in range(4)]
        knew = sb("knew", [128, 4, 64])
        knew_r = Res()
        NB_ = 8
        banks = [es.enter_context(nc.psum_tensor("bank%d" % i, [128, 512], F32)) for i in range(NB_)]
        bank_r = RL(NB_)
        for _r in bank_r:
            _r.x = True
        bnext = [0]
        banks_bf = [banks[i][:, :].bitcast(BF16) for i in range(NB_)]
        pbf = [banks_bf[6], banks_bf[7]]
        pbf_r = [bank_r[6], bank_r[7]]
        pbn = [0]

        def psum():
            i = bnext[0]
            bnext[0] = (i + 1) % NB_
            return banks[i], bank_r[i]

        def psumbf():
            i = pbn[0]
            pbn[0] = 1 - i
            return pbf[i], pbf_r[i]

        ident_r, nrm_r, ones_r, c_r = Res(), Res(), Res(), Res()
        P.dma("sp", lambda e: e.dma_start(out=ident[:, :], in_=ident_d[:, :]), writes=[ident_r])
        P.dma("sp", lambda e: e.dma_start(out=nrm_sb[:, :, :], in_=nrm[:, :, :]), writes=[nrm_r])
        V(lambda e: e.memset(ones_bf[:, :], 1.0), [], [ones_r])
        cl = [Res() for _ in range(8)]
        P.dma("sp", lambda e: e.dma_start(out=cw[:, :, :], in_=cw_d[:, :, :]), writes=[cl[0]])
        P.dma("sp", lambda e: e.dma_start(out=gv[:, :], in_=gv_d[0:1, :].broadcast_to([128, 160])), writes=[cl[1]])
        P.dma("sp", lambda e: e.dma_start(out=rmT[:, :], in_=rmT_d[:, :]), writes=[cl[2]])
        P.dma("sp", lambda e: e.dma_start(out=mk[:, :, :], in_=mk_d[:, :, :]), writes=[cl[3]])
        P.dma("sp", lambda e: e.dma_start(out=sm[:, :], in_=sm_d[:, :]), writes=[cl[4]])
        _t, _tr = tf()
        P.dma("sp", lambda e: e.dma_start(out=_t[:, 0:384], in_=mkb_d[:, :, :].rearrange("p a b -> p (a b)")), writes=[_tr])
        V(lambda e: e.tensor_copy(mkb[:, 0:3, :].rearrange("p a b -> p (a b)"), _t[:, 0:384]), [_tr], [cl[5]])
        _t3, _t3r = tf()
        P.dma("sp", lambda e: e.dma_start(out=_t3[:, 0:128], in_=mprev0_d[:, :]), writes=[_t3r])
        V(lambda e: e.tensor_copy(mkb[:, 3, :], _t3[:, 0:128]), [_t3r], [cl[7]])
        V(lambda e: e.memset(kcar[:, :, :], 0.0), [], [kcar_r])
        V(lambda e: e.memset(vs_tok[:, 0, :, :], 0.0), [], [vs_tok_r])
        V(lambda e: e.tensor_copy(ident_bf[:, :], ident[:, :]), [ident_r], [cl[6]])
        V(lambda e: e.memset(ones_f[:, :], 1.0), [], [cl[6]])
        A(lambda e: e.activation(negA[:, :], gv[:, 0:8], AF.Exp), [cl[1]], [c_r])
        V(lambda e: e.tensor_scalar(negA[:, :], negA[:, :], -1.0, None, ALU.mult), [c_r], [c_r])
        A(lambda e: e.activation(esink_f[:, :], gv[0:1, 144:160], AF.Exp), [cl[1]], [c_r])
        for par in range(2):
            src = esink_f[0:1, :].rearrange("o (k i par) -> o par k i", k=4, i=2)[:, par]
            dstv = esink[0:1, :, :].rearrange("o k (par i q) -> o par k i q", par=2, i=2)[:, par]
            V(lambda e, src=src, dstv=dstv: e.tensor_copy(dstv, src.unsqueeze(3).broadcast_to([1, 4, 2, 128])), [c_r], [c_r])
            for kvh in range(4):
                src2 = esink_f[0:1, 4 * kvh:4 * kvh + 4].rearrange("o (i par) -> o par i", i=2)[:, par]
                dst2 = esinkS[0:1, kvh, :].rearrange("o (par s i t) -> o par s i t", par=2, s=NSQ, i=2)[:, par]
                V(lambda e, src2=src2, dst2=dst2: e.tensor_copy(dst2, src2.unsqueeze(1).unsqueeze(3).broadcast_to([1, NSQ, 2, TS])), [c_r], [c_r])
        _t2, _t2r = tf()
        P.dma("sp", lambda e: e.dma_start(out=_t2[:, 0:512], in_=mkS_d[:, :]), writes=[_t2r])
        V(lambda e: e.tensor_copy(mkS[:, :], _t2[:, 0:512]), [_t2r], [c_r])
        V(lambda e: e.memset(cst[:, :, :], 0.0), [], [cst_r])
        for _i in range(2):
            V(lambda e, _i=_i: e.memset(vnew[_i][:, :], 0.0), [], [vnew_r[_i]])
        V(lambda e: e.memset(Sf[:, 0:8, :], 0.0), [], Sf_r[0:8])
        V(lambda e: e.memset(Sbf[:, 0:8, :], 0.0), [], Sbf_r[0:8])
        CALL = [ident_r, ones_r, c_r] + cl

        wscr = nc.dram_tensor("wscr", [200, 128, SLOT], BF16).ap()
        wreg = {}
        half_r = [[Res(), Res()] for _k in range(NSTG)]
        _hv = [wstg[_k][:, :].bitcast(BF16) for _k in range(NSTG)]
        fslots = [(wslots[0], wslot_r[0], None),
                  (_hv[0][:, 0:SLOT], half_r[0][0], wstg_r[0]),
                  (_hv[1][:, 0:SLOT], half_r[1][0], wstg_r[1]),
                  (wslots[1], wslot_r[1], None),
                  (_hv[0][:, SLOT:2 * SLOT], half_r[0][1], wstg_r[0]),
                  (_hv[1][:, SLOT:2 * SLOT], half_r[1][1], wstg_r[1])]
        fnext = [0]
        cnext = [0]

        def _wload(key, src, n1, ncol):
            if key in wreg:
                tid, tres = wreg[key]
                j = fnext[0]
                fnext[0] = (j + 1) % len(fslots)
                buf, bres, extra = fslots[j]
                wr = [bres]
                if extra is not None:
                    wr.append(extra)
                P.dma("sp", lambda e: e.dma_start(out=buf[:, 0:n1 * ncol], in_=wscr[tid, :, 0:n1 * ncol]), reads=[tres], writes=wr)
                return buf[:, 0:n1 * ncol].rearrange("p (c n) -> p c n", c=n1), bres
            k = snext[0]
            snext[0] = (k + 1) % NSTG
            i = wnext[0]
            wnext[0] = (i + 1) % NSLOT
            sview = wstg[k][:, 0:n1 * ncol].rearrange("p (c n) -> p c n", c=n1)
            view = wslots[i][:, 0:n1 * ncol].rearrange("p (c n) -> p c n", c=n1)
            P.dma("sp", lambda e: e.dma_start(out=sview, in_=src), writes=[wstg_r[k], half_r[k][0], half_r[k][1]])
            ce = ("pool", "act", "dve")[cnext[0] % 3]
            cnext[0] += 1
            if ce == "act":
                P.op("act", lambda e: e.copy(wslots[i][:, 0:n1 * ncol], wstg[k][:, 0:n1 * ncol]),
                     reads=[wstg_r[k]], writes=[wslot_r[i]])
            else:
                P.op(ce, lambda e: e.tensor_copy(wslots[i][:, 0:n1 * ncol], wstg[k][:, 0:n1 * ncol]),
                     reads=[wstg_r[k]], writes=[wslot_r[i]])
            tid = len(wreg)
            tres = Res()
            wreg[key] = (tid, tres)
            P.dma("sp", lambda e: e.dma_start(out=wscr[tid, :, 0:n1 * ncol], in_=wslots[i][:, 0:n1 * ncol]),
                  reads=[wslot_r[i]], writes=[tres])
            return view, wslot_r[i]

        def wload_in(W, col0, ncol):
            src = W.rearrange("(c p) n -> p c n", p=128)[:, :, col0:col0 + ncol]
            return _wload((id(W), "i", col0, ncol), src, DC, ncol)

        def wload_out(W, nj, col0, ncol, r0=0):
            src = W[r0:r0 + nj * 128, :].rearrange("(j p) n -> p j n", p=128)[:, :, col0:col0 + ncol]
            return _wload((id(W), "o", col0, ncol, r0, nj), src, nj, ncol)

        def load_x(xd, tok0, ntok):
            nb = ntok // 128
            src = xd[tok0:tok0 + ntok, :].rearrange("(b p) d -> p b d", p=128)
            P.dma("sp", lambda e: e.dma_start(out=xin[:, 0:nb, :], in_=src), writes=xin_r[0:nb])
            for c in range(DC):
                ps, pr = psum()
                for b in range(nb):
                    P.op("pe", lambda e, b=b, c=c, ps=ps: e.transpose(
                        ps[:, b * 128:(b + 1) * 128], xin[:, b, c * 128:(c + 1) * 128], ident[:, :]),
                        reads=[xin_r[b], ident_r], writes=[pr], inc=(b == nb - 1))
                eng = "act" if c % 2 == 0 else "dve"
                if eng == "act":
                    P.op("act", lambda e, c=c, ps=ps: e.copy(xT[:, c, 0:ntok], ps[:, 0:ntok]),
                         reads=[pr], writes=[xT_r[c]])
                else:
                    P.op("dve", lambda e, c=c, ps=ps: e.tensor_copy(xT[:, c, 0:ntok], ps[:, 0:ntok]),
                         reads=[pr], writes=[xT_r[c]])

        def rmsnorm(which, out, out_r, ntok):
            ps, pr = psum()
            for c in range(DC):
                s = c % 2
                P.op("act", lambda e, c=c, s=s: e.activation(sq[s][:, 0:ntok], xT[:, c, 0:ntok], AF.Square),
                     reads=[xT_r[c]], writes=[sq_r[s]])
                P.op("pe", lambda e, c=c, s=s, ps=ps: e.matmul(ps[:, 0:ntok], ones_bf[:, :], sq[s][:, 0:ntok],
                                                               start=(c == 0), stop=(c == DC - 1)),
                     reads=[sq_r[s], ones_r], writes=[pr], inc=True)
            sd, sd_r = tf()
            rstd, rstd_r = tf()
            P.op("act", lambda e, ps=ps: e.activation(sd[:, 0:ntok], ps[:, 0:ntok], AF.Sqrt, bias=EPS, scale=1.0 / D),
                 reads=[pr], writes=[sd_r])
            P.op("dve", lambda e: e.reciprocal(rstd[:, 0:ntok], sd[:, 0:ntok]), reads=[sd_r], writes=[rstd_r])
            for c in range(DC):
                P.op("dve", lambda e, c=c: e.scalar_tensor_tensor(
                    out[:, c, 0:ntok], xT[:, c, 0:ntok], nrm_sb[:, which, c:c + 1], rstd[:, 0:ntok],
                    ALU.mult, ALU.mult),
                    reads=[xT_r[c], rstd_r, nrm_r], writes=[out_r[c]])

        def ffn(Wi, Wo, ntok):
            for j in range(FC):
                wg, wgr = wload_in(Wi, 128 * j, 128)
                wu, wur = wload_in(Wi, DFF + 128 * j, 128)
                pg, pgr = psum()
                for c in range(DC):
                    P.op("pe", lambda e, c=c, pg=pg, wg=wg: e.matmul(
                        pg[:, 0:ntok], wg[:, c, :], xn[:, c, 0:ntok], start=(c == 0), stop=(c == DC - 1)),
                        reads=[wgr, xn_r[c]], writes=[pgr], inc=(c == DC - 1))
                pu, pur = psum()
                for c in range(DC):
                    P.op("pe", lambda e, c=c, pu=pu, wu=wu: e.matmul(
                        pu[:, 0:ntok], wu[:, c, :], xn[:, c, 0:ntok], start=(c == 0), stop=(c == DC - 1)),
                        reads=[wur, xn_r[c]], writes=[pur], inc=(c == DC - 1))
                sgb, sgr = tf()
                P.op("act", lambda e, sgb=sgb, pg=pg: e.activation(sgb[:, 0:ntok], pg[:, 0:ntok], AF.Silu),
                     reads=[pgr], writes=[sgr])
                P.op("dve", lambda e, sgb=sgb, pu=pu, j=j: e.tensor_tensor(
                    hT[:, j, 0:ntok], sgb[:, 0:ntok], pu[:, 0:ntok], ALU.mult),
                    reads=[sgr, pur], writes=[hT_r[j]])
            for c in range(DC):
                po, por = psum()
                for half in range(2):
                    wo, wor = wload_out(Wo, 11, 128 * c, 128, r0=half * 11 * 128)
                    for jj in range(11):
                        j = half * 11 + jj
                        P.op("pe", lambda e, j=j, jj=jj, po=po, wo=wo: e.matmul(
                            po[:, 0:ntok], wo[:, jj, :], hT[:, j, 0:ntok], start=(j == 0), stop=(j == FC - 1)),
                            reads=[wor, hT_r[j]], writes=[por], inc=(j == FC - 1 or jj == 10))
                P.op("dve", lambda e, c=c, po=po: e.scalar_tensor_tensor(
                    xT[:, c, 0:ntok], po[:, 0:ntok], 0.5, xT[:, c, 0:ntok], ALU.mult, ALU.add),
                    reads=[por, xT_r[c]], writes=[xT_r[c]])

        def store_y(yd, tok0, ntok):
            nb = ntok // 128
            for b in range(nb):
                for cg in range(2):
                    ps, pr = psum()
                    for ci in range(4):
                        c = cg * 4 + ci
                        P.op("pe", lambda e, b=b, c=c, ci=ci, ps=ps: e.transpose(
                            ps[:, ci * 128:(ci + 1) * 128], xT[:, c, b * 128:(b + 1) * 128], ident[:, :]),
                            reads=[xT_r[c], ident_r], writes=[pr], inc=(ci == 3))
                    if cg == 0:
                        P.op("act", lambda e, b=b, cg=cg, ps=ps: e.copy(xin[:, b, cg * 512:(cg + 1) * 512], ps[:, :]),
                             reads=[pr], writes=[xin_r[b]])
                    else:
                        P.op("dve", lambda e, b=b, cg=cg, ps=ps: e.tensor_copy(xin[:, b, cg * 512:(cg + 1) * 512], ps[:, :]),
                             reads=[pr], writes=[xin_r[b]])
            dst = yd[tok0:tok0 + ntok, :].rearrange("(b p) d -> p b d", p=128)
            P.dma("sp", lambda e: e.dma_start(out=dst, in_=xin[:, 0:nb, :]), reads=xin_r[0:nb])


        class UC:
            def __init__(self, u):
                self.u = u
                self.c = {}
                self.pn = 0

            def n(self, name, n):
                i = self.c.get(name, 0)
                self.c[name] = (i + 1) % n
                return self.u * n + i

            def ps(self):
                i = 2 * self.u + self.pn
                self.pn = 1 - self.pn
                self.last = i
                return banks[i], bank_r[i]

            def pb(self):
                i = 2 * self.u + self.pn
                self.pn = 1 - self.pn
                self.last = i
                return banks_bf[i], bank_r[i]

            def pb_of(self, i):
                return banks_bf[i], bank_r[i]

            def tf(self, k):
                return TF[2 * self.u + k], TF_r[2 * self.u + k]

            def sq(self):
                return sqx[self.u], sqx_r[self.u]
        UCS = [UC(0), UC(1), UC(2), UC(3)]
        sqx = [sq[0], sq[1], pT[0], pT[1]]
        sqx_r = [sq_r[0], sq_r[1], pT_r[0], pT_r[1]]
        pTx = [pT[0], pT[1], sq[0], sq[1]]
        pTx_r = [pT_r[0], pT_r[1], sq_r[0], sq_r[1]]

        def run_interleaved(gens, W):
            active = []
            it = iter(gens)
            slots = list(range(W))
            pending = True
            while True:
                while len(active) < W and pending:
                    try:
                        mk_ = next(it)
                    except StopIteration:
                        pending = False
                        break
                    u = slots.pop(0)
                    active.append((u, mk_(UCS[u])))
                if not active:
                    break
                for ent in list(active):
                    u, g = ent
                    try:
                        next(g)
                    except StopIteration:
                        active.remove(ent)
                        slots.append(u)

        def chunk_gen(U, W, off, ch, ntok, fn):
            w, wr = wload_in(W, off + 128 * ch, 128)
            ps, pr = U.ps()
            for c in range(DC):
                M(lambda e, c=c: e.matmul(ps[:, 0:ntok], w[:, c, :], xn[:, c, 0:ntok],
                                          start=(c == 0), stop=(c == DC - 1)), [wr, xn_r[c]], [pr], inc=(c == DC - 1))
            yield
            yield from fn(U, ch, ps, pr)

        def l2norm(U, y, yr, tmp, tmpr, out, outr, ntok, scale):
            sqb, sqr = U.sq()
            A(lambda e: e.activation(sqb[:, 0:ntok], y[:, 0:ntok], AF.Square), [yr], [sqr])
            ps, pr = U.ps()
            M(lambda e: e.matmul(ps[:, 0:ntok], ones_bf[:, :], sqb[:, 0:ntok], start=True, stop=True),
              [sqr, ones_r], [pr])
            yield
            A(lambda e: e.activation(tmp[:, 0:ntok], ps[:, 0:ntok], AF.Ln, bias=EPS, scale=1.0), [pr], [tmpr])
            A(lambda e: e.activation(tmp[:, 0:ntok], tmp[:, 0:ntok], AF.Exp, scale=-0.5), [tmpr], [tmpr])
            yield
            V(lambda e: e.scalar_tensor_tensor(out, y[:, 0:ntok], scale, tmp[:, 0:ntok], ALU.mult, ALU.mult),
              [yr, tmpr], [outr])
            yield

        def to_tok(U, src, srcr, dst, dstr, nb):
            pb, pbr = U.pb()
            for tb in range(nb):
                M(lambda e, tb=tb: e.transpose(pb[:, tb * 128:(tb + 1) * 128], src[:, tb * 128:(tb + 1) * 128], ident_bf[:, :]),
                  [srcr] + CALL, [pbr], inc=(tb == nb - 1))
            yield
            A(lambda e: e.copy(dst, pb[:, 0:nb * 128].rearrange("p (b d) -> p b d", b=nb)), [pbr], [dstr])

        def inproj(mode, tok0, ntok, lastx=False):
            nb = ntok // 128
            isP = mode in ("P", "X")
            isX = mode == "X"
            NSEQ, T = (1, ntok) if isP else (NSQ, TS)
            cs_, sn_ = (cosP, sinP) if mode == "P" else ((cosX, sinX) if isX else (cosS, sinS))
            c0 = tok0 if isP else 0
            P.dma("sp", lambda e: e.dma_start(out=csn[:, 0, 0:ntok], in_=cs_[:, c0:c0 + ntok]), writes=[csn_r])
            P.dma("sp", lambda e: e.dma_start(out=csn[:, 1, 0:ntok], in_=sn_[:, c0:c0 + ntok]), writes=[csn_r])
            wab, wabr = wload_in(win, OA, 16)
            psab, psabr = psum()
            for tb in range(nb):
                for c in range(DC):
                    M(lambda e, tb=tb, c=c: e.matmul(psab[:, tb * 16:(tb + 1) * 16], xn[:, c, tb * 128:(tb + 1) * 128],
                                                     wab[:, c, 0:16], start=(c == 0), stop=(c == DC - 1)),
                      [wabr, xn_r[c]], [psabr], inc=(c == DC - 1))
            gates("P" if isX else mode, nb, psab, psabr)
            gens = []

            def add(W, off, nch, fn):
                for ch in range(nch):
                    gens.append(lambda U, ch=ch: chunk_gen(U, W, off, ch, ntok, fn))

            def conv_ep(grp):
                def f(U, ch, ps, pr):
                    cidx = grp * 8 + ch
                    rawb, rawr = U.tf(0)
                    accb, accr = U.tf(1)
                    rb = rawb[:, 0:NSEQ * (3 + T)].rearrange("p (s t) -> p s t", s=NSEQ)
                    psv = ps[:, 0:ntok].rearrange("p (s t) -> p s t", s=NSEQ)
                    A(lambda e: e.copy(rb[:, :, 3:3 + T], psv), [pr], [rawr])
                    cv_ = cst[:, cidx, 0:NSEQ * 3].rearrange("p (s j) -> p s j", s=NSEQ)
                    V(lambda e: e.tensor_copy(rb[:, :, 0:3], cv_), [cst_r], [rawr])
                    av = accb[:, 0:ntok].rearrange("p (s t) -> p s t", s=NSEQ)
                    if not (isX and grp == 0):
                        A(lambda e: e.mul(av, psv, cw[:, cidx, 3:4]), [pr] + CALL, [accr])
                    yield
                    V(lambda e: e.tensor_copy(cv_, rb[:, :, T:T + 3]), [rawr], [cst_r])
                    if isX and grp == 0:
                        return
                    for j in range(0, 3):
                        V(lambda e, j=j: e.scalar_tensor_tensor(av, rb[:, :, j:j + T], cw[:, cidx, j:j + 1], av,
                                                                ALU.mult, ALU.add), [rawr, accr], [accr])
                    yield
                    if grp == 2:
                        vt_, vtr_ = U.sq()
                        A(lambda e: e.activation(vt_[:, 0:ntok], accb[:, 0:ntok], AF.Silu), [accr], [vtr_])
                        yield
                        yield from to_tok(U, vt_, vtr_, v_tok[:, 0:nb, ch, :], v_tok_r[ch], nb)
                    else:
                        A(lambda e: e.activation(rawb[:, 0:ntok], accb[:, 0:ntok], AF.Silu), [accr], [rawr])
                        yield
                        if grp == 0:
                            yield from l2norm(U, rawb, rawr, accb, accr, qT[:, ch, 0:ntok], qT_r[ch], ntok, 128.0 ** -0.5)
                        else:
                            yield from l2norm(U, rawb, rawr, accb, accr, kT[:, ch, 0:ntok], kT_r[ch], ntok, 1.0)
                            yield from to_tok(U, kT[:, ch, :], kT_r[ch], k_tok[:, 0:nb, ch, :], k_tok_r[ch], nb)
                return f
            if (not isX) or lastx:
                add(win, OQ, 8, conv_ep(0))
            add(win, OK_, 8, conv_ep(1))
            add(win, OV, 8, conv_ep(2))

            def z_ep(U, ch, ps, pr):
                A(lambda e: e.activation(zg[:, ch, 0:ntok], ps[:, 0:ntok], AF.Silu), [pr], [zg_r[ch]])
                yield
            if not isX:
                add(win, OZ, 8, z_ep)

            def ga_ep(U, ch, ps, pr):
                t1b, t1_r = U.tf(0)
                A(lambda e: e.activation(t1b[:, 0:ntok], ps[:, 0:ntok], AF.Sigmoid), [pr], [t1_r])
                yield
                V(lambda e: e.tensor_tensor(zg[:, ch, 0:ntok], zg[:, ch, 0:ntok], t1b[:, 0:ntok], ALU.mult),
                  [t1_r, zg_r[ch]], [zg_r[ch]])
            if not isX:
                add(win, OGA, 8, ga_ep)

            def gb_ep(U, ch, ps, pr):
                A(lambda e: e.activation(gbT[:, ch, 0:ntok], ps[:, 0:ntok], AF.Sigmoid), [pr], [gbT_r[ch]])
                yield
            if not isX:
                add(win, OGB, 8, gb_ep)

            def rope_ep(kind):
                def f(U, ch, ps, pr):
                    yb, ybr = U.tf(0)
                    t2b, t2_r = U.tf(1)
                    t1b, t1_r = yb, ybr
                    A(lambda e: e.copy(yb[:, 0:ntok], ps[:, 0:ntok]), [pr], [ybr])
                    yield
                    p2, p2r = U.ps()
                    M(lambda e: e.matmul(p2[:, 0:ntok], rmT[:, :], yb[:, 0:ntok], start=True, stop=True),
                      [ybr] + CALL, [p2r])
                    V(lambda e: e.tensor_tensor(t2b[:, 0:ntok], yb[:, 0:ntok], csn[:, 0, 0:ntok], ALU.mult),
                      [ybr, csn_r], [t2_r])
                    yield
                    V(lambda e: e.tensor_tensor(t1b[:, 0:ntok], p2[:, 0:ntok], csn[:, 1, 0:ntok], ALU.mult),
                      [p2r, csn_r], [t1_r])
                    if kind == "q":
                        V(lambda e: e.tensor_tensor(qsT[:, ch, 0:ntok], t1b[:, 0:ntok], t2b[:, 0:ntok], ALU.add),
                          [t1_r, t2_r], [qsT_r[ch]])
                    else:
                        V(lambda e: e.tensor_tensor(ksT[:, ch, 128:128 + ntok], t1b[:, 0:ntok], t2b[:, 0:ntok], ALU.add),
                          [t1_r, t2_r], [ksT_r])
                        V(lambda e: e.tensor_tensor(ks32[:, ch, :], t1b[:, ntok - 128:ntok], t2b[:, ntok - 128:ntok], ALU.add),
                          [t1_r, t2_r], [ks32_r])
                    yield
                return f
            if not isX:
                add(win, OQS, 8, rope_ep("q"))
            if (not isX) or lastx:
                add(wks, 0, 4, rope_ep("k"))

            def vs_ep(U, ch, ps, pr):
                yb, ybr = U.tf(0)
                A(lambda e: e.copy(yb[:, 0:ntok], ps[:, 0:ntok]), [pr], [ybr])
                yield
                p2, p2r = U.ps()
                for tb in range(nb):
                    M(lambda e, tb=tb: e.transpose(p2[:, tb * 128:(tb + 1) * 128], yb[:, tb * 128:(tb + 1) * 128], ident[:, :]),
                      [ybr, ident_r], [p2r], inc=(tb == nb - 1))
                yield
                src = p2[:, 0:nb * 128].rearrange("p (b h d) -> p b h d", b=nb, h=2)
                for dup in range(2):
                    V(lambda e, dup=dup: e.tensor_copy(vs_tok[:, 1:1 + nb, 2 * ch:2 * ch + 2, dup * 64:(dup + 1) * 64], src),
                      [p2r], [vs_tok_r])
                V(lambda e: e.tensor_copy(vs32[:, ch * 128:(ch + 1) * 128], p2[:, (nb - 1) * 128:nb * 128]), [p2r], [vs32_r])
            if (not isX) or lastx:
                add(win, OVS, 2, vs_ep)
            run_interleaved(gens, 3)

        def gates(mode, nb, psab, psabr):
            isS = 0 if mode == "P" else 1
            NBK = 2 if mode == "P" else 16
            G = gates_f
            ab = psab[:, 0:nb * 16].rearrange("p (b x) -> p b x", b=nb)

            def g3(i):
                return G[:, i, 0:nb * 8].rearrange("p (b h) -> p b h", b=nb)
            V(lambda e: e.tensor_tensor(g3(0), ab[:, :, 0:8], gv[:, 8:16].unsqueeze(1).broadcast_to([128, nb, 8]), ALU.add),
              [psabr] + CALL, [gts_r])
            A(lambda e: e.activation(g3(0), g3(0), AF.Exp), [gts_r], [gts_r])
            A(lambda e: e.activation(g3(0), g3(0), AF.Ln, bias=1.0), [gts_r], [gts_r])
            V(lambda e: e.tensor_tensor(g3(1), g3(0), negA[:, :].unsqueeze(1).broadcast_to([128, nb, 8]), ALU.mult),
              [gts_r] + CALL, [gts_r])
            A(lambda e: e.activation(g3(2), ab[:, :, 8:16], AF.Sigmoid), [psabr], [gts_r])
            V(lambda e: e.tensor_scalar(g3(3), g3(2), -1.0, None, ALU.mult), [gts_r], [gts_r])
            pg, pgr = psum()
            pgT, pgTr = psum()
            for tb in range(nb):
                M(lambda e, tb=tb: e.matmul(pg[:, tb * 8:(tb + 1) * 8], mk[:, M_TRI + 4 * isS, :], G[:, 1, tb * 8:(tb + 1) * 8],
                                            start=True, stop=True), [gts_r] + CALL, [pgr])
                M(lambda e, tb=tb: e.matmul(pgT[0:8, tb * 128:(tb + 1) * 128], G[:, 1, tb * 8:(tb + 1) * 8], mk[:, M_TRI + 4 * isS, :],
                                            start=True, stop=True), [gts_r] + CALL, [pgTr])
            V(lambda e: e.tensor_copy(G[:, 4, 0:nb * 8], pg[:, 0:nb * 8]), [pgr], [gts_r])
            A(lambda e: e.copy(gcT[:, 0:nb * 128], pgT[0:8, 0:nb * 128]), [pgTr], [gcT_r])
            A(lambda e: e.activation(G[:, 5, 0:nb * 8], G[:, 4, 0:nb * 8], AF.Exp), [gts_r], [gts_r])
            V(lambda e: e.tensor_tensor(G[:, 6, 0:nb * 8], G[:, 2, 0:nb * 8], G[:, 5, 0:nb * 8], ALU.mult), [gts_r], [gts_r])
            lm = sm[:, SM_LASTP:SM_LASTP + 2] if not isS else sm[:, SM_LASTS:SM_LASTS + 16]
            pl, plr = psum()
            rlt, rl_r = tf()
            rl = rlt[:, 0:512].rearrange("p (b x) -> p b x", b=4)
            for tb in range(nb):
                V(lambda e, tb=tb: e.tensor_tensor(
                    rl[:, tb, 0:NBK * 8].rearrange("p (k h) -> p k h", k=NBK),
                    G[:, 4, tb * 8:(tb + 1) * 8].unsqueeze(1).broadcast_to([128, NBK, 8]),
                    lm.unsqueeze(2).broadcast_to([128, NBK, 8]), ALU.mult), [gts_r] + CALL, [rl_r])
                M(lambda e, tb=tb: e.matmul(pl[:, tb * 128:tb * 128 + NBK * 8], ones_f[:, :], rl[:, tb, 0:NBK * 8],
                                            start=True, stop=True), [rl_r] + CALL, [plr])
            for tb in range(nb):
                A(lambda e, tb=tb: e.activation(gl[:, tb, 0:NBK * 8], pl[:, tb * 128:tb * 128 + NBK * 8], AF.Exp), [plr], [gl_r])
            anyc = sm[:, SM_ANYP + isS:SM_ANYP + isS + 1]
            V(lambda e: e.tensor_scalar(G[:, 8, 0:nb * 8], G[:, 4, 0:nb * 8], anyc, None, ALU.mult), [gts_r] + CALL, [gts_r])
            po, por = psum()
            for tb in range(nb):
                M(lambda e, tb=tb: e.matmul(po[:, tb * 8:(tb + 1) * 8], mk[:, M_BLK + 4 * isS, :], G[:, 8, tb * 8:(tb + 1) * 8],
                                            start=True, stop=True), [gts_r] + CALL, [por])
            V(lambda e: e.tensor_tensor(G[:, 7, 0:nb * 8], po[:, 0:nb * 8], G[:, 4, 0:nb * 8], ALU.subtract), [por, gts_r], [gts_r])
            A(lambda e: e.activation(G[:, 7, 0:nb * 8], G[:, 7, 0:nb * 8], AF.Exp), [gts_r], [gts_r])

        def gdn_prep(U, mode, tb, h, d, so=False):
            isS = 0 if mode == "P" else 1
            NL = 6 if mode == "P" else 3
            G = gates_f
            u = U.u
            cols = slice(tb * 128, (tb + 1) * 128)
            gcol = G[:, 4, tb * 8 + h:tb * 8 + h + 1]
            i = u
            d["i"] = i
            psG, psGr = U.ps()
            M(lambda e: e.matmul(psG[:, 0:128], kT[:, h, cols], kT[:, h, cols], start=True, stop=True), [kT_r[h]], [psGr])
            if not so:
                M(lambda e: e.matmul(psG[:, 128:256], kT[:, h, cols], qT[:, h, cols], start=True, stop=True), [kT_r[h], qT_r[h]], [psGr])
            V(lambda e: e.tensor_scalar(gch[u][:, :], gcT[:, cols], ident[0:8, h:h + 1], None, ALU.mult), [gcT_r, ident_r], [gch_r[u]])
            psR, psRr = U.ps()
            M(lambda e: e.matmul(psR[:, 0:128], ones_f[0:8, :], gch[u][:, :], start=True, stop=True), [gch_r[u]] + CALL, [psRr])
            V(lambda e: e.tensor_scalar(vb[i][:, :], v_tok[:, tb, h, :], G[:, 2, tb * 8 + h:tb * 8 + h + 1], None, ALU.mult),
               [v_tok_r[h], gts_r], [vb_r[i]])
            V(lambda e: e.tensor_scalar(kbg[i][:, :], k_tok[:, tb, h, :], G[:, 6, tb * 8 + h:tb * 8 + h + 1], None, ALU.mult),
               [k_tok_r[h], gts_r], [kbg_r[i]])
            yield
            ia = U.n("e1", 2)
            ib = U.n("e1", 2)
            V(lambda e: e.scalar_tensor_tensor(e1[ia][:, :], psR[:, 0:128], gcol, mk[:, M_M1 + 4 * isS, :], ALU.subtract, ALU.add),
              [psRr, gts_r] + CALL, [e1_r[ia]])
            if not so:
                V(lambda e: e.scalar_tensor_tensor(e1[ib][:, :], psR[:, 0:128], gcol, mk[:, M_M2 + 4 * isS, :], ALU.subtract, ALU.add),
                  [psRr, gts_r] + CALL, [e1_r[ib]])
                A(lambda e: e.activation(EG[i][:, :], psR[:, 0:128], AF.Exp), [psRr], [EG_r[i]])
            yield
            A(lambda e: e.activation(Dm[i][:, :], e1[ia][:, :], AF.Exp, scale=-1.0), [e1_r[ia]], [Dm_r[i]])
            if not so:
                A(lambda e: e.activation(DTm[i][:, :], e1[ib][:, :], AF.Exp), [e1_r[ib]], [DTm_r[i]])
                V(lambda e: e.tensor_tensor(qg[i][:, :], qT[:, h, cols], EG[i][:, :], ALU.mult), [qT_r[h], EG_r[i]], [qg_r[i]])
            yield
            V(lambda e: e.scalar_tensor_tensor(Bm[i][:, :], psG[:, 0:128], G[:, 3, tb * 8 + h:tb * 8 + h + 1], Dm[i][:, :],
                                               ALU.mult, ALU.mult), [psGr, gts_r, Dm_r[i]], [Bm_r[i]])
            if not so:
                V(lambda e: e.tensor_tensor(QKD[i][:, :], psG[:, 128:256], DTm[i][:, :], ALU.mult), [psGr, DTm_r[i]], [QKD_r[i]])
            yield
            pb, pbr = U.pb()
            M(lambda e: e.transpose(pb[:, 0:128], Bm[i][:, :], ident_bf[:, :]), [Bm_r[i]] + CALL, [pbr])
            yield
            p0 = U.n("Pm", 3)
            A(lambda e: e.copy(PTm[p0][:, :], pb[:, 0:128]), [pbr], [PTm_r[p0]])
            t0 = U.n("TT", 2)
            V(lambda e, t0=t0: e.tensor_tensor(TT[t0][:, :], pb[:, 0:128], ident_bf[:, :], ALU.add), [pbr] + CALL, [TT_r[t0]])
            yield
            Pc, Pcr = Bm[i], Bm_r[i]
            PTc, PTcr = PTm[p0], PTm_r[p0]
            for k in range(NL - 1):
                ps, psr = U.ps()
                M(lambda e, PTc=PTc, Pc=Pc, ps=ps: e.matmul(ps[:, 0:128], PTc[:, :], Pc[:, :], start=True, stop=True),
                  [PTcr, Pcr], [psr])
                if k < NL - 2:
                    M(lambda e, PTc=PTc, Pc=Pc, ps=ps: e.matmul(ps[:, 128:256], Pc[:, :], PTc[:, :], start=True, stop=True),
                      [PTcr, Pcr], [psr])
                yield
                pn = U.n("Pm", 3)
                A(lambda e, pn=pn, ps=ps: e.copy(Pm[pn][:, :], ps[:, 0:128]), [psr], [Pm_r[pn]])
                if k < NL - 2:
                    A(lambda e, pn=pn, ps=ps: e.copy(PTm[pn][:, :], ps[:, 128:256]), [psr], [PTm_r[pn]])
                Pc, Pcr = Pm[pn], Pm_r[pn]
                PTc, PTcr = PTm[pn], PTm_r[pn]
                yield
                ps2, ps2r = U.ps()
                M(lambda e, Pc=Pc, t0=t0, ps2=ps2: e.matmul(ps2[:, 0:128], Pc[:, :], TT[t0][:, :], start=True, stop=True),
                  [Pcr, TT_r[t0]], [ps2r])
                yield
                t1 = U.n("TT", 2)
                V(lambda e, t0=t0, t1=t1, ps2=ps2: e.tensor_tensor(TT[t1][:, :], ps2[:, 0:128], TT[t0][:, :], ALU.add),
                  [ps2r, TT_r[t0]], [TT_r[t1]])
                t0 = t1
                yield
            d["TT"] = TT[t0]
            d["TTr"] = TT_r[t0]

        def o_finish(U, pso, psor, tb, h, pbsel=None):
            cols = slice(tb * 128, (tb + 1) * 128)
            j = U.u
            sm_ = small[:, 4 * j:4 * j + 4]
            smr = small_r[j]
            A(lambda e: e.activation(otmp[j][:, :], pso, AF.Square, accum_out=sm_[:, 0:1]), [psor], [otmp_r[j], smr])
            A(lambda e: e.activation(sm_[:, 1:2], sm_[:, 0:1], AF.Ln, bias=EPS, scale=1.0 / 128), [smr], [smr])
            A(lambda e: e.activation(sm_[:, 2:3], sm_[:, 1:2], AF.Exp, scale=-0.5), [smr], [smr])
            yield
            V(lambda e: e.scalar_tensor_tensor(ogn[j][:, :], pso, sm_[:, 2:3], gnw, ALU.mult, ALU.mult),
              [psor, smr] + CALL, [ogn_r[j]])
            yield
            pb, pbr = pbsel if pbsel is not None else U.pb()
            M(lambda e: e.transpose(pb[:, 0:128], ogn[j][:, :], ident_bf[:, :]), [ogn_r[j]] + CALL, [pbr])
            yield
            V(lambda e: e.tensor_tensor(otmp[j][:, :], pb[:, 0:128], zg[:, h, cols], ALU.mult), [pbr, zg_r[h]], [otmp_r[j]])
            V(lambda e: e.tensor_tensor(mixT[:, h, cols], mixT[:, h, cols], otmp[j][:, :], ALU.add), [otmp_r[j], mixT_r[h]], [mixT_r[h]])

        def gdn_prompt(nb, so=False):
            G = gates_f

            def unit(U, tb, h):
                d = {}
                yield from gdn_prep(U, "P", tb, h, d, so)
                i, TTc, TTr = d["i"], d["TT"], d["TTr"]
                V(lambda e: e.tensor_scalar(kgt[i][:, :], k_tok[:, tb, h, :], G[:, 7, tb * 8 + h:tb * 8 + h + 1], None, ALU.mult),
                   [k_tok_r[h], gts_r], [kgt_r[i]])
                pu, pur = U.ps()
                M(lambda e: e.matmul(pu[:, 0:128], TTc[:, :], vb[i][:, :], start=True, stop=True), [TTr, vb_r[i]], [pur])
                M(lambda e: e.matmul(pu[:, 128:256], kbg[i][:, :], TTc[:, :], start=True, stop=True), [TTr, kbg_r[i]], [pur])
                yield
                A(lambda e: e.copy(ub[i][:, :], pu[:, 0:128]), [pur], [ub_r[i]])
                A(lambda e: e.copy(wTb[i][:, :], pu[:, 128:256]), [pur], [wTb_r[i]])
                yield
                pw, pwr = U.ps()
                pS, pSr = U.ps()
                pS_i = U.last
                for cc in range(2):
                    r = slice(64 * cc, 64 * cc + 64)
                    M(lambda e, r=r: e.matmul(pw[r, 0:128], wTb[i][:, r], Sbf[:, h, :], start=True, stop=True),
                      [wTb_r[i], Sbf_r[h]], [pwr])
                    yield
                    V(lambda e, r=r: e.tensor_tensor(vnew[i][r, :], ub[i][r, :], pw[r, 0:128], ALU.subtract),
                      [ub_r[i], pwr], [vnew_r[i]])
                    yield
                    if not so:
                        M(lambda e, r=r: e.matmul(pw[r, 128:256], qg[i][:, r], Sbf[:, h, :], start=True, stop=False),
                          [qg_r[i], Sbf_r[h]], [pwr], inc=False)
                        M(lambda e, r=r: e.matmul(pw[r, 128:256], QKD[i][:, r], vnew[i][:, :], start=False, stop=True),
                          [QKD_r[i], vnew_r[i]], [pwr])
                    M(lambda e, r=r, cc=cc, pS=pS: e.matmul(pS[:, 0:128], kgt[i][r, :], vnew[i][r, :], start=True, stop=True),
                      [kgt_r[i], vnew_r[i]], [pSr])
                    yield
                    V(lambda e, cc=cc, pS=pS: e.scalar_tensor_tensor(Sf[:, h, :], Sf[:, h, :], gl[:, tb, cc * 8 + h:cc * 8 + h + 1],
                                                                     pS[:, 0:128], ALU.mult, ALU.add),
                      [Sf_r[h], gl_r, pSr], [Sf_r[h]])
                    A(lambda e: e.copy(Sbf[:, h, :], Sf[:, h, :]), [Sf_r[h]], [Sbf_r[h]])
                    yield
                if not so:
                    yield from o_finish(U, pw[:, 128:256], pwr, tb, h, U.pb_of(pS_i))

            gens = []
            for tb in range(nb):
                for h in range(8):
                    gens.append(lambda U, tb=tb, h=h: unit(U, tb, h))
            V(lambda e: e.memset(vnew[2][:, :], 0.0), [], [xin_r[2], xin_r[3]] + s2res)
            V(lambda e: e.memset(vnew[3][:, :], 0.0), [], TF_r[0:4] + s3res)
            run_interleaved(gens, 4)
            V(lambda e: e.memset(dummy[:, 0:1], 0.0), [], [xin_r[2], xin_r[3], dummy_r] + s2res)
            V(lambda e: e.memset(dummy[:, 1:2], 0.0), [], TF_r[0:4] + [dummy_r] + s3res)

        def gdn_sample():
            G = gates_f
            tb = 0

            def unit(h):
                P.dma("sp", lambda e, h=h: e.dma_start(out=Sf[:, 0:NSQ, :], in_=sgdn[:, h].rearrange("s k v -> k s v")), writes=Sf_r)
                A(lambda e: e.copy(Sbf[:, 0:NSQ, :], Sf[:, 0:NSQ, :]), Sf_r, Sbf_r)
                d = {}
                for _ in gdn_prep(UCS[0], "S", tb, h, d):
                    pass
                i, TTc, TTr = d["i"], d["TT"], d["TTr"]
                pu, pur = psum()
                M(lambda e: e.matmul(pu[:, 0:128], vb[i][:, :], TTc[:, :], start=True, stop=True), [TTr, vb_r[i]], [pur])
                M(lambda e: e.matmul(pu[:, 128:256], kbg[i][:, :], TTc[:, :], start=True, stop=True), [TTr, kbg_r[i]], [pur])
                A(lambda e: e.copy(ub[i][:, :], pu[:, 0:128]), [pur], [ub_r[i]])
                A(lambda e: e.copy(wTb[i][:, :], pu[:, 128:256]), [pur], [wTb_r[i]])
                pw, pwr = psum()
                for s in range(NSQ):
                    c8 = slice(8 * s, 8 * s + 8)
                    M(lambda e, s=s, c8=c8: e.matmul(pw[:, 8 * s:8 * s + 8], Sbf[:, s, :], wTb[i][:, c8], start=True, stop=True),
                      [wTb_r[i], Sbf_r[s]], [pwr], inc=False)
                    M(lambda e, s=s, c8=c8: e.matmul(pw[:, 128 + 8 * s:128 + 8 * s + 8], Sbf[:, s, :], qg[i][:, c8], start=True, stop=True),
                      [qg_r[i], Sbf_r[s]], [pwr], inc=(s == NSQ - 1))
                V(lambda e: e.tensor_tensor(vnf[i][:, :], ub[i][:, :], pw[:, 0:128], ALU.subtract), [ub_r[i], pwr], [vnf_r[i]])
                A(lambda e: e.copy(oq[i][:, :], pw[:, 128:256]), [pwr], [oq_r[i]])
                p2, p2r = psum()
                M(lambda e: e.transpose(p2[:, 0:128], vnf[i][:, :], ident[:, :]), [vnf_r[i], ident_r], [p2r])
                V(lambda e: e.tensor_copy(vnew[i][:, :], p2[:, 0:128]), [p2r], [vnew_r[i]])
                M(lambda e: e.matmul(p2[:, 128:256], vnew[i][:, :], QKD[i][:, :], start=True, stop=True), [vnew_r[i], QKD_r[i]], [p2r])
                V(lambda e: e.tensor_tensor(oq[i][:, :], oq[i][:, :], p2[:, 128:256], ALU.add), [oq_r[i], p2r], [oq_r[i]])
                M(lambda e: e.transpose(p2[:, 256:384], oq[i][:, :], ident[:, :]), [oq_r[i], ident_r], [p2r])
                for _ in o_finish(UCS[0], p2[:, 256:384], p2r, tb, h):
                    pass
                for s in range(NSQ):
                    kk = nxt("kgt", 2)
                    V(lambda e, s=s, kk=kk: e.tensor_scalar(kgt[kk][:, :], k_tok[:, tb, h, :], G[:, 7, h:h + 1],
                                                            sm[:, SM_SEQ + s:SM_SEQ + s + 1], ALU.mult, ALU.mult),
                      [k_tok_r[h], gts_r] + CALL, [kgt_r[kk]])
                    pS, pSr = psum()
                    M(lambda e, kk=kk: e.matmul(pS[:, 0:128], kgt[kk][:, :], vnew[i][:, :], start=True, stop=True),
                      [kgt_r[kk], vnew_r[i]], [pSr])
                    V(lambda e, s=s: e.scalar_tensor_tensor(Sf[:, s, :], Sf[:, s, :], gl[:, 0, s * 8 + h:s * 8 + h + 1], pS[:, 0:128],
                                                            ALU.mult, ALU.add), [Sf_r[s], gl_r, pSr], [Sf_r[s]])
                P.dma("sp", lambda e, h=h: e.dma_start(out=gdns[:, h].rearrange("s k v -> k s v"), in_=Sf[:, 0:NSQ, :]), reads=Sf_r)
            for h in range(8):
                unit(h)

        def swa_block(U, mode, tb, kvh, first_tile, cache_fn=None):
            cols = slice(tb * 128, (tb + 1) * 128)
            isS = mode == "S"
            kbs = []
            if not isS:
                kbs.append((tb, 3 if (first_tile and tb == 0) else 2))
                kbs.append((tb + 1, 0))
            else:
                kbs.append((1, 1))
            u = U.u
            psO, psOr = banks[2 * u], bank_r[2 * u]
            psD, psDr = banks[2 * u + 1], bank_r[2 * u + 1]
            npt = len(kbs)
            for n_, (blk, mi) in enumerate(kbs):
                psS2 = [(banks[2 * u + 4], bank_r[2 * u + 4]), (banks[2 * u + 5], bank_r[2 * u + 5])]
                pi = 2 * u + n_
                for par in range(2):
                    r = slice(64 * par, 64 * par + 64)
                    psS, psSr = psS2[par]
                    out = psS[:, 0:256]
                    if isS:
                        rhs = qsT[r, 2 * kvh:2 * kvh + 2, 0:128].rearrange("p i (s t) -> p s i t", s=NSQ)
                    else:
                        rhs = qsT[r, 2 * kvh:2 * kvh + 2, cols]
                    M(lambda e, r=r, out=out, blk=blk, rhs=rhs: e.matmul(out, ksT[r, kvh, blk * 128:(blk + 1) * 128], rhs,
                                                                          start=True, stop=True),
                      [ksT_r, qsT_r[2 * kvh], qsT_r[2 * kvh + 1]], [psSr])
                    yield
                    A(lambda e, pi=pi, psS=psS, par=par: e.activation(pTx[pi][:, par * 256:(par + 1) * 256], psS[:, 0:256], AF.Exp, scale=0.125),
                      [psSr], [pTx_r[pi]])
                yield
                if isS:
                    V(lambda e, pi=pi: e.tensor_tensor(pTx[pi][:, :], pTx[pi][:, :], mkS[:, :], ALU.mult),
                      [pTx_r[pi]] + CALL, [pTx_r[pi]])
                else:
                    pv = pTx[pi][:, :].rearrange("p (g q) -> p g q", g=4)
                    V(lambda e, pv=pv, mi=mi: e.tensor_tensor(pv, pv, mkb[:, mi, :].unsqueeze(1).broadcast_to([128, 4, 128]), ALU.mult),
                       [pTx_r[pi]] + CALL, [pTx_r[pi]])
                yield
                last = (n_ == npt - 1)
                M(lambda e, pi=pi, blk=blk, n_=n_, last=last: e.matmul(psO[:, :], vs_tok[:, blk, kvh, :], pTx[pi][:, :],
                                                                       start=(n_ == 0), stop=last),
                  [vs_tok_r, pTx_r[pi]], [psOr])
                M(lambda e, pi=pi, n_=n_: e.matmul(psD[:, :], ones_bf[:, :], pTx[pi][:, :], start=(n_ == 0), stop=False),
                  [pTx_r[pi], ones_r], [psDr])
            if cache_fn is not None:
                cache_fn(kvh, psO, psOr, psD, psDr)
            esk = esinkS if isS else esink
            M(lambda e: e.matmul(psD[:, :], ones_bf[0:1, :], esk[0:1, kvh, :], start=False, stop=True), [ones_r] + CALL, [psDr])
            yield
            rden, rden_r = U.tf(0)
            osw, osw_r = U.tf(1)
            V(lambda e: e.reciprocal(rden[:, 0:512], psD[:, :]), [psDr], [rden_r])
            V(lambda e: e.tensor_tensor(osw[:, 0:512], psO[:, :], rden[:, 0:512], ALU.mult), [psOr, rden_r], [osw_r])
            yield
            for par in range(2):
                r = slice(64 * par, 64 * par + 64)
                if isS:
                    src = osw[r, par * 256:(par + 1) * 256].rearrange("p (s i t) -> p i s t", s=NSQ, i=2)
                    dst = mixT[r, 2 * kvh:2 * kvh + 2, 0:128].rearrange("p i (s t) -> p i s t", s=NSQ)
                    gb_ = gbT[r, 2 * kvh:2 * kvh + 2, 0:128].rearrange("p i (s t) -> p i s t", s=NSQ)
                else:
                    src = osw[r, par * 256:(par + 1) * 256].rearrange("p (i q) -> p i q", i=2)
                    dst = mixT[r, 2 * kvh:2 * kvh + 2, cols]
                    gb_ = gbT[r, 2 * kvh:2 * kvh + 2, cols]
                V(lambda e, src=src, dst=dst, gb_=gb_: e.tensor_tensor(dst, src, gb_, ALU.mult),
                  [osw_r, gbT_r[2 * kvh], gbT_r[2 * kvh + 1]], [mixT_r[2 * kvh], mixT_r[2 * kvh + 1]])

        def swa_prompt(nb, first_tile, ntok, carry_only=False):
            V(lambda e: e.tensor_copy(ksT[:, :, 0:128], kcar[:, :, :]), [kcar_r], [ksT_r])
            gens = []
            for tb in range(0 if carry_only else nb):
                for kvh in range(4):
                    gens.append(lambda U, tb=tb, kvh=kvh: swa_block(U, "P", tb, kvh, first_tile))
            run_interleaved(gens, 2)
            V(lambda e: e.tensor_copy(kcar[:, :, :], ksT[:, :, ntok:ntok + 128]), [ksT_r], [kcar_r])
            V(lambda e: e.tensor_copy(vs_tok[:, 0, :, :], vs_tok[:, nb, :, :]), [vs_tok_r], [vs_tok_r])

        mcb = sb("mcb", [128, TS], BF16)
        mcb_r = Res()

        def swa_sample():
            V(lambda e: e.tensor_copy(mcb[:, :], sm[:, SM_MC:SM_MC + TS]), CALL, [mcb_r])
            for s in range(NSQ):
                for dup in range(2):
                    P.dma("sp", lambda e, s=s, dup=dup: e.dma_start(out=kcs[:, :, dup, :], in_=ck[s].rearrange("r (k d) -> r k d", k=4)), writes=[kcs_r])
                p2, p2r = psum()
                for kvh in range(4):
                    M(lambda e, kvh=kvh, p2=p2: e.transpose(p2[:, kvh * 128:(kvh + 1) * 128], kcs[:, kvh, :, :].rearrange("p a d -> p (a d)"), ident[:, :]),
                      [kcs_r, ident_r], [p2r], inc=(kvh == 3))
                ki = nxt("kcT", 2)
                A(lambda e, ki=ki, p2=p2: e.copy(kcT[ki][:, :, :], p2[:, :].rearrange("p (k r) -> p k r", k=4)), [p2r], [kcT_r[ki]])
                psS2 = [psum(), psum()]
                for par in range(2):
                    psS, psSr = psS2[par]
                    r = slice(64 * par, 64 * par + 64)
                    for kvh in range(4):
                        out = psS[:, kvh * 16:kvh * 16 + 16]
                        M(lambda e, r=r, out=out, s=s, kvh=kvh, ki=ki: e.matmul(out, kcT[ki][r, kvh, :],
                                                                            qsT[r, 2 * kvh:2 * kvh + 2, 8 * s:8 * s + 8], start=True, stop=True),
                          [kcT_r[ki], qsT_r[2 * kvh], qsT_r[2 * kvh + 1]], [psSr], inc=(kvh == 3))
                    dst = pTc[:, :, :].rearrange("p k (par s x) -> p par s k x", par=2, s=NSQ)[:, par, s]
                    A(lambda e, dst=dst, psS=psS: e.activation(dst, psS[:, 0:64].rearrange("p (k x) -> p k x", k=4), AF.Exp, scale=0.125),
                      [psSr], Sbf_r)
            for kvh in range(4):
                pv = pTc[:, kvh, :].rearrange("p (x t) -> p x t", t=TS)
                V(lambda e, pv=pv: e.tensor_tensor(pv, pv, mcb[:, :].unsqueeze(1).broadcast_to([128, 64, TS]), ALU.mult),
                  pTc_r[kvh] + [mcb_r], pTc_r[kvh])

            def cache_fn(kvh, psO, psOr, psD, psDr):
                M(lambda e: e.matmul(psD[:, :], ones_bf[:, :], pTc[:, kvh, :], start=False, stop=False), pTc_r[kvh] + [ones_r], [psDr])
                for s in range(NSQ):
                    for par in range(2):
                        c16 = slice(par * 256 + s * 16, par * 256 + s * 16 + 16)
                        for half in range(2):
                            lastm = (s == NSQ - 1 and par == 1 and half == 1)
                            M(lambda e, c16=c16, s=s, half=half, lastm=lastm: e.matmul(
                                psO[64 * half:64 * half + 64, c16], vcall[:, s, kvh, :], pTc[:, kvh, c16], start=False, stop=True, skip_group_check=True),
                              [vcall_r[s]] + pTc_r[kvh], [psOr], inc=lastm)
            for kvh in range(4):
                for _ in swa_block(UCS[0], "S", 0, kvh, False, cache_fn):
                    pass

        def load_cache():
            for s in range(NSQ):
                vt, vtr = tf()
                P.dma("sp", lambda e, s=s, vt=vt: e.dma_start(out=vt[:, 0:256], in_=cv[s]), writes=[vtr])
                V(lambda e, s=s, vt=vt: e.tensor_copy(vcall[:, s, :, :].rearrange("p k d -> p (k d)"), vt[:, 0:256]), [vtr], [vcall_r[s]])

        def load_sconv():
            P.dma("sp", lambda e: e.dma_start(out=xin[0:NSQ * 3, 0, :], in_=sconv[:, 0:1024]), writes=[xin_r[0]])
            P.dma("sp", lambda e: e.dma_start(out=xin[0:NSQ * 3, 1, :], in_=sconv[:, 1024:2048]), writes=[xin_r[1]])
            P.dma("sp", lambda e: e.dma_start(out=xin[0:NSQ * 3, 2, :], in_=sconv[:, 2048:3072]), writes=[xin_r[2]])
            for g in range(6):
                p2, p2r = psum()
                for ci in range(4):
                    cidx = g * 4 + ci
                    M(lambda e, cidx=cidx, ci=ci, p2=p2: e.transpose(p2[:, ci * 128:ci * 128 + NSQ * 3],
                                                              xin[0:NSQ * 3, cidx // 8, (cidx % 8) * 128:(cidx % 8 + 1) * 128], ident[0:NSQ * 3, 0:NSQ * 3]),
                      [xin_r[cidx // 8], ident_r], [p2r], inc=(ci == 3))
                V(lambda e, g=g, p2=p2: e.tensor_copy(cst[:, g * 4:(g + 1) * 4, 0:NSQ * 3],
                                               p2[:, :].rearrange("p (c x) -> p c x", c=4)[:, :, 0:NSQ * 3]), [p2r], [cst_r])

        def store_conv(dst, nrow):
            for g in range(6):
                p2, p2r = psum()
                for ci in range(4):
                    cidx = g * 4 + ci
                    M(lambda e, cidx=cidx, ci=ci, p2=p2: e.transpose(p2[0:nrow, ci * 128:(ci + 1) * 128], cst[:, cidx, 0:nrow], ident[:, :]),
                      [cst_r, ident_r], [p2r], inc=(ci == 3))
                A(lambda e, g=g, p2=p2: e.copy(xin[0:nrow, g // 2, (g % 2) * 512:(g % 2 + 1) * 512], p2[0:nrow, :]), [p2r], [xin_r[g // 2]])
            for q in range(3):
                P.dma("sp", lambda e, q=q: e.dma_start(out=dst[:, q * 1024:(q + 1) * 1024], in_=xin[0:nrow, q, :]), reads=[xin_r[q]])

        def store_kv(kdst, vdst):
            p2, p2r = psum()
            for kvh in range(4):
                M(lambda e, kvh=kvh: e.transpose(p2[:, kvh * 128:(kvh + 1) * 128], ks32[:, kvh, :], ident[:, :]),
                  [ks32_r, ident_r], [p2r], inc=(kvh == 3))
            A(lambda e: e.copy(knew[:, :, :], p2[:, :].rearrange("p (k x) -> p k x", k=4)[:, :, 0:64]), [p2r], [knew_r])
            P.dma("sp", lambda e: e.dma_start(out=kdst, in_=knew[:, :, :]), reads=[knew_r])
            P.dma("sp", lambda e: e.dma_start(out=vdst, in_=vs32[:, :].rearrange("p (k d) -> p k d", k=4)), reads=[vs32_r])

        def outproj(ntok):
            for c in range(DC):
                wo, wor = wload_out(wout, DC, 128 * c, 128)
                po, por = psum()
                for j in range(DC):
                    M(lambda e, j=j, po=po, wo=wo: e.matmul(po[:, 0:ntok], wo[:, j, :], mixT[:, j, 0:ntok],
                                                            start=(j == 0), stop=(j == DC - 1)),
                      [wor, mixT_r[j]], [por], inc=(j == DC - 1))
                V(lambda e, c=c, po=po: e.tensor_tensor(xT[:, c, 0:ntok], po[:, 0:ntok], xT[:, c, 0:ntok], ALU.add),
                  [por, xT_r[c]], [xT_r[c]])

        def tile_pass(mode, xd, yd, tok0, ntok, first, last):
            nb = ntok // 128
            load_x(xd, tok0, ntok)
            rmsnorm(0, xn, xn_r, ntok)
            ffn(w1i, w1o, ntok)
            if stages >= 2:
                rmsnorm(1, xn, xn_r, ntok)
                if mode == "X":
                    inproj("X", tok0, ntok, lastx=last)
                    if last:
                        swa_prompt(nb, False, ntok, carry_only=True)
                    gdn_prompt(nb, so=True)
                    return
                if MIX & 1:
                    inproj(mode, tok0, ntok)
                if mode == "P":
                    if MIX & 2:
                        swa_prompt(nb, first, ntok)
                    if MIX & 4:
                        gdn_prompt(nb)
                else:
                    if MIX & 8:
                        swa_sample()
                    if MIX & 16:
                        gdn_sample()
                if MIX & 32:
                    outproj(ntok)
            if stages >= 3:
                rmsnorm(2, xn, xn_r, ntok)
                ffn(w2i, w2o, ntok)
            rmsnorm(3, xT, xT_r, ntok)
            store_y(yd, tok0, ntok)

        NX = LPRE // 512
        for t in range(NX):
            tile_pass("X", xpre, None, t * 512, 512, False, t == NX - 1)
        NT = LP // 512
        for t in range(NT):
            tile_pass("P", xp, yp, t * 512, 512, t == 0, t == NT - 1)
        if stages >= 2:
            if MIX & 64:
                store_conv(convp, 3)
            if MIX & 128:
                store_kv(kp.rearrange("r (k d) -> r k d", k=4), vp.rearrange("r (k d) -> r k d", k=4))
            if MIX & 256:
                P.dma("sp", lambda e: e.dma_start(out=gdnp.rearrange("h k v -> k h v"), in_=Sf[:, 0:8, :]), reads=Sf_r[0:8])
            if MIX & 512:
                load_sconv()
            if MIX & 1024:
                load_cache()
            if MIX & 2048:
                P.dma("sp", lambda e: e.dma_start(out=ksd[:, 0:120, :], in_=ck[:, 8:128, :]))
                P.dma("sp", lambda e: e.dma_start(out=vsd[:, 0:120, :], in_=cv[:, 8:128, :]))
        tile_pass("S", xs, ys, 0, NS_TOK, False, True)
        if stages >= 2:
            if MIX & 4096:
                store_conv(convs, NSQ * 3)
            if MIX & 8192:
                store_kv(ksd[:, 120:128, :].rearrange("s t (k d) -> s t k d", k=4), vsd[:, 120:128, :].rearrange("s t (k d) -> s t k d", k=4))
        P.finish()

        with nc.Block() as block:
            @block.tensor
            def _(eng):
                P.replay(eng, "pe")

            @block.scalar
            def _(eng):
                P.replay(eng, "act")

            @block.vector
            def _(eng):
                P.replay(eng, "dve")

            @block.gpsimd
            def _(eng):
                P.replay(eng, "pool")

            @block.sync
            def _(eng):
                P.replay(eng, "sp")
    return nc


def host_consts(NSQ, TS):
    f = np.float32
    ident = np.eye(128, dtype=f)
    p = np.arange(128)
    rm = np.zeros((128, 128), f)
    for m in range(128):
        d = m % 64
        if d < 8:
            rm[m, m + 8] = -1.0
        elif d < 16:
            rm[m, m - 8] = 1.0
    rmT = np.ascontiguousarray(rm.T)
    mk = np.zeros((128, 8, 128), f)
    for isS, C in ((0, 64), (1, TS)):
        same = (p[:, None] // C) == (p[None, :] // C)
        mk[:, 0 + 4 * isS, :] = (same & (p[:, None] <= p[None, :])).astype(f)
        mk[:, 1 + 4 * isS, :] = np.where(same & (p[:, None] > p[None, :]), 0.0, BIG)
        mk[:, 2 + 4 * isS, :] = np.where(same & (p[None, :] >= p[:, None]), 0.0, -BIG)
        mk[:, 3 + 4 * isS, :] = same.astype(f)
    mkb = np.zeros((128, 3, 128), f)
    mprev = (p[:, None] > p[None, :]).astype(f)
    mkb[:, 0, :] = (p[:, None] <= p[None, :]).astype(f)
    mkb[:, 1, :] = (((p[:, None] // TS) == (p[None, :] // TS)) & (p[:, None] <= p[None, :])).astype(f)
    mkb[:, 2, :] = (p[:, None] > p[None, :]).astype(f)
    mkS = np.zeros((128, 2, NSQ, 2, TS), f)
    for k in range(128):
        ks_, kt_ = k // TS, k % TS
        mkS[k, :, ks_, :, kt_:] = 1.0
    mkS = np.ascontiguousarray(mkS.reshape(128, 512))
    sm = np.zeros((128, 64), f)
    for b in range(2):
        sm[64 * b + 63, SM_LASTP + b] = 1.0
    for s in range(NSQ):
        sm[TS * s + TS - 1, SM_LASTS + s] = 1.0
        sm[TS * s:TS * (s + 1), SM_SEQ + s] = 1.0
    sm[:, SM_ANYP] = (p % 64 == 63).astype(f)
    sm[:, SM_ANYS] = (p % TS == TS - 1).astype(f)
    for t in range(TS):
        sm[:, SM_MC + t] = (p > t).astype(f)
    inv = (np.float32(500000.0) ** (-(np.arange(0, 16, 2, dtype=f)) / np.float32(16))).astype(f)

    def tables(pos):
        ang = (pos.astype(f)[:, None] * inv[None, :]).astype(f)
        c = np.ones((128, len(pos)), f)
        s = np.zeros((128, len(pos)), f)
        for pp in range(128):
            d = pp % 64
            if d < 16:
                c[pp] = np.cos(ang[:, d % 8])
                s[pp] = np.sin(ang[:, d % 8])
        return np.ascontiguousarray(c), np.ascontiguousarray(s)
    cosS, sinS = tables(np.tile(16384 + np.arange(TS), NSQ))
    return dict(ident=ident, rmT=rmT, mk=mk, mkb=mkb, mkS=mkS, sm=sm, cosS=cosS, sinS=sinS), tables, mprev


def pvec(v):
    return np.ascontiguousarray(v.reshape(DC, 128).T)


_CACHE = {}
STAGES = 99
MIXF = 0xFFFF


def kernel(x_prompt, x_sample, state_conv, state_gdn, cache_swa_k, cache_swa_v,
           norm_ffn1, w_ffn1_in, w_ffn1_out, norm_mix, w_in, conv_w, gdn_a_log, gdn_dt_bias,
           gdn_norm, swa_sinks, w_out, norm_ffn2, w_ffn2_in, w_ffn2_out, norm_final):
    f = np.float32
    A_ = lambda a: np.asarray(a, f)
    x_prompt = A_(x_prompt)
    x_sample = A_(x_sample)
    B, L, _ = x_prompt.shape
    NSQ = x_sample.shape[0] // NCORES
    T = x_sample.shape[1]
    LP = L // 2
    key = (L, STAGES, MIXF)
    if key not in _CACHE:
        _CACHE[key] = build(LP, LP, NSQ, T, stages=STAGES, MIX=MIXF)
    nc = _CACHE[key]
    nrm = np.ascontiguousarray(np.stack([pvec(A_(norm_ffn1)[0]), pvec(A_(norm_mix)[0]),
                                         pvec(A_(norm_ffn2)[0]), pvec(A_(norm_final))], axis=1))
    hc, tables, mprev = host_consts(NSQ, T)
    tabs = [tables(np.arange(hf * LP, (hf + 1) * LP)) for hf in range(2)]
    zx = np.zeros((LP, D), f)
    zm = np.zeros((128, 128), f)
    win = np.ascontiguousarray(A_(w_in)[0])
    wks = np.zeros((D, 512), f)
    for kvh in range(4):
        for dup in range(2):
            wks[:, kvh * 128 + dup * 64:kvh * 128 + (dup + 1) * 64] = win[:, OKS + kvh * 64:OKS + (kvh + 1) * 64]
    cw = np.ascontiguousarray(A_(conv_w)[0].reshape(4, 24, 128).transpose(2, 1, 0))
    gv = np.concatenate([A_(gdn_a_log)[0], A_(gdn_dt_bias)[0], A_(gdn_norm)[0], A_(swa_sinks)[0]])[None, :].astype(f)
    shared = {
        "w1i": np.ascontiguousarray(A_(w_ffn1_in)[0]), "w1o": np.ascontiguousarray(A_(w_ffn1_out)[0]),
        "w2i": np.ascontiguousarray(A_(w_ffn2_in)[0]), "w2o": np.ascontiguousarray(A_(w_ffn2_out)[0]),
        "nrm": nrm, "win": win, "wks": wks, "wout": np.ascontiguousarray(A_(w_out)[0]),
        "cw": cw, "gv": np.ascontiguousarray(gv),
    }
    shared.update(hc)
    sc, sg_, ck_, cv_ = A_(state_conv)[0], A_(state_gdn)[0], A_(cache_swa_k)[0], A_(cache_swa_v)[0]
    in_maps = []
    for c in range(NCORES):
        m = dict(shared)
        sl = slice(c * NSQ, (c + 1) * NSQ)
        sq_, hf = c // 2, c % 2
        m["xp"] = np.ascontiguousarray(x_prompt[sq_, hf * LP:(hf + 1) * LP])
        m["xpre"] = np.ascontiguousarray(x_prompt[sq_, 0:LP]) if hf == 1 else zx
        m["cosP"], m["sinP"] = tabs[hf]
        m["cosX"], m["sinX"] = tabs[0]
        m["mprev0"] = mprev if hf == 1 else zm
        m["xs"] = np.ascontiguousarray(x_sample[sl].reshape(NSQ * T, D))
        m["sconv"] = np.ascontiguousarray(sc[sl].reshape(NSQ * 3, 3072))
        m["sgdn"] = np.ascontiguousarray(sg_[sl])
        m["ck"] = np.ascontiguousarray(ck_[sl].reshape(NSQ, 128, 256))
        m["cv"] = np.ascontiguousarray(cv_[sl].reshape(NSQ, 128, 256))
        in_maps.append(m)
    res = run_bass_kernel_spmd(nc, in_maps, core_ids=list(range(NCORES)))
    R = res.results
    cat = lambda k, shp: np.concatenate([np.asarray(R[c][k], f).reshape(shp) for c in range(NCORES)], axis=0)
    stk = lambda k, shp: np.stack([np.asarray(R[2 * b + 1][k], f).reshape(shp) for b in range(B)], axis=0)
    y_prompt = np.stack([np.concatenate([np.asarray(R[2 * b]["yp"], f), np.asarray(R[2 * b + 1]["yp"], f)], axis=0) for b in range(B)], axis=0)
    y_sample = cat("ys", (NSQ, T, D))
    conv_p = stk("convp", (3, 3072))[None]
    conv_s = cat("convs", (NSQ, 3, 3072))[None]
    gdn_p = stk("gdnp", (8, 128, 128))[None]
    gdn_s = cat("gdns", (NSQ, 8, 128, 128))[None]
    k_p = stk("kp", (128, 4, 64))[None]
    k_s = cat("ksd", (NSQ, 128, 4, 64))[None]
    v_p = stk("vp", (128, 4, 64))[None]
    v_s = cat("vsd", (NSQ, 128, 4, 64))[None]
    return (y_prompt, y_sample, conv_p, conv_s, gdn_p, gdn_s, k_p, k_s, v_p, v_s)
```

```python
import math
from contextlib import ExitStack
import numpy as np
import concourse.bass as bass
import concourse.mybir as mybir
from concourse.bass_utils import run_bass_kernel_spmd

F32 = mybir.dt.float32
BF16 = mybir.dt.bfloat16
AF = mybir.ActivationFunctionType
ALU = mybir.AluOpType
AX = mybir.AxisListType

D = 1024
DC = 8
DFF = 2816
FC = 22
DIN = 7696
EPS = 1e-6
NCORES = 8
SLOT = 1408


class Res:
    __slots__ = ("w", "r", "x")

    def __init__(self):
        self.w = None
        self.r = {}
        self.x = False


def RL(n):
    return [Res() for _ in range(n)]


class EngS:
    def __init__(self, name, sem):
        self.name = name
        self.sem = sem
        self.count = 0
        self.ops = []
        self.waited = {}


class Prog:
    def __init__(self, sems, dsems):
        self.E = {n: EngS(n, sems[n]) for n in ("pe", "act", "dve", "pool", "sp")}
        self.dsems = dsems
        self.dcount = [0] * len(dsems)
        self.dnext = 0

    def _deps(self, reads, writes, own=None):
        need = {}

        def add(k, v):
            if need.get(k, 0) < v:
                need[k] = v

        for r in reads:
            if r.w:
                add(*r.w)
            if r.x:
                for k, v in r.r.items():
                    if k != own:
                        add(k, v)
        for w in writes:
            if w.w:
                add(*w.w)
            for k, v in w.r.items():
                add(k, v)
        return need

    def _waits(self, e, need, skip_own=False):
        for k, v in need.items():
            if skip_own and k == e.name:
                continue
            if e.waited.get(k, 0) < v:
                e.waited[k] = v
                e.ops.append(("wait", k, v))

    def op(self, eng, fn, reads=(), writes=(), inc=True):
        e = self.E[eng]
        need = self._deps(reads, writes, own=e.name)
        self._waits(e, need, skip_own=(eng == "pe"))
        e.ops.append(("op", fn, inc))
        idx = e.count + 1
        if inc:
            e.count = idx
        for r in reads:
            if r.r.get(e.name, 0) < idx:
                r.r[e.name] = idx
        for w in writes:
            w.w = (e.name, idx)
            w.r = {}

    def dma(self, eng, fn, reads=(), writes=()):
        e = self.E[eng]
        need = self._deps(reads, writes)
        j = self.dnext
        self.dnext = (j + 1) % len(self.dsems)
        key = ("d", j)
        if self.dcount[j] > 0:
            v = 16 * self.dcount[j]
            if need.get(key, 0) < v:
                need[key] = v
        self._waits(e, need)
        self.dcount[j] += 1
        val = 16 * self.dcount[j]
        e.ops.append(("dma", fn, key))
        for r in reads:
            if r.r.get(key, 0) < val:
                r.r[key] = val
        for w in writes:
            w.w = (key, val)
            w.r = {}

    def semof(self, key):
        if isinstance(key, tuple):
            return self.dsems[key[1]]
        return self.E[key].sem

    def replay(self, eng, name):
        e = self.E[name]
        for item in e.ops:
            if item[0] == "wait":
                eng.wait_ge(self.semof(item[1]), item[2])
            elif item[0] == "op":
                ins = item[1](eng)
                if item[2]:
                    ins.then_inc(e.sem, 1)
            else:
                ins = item[1](eng)
                ins.then_inc(self.semof(item[2]), 16)

    def finish(self):
        e = self.E["sp"]
        for j, c in enumerate(self.dcount):
            if c > 0:
                e.ops.append(("wait", ("d", j), 16 * c))
        for n in ("pe", "act", "dve", "pool"):
            if self.E[n].count > 0:
                e.ops.append(("wait", n, self.E[n].count))


OQ, OK_, OV, OZ, OA, OB, OQS, OKS, OVS, OGA, OGB = 0, 1024, 2048, 3072, 4096, 4104, 4112, 5136, 5392, 5648, 6672
BIG = 30000.0
M_TRI, M_M1, M_M2, M_BLK = 0, 1, 2, 3
SM_LASTP, SM_LASTS, SM_ANYP, SM_ANYS, SM_SEQ, SM_MC = 0, 2, 18, 19, 20, 36


def build(LP, LPRE, NSQ=16, TS=8, stages=99, MIX=0xFFFF, GW=2):
    NS_TOK = NSQ * TS
    nc = bass.Bass("TRN2", target_bir_lowering=False)

    def din(name, shape, dt=F32):
        return nc.dram_tensor(name, list(shape), dt, kind="ExternalInput").ap()

    def dout(name, shape, dt=F32):
        return nc.dram_tensor(name, list(shape), dt, kind="ExternalOutput").ap()

    xp = din("xp", [LP, D])
    xpre = din("xpre", [LPRE, D])
    cosX = din("cosX", [128, LPRE])
    sinX = din("sinX", [128, LPRE])
    mprev0_d = din("mprev0", [128, 128])
    xs = din("xs", [NS_TOK, D])
    w1i = din("w1i", [D, 2 * DFF])
    w1o = din("w1o", [DFF, D])
    w2i = din("w2i", [D, 2 * DFF])
    w2o = din("w2o", [DFF, D])
    nrm = din("nrm", [128, 4, DC])
    ident_d = din("ident", [128, 128])
    win = din("win", [D, DIN])
    wks = din("wks", [D, 512])
    wout = din("wout", [D, D])
    cw_d = din("cw", [128, 24, 4])
    gv_d = din("gv", [1, 160])
    cosP = din("cosP", [128, LP])
    sinP = din("sinP", [128, LP])
    cosS = din("cosS", [128, NS_TOK])
    sinS = din("sinS", [128, NS_TOK])
    rmT_d = din("rmT", [128, 128])
    mk_d = din("mk", [128, 8, 128])
    mkb_d = din("mkb", [128, 3, 128])
    mkS_d = din("mkS", [128, 512])
    sm_d = din("sm", [128, 64])
    sconv = din("sconv", [NSQ * 3, 3072])
    sgdn = din("sgdn", [NSQ, 8, 128, 128])
    ck = din("ck", [NSQ, 128, 256])
    cv = din("cv", [NSQ, 128, 256])
    yp = dout("yp", [LP, D])
    ys = dout("ys", [NS_TOK, D])
    convp = dout("convp", [3, 3072])
    convs = dout("convs", [NSQ * 3, 3072])
    gdnp = dout("gdnp", [8, 128, 128])
    gdns = dout("gdns", [NSQ, 8, 128, 128])
    kp = dout("kp", [128, 256])
    ksd = dout("ksd", [NSQ, 128, 256])
    vp = dout("vp", [128, 256])
    vsd = dout("vsd", [NSQ, 128, 256])

    es = ExitStack()
    with es:
        es.enter_context(nc.allow_low_precision("bf16 matmul operands, fp32 accumulate"))
        sems = {n: es.enter_context(nc.semaphore("sem_" + n)) for n in ("pe", "act", "dve", "pool", "sp")}
        dsems = [es.enter_context(nc.semaphore("dsem%d" % i)) for i in range(24)]
        P = Prog(sems, dsems)

        def sb(name, shape, dt=F32):
            return es.enter_context(nc.sbuf_tensor(name, list(shape), dt))

        def A(fn, reads, writes):
            P.op("act", fn, reads, writes)

        def V(fn, reads, writes):
            P.op("dve", fn, reads, writes)

        def M(fn, reads, writes, inc=True):
            P.op("pe", fn, reads, writes, inc)

        def G_(fn, reads, writes):
            P.op("pool", fn, reads, writes)

        xin = sb("xin", [128, 4, D])
        xin_r = RL(4)
        xT = sb("xT", [128, DC, 512])
        xT_r = RL(DC)
        xn = sb("xn", [128, DC, 512], BF16)
        xn_r = RL(DC)
        hT = sb("hT", [128, FC, 512], BF16)
        hT_r = RL(FC)
        NSLOT = 2
        NSTG = 2
        wslots = [sb("wslot%d" % i, [128, SLOT], BF16) for i in range(NSLOT)]
        wslot_r = RL(NSLOT)
        wnext = [0]
        wstg = [sb("wstg%d" % i, [128, SLOT]) for i in range(NSTG)]
        wstg_r = RL(NSTG)
        snext = [0]
        sq = [sb("sq%d" % i, [128, 512], BF16) for i in range(2)]
        sq_r = RL(2)
        TF = [sb("tf%d" % i, [128, 520]) for i in range(6)]
        TF_r = RL(6)
        tfn = [0]

        def tf():
            i = tfn[0]
            tfn[0] = (i + 1) % 6
            return TF[i], TF_r[i]
        ident = sb("ident_sb", [128, 128])
        ident_bf = sb("ident_bf", [128, 128], BF16)
        ones_bf = sb("ones_bf", [128, 128], BF16)
        ones_f = sb("ones_f", [128, 128])
        nrm_sb = sb("nrm_sb", [128, 4, DC])
        qT = hT[:, 0:8, :]
        qT_r = hT_r[0:8]
        kT = hT[:, 8:16, :]
        kT_r = hT_r[8:16]
        ksT = hT[:, 16:22, :].rearrange("p j n -> p (j n)")[:, 0:2560].rearrange("p (k n) -> p k n", k=4)
        ksT_r = Res()
        for _j in range(16, 22):
            hT_r[_j] = ksT_r
        mixT = xn
        mixT_r = xn_r
        k_tok = sb("k_tok", [128, 4, 8, 128], BF16)
        k_tok_r = RL(8)
        v_tok = sb("v_tok", [128, 4, 8, 128], BF16)
        v_tok_r = RL(8)
        zg = xin[:, 0:2, :].rearrange("p b d -> p (b d)").bitcast(BF16).rearrange("p (h n) -> p h n", h=8)
        zg_r = [xin_r[h // 4] for h in range(8)]
        vcall = xin[:, 2:4, :].rearrange("p b d -> p (b d)").bitcast(BF16).rearrange("p (s k d) -> p s k d", s=16, k=4)
        vcall_r = [xin_r[2 + s // 8] for s in range(16)]
        gbT = sb("gbT", [128, 8, 512], BF16)
        gbT_r = RL(8)
        qsT = sb("qsT", [128, 8, 512], BF16)
        qsT_r = RL(8)
        vs_tok = sb("vs_tok", [128, 5, 4, 128], BF16)
        vs_tok_r = Res()
        kcar = sb("kcar", [128, 4, 128], BF16)
        kcar_r = Res()
        csn = sb("csn", [128, 2, 512])
        csn_r = Res()
        ks32 = sb("ks32", [128, 4, 128])
        ks32_r = Res()
        vs32 = sb("vs32", [128, 256])
        vs32_r = Res()
        cst = sb("cst", [128, 24, 48])
        cst_r = Res()
        cw = sb("cw_sb", [128, 24, 4])
        gv = sb("gv_sb", [128, 160])
        gnw = gv[:, 16:144]
        negA = sb("negA", [128, 8])
        esink = sb("esink", [1, 4, 512], BF16)
        esinkS = sb("esinkS", [1, 4, 512], BF16)
        mkS = sb("mkS_sb", [128, 512], BF16)
        esink_f = sb("esink_f", [1, 16])
        rmT = sb("rmT_sb", [128, 128])
        mk = sb("mk_sb", [128, 8, 128])
        mkb = sb("mkb_sb", [128, 4, 128], BF16)
        sm = sb("sm_sb", [128, 64])
        gch = [sb("gch%d" % i, [8, 128]) for i in range(4)]
        gch_r = RL(4)
        gates_f = sb("gates_f", [128, 9, 32])
        gts_r = Res()
        gcT = sb("gcT", [8, 512])
        gcT_r = Res()
        gl = sb("gl", [128, 4, 128])
        gl_r = Res()
        x2pool = xin[:, 2:4, :].rearrange("p b d -> p (b d)")
        x2off = [0]
        s2res = []
        x3off = [0]
        s3res = []

        def pool3(nel):
            k, o = divmod(x3off[0], 512)
            assert o + nel <= 512 and k < 4
            x3off[0] += nel
            return TF[k][:, o:o + nel]

        def wt(name, dt=F32, n=1, s2=True):
            tiles = [sb("%s%d" % (name, i), [128, 128], dt) for i in range(2 * n)]
            res = RL(2 * n)
            if s2:
                for i in range(n):
                    if dt == F32:
                        v = x2pool[:, x2off[0]:x2off[0] + 128]
                        x2off[0] += 128
                    else:
                        v = x2pool[:, x2off[0]:x2off[0] + 64].bitcast(BF16)
                        x2off[0] += 64
                    r_ = Res()
                    tiles.append(v)
                    res.append(r_)
                    s2res.append(r_)
                for i in range(n):
                    v = pool3(128) if dt == F32 else pool3(64).bitcast(BF16)
                    r_ = Res()
                    tiles.append(v)
                    res.append(r_)
                    s3res.append(r_)
            return tiles, res
        e1, e1_r = wt("e1", F32, 2)
        Dm, Dm_r = wt("Dm", F32, 1)
        DTm, DTm_r = wt("DTm", F32, 1)
        EG, EG_r = wt("EG", F32, 1)
        qg, qg_r = wt("qg", BF16, 1)
        Bm, Bm_r = wt("Bm", BF16, 1)
        Pm, Pm_r = wt("Pm", BF16, 3)
        PTm, PTm_r = wt("PTm", BF16, 3)
        TT, TT_r = wt("TT", BF16, 2)
        QKD, QKD_r = wt("QKD", BF16, 1)
        vb, vb_r = wt("vb", BF16, 1)
        kbg, kbg_r = wt("kbg", BF16, 1)
        kgt, kgt_r = wt("kgt", BF16, 1)
        ub, ub_r = wt("ub", F32, 1)
        wTb, wTb_r = wt("wTb", BF16, 1)
        vnew, vnew_r = wt("vnew", BF16, 1)
        vnf, vnf_r = wt("vnf", F32, 1, s2=False)
        ogn, ogn_r = wt("ogn", BF16, 1)
        oq, oq_r = wt("oq", F32, 1, s2=False)
        otmp, otmp_r = wt("otmp", F32, 1)
        assert x2off[0] <= 2048, x2off[0]
        dummy = sb("dummy_sb", [128, 2])
        dummy_r = Res()
        ctr = {}

        def nxt(name, n):
            i = ctr.get(name, 0)
            ctr[name] = (i + 1) % n
            return i
        small = sb("small", [128, 16])
        small_r = RL(4)
        Sf = sb("Sf", [128, 16, 128])
        Sf_r = RL(16)
        Sbf = sb("Sbf", [128, 16, 128], BF16)
        Sbf_r = RL(16)
        pT = [sb("pT%d" % i, [128, 512], BF16) for i in range(2)]
        pT_r = RL(2)
        kcs = sb("kcs", [128, 4, 2, 64])
        kcs_r = Res()
        kcT = [sb("kcT%d" % i, [128, 4, 128], BF16) for i in range(2)]
        kcT_r = RL(2)
        pTc = Sbf[:, :, :].rearrange("p (k a) n -> p k (a n)", k=4)
        pTc_r = [Sbf_r[4 * k:4 * k + 4] for k in range(4)]
        knew = sb("knew", [128, 4, 64])
        knew_r = Res()
        NB_ = 8
        banks = [es.enter_context(nc.psum_tensor("bank%d" % i, [128, 512], F32)) for i in range(NB_)]
        bank_r = RL(NB_)
        for _r in bank_r:
            _r.x = True
        bnext = [0]
        banks_bf = [banks[i][:, :].bitcast(BF16) for i in range(NB_)]
        pbf = [banks_bf[6], banks_bf[7]]
        pbf_r = [bank_r[6], bank_r[7]]
        pbn = [0]

        def psum():
            i = bnext[0]
            bnext[0] = (i + 1) % NB_
            return banks[i], bank_r[i]

        def psumbf():
            i = pbn[0]
            pbn[0] = 1 - i
            return pbf[i], pbf_r[i]

        ident_r, nrm_r, ones_r, c_r = Res(), Res(), Res(), Res()
        P.dma("sp", lambda e: e.dma_start(out=ident[:, :], in_=ident_d[:, :]), writes=[ident_r])
        P.dma("sp", lambda e: e.dma_start(out=nrm_sb[:, :, :], in_=nrm[:, :, :]), writes=[nrm_r])
        V(lambda e: e.memset(ones_bf[:, :], 1.0), [], [ones_r])
        cl = [Res() for _ in range(8)]
        P.dma("sp", lambda e: e.dma_start(out=cw[:, :, :], in_=cw_d[:, :, :]), writes=[cl[0]])
        P.dma("sp", lambda e: e.dma_start(out=gv[:, :], in_=gv_d[0:1, :].broadcast_to([128, 160])), writes=[cl[1]])
        P.dma("sp", lambda e: e.dma_start(out=rmT[:, :], in_=rmT_d[:, :]), writes=[cl[2]])
        P.dma("sp", lambda e: e.dma_start(out=mk[:, :, :], in_=mk_d[:, :, :]), writes=[cl[3]])
        P.dma("sp", lambda e: e.dma_start(out=sm[:, :], in_=sm_d[:, :]), writes=[cl[4]])
        _t, _tr = tf()
        P.dma("sp", lambda e: e.dma_start(out=_t[:, 0:384], in_=mkb_d[:, :, :].rearrange("p a b -> p (a b)")), writes=[_tr])
        V(lambda e: e.tensor_copy(mkb[:, 0:3, :].rearrange("p a b -> p (a b)"), _t[:, 0:384]), [_tr], [cl[5]])
        _t3, _t3r = tf()
        P.dma("sp", lambda e: e.dma_start(out=_t3[:, 0:128], in_=mprev0_d[:, :]), writes=[_t3r])
        V(lambda e: e.tensor_copy(mkb[:, 3, :], _t3[:, 0:128]), [_t3r], [cl[7]])
        V(lambda e: e.memset(kcar[:, :, :], 0.0), [], [kcar_r])
        V(lambda e: e.memset(vs_tok[:, 0, :, :], 0.0), [], [vs_tok_r])
        V(lambda e: e.tensor_copy(ident_bf[:, :], ident[:, :]), [ident_r], [cl[6]])
        V(lambda e: e.memset(ones_f[:, :], 1.0), [], [cl[6]])
        A(lambda e: e.activation(negA[:, :], gv[:, 0:8], AF.Exp), [cl[1]], [c_r])
        V(lambda e: e.tensor_scalar(negA[:, :], negA[:, :], -1.0, None, ALU.mult), [c_r], [c_r])
        A(lambda e: e.activation(esink_f[:, :], gv[0:1, 144:160], AF.Exp), [cl[1]], [c_r])
        for par in range(2):
            src = esink_f[0:1, :].rearrange("o (k i par) -> o par k i", k=4, i=2)[:, par]
            dstv = esink[0:1, :, :].rearrange("o k (par i q) -> o par k i q", par=2, i=2)[:, par]
            V(lambda e, src=src, dstv=dstv: e.tensor_copy(dstv, src.unsqueeze(3).broadcast_to([1, 4, 2, 128])), [c_r], [c_r])
            for kvh in range(4):
                src2 = esink_f[0:1, 4 * kvh:4 * kvh + 4].rearrange("o (i par) -> o par i", i=2)[:, par]
                dst2 = esinkS[0:1, kvh, :].rearrange("o (par s i t) -> o par s i t", par=2, s=NSQ, i=2)[:, par]
                V(lambda e, src2=src2, dst2=dst2: e.tensor_copy(dst2, src2.unsqueeze(1).unsqueeze(3).broadcast_to([1, NSQ, 2, TS])), [c_r], [c_r])
        _t2, _t2r = tf()
        P.dma("sp", lambda e: e.dma_start(out=_t2[:, 0:512], in_=mkS_d[:, :]), writes=[_t2r])
        V(lambda e: e.tensor_copy(mkS[:, :], _t2[:, 0:512]), [_t2r], [c_r])
        V(lambda e: e.memset(cst[:, :, :], 0.0), [], [cst_r])
        for _i in range(2):
            V(lambda e, _i=_i: e.memset(vnew[_i][:, :], 0.0), [], [vnew_r[_i]])
        V(lambda e: e.memset(Sf[:, 0:8, :], 0.0), [], Sf_r[0:8])
        V(lambda e: e.memset(Sbf[:, 0:8, :], 0.0), [], Sbf_r[0:8])
        CALL = [ident_r, ones_r, c_r] + cl

        wscr = nc.dram_tensor("wscr", [200, 128, SLOT], BF16).ap()
        wreg = {}
        half_r = [[Res(), Res()] for _k in range(NSTG)]
        _hv = [wstg[_k][:, :].bitcast(BF16) for _k in range(NSTG)]
        fslots = [(wslots[0], wslot_r[0], None),
                  (_hv[0][:, 0:SLOT], half_r[0][0], wstg_r[0]),
                  (_hv[1][:, 0:SLOT], half_r[1][0], wstg_r[1]),
                  (wslots[1], wslot_r[1], None),
                  (_hv[0][:, SLOT:2 * SLOT], half_r[0][1], wstg_r[0]),
                  (_hv[1][:, SLOT:2 * SLOT], half_r[1][1], wstg_r[1])]
        fnext = [0]
        cnext = [0]

        def _wload(key, src, n1, ncol):
            if key in wreg:
                tid, tres = wreg[key]
                j = fnext[0]
                fnext[0] = (j + 1) % len(fslots)
                buf, bres, extra = fslots[j]
                wr = [bres]
                if extra is not None:
                    wr.append(extra)
                P.dma("sp", lambda e: e.dma_start(out=buf[:, 0:n1 * ncol], in_=wscr[tid, :, 0:n1 * ncol]), reads=[tres], writes=wr)
                return buf[:, 0:n1 * ncol].rearrange("p (c n) -> p c n", c=n1), bres
            k = snext[0]
            snext[0] = (k + 1) % NSTG
            i = wnext[0]
            wnext[0] = (i + 1) % NSLOT
            sview = wstg[k][:, 0:n1 * ncol].rearrange("p (c n) -> p c n", c=n1)
            view = wslots[i][:, 0:n1 * ncol].rearrange("p (c n) -> p c n", c=n1)
            P.dma("sp", lambda e: e.dma_start(out=sview, in_=src), writes=[wstg_r[k], half_r[k][0], half_r[k][1]])
            ce = ("pool", "act", "dve")[cnext[0] % 3]
            cnext[0] += 1
            if ce == "act":
                P.op("act", lambda e: e.copy(wslots[i][:, 0:n1 * ncol], wstg[k][:, 0:n1 * ncol]),
                     reads=[wstg_r[k]], writes=[wslot_r[i]])
            else:
                P.op(ce, lambda e: e.tensor_copy(wslots[i][:, 0:n1 * ncol], wstg[k][:, 0:n1 * ncol]),
                     reads=[wstg_r[k]], writes=[wslot_r[i]])
            tid = len(wreg)
            tres = Res()
            wreg[key] = (tid, tres)
            P.dma("sp", lambda e: e.dma_start(out=wscr[tid, :, 0:n1 * ncol], in_=wslots[i][:, 0:n1 * ncol]),
                  reads=[wslot_r[i]], writes=[tres])
            return view, wslot_r[i]

        def wload_in(W, col0, ncol):
            src = W.rearrange("(c p) n -> p c n", p=128)[:, :, col0:col0 + ncol]
            return _wload((id(W), "i", col0, ncol), src, DC, ncol)

        def wload_out(W, nj, col0, ncol, r0=0):
            src = W[r0:r0 + nj * 128, :].rearrange("(j p) n -> p j n", p=128)[:, :, col0:col0 + ncol]
            return _wload((id(W), "o", col0, ncol, r0, nj), src, nj, ncol)

        def load_x(xd, tok0, ntok):
            nb = ntok // 128
            src = xd[tok0:tok0 + ntok, :].rearrange("(b p) d -> p b d", p=128)
            P.dma("sp", lambda e: e.dma_start(out=xin[:, 0:nb, :], in_=src), writes=xin_r[0:nb])
            for c in range(DC):
                ps, pr = psum()
                for b in range(nb):
                    P.op("pe", lambda e, b=b, c=c, ps=ps: e.transpose(
                        ps[:, b * 128:(b + 1) * 128], xin[:, b, c * 128:(c + 1) * 128], ident[:, :]),
                        reads=[xin_r[b], ident_r], writes=[pr], inc=(b == nb - 1))
                eng = "act" if c % 2 == 0 else "dve"
                if eng == "act":
                    P.op("act", lambda e, c=c, ps=ps: e.copy(xT[:, c, 0:ntok], ps[:, 0:ntok]),
                         reads=[pr], writes=[xT_r[c]])
                else:
                    P.op("dve", lambda e, c=c, ps=ps: e.tensor_copy(xT[:, c, 0:ntok], ps[:, 0:ntok]),
                         reads=[pr], writes=[xT_r[c]])

        def rmsnorm(which, out, out_r, ntok):
            ps, pr = psum()
            for c in range(DC):
                s = c % 2
                P.op("act", lambda e, c=c, s=s: e.activation(sq[s][:, 0:ntok], xT[:, c, 0:ntok], AF.Square),
                     reads=[xT_r[c]], writes=[sq_r[s]])
                P.op("pe", lambda e, c=c, s=s, ps=ps: e.matmul(ps[:, 0:ntok], ones_bf[:, :], sq[s][:, 0:ntok],
                                                               start=(c == 0), stop=(c == DC - 1)),
                     reads=[sq_r[s], ones_r], writes=[pr], inc=True)
            sd, sd_r = tf()
            rstd, rstd_r = tf()
            P.op("act", lambda e, ps=ps: e.activation(sd[:, 0:ntok], ps[:, 0:ntok], AF.Sqrt, bias=EPS, scale=1.0 / D),
                 reads=[pr], writes=[sd_r])
            P.op("dve", lambda e: e.reciprocal(rstd[:, 0:ntok], sd[:, 0:ntok]), reads=[sd_r], writes=[rstd_r])
            for c in range(DC):
                P.op("dve", lambda e, c=c: e.scalar_tensor_tensor(
                    out[:, c, 0:ntok], xT[:, c, 0:ntok], nrm_sb[:, which, c:c + 1], rstd[:, 0:ntok],
                    ALU.mult, ALU.mult),
                    reads=[xT_r[c], rstd_r, nrm_r], writes=[out_r[c]])

        def ffn(Wi, Wo, ntok):
            for j in range(FC):
                wg, wgr = wload_in(Wi, 128 * j, 128)
                wu, wur = wload_in(Wi, DFF + 128 * j, 128)
                pg, pgr = psum()
                for c in range(DC):
                    P.op("pe", lambda e, c=c, pg=pg, wg=wg: e.matmul(
                        pg[:, 0:ntok], wg[:, c, :], xn[:, c, 0:ntok], start=(c == 0), stop=(c == DC - 1)),
                        reads=[wgr, xn_r[c]], writes=[pgr], inc=(c == DC - 1))
                pu, pur = psum()
                for c in range(DC):
                    P.op("pe", lambda e, c=c, pu=pu, wu=wu: e.matmul(
                        pu[:, 0:ntok], wu[:, c, :], xn[:, c, 0:ntok], start=(c == 0), stop=(c == DC - 1)),
                        reads=[wur, xn_r[c]], writes=[pur], inc=(c == DC - 1))
                sgb, sgr = tf()
                P.op("act", lambda e, sgb=sgb, pg=pg: e.activation(sgb[:, 0:ntok], pg[:, 0:ntok], AF.Silu),
                     reads=[pgr], writes=[sgr])
                P.op("dve", lambda e, sgb=sgb, pu=pu, j=j: e.tensor_tensor(
                    hT[:, j, 0:ntok], sgb[:, 0:ntok], pu[:, 0:ntok], ALU.mult),
                    reads=[sgr, pur], writes=[hT_r[j]])
            for c in range(DC):
                po, por = psum()
                for half in range(2):
                    wo, wor = wload_out(Wo, 11, 128 * c, 128, r0=half * 11 * 128)
                    for jj in range(11):
                        j = half * 11 + jj
                        P.op("pe", lambda e, j=j, jj=jj, po=po, wo=wo: e.matmul(
                            po[:, 0:ntok], wo[:, jj, :], hT[:, j, 0:ntok], start=(j == 0), stop=(j == FC - 1)),
                            reads=[wor, hT_r[j]], writes=[por], inc=(j == FC - 1 or jj == 10))
                P.op("dve", lambda e, c=c, po=po: e.scalar_tensor_tensor(
                    xT[:, c, 0:ntok], po[:, 0:ntok], 0.5, xT[:, c, 0:ntok], ALU.mult, ALU.add),
                    reads=[por, xT_r[c]], writes=[xT_r[c]])

        def store_y(yd, tok0, ntok):
            nb = ntok // 128
            for b in range(nb):
                for cg in range(2):
                    ps, pr = psum()
                    for ci in range(4):
                        c = cg * 4 + ci
                        P.op("pe", lambda e, b=b, c=c, ci=ci, ps=ps: e.transpose(
                            ps[:, ci * 128:(ci + 1) * 128], xT[:, c, b * 128:(b + 1) * 128], ident[:, :]),
                            reads=[xT_r[c], ident_r], writes=[pr], inc=(ci == 3))
                    if cg == 0:
                        P.op("act", lambda e, b=b, cg=cg, ps=ps: e.copy(xin[:, b, cg * 512:(cg + 1) * 512], ps[:, :]),
                             reads=[pr], writes=[xin_r[b]])
                    else:
                        P.op("dve", lambda e, b=b, cg=cg, ps=ps: e.tensor_copy(xin[:, b, cg * 512:(cg + 1) * 512], ps[:, :]),
                             reads=[pr], writes=[xin_r[b]])
            dst = yd[tok0:tok0 + ntok, :].rearrange("(b p) d -> p b d", p=128)
            P.dma("sp", lambda e: e.dma_start(out=dst, in_=xin[:, 0:nb, :]), reads=xin_r[0:nb])


        class UC:
            def __init__(self, u):
                self.u = u
                self.c = {}
                self.pn = 0

            def n(self, name, n):
                i = self.c.get(name, 0)
                self.c[name] = (i + 1) % n
                return self.u * n + i

            def ps(self):
                i = 2 * self.u + self.pn
                self.pn = 1 - self.pn
                self.last = i
                return banks[i], bank_r[i]

            def pb(self):
                i = 2 * self.u + self.pn
                self.pn = 1 - self.pn
                self.last = i
                return banks_bf[i], bank_r[i]

            def pb_of(self, i):
                return banks_bf[i], bank_r[i]

            def tf(self, k):
                return TF[2 * self.u + k], TF_r[2 * self.u + k]

            def sq(self):
                return sqx[self.u], sqx_r[self.u]
        UCS = [UC(0), UC(1), UC(2), UC(3)]
        sqx = [sq[0], sq[1], pT[0], pT[1]]
        sqx_r = [sq_r[0], sq_r[1], pT_r[0], pT_r[1]]
        pTx = [pT[0], pT[1], sq[0], sq[1]]
        pTx_r = [pT_r[0], pT_r[1], sq_r[0], sq_r[1]]

        def run_interleaved(gens, W):
            active = []
            it = iter(gens)
            slots = list(range(W))
            pending = True
            while True:
                while len(active) < W and pending:
                    try:
                        mk_ = next(it)
                    except StopIteration:
                        pending = False
                        break
                    u = slots.pop(0)
                    active.append((u, mk_(UCS[u])))
                if not active:
                    break
                for ent in list(active):
                    u, g = ent
                    try:
                        next(g)
                    except StopIteration:
                        active.remove(ent)
                        slots.append(u)

        def chunk_gen(U, W, off, ch, ntok, fn):
            w, wr = wload_in(W, off + 128 * ch, 128)
            ps, pr = U.ps()
            for c in range(DC):
                M(lambda e, c=c: e.matmul(ps[:, 0:ntok], w[:, c, :], xn[:, c, 0:ntok],
                                          start=(c == 0), stop=(c == DC - 1)), [wr, xn_r[c]], [pr], inc=(c == DC - 1))
            yield
            yield from fn(U, ch, ps, pr)

        def l2norm(U, y, yr, tmp, tmpr, out, outr, ntok, scale):
            sqb, sqr = U.sq()
            A(lambda e: e.activation(sqb[:, 0:ntok], y[:, 0:ntok], AF.Square), [yr], [sqr])
            ps, pr = U.ps()
            M(lambda e: e.matmul(ps[:, 0:ntok], ones_bf[:, :], sqb[:, 0:ntok], start=True, stop=True),
              [sqr, ones_r], [pr])
            yield
            A(lambda e: e.activation(tmp[:, 0:ntok], ps[:, 0:ntok], AF.Ln, bias=EPS, scale=1.0), [pr], [tmpr])
            A(lambda e: e.activation(tmp[:, 0:ntok], tmp[:, 0:ntok], AF.Exp, scale=-0.5), [tmpr], [tmpr])
            yield
            V(lambda e: e.scalar_tensor_tensor(out, y[:, 0:ntok], scale, tmp[:, 0:ntok], ALU.mult, ALU.mult),
              [yr, tmpr], [outr])
            yield

        def to_tok(U, src, srcr, dst, dstr, nb):
            pb, pbr = U.pb()
            for tb in range(nb):
                M(lambda e, tb=tb: e.transpose(pb[:, tb * 128:(tb + 1) * 128], src[:, tb * 128:(tb + 1) * 128], ident_bf[:, :]),
                  [srcr] + CALL, [pbr], inc=(tb == nb - 1))
            yield
            A(lambda e: e.copy(dst, pb[:, 0:nb * 128].rearrange("p (b d) -> p b d", b=nb)), [pbr], [dstr])

        def inproj(mode, tok0, ntok, lastx=False):
            nb = ntok // 128
            isP = mode in ("P", "X")
            isX = mode == "X"
            NSEQ, T = (1, ntok) if isP else (NSQ, TS)
            cs_, sn_ = (cosP, sinP) if mode == "P" else ((cosX, sinX) if isX else (cosS, sinS))
            c0 = tok0 if isP else 0
            P.dma("sp", lambda e: e.dma_start(out=csn[:, 0, 0:ntok], in_=cs_[:, c0:c0 + ntok]), writes=[csn_r])
            P.dma("sp", lambda e: e.dma_start(out=csn[:, 1, 0:ntok], in_=sn_[:, c0:c0 + ntok]), writes=[csn_r])
            wab, wabr = wload_in(win, OA, 16)
            psab, psabr = psum()
            for tb in range(nb):
                for c in range(DC):
                    M(lambda e, tb=tb, c=c: e.matmul(psab[:, tb * 16:(tb + 1) * 16], xn[:, c, tb * 128:(tb + 1) * 128],
                                                     wab[:, c, 0:16], start=(c == 0), stop=(c == DC - 1)),
                      [wabr, xn_r[c]], [psabr], inc=(c == DC - 1))
            gates("P" if isX else mode, nb, psab, psabr)
            gens = []

            def add(W, off, nch, fn):
                for ch in range(nch):
                    gens.append(lambda U, ch=ch: chunk_gen(U, W, off, ch, ntok, fn))

            def conv_ep(grp):
                def f(U, ch, ps, pr):
                    cidx = grp * 8 + ch
                    rawb, rawr = U.tf(0)
                    accb, accr = U.tf(1)
                    rb = rawb[:, 0:NSEQ * (3 + T)].rearrange("p (s t) -> p s t", s=NSEQ)
                    psv = ps[:, 0:ntok].rearrange("p (s t) -> p s t", s=NSEQ)
                    A(lambda e: e.copy(rb[:, :, 3:3 + T], psv), [pr], [rawr])
                    cv_ = cst[:, cidx, 0:NSEQ * 3].rearrange("p (s j) -> p s j", s=NSEQ)
                    V(lambda e: e.tensor_copy(rb[:, :, 0:3], cv_), [cst_r], [rawr])
                    av = accb[:, 0:ntok].rearrange("p (s t) -> p s t", s=NSEQ)
                    if not (isX and grp == 0):
                        A(lambda e: e.mul(av, psv, cw[:, cidx, 3:4]), [pr] + CALL, [accr])
                    yield
                    V(lambda e: e.tensor_copy(cv_, rb[:, :, T:T + 3]), [rawr], [cst_r])
                    if isX and grp == 0:
                        return
                    for j in range(0, 3):
                        V(lambda e, j=j: e.scalar_tensor_tensor(av, rb[:, :, j:j + T], cw[:, cidx, j:j + 1], av,
                                                                ALU.mult, ALU.add), [rawr, accr], [accr])
                    yield
                    if grp == 2:
                        vt_, vtr_ = U.sq()
                        A(lambda e: e.activation(vt_[:, 0:ntok], accb[:, 0:ntok], AF.Silu), [accr], [vtr_])
                        yield
                        yield from to_tok(U, vt_, vtr_, v_tok[:, 0:nb, ch, :], v_tok_r[ch], nb)
                    else:
                        A(lambda e: e.activation(rawb[:, 0:ntok], accb[:, 0:ntok], AF.Silu), [accr], [rawr])
                        yield
                        if grp == 0:
                            yield from l2norm(U, rawb, rawr, accb, accr, qT[:, ch, 0:ntok], qT_r[ch], ntok, 128.0 ** -0.5)
                        else:
                            yield from l2norm(U, rawb, rawr, accb, accr, kT[:, ch, 0:ntok], kT_r[ch], ntok, 1.0)
                            yield from to_tok(U, kT[:, ch, :], kT_r[ch], k_tok[:, 0:nb, ch, :], k_tok_r[ch], nb)
                return f
            if (not isX) or lastx:
                add(win, OQ, 8, conv_ep(0))
            add(win, OK_, 8, conv_ep(1))
            add(win, OV, 8, conv_ep(2))

            def z_ep(U, ch, ps, pr):
                A(lambda e: e.activation(zg[:, ch, 0:ntok], ps[:, 0:ntok], AF.Silu), [pr], [zg_r[ch]])
                yield
            if not isX:
                add(win, OZ, 8, z_ep)

            def ga_ep(U, ch, ps, pr):
                t1b, t1_r = U.tf(0)
                A(lambda e: e.activation(t1b[:, 0:ntok], ps[:, 0:ntok], AF.Sigmoid), [pr], [t1_r])
                yield
                V(lambda e: e.tensor_tensor(zg[:, ch, 0:ntok], zg[:, ch, 0:ntok], t1b[:, 0:ntok], ALU.mult),
                  [t1_r, zg_r[ch]], [zg_r[ch]])
            if not isX:
                add(win, OGA, 8, ga_ep)

            def gb_ep(U, ch, ps, pr):
                A(lambda e: e.activation(gbT[:, ch, 0:ntok], ps[:, 0:ntok], AF.Sigmoid), [pr], [gbT_r[ch]])
                yield
            if not isX:
                add(win, OGB, 8, gb_ep)

            def rope_ep(kind):
                def f(U, ch, ps, pr):
                    yb, ybr = U.tf(0)
                    t2b, t2_r = U.tf(1)
                    t1b, t1_r = yb, ybr
                    A(lambda e: e.copy(yb[:, 0:ntok], ps[:, 0:ntok]), [pr], [ybr])
                    yield
                    p2, p2r = U.ps()
                    M(lambda e: e.matmul(p2[:, 0:ntok], rmT[:, :], yb[:, 0:ntok], start=True, stop=True),
                      [ybr] + CALL, [p2r])
                    V(lambda e: e.tensor_tensor(t2b[:, 0:ntok], yb[:, 0:ntok], csn[:, 0, 0:ntok], ALU.mult),
                      [ybr, csn_r], [t2_r])
                    yield
                    V(lambda e: e.tensor_tensor(t1b[:, 0:ntok], p2[:, 0:ntok], csn[:, 1, 0:ntok], ALU.mult),
                      [p2r, csn_r], [t1_r])
                    if kind == "q":
                        V(lambda e: e.tensor_tensor(qsT[:, ch, 0:ntok], t1b[:, 0:ntok], t2b[:, 0:ntok], ALU.add),
                          [t1_r, t2_r], [qsT_r[ch]])
                    else:
                        V(lambda e: e.tensor_tensor(ksT[:, ch, 128:128 + ntok], t1b[:, 0:ntok], t2b[:, 0:ntok], ALU.add),
                          [t1_r, t2_r], [ksT_r])
                        V(lambda e: e.tensor_tensor(ks32[:, ch, :], t1b[:, ntok - 128:ntok], t2b[:, ntok - 128:ntok], ALU.add),
                          [t1_r, t2_r], [ks32_r])
                    yield
                return f
            if not isX:
                add(win, OQS, 8, rope_ep("q"))
            if (not isX) or lastx:
                add(wks, 0, 4, rope_ep("k"))

            def vs_ep(U, ch, ps, pr):
                yb, ybr = U.tf(0)
                A(lambda e: e.copy(yb[:, 0:ntok], ps[:, 0:ntok]), [pr], [ybr])
                yield
                p2, p2r = U.ps()
                for tb in range(nb):
                    M(lambda e, tb=tb: e.transpose(p2[:, tb * 128:(tb + 1) * 128], yb[:, tb * 128:(tb + 1) * 128], ident[:, :]),
                      [ybr, ident_r], [p2r], inc=(tb == nb - 1))
                yield
                src = p2[:, 0:nb * 128].rearrange("p (b h d) -> p b h d", b=nb, h=2)
                for dup in range(2):
                    V(lambda e, dup=dup: e.tensor_copy(vs_tok[:, 1:1 + nb, 2 * ch:2 * ch + 2, dup * 64:(dup + 1) * 64], src),
                      [p2r], [vs_tok_r])
                V(lambda e: e.tensor_copy(vs32[:, ch * 128:(ch + 1) * 128], p2[:, (nb - 1) * 128:nb * 128]), [p2r], [vs32_r])
            if (not isX) or lastx:
                add(win, OVS, 2, vs_ep)
            run_interleaved(gens, 3)

        def gates(mode, nb, psab, psabr):
            isS = 0 if mode == "P" else 1
            NBK = 2 if mode == "P" else 16
            G = gates_f
            ab = psab[:, 0:nb * 16].rearrange("p (b x) -> p b x", b=nb)

            def g3(i):
                return G[:, i, 0:nb * 8].rearrange("p (b h) -> p b h", b=nb)
            V(lambda e: e.tensor_tensor(g3(0), ab[:, :, 0:8], gv[:, 8:16].unsqueeze(1).broadcast_to([128, nb, 8]), ALU.add),
              [psabr] + CALL, [gts_r])
            A(lambda e: e.activation(g3(0), g3(0), AF.Exp), [gts_r], [gts_r])
            A(lambda e: e.activation(g3(0), g3(0), AF.Ln, bias=1.0), [gts_r], [gts_r])
            V(lambda e: e.tensor_tensor(g3(1), g3(0), negA[:, :].unsqueeze(1).broadcast_to([128, nb, 8]), ALU.mult),
              [gts_r] + CALL, [gts_r])
            A(lambda e: e.activation(g3(2), ab[:, :, 8:16], AF.Sigmoid), [psabr], [gts_r])
            V(lambda e: e.tensor_scalar(g3(3), g3(2), -1.0, None, ALU.mult), [gts_r], [gts_r])
            pg, pgr = psum()
            pgT, pgTr = psum()
            for tb in range(nb):
                M(lambda e, tb=tb: e.matmul(pg[:, tb * 8:(tb + 1) * 8], mk[:, M_TRI + 4 * isS, :], G[:, 1, tb * 8:(tb + 1) * 8],
                                            start=True, stop=True), [gts_r] + CALL, [pgr])
                M(lambda e, tb=tb: e.matmul(pgT[0:8, tb * 128:(tb + 1) * 128], G[:, 1, tb * 8:(tb + 1) * 8], mk[:, M_TRI + 4 * isS, :],
                                            start=True, stop=True), [gts_r] + CALL, [pgTr])
            V(lambda e: e.tensor_copy(G[:, 4, 0:nb * 8], pg[:, 0:nb * 8]), [pgr], [gts_r])
            A(lambda e: e.copy(gcT[:, 0:nb * 128], pgT[0:8, 0:nb * 128]), [pgTr], [gcT_r])
            A(lambda e: e.activation(G[:, 5, 0:nb * 8], G[:, 4, 0:nb * 8], AF.Exp), [gts_r], [gts_r])
            V(lambda e: e.tensor_tensor(G[:, 6, 0:nb * 8], G[:, 2, 0:nb * 8], G[:, 5, 0:nb * 8], ALU.mult), [gts_r], [gts_r])
            lm = sm[:, SM_LASTP:SM_LASTP + 2] if not isS else sm[:, SM_LASTS:SM_LASTS + 16]
            pl, plr = psum()
            rlt, rl_r = tf()
            rl = rlt[:, 0:512].rearrange("p (b x) -> p b x", b=4)
            for tb in range(nb):
                V(lambda e, tb=tb: e.tensor_tensor(
                    rl[:, tb, 0:NBK * 8].rearrange("p (k h) -> p k h", k=NBK),
                    G[:, 4, tb * 8:(tb + 1) * 8].unsqueeze(1).broadcast_to([128, NBK, 8]),
                    lm.unsqueeze(2).broadcast_to([128, NBK, 8]), ALU.mult), [gts_r] + CALL, [rl_r])
                M(lambda e, tb=tb: e.matmul(pl[:, tb * 128:tb * 128 + NBK * 8], ones_f[:, :], rl[:, tb, 0:NBK * 8],
                                            start=True, stop=True), [rl_r] + CALL, [plr])
            for tb in range(nb):
                A(lambda e, tb=tb: e.activation(gl[:, tb, 0:NBK * 8], pl[:, tb * 128:tb * 128 + NBK * 8], AF.Exp), [plr], [gl_r])
            anyc = sm[:, SM_ANYP + isS:SM_ANYP + isS + 1]
            V(lambda e: e.tensor_scalar(G[:, 8, 0:nb * 8], G[:, 4, 0:nb * 8], anyc, None, ALU.mult), [gts_r] + CALL, [gts_r])
            po, por = psum()
            for tb in range(nb):
                M(lambda e, tb=tb: e.matmul(po[:, tb * 8:(tb + 1) * 8], mk[:, M_BLK + 4 * isS, :], G[:, 8, tb * 8:(tb + 1) * 8],
                                            start=True, stop=True), [gts_r] + CALL, [por])
            V(lambda e: e.tensor_tensor(G[:, 7, 0:nb * 8], po[:, 0:nb * 8], G[:, 4, 0:nb * 8], ALU.subtract), [por, gts_r], [gts_r])
            A(lambda e: e.activation(G[:, 7, 0:nb * 8], G[:, 7, 0:nb * 8], AF.Exp), [gts_r], [gts_r])

        def gdn_prep(U, mode, tb, h, d, so=False):
            isS = 0 if mode == "P" else 1
            NL = 6 if mode == "P" else 3
            G = gates_f
            u = U.u
            cols = slice(tb * 128, (tb + 1) * 128)
            gcol = G[:, 4, tb * 8 + h:tb * 8 + h + 1]
            i = u
            d["i"] = i
            psG, psGr = U.ps()
            M(lambda e: e.matmul(psG[:, 0:128], kT[:, h, cols], kT[:, h, cols], start=True, stop=True), [kT_r[h]], [psGr])
            if not so:
                M(lambda e: e.matmul(psG[:, 128:256], kT[:, h, cols], qT[:, h, cols], start=True, stop=True), [kT_r[h], qT_r[h]], [psGr])
            V(lambda e: e.tensor_scalar(gch[u][:, :], gcT[:, cols], ident[0:8, h:h + 1], None, ALU.mult), [gcT_r, ident_r], [gch_r[u]])
            psR, psRr = U.ps()
            M(lambda e: e.matmul(psR[:, 0:128], ones_f[0:8, :], gch[u][:, :], start=True, stop=True), [gch_r[u]] + CALL, [psRr])
            V(lambda e: e.tensor_scalar(vb[i][:, :], v_tok[:, tb, h, :], G[:, 2, tb * 8 + h:tb * 8 + h + 1], None, ALU.mult),
               [v_tok_r[h], gts_r], [vb_r[i]])
            V(lambda e: e.tensor_scalar(kbg[i][:, :], k_tok[:, tb, h, :], G[:, 6, tb * 8 + h:tb * 8 + h + 1], None, ALU.mult),
               [k_tok_r[h], gts_r], [kbg_r[i]])
            yield
            ia = U.n("e1", 2)
            ib = U.n("e1", 2)
            V(lambda e: e.scalar_tensor_tensor(e1[ia][:, :], psR[:, 0:128], gcol, mk[:, M_M1 + 4 * isS, :], ALU.subtract, ALU.add),
              [psRr, gts_r] + CALL, [e1_r[ia]])
            if not so:
                V(lambda e: e.scalar_tensor_tensor(e1[ib][:, :], psR[:, 0:128], gcol, mk[:, M_M2 + 4 * isS, :], ALU.subtract, ALU.add),
                  [psRr, gts_r] + CALL, [e1_r[ib]])
                A(lambda e: e.activation(EG[i][:, :], psR[:, 0:128], AF.Exp), [psRr], [EG_r[i]])
            yield
            A(lambda e: e.activation(Dm[i][:, :], e1[ia][:, :], AF.Exp, scale=-1.0), [e1_r[ia]], [Dm_r[i]])
            if not so:
                A(lambda e: e.activation(DTm[i][:, :], e1[ib][:, :], AF.Exp), [e1_r[ib]], [DTm_r[i]])
                V(lambda e: e.tensor_tensor(qg[i][:, :], qT[:, h, cols], EG[i][:, :], ALU.mult), [qT_r[h], EG_r[i]], [qg_r[i]])
            yield
            V(lambda e: e.scalar_tensor_tensor(Bm[i][:, :], psG[:, 0:128], G[:, 3, tb * 8 + h:tb * 8 + h + 1], Dm[i][:, :],
                                               ALU.mult, ALU.mult), [psGr, gts_r, Dm_r[i]], [Bm_r[i]])
            if not so:
                V(lambda e: e.tensor_tensor(QKD[i][:, :], psG[:, 128:256], DTm[i][:, :], ALU.mult), [psGr, DTm_r[i]], [QKD_r[i]])
            yield
            pb, pbr = U.pb()
            M(lambda e: e.transpose(pb[:, 0:128], Bm[i][:, :], ident_bf[:, :]), [Bm_r[i]] + CALL, [pbr])
            yield
            p0 = U.n("Pm", 3)
            A(lambda e: e.copy(PTm[p0][:, :], pb[:, 0:128]), [pbr], [PTm_r[p0]])
            t0 = U.n("TT", 2)
            V(lambda e, t0=t0: e.tensor_tensor(TT[t0][:, :], pb[:, 0:128], ident_bf[:, :], ALU.add), [pbr] + CALL, [TT_r[t0]])
            yield
            Pc, Pcr = Bm[i], Bm_r[i]
            PTc, PTcr = PTm[p0], PTm_r[p0]
            for k in range(NL - 1):
                ps, psr = U.ps()
                M(lambda e, PTc=PTc, Pc=Pc, ps=ps: e.matmul(ps[:, 0:128], PTc[:, :], Pc[:, :], start=True, stop=True),
                  [PTcr, Pcr], [psr])
                if k < NL - 2:
                    M(lambda e, PTc=PTc, Pc=Pc, ps=ps: e.matmul(ps[:, 128:256], Pc[:, :], PTc[:, :], start=True, stop=True),
                      [PTcr, Pcr], [psr])
                yield
                pn = U.n("Pm", 3)
                A(lambda e, pn=pn, ps=ps: e.copy(Pm[pn][:, :], ps[:, 0:128]), [psr], [Pm_r[pn]])
                if k < NL - 2:
                    A(lambda e, pn=pn, ps=ps: e.copy(PTm[pn][:, :], ps[:, 128:256]), [psr], [PTm_r[pn]])
                Pc, Pcr = Pm[pn], Pm_r[pn]
                PTc, PTcr = PTm[pn], PTm_r[pn]
                yield
                ps2, ps2r = U.ps()
                M(lambda e, Pc=Pc, t0=t0, ps2=ps2: e.matmul(ps2[:, 0:128], Pc[:, :], TT[t0][:, :], start=True, stop=True),
                  [Pcr, TT_r[t0]], [ps2r])
                yield
                t1 = U.n("TT", 2)
                V(lambda e, t0=t0, t1=t1, ps2=ps2: e.tensor_tensor(TT[t1][:, :], ps2[:, 0:128], TT[t0][:, :], ALU.add),
                  [ps2r, TT_r[t0]], [TT_r[t1]])
                t0 = t1
                yield
            d["TT"] = TT[t0]
            d["TTr"] = TT_r[t0]

        def o_finish(U, pso, psor, tb, h, pbsel=None):
            cols = slice(tb * 128, (tb + 1) * 128)
            j = U.u
            sm_ = small[:, 4 * j:4 * j + 4]
            smr = small_r[j]
            A(lambda e: e.activation(otmp[j][:, :], pso, AF.Square, accum_out=sm_[:, 0:1]), [psor], [otmp_r[j], smr])
            A(lambda e: e.activation(sm_[:, 1:2], sm_[:, 0:1], AF.Ln, bias=EPS, scale=1.0 / 128), [smr], [smr])
            A(lambda e: e.activation(sm_[:, 2:3], sm_[:, 1:2], AF.Exp, scale=-0.5), [smr], [smr])
            yield
            V(lambda e: e.scalar_tensor_tensor(ogn[j][:, :], pso, sm_[:, 2:3], gnw, ALU.mult, ALU.mult),
              [psor, smr] + CALL, [ogn_r[j]])
            yield
            pb, pbr = pbsel if pbsel is not None else U.pb()
            M(lambda e: e.transpose(pb[:, 0:128], ogn[j][:, :], ident_bf[:, :]), [ogn_r[j]] + CALL, [pbr])
            yield
            V(lambda e: e.tensor_tensor(otmp[j][:, :], pb[:, 0:128], zg[:, h, cols], ALU.mult), [pbr, zg_r[h]], [otmp_r[j]])
            V(lambda e: e.tensor_tensor(mixT[:, h, cols], mixT[:, h, cols], otmp[j][:, :], ALU.add), [otmp_r[j], mixT_r[h]], [mixT_r[h]])

        def gdn_prompt(nb, so=False):
            G = gates_f

            def unit(U, tb, h):
                d = {}
                yield from gdn_prep(U, "P", tb, h, d, so)
                i, TTc, TTr = d["i"], d["TT"], d["TTr"]
                V(lambda e: e.tensor_scalar(kgt[i][:, :], k_tok[:, tb, h, :], G[:, 7, tb * 8 + h:tb * 8 + h + 1], None, ALU.mult),
                   [k_tok_r[h], gts_r], [kgt_r[i]])
                pu, pur = U.ps()
                M(lambda e: e.matmul(pu[:, 0:128], TTc[:, :], vb[i][:, :], start=True, stop=True), [TTr, vb_r[i]], [pur])
                M(lambda e: e.matmul(pu[:, 128:256], kbg[i][:, :], TTc[:, :], start=True, stop=True), [TTr, kbg_r[i]], [pur])
                yield
                A(lambda e: e.copy(ub[i][:, :], pu[:, 0:128]), [pur], [ub_r[i]])
                A(lambda e: e.copy(wTb[i][:, :], pu[:, 128:256]), [pur], [wTb_r[i]])
                yield
                pw, pwr = U.ps()
                pS, pSr = U.ps()
                pS_i = U.last
                for cc in range(2):
                    r = slice(64 * cc, 64 * cc + 64)
                    M(lambda e, r=r: e.matmul(pw[r, 0:128], wTb[i][:, r], Sbf[:, h, :], start=True, stop=True),
                      [wTb_r[i], Sbf_r[h]], [pwr])
                    yield
                    V(lambda e, r=r: e.tensor_tensor(vnew[i][r, :], ub[i][r, :], pw[r, 0:128], ALU.subtract),
                      [ub_r[i], pwr], [vnew_r[i]])
                    yield
                    if not so:
                        M(lambda e, r=r: e.matmul(pw[r, 128:256], qg[i][:, r], Sbf[:, h, :], start=True, stop=False),
                          [qg_r[i], Sbf_r[h]], [pwr], inc=False)
                        M(lambda e, r=r: e.matmul(pw[r, 128:256], QKD[i][:, r], vnew[i][:, :], start=False, stop=True),
                          [QKD_r[i], vnew_r[i]], [pwr])
                    M(lambda e, r=r, cc=cc, pS=pS: e.matmul(pS[:, 0:128], kgt[i][r, :], vnew[i][r, :], start=True, stop=True),
                      [kgt_r[i], vnew_r[i]], [pSr])
                    yield
                    V(lambda e, cc=cc, pS=pS: e.scalar_tensor_tensor(Sbf[:, h, :], Sf[:, h, :], gl[:, tb, cc * 8 + h:cc * 8 + h + 1],
                                                                     pS[:, 0:128], ALU.mult, ALU.add),
                      [Sf_r[h], gl_r, pSr], [Sbf_r[h]])
                    V(lambda e, cc=cc, pS=pS: e.scalar_tensor_tensor(Sf[:, h, :], Sf[:, h, :], gl[:, tb, cc * 8 + h:cc * 8 + h + 1],
                                                                     pS[:, 0:128], ALU.mult, ALU.add),
                      [Sf_r[h], gl_r, pSr], [Sf_r[h]])
                    yield
                if not so:
                    yield from o_finish(U, pw[:, 128:256], pwr, tb, h, U.pb_of(pS_i))

            gens = []
            for tb in range(nb):
                for h in range(8):
                    gens.append(lambda U, tb=tb, h=h: unit(U, tb, h))
            V(lambda e: e.memset(vnew[2][:, :], 0.0), [], [xin_r[2], xin_r[3]] + s2res)
            V(lambda e: e.memset(vnew[3][:, :], 0.0), [], TF_r[0:4] + s3res)
            run_interleaved(gens, 4)
            V(lambda e: e.memset(dummy[:, 0:1], 0.0), [], [xin_r[2], xin_r[3], dummy_r] + s2res)
            V(lambda e: e.memset(dummy[:, 1:2], 0.0), [], TF_r[0:4] + [dummy_r] + s3res)

        def gdn_sample():
            G = gates_f
            tb = 0

            def unit(h):
                P.dma("sp", lambda e, h=h: e.dma_start(out=Sf[:, 0:NSQ, :], in_=sgdn[:, h].rearrange("s k v -> k s v")), writes=Sf_r)
                A(lambda e: e.copy(Sbf[:, 0:NSQ, :], Sf[:, 0:NSQ, :]), Sf_r, Sbf_r)
                d = {}
                for _ in gdn_prep(UCS[0], "S", tb, h, d):
                    pass
                i, TTc, TTr = d["i"], d["TT"], d["TTr"]
                pu, pur = psum()
                M(lambda e: e.matmul(pu[:, 0:128], vb[i][:, :], TTc[:, :], start=True, stop=True), [TTr, vb_r[i]], [pur])
                M(lambda e: e.matmul(pu[:, 128:256], kbg[i][:, :], TTc[:, :], start=True, stop=True), [TTr, kbg_r[i]], [pur])
                A(lambda e: e.copy(ub[i][:, :], pu[:, 0:128]), [pur], [ub_r[i]])
                A(lambda e: e.copy(wTb[i][:, :], pu[:, 128:256]), [pur], [wTb_r[i]])
                pw, pwr = psum()
                for s in range(NSQ):
                    c8 = slice(8 * s, 8 * s + 8)
                    M(lambda e, s=s, c8=c8: e.matmul(pw[:, 8 * s:8 * s + 8], Sbf[:, s, :], wTb[i][:, c8], start=True, stop=True),
                      [wTb_r[i], Sbf_r[s]], [pwr], inc=False)
                    M(lambda e, s=s, c8=c8: e.matmul(pw[:, 128 + 8 * s:128 + 8 * s + 8], Sbf[:, s, :], qg[i][:, c8], start=True, stop=True),
                      [qg_r[i], Sbf_r[s]], [pwr], inc=(s == NSQ - 1))
                V(lambda e: e.tensor_tensor(vnf[i][:, :], ub[i][:, :], pw[:, 0:128], ALU.subtract), [ub_r[i], pwr], [vnf_r[i]])
                A(lambda e: e.copy(oq[i][:, :], pw[:, 128:256]), [pwr], [oq_r[i]])
                p2, p2r = psum()
                M(lambda e: e.transpose(p2[:, 0:128], vnf[i][:, :], ident[:, :]), [vnf_r[i], ident_r], [p2r])
                V(lambda e: e.tensor_copy(vnew[i][:, :], p2[:, 0:128]), [p2r], [vnew_r[i]])
                M(lambda e: e.matmul(p2[:, 128:256], vnew[i][:, :], QKD[i][:, :], start=True, stop=True), [vnew_r[i], QKD_r[i]], [p2r])
                V(lambda e: e.tensor_tensor(oq[i][:, :], oq[i][:, :], p2[:, 128:256], ALU.add), [oq_r[i], p2r], [oq_r[i]])
                M(lambda e: e.transpose(p2[:, 256:384], oq[i][:, :], ident[:, :]), [oq_r[i], ident_r], [p2r])
                for _ in o_finish(UCS[0], p2[:, 256:384], p2r, tb, h):
                    pass
                for s in range(NSQ):
                    kk = nxt("kgt", 2)
                    V(lambda e, s=s, kk=kk: e.tensor_scalar(kgt[kk][:, :], k_tok[:, tb, h, :], G[:, 7, h:h + 1],
                                                            sm[:, SM_SEQ + s:SM_SEQ + s + 1], ALU.mult, ALU.mult),
                      [k_tok_r[h], gts_r] + CALL, [kgt_r[kk]])
                    pS, pSr = psum()
                    M(lambda e, kk=kk: e.matmul(pS[:, 0:128], kgt[kk][:, :], vnew[i][:, :], start=True, stop=True),
                      [kgt_r[kk], vnew_r[i]], [pSr])
                    V(lambda e, s=s: e.scalar_tensor_tensor(Sf[:, s, :], Sf[:, s, :], gl[:, 0, s * 8 + h:s * 8 + h + 1], pS[:, 0:128],
                                                            ALU.mult, ALU.add), [Sf_r[s], gl_r, pSr], [Sf_r[s]])
                P.dma("sp", lambda e, h=h: e.dma_start(out=gdns[:, h].rearrange("s k v -> k s v"), in_=Sf[:, 0:NSQ, :]), reads=Sf_r)
            for h in range(8):
                unit(h)

        def swa_block(U, mode, tb, kvh, first_tile, cache_fn=None):
            cols = slice(tb * 128, (tb + 1) * 128)
            isS = mode == "S"
            kbs = []
            if not isS:
                kbs.append((tb, 3 if (first_tile and tb == 0) else 2))
                kbs.append((tb + 1, 0))
            else:
                kbs.append((1, 1))
            u = U.u
            psO, psOr = banks[2 * u], bank_r[2 * u]
            psD, psDr = banks[2 * u + 1], bank_r[2 * u + 1]
            npt = len(kbs)
            for n_, (blk, mi) in enumerate(kbs):
                psS2 = [(banks[2 * u + 4], bank_r[2 * u + 4]), (banks[2 * u + 5], bank_r[2 * u + 5])]
                pi = 2 * u + n_
                for par in range(2):
                    r = slice(64 * par, 64 * par + 64)
                    psS, psSr = psS2[par]
                    out = psS[:, 0:256]
                    if isS:
                        rhs = qsT[r, 2 * kvh:2 * kvh + 2, 0:128].rearrange("p i (s t) -> p s i t", s=NSQ)
                    else:
                        rhs = qsT[r, 2 * kvh:2 * kvh + 2, cols]
                    M(lambda e, r=r, out=out, blk=blk, rhs=rhs: e.matmul(out, ksT[r, kvh, blk * 128:(blk + 1) * 128], rhs,
                                                                          start=True, stop=True),
                      [ksT_r, qsT_r[2 * kvh], qsT_r[2 * kvh + 1]], [psSr])
                    yield
                    A(lambda e, pi=pi, psS=psS, par=par: e.activation(pTx[pi][:, par * 256:(par + 1) * 256], psS[:, 0:256], AF.Exp, scale=0.125),
                      [psSr], [pTx_r[pi]])
                yield
                if isS:
                    V(lambda e, pi=pi: e.tensor_tensor(pTx[pi][:, :], pTx[pi][:, :], mkS[:, :], ALU.mult),
                      [pTx_r[pi]] + CALL, [pTx_r[pi]])
                else:
                    pv = pTx[pi][:, :].rearrange("p (g q) -> p g q", g=4)
                    V(lambda e, pv=pv, mi=mi: e.tensor_tensor(pv, pv, mkb[:, mi, :].unsqueeze(1).broadcast_to([128, 4, 128]), ALU.mult),
                       [pTx_r[pi]] + CALL, [pTx_r[pi]])
                yield
                last = (n_ == npt - 1)
                M(lambda e, pi=pi, blk=blk, n_=n_, last=last: e.matmul(psO[:, :], vs_tok[:, blk, kvh, :], pTx[pi][:, :],
                                                                       start=(n_ == 0), stop=last),
                  [vs_tok_r, pTx_r[pi]], [psOr])
                M(lambda e, pi=pi, n_=n_: e.matmul(psD[:, :], ones_bf[:, :], pTx[pi][:, :], start=(n_ == 0), stop=False),
                  [pTx_r[pi], ones_r], [psDr])
            if cache_fn is not None:
                cache_fn(kvh, psO, psOr, psD, psDr)
            esk = esinkS if isS else esink
            M(lambda e: e.matmul(psD[:, :], ones_bf[0:1, :], esk[0:1, kvh, :], start=False, stop=True), [ones_r] + CALL, [psDr])
            yield
            rden, rden_r = U.tf(0)
            osw, osw_r = U.tf(1)
            V(lambda e: e.reciprocal(rden[:, 0:512], psD[:, :]), [psDr], [rden_r])
            V(lambda e: e.tensor_tensor(osw[:, 0:512], psO[:, :], rden[:, 0:512], ALU.mult), [psOr, rden_r], [osw_r])
            yield
            for par in range(2):
                r = slice(64 * par, 64 * par + 64)
                if isS:
                    src = osw[r, par * 256:(par + 1) * 256].rearrange("p (s i t) -> p i s t", s=NSQ, i=2)
                    dst = mixT[r, 2 * kvh:2 * kvh + 2, 0:128].rearrange("p i (s t) -> p i s t", s=NSQ)
                    gb_ = gbT[r, 2 * kvh:2 * kvh + 2, 0:128].rearrange("p i (s t) -> p i s t", s=NSQ)
                else:
                    src = osw[r, par * 256:(par + 1) * 256].rearrange("p (i q) -> p i q", i=2)
                    dst = mixT[r, 2 * kvh:2 * kvh + 2, cols]
                    gb_ = gbT[r, 2 * kvh:2 * kvh + 2, cols]
                V(lambda e, src=src, dst=dst, gb_=gb_: e.tensor_tensor(dst, src, gb_, ALU.mult),
                  [osw_r, gbT_r[2 * kvh], gbT_r[2 * kvh + 1]], [mixT_r[2 * kvh], mixT_r[2 * kvh + 1]])

        def swa_prompt(nb, first_tile, ntok, carry_only=False):
            V(lambda e: e.tensor_copy(ksT[:, :, 0:128], kcar[:, :, :]), [kcar_r], [ksT_r])
            gens = []
            for tb in range(0 if carry_only else nb):
                for kvh in range(4):
                    gens.append(lambda U, tb=tb, kvh=kvh: swa_block(U, "P", tb, kvh, first_tile))
            run_interleaved(gens, 2)
            V(lambda e: e.tensor_copy(kcar[:, :, :], ksT[:, :, ntok:ntok + 128]), [ksT_r], [kcar_r])
            V(lambda e: e.tensor_copy(vs_tok[:, 0, :, :], vs_tok[:, nb, :, :]), [vs_tok_r], [vs_tok_r])

        mcb = sb("mcb", [128, TS], BF16)
        mcb_r = Res()

        def swa_sample():
            V(lambda e: e.tensor_copy(mcb[:, :], sm[:, SM_MC:SM_MC + TS]), CALL, [mcb_r])
            for s in range(NSQ):
                for dup in range(2):
                    P.dma("sp", lambda e, s=s, dup=dup: e.dma_start(out=kcs[:, :, dup, :], in_=ck[s].rearrange("r (k d) -> r k d", k=4)), writes=[kcs_r])
                p2, p2r = psum()
                for kvh in range(4):
                    M(lambda e, kvh=kvh, p2=p2: e.transpose(p2[:, kvh * 128:(kvh + 1) * 128], kcs[:, kvh, :, :].rearrange("p a d -> p (a d)"), ident[:, :]),
                      [kcs_r, ident_r], [p2r], inc=(kvh == 3))
                ki = nxt("kcT", 2)
                A(lambda e, ki=ki, p2=p2: e.copy(kcT[ki][:, :, :], p2[:, :].rearrange("p (k r) -> p k r", k=4)), [p2r], [kcT_r[ki]])
                psS2 = [psum(), psum()]
                for par in range(2):
                    psS, psSr = psS2[par]
                    r = slice(64 * par, 64 * par + 64)
                    for kvh in range(4):
                        out = psS[:, kvh * 16:kvh * 16 + 16]
                        M(lambda e, r=r, out=out, s=s, kvh=kvh, ki=ki: e.matmul(out, kcT[ki][r, kvh, :],
                                                                            qsT[r, 2 * kvh:2 * kvh + 2, 8 * s:8 * s + 8], start=True, stop=True),
                          [kcT_r[ki], qsT_r[2 * kvh], qsT_r[2 * kvh + 1]], [psSr], inc=(kvh == 3))
                    dst = pTc[:, :, :].rearrange("p k (par s x) -> p par s k x", par=2, s=NSQ)[:, par, s]
                    A(lambda e, dst=dst, psS=psS: e.activation(dst, psS[:, 0:64].rearrange("p (k x) -> p k x", k=4), AF.Exp, scale=0.125),
                      [psSr], Sbf_r)
            for kvh in range(4):
                pv = pTc[:, kvh, :].rearrange("p (x t) -> p x t", t=TS)
                V(lambda e, pv=pv: e.tensor_tensor(pv, pv, mcb[:, :].unsqueeze(1).broadcast_to([128, 64, TS]), ALU.mult),
                  pTc_r[kvh] + [mcb_r], pTc_r[kvh])

            def cache_fn(kvh, psO, psOr, psD, psDr):
                M(lambda e: e.matmul(psD[:, :], ones_bf[:, :], pTc[:, kvh, :], start=False, stop=False), pTc_r[kvh] + [ones_r], [psDr])
                for s in range(NSQ):
                    for par in range(2):
                        c16 = slice(par * 256 + s * 16, par * 256 + s * 16 + 16)
                        for half in range(2):
                            lastm = (s == NSQ - 1 and par == 1 and half == 1)
                            M(lambda e, c16=c16, s=s, half=half, lastm=lastm: e.matmul(
                                psO[64 * half:64 * half + 64, c16], vcall[:, s, kvh, :], pTc[:, kvh, c16], start=False, stop=True, skip_group_check=True),
                              [vcall_r[s]] + pTc_r[kvh], [psOr], inc=lastm)
            for kvh in range(4):
                for _ in swa_block(UCS[0], "S", 0, kvh, False, cache_fn):
                    pass

        def load_cache():
            for s in range(NSQ):
                vt, vtr = tf()
                P.dma("sp", lambda e, s=s, vt=vt: e.dma_start(out=vt[:, 0:256], in_=cv[s]), writes=[vtr])
                V(lambda e, s=s, vt=vt: e.tensor_copy(vcall[:, s, :, :].rearrange("p k d -> p (k d)"), vt[:, 0:256]), [vtr], [vcall_r[s]])

        def load_sconv():
            P.dma("sp", lambda e: e.dma_start(out=xin[0:NSQ * 3, 0, :], in_=sconv[:, 0:1024]), writes=[xin_r[0]])
            P.dma("sp", lambda e: e.dma_start(out=xin[0:NSQ * 3, 1, :], in_=sconv[:, 1024:2048]), writes=[xin_r[1]])
            P.dma("sp", lambda e: e.dma_start(out=xin[0:NSQ * 3, 2, :], in_=sconv[:, 2048:3072]), writes=[xin_r[2]])
            for g in range(6):
                p2, p2r = psum()
                for ci in range(4):
                    cidx = g * 4 + ci
                    M(lambda e, cidx=cidx, ci=ci, p2=p2: e.transpose(p2[:, ci * 128:ci * 128 + NSQ * 3],
                                                              xin[0:NSQ * 3, cidx // 8, (cidx % 8) * 128:(cidx % 8 + 1) * 128], ident[0:NSQ * 3, 0:NSQ * 3]),
                      [xin_r[cidx // 8], ident_r], [p2r], inc=(ci == 3))
                V(lambda e, g=g, p2=p2: e.tensor_copy(cst[:, g * 4:(g + 1) * 4, 0:NSQ * 3],
                                               p2[:, :].rearrange("p (c x) -> p c x", c=4)[:, :, 0:NSQ * 3]), [p2r], [cst_r])

        def store_conv(dst, nrow):
            for g in range(6):
                p2, p2r = psum()
                for ci in range(4):
                    cidx = g * 4 + ci
                    M(lambda e, cidx=cidx, ci=ci, p2=p2: e.transpose(p2[0:nrow, ci * 128:(ci + 1) * 128], cst[:, cidx, 0:nrow], ident[:, :]),
                      [cst_r, ident_r], [p2r], inc=(ci == 3))
                A(lambda e, g=g, p2=p2: e.copy(xin[0:nrow, g // 2, (g % 2) * 512:(g % 2 + 1) * 512], p2[0:nrow, :]), [p2r], [xin_r[g // 2]])
            for q in range(3):
                P.dma("sp", lambda e, q=q: e.dma_start(out=dst[:, q * 1024:(q + 1) * 1024], in_=xin[0:nrow, q, :]), reads=[xin_r[q]])

        def store_kv(kdst, vdst):
            p2, p2r = psum()
            for kvh in range(4):
                M(lambda e, kvh=kvh: e.transpose(p2[:, kvh * 128:(kvh + 1) * 128], ks32[:, kvh, :], ident[:, :]),
                  [ks32_r, ident_r], [p2r], inc=(kvh == 3))
            A(lambda e: e.copy(knew[:, :, :], p2[:, :].rearrange("p (k x) -> p k x", k=4)[:, :, 0:64]), [p2r], [knew_r])
            P.dma("sp", lambda e: e.dma_start(out=kdst, in_=knew[:, :, :]), reads=[knew_r])
            P.dma("sp", lambda e: e.dma_start(out=vdst, in_=vs32[:, :].rearrange("p (k d) -> p k d", k=4)), reads=[vs32_r])

        def outproj(ntok):
            for c in range(DC):
                wo, wor = wload_out(wout, DC, 128 * c, 128)
                po, por = psum()
                for j in range(DC):
                    M(lambda e, j=j, po=po, wo=wo: e.matmul(po[:, 0:ntok], wo[:, j, :], mixT[:, j, 0:ntok],
                                                            start=(j == 0), stop=(j == DC - 1)),
                      [wor, mixT_r[j]], [por], inc=(j == DC - 1))
                V(lambda e, c=c, po=po: e.tensor_tensor(xT[:, c, 0:ntok], po[:, 0:ntok], xT[:, c, 0:ntok], ALU.add),
                  [por, xT_r[c]], [xT_r[c]])

        def tile_pass(mode, xd, yd, tok0, ntok, first, last):
            nb = ntok // 128
            load_x(xd, tok0, ntok)
            rmsnorm(0, xn, xn_r, ntok)
            ffn(w1i, w1o, ntok)
            if stages >= 2:
                rmsnorm(1, xn, xn_r, ntok)
                if mode == "X":
                    inproj("X", tok0, ntok, lastx=last)
                    if last:
                        swa_prompt(nb, False, ntok, carry_only=True)
                    gdn_prompt(nb, so=True)
                    return
                if MIX & 1:
                    inproj(mode, tok0, ntok)
                if mode == "P":
                    if MIX & 2:
                        swa_prompt(nb, first, ntok)
                    if MIX & 4:
                        gdn_prompt(nb)
                else:
                    if MIX & 8:
                        swa_sample()
                    if MIX & 16:
                        gdn_sample()
                if MIX & 32:
                    outproj(ntok)
            if stages >= 3:
                rmsnorm(2, xn, xn_r, ntok)
                ffn(w2i, w2o, ntok)
            rmsnorm(3, xT, xT_r, ntok)
            store_y(yd, tok0, ntok)

        NX = LPRE // 512
        for t in range(NX):
            tile_pass("X", xpre, None, t * 512, 512, False, t == NX - 1)
        NT = LP // 512
        for t in range(NT):
            tile_pass("P", xp, yp, t * 512, 512, t == 0, t == NT - 1)
        if stages >= 2:
            if MIX & 64:
                store_conv(convp, 3)
            if MIX & 128:
                store_kv(kp.rearrange("r (k d) -> r k d", k=4), vp.rearrange("r (k d) -> r k d", k=4))
            if MIX & 256:
                P.dma("sp", lambda e: e.dma_start(out=gdnp.rearrange("h k v -> k h v"), in_=Sf[:, 0:8, :]), reads=Sf_r[0:8])
            if MIX & 512:
                load_sconv()
            if MIX & 1024:
                load_cache()
            if MIX & 2048:
                P.dma("sp", lambda e: e.dma_start(out=ksd[:, 0:120, :], in_=ck[:, 8:128, :]))
                P.dma("sp", lambda e: e.dma_start(out=vsd[:, 0:120, :], in_=cv[:, 8:128, :]))
        tile_pass("S", xs, ys, 0, NS_TOK, False, True)
        if stages >= 2:
            if MIX & 4096:
                store_conv(convs, NSQ * 3)
            if MIX & 8192:
                store_kv(ksd[:, 120:128, :].rearrange("s t (k d) -> s t k d", k=4), vsd[:, 120:128, :].rearrange("s t (k d) -> s t k d", k=4))
        P.finish()

        with nc.Block() as block:
            @block.tensor
            def _(eng):
                P.replay(eng, "pe")

            @block.scalar
            def _(eng):
                P.replay(eng, "act")

            @block.vector
            def _(eng):
                P.replay(eng, "dve")

            @block.gpsimd
            def _(eng):
                P.replay(eng, "pool")

            @block.sync
            def _(eng):
                P.replay(eng, "sp")
    return nc


def host_consts(NSQ, TS):
    f = np.float32
    ident = np.eye(128, dtype=f)
    p = np.arange(128)
    rm = np.zeros((128, 128), f)
    for m in range(128):
        d = m % 64
        if d < 8:
            rm[m, m + 8] = -1.0
        elif d < 16:
            rm[m, m - 8] = 1.0
    rmT = np.ascontiguousarray(rm.T)
    mk = np.zeros((128, 8, 128), f)
    for isS, C in ((0, 64), (1, TS)):
        same = (p[:, None] // C) == (p[None, :] // C)
        mk[:, 0 + 4 * isS, :] = (same & (p[:, None] <= p[None, :])).astype(f)
        mk[:, 1 + 4 * isS, :] = np.where(same & (p[:, None] > p[None, :]), 0.0, BIG)
        mk[:, 2 + 4 * isS, :] = np.where(same & (p[None, :] >= p[:, None]), 0.0, -BIG)
        mk[:, 3 + 4 * isS, :] = same.astype(f)
    mkb = np.zeros((128, 3, 128), f)
    mprev = (p[:, None] > p[None, :]).astype(f)
    mkb[:, 0, :] = (p[:, None] <= p[None, :]).astype(f)
    mkb[:, 1, :] = (((p[:, None] // TS) == (p[None, :] // TS)) & (p[:, None] <= p[None, :])).astype(f)
    mkb[:, 2, :] = (p[:, None] > p[None, :]).astype(f)
    mkS = np.zeros((128, 2, NSQ, 2, TS), f)
    for k in range(128):
        ks_, kt_ = k // TS, k % TS
        mkS[k, :, ks_, :, kt_:] = 1.0
    mkS = np.ascontiguousarray(mkS.reshape(128, 512))
    sm = np.zeros((128, 64), f)
    for b in range(2):
        sm[64 * b + 63, SM_LASTP + b] = 1.0
    for s in range(NSQ):
        sm[TS * s + TS - 1, SM_LASTS + s] = 1.0
        sm[TS * s:TS * (s + 1), SM_SEQ + s] = 1.0
    sm[:, SM_ANYP] = (p % 64 == 63).astype(f)
    sm[:, SM_ANYS] = (p % TS == TS - 1).astype(f)
    for t in range(TS):
        sm[:, SM_MC + t] = (p > t).astype(f)
    inv = (np.float32(500000.0) ** (-(np.arange(0, 16, 2, dtype=f)) / np.float32(16))).astype(f)

    def tables(pos):
        ang = (pos.astype(f)[:, None] * inv[None, :]).astype(f)
        c = np.ones((128, len(pos)), f)
        s = np.zeros((128, len(pos)), f)
        for pp in range(128):
            d = pp % 64
            if d < 16:
                c[pp] = np.cos(ang[:, d % 8])
                s[pp] = np.sin(ang[:, d % 8])
        return np.ascontiguousarray(c), np.ascontiguousarray(s)
    cosS, sinS = tables(np.tile(16384 + np.arange(TS), NSQ))
    return dict(ident=ident, rmT=rmT, mk=mk, mkb=mkb, mkS=mkS, sm=sm, cosS=cosS, sinS=sinS), tables, mprev


def pvec(v):
    return np.ascontiguousarray(v.reshape(DC, 128).T)


_CACHE = {}
STAGES = 99
MIXF = 0xFFFF


def kernel(x_prompt, x_sample, state_conv, state_gdn, cache_swa_k, cache_swa_v,
           norm_ffn1, w_ffn1_in, w_ffn1_out, norm_mix, w_in, conv_w, gdn_a_log, gdn_dt_bias,
           gdn_norm, swa_sinks, w_out, norm_ffn2, w_ffn2_in, w_ffn2_out, norm_final):
    f = np.float32
    A_ = lambda a: np.asarray(a, f)
    x_prompt = A_(x_prompt)
    x_sample = A_(x_sample)
    B, L, _ = x_prompt.shape
    NSQ = x_sample.shape[0] // NCORES
    T = x_sample.shape[1]
    LP = L // 2
    key = (L, STAGES, MIXF)
    if key not in _CACHE:
        _CACHE[key] = build(LP, LP, NSQ, T, stages=STAGES, MIX=MIXF)
    nc = _CACHE[key]
    nrm = np.ascontiguousarray(np.stack([pvec(A_(norm_ffn1)[0]), pvec(A_(norm_mix)[0]),
                                         pvec(A_(norm_ffn2)[0]), pvec(A_(norm_final))], axis=1))
    hc, tables, mprev = host_consts(NSQ, T)
    tabs = [tables(np.arange(hf * LP, (hf + 1) * LP)) for hf in range(2)]
    zx = np.zeros((LP, D), f)
    zm = np.zeros((128, 128), f)
    win = np.ascontiguousarray(A_(w_in)[0])
    wks = np.zeros((D, 512), f)
    for kvh in range(4):
        for dup in range(2):
            wks[:, kvh * 128 + dup * 64:kvh * 128 + (dup + 1) * 64] = win[:, OKS + kvh * 64:OKS + (kvh + 1) * 64]
    cw = np.ascontiguousarray(A_(conv_w)[0].reshape(4, 24, 128).transpose(2, 1, 0))
    gv = np.concatenate([A_(gdn_a_log)[0], A_(gdn_dt_bias)[0], A_(gdn_norm)[0], A_(swa_sinks)[0]])[None, :].astype(f)
    shared = {
        "w1i": np.ascontiguousarray(A_(w_ffn1_in)[0]), "w1o": np.ascontiguousarray(A_(w_ffn1_out)[0]),
        "w2i": np.ascontiguousarray(A_(w_ffn2_in)[0]), "w2o": np.ascontiguousarray(A_(w_ffn2_out)[0]),
        "nrm": nrm, "win": win, "wks": wks, "wout": np.ascontiguousarray(A_(w_out)[0]),
        "cw": cw, "gv": np.ascontiguousarray(gv),
    }
    shared.update(hc)
    sc, sg_, ck_, cv_ = A_(state_conv)[0], A_(state_gdn)[0], A_(cache_swa_k)[0], A_(cache_swa_v)[0]
    in_maps = []
    for c in range(NCORES):
        m = dict(shared)
        sl = slice(c * NSQ, (c + 1) * NSQ)
        sq_, hf = c // 2, c % 2
        m["xp"] = np.ascontiguousarray(x_prompt[sq_, hf * LP:(hf + 1) * LP])
        m["xpre"] = np.ascontiguousarray(x_prompt[sq_, 0:LP]) if hf == 1 else zx
        m["cosP"], m["sinP"] = tabs[hf]
        m["cosX"], m["sinX"] = tabs[0]
        m["mprev0"] = mprev if hf == 1 else zm
        m["xs"] = np.ascontiguousarray(x_sample[sl].reshape(NSQ * T, D))
        m["sconv"] = np.ascontiguousarray(sc[sl].reshape(NSQ * 3, 3072))
        m["sgdn"] = np.ascontiguousarray(sg_[sl])
        m["ck"] = np.ascontiguousarray(ck_[sl].reshape(NSQ, 128, 256))
        m["cv"] = np.ascontiguousarray(cv_[sl].reshape(NSQ, 128, 256))
        in_maps.append(m)
    res = run_bass_kernel_spmd(nc, in_maps, core_ids=list(range(NCORES)))
    R = res.results
    cat = lambda k, shp: np.concatenate([np.asarray(R[c][k], f).reshape(shp) for c in range(NCORES)], axis=0)
    stk = lambda k, shp: np.stack([np.asarray(R[2 * b + 1][k], f).reshape(shp) for b in range(B)], axis=0)
    y_prompt = np.stack([np.concatenate([np.asarray(R[2 * b]["yp"], f), np.asarray(R[2 * b + 1]["yp"], f)], axis=0) for b in range(B)], axis=0)
    y_sample = cat("ys", (NSQ, T, D))
    conv_p = stk("convp", (3, 3072))[None]
    conv_s = cat("convs", (NSQ, 3, 3072))[None]
    gdn_p = stk("gdnp", (8, 128, 128))[None]
    gdn_s = cat("gdns", (NSQ, 8, 128, 128))[None]
    k_p = stk("kp", (128, 4, 64))[None]
    k_s = cat("ksd", (NSQ, 128, 4, 64))[None]
    v_p = stk("vp", (128, 4, 64))[None]
    v_s = cat("vsd", (NSQ, 128, 4, 64))[None]
    return (y_prompt, y_sample, conv_p, conv_s, gdn_p, gdn_s, k_p, k_s, v_p, v_s)
```

```python
import math
from contextlib import ExitStack
import numpy as np
import concourse.bass as bass
import concourse.mybir as mybir
from concourse.bass_utils import run_bass_kernel_spmd

F32 = mybir.dt.float32
BF16 = mybir.dt.bfloat16
AF = mybir.ActivationFunctionType
ALU = mybir.AluOpType
AX = mybir.AxisListType

D = 1024
DC = 8
DFF = 2816
FC = 22
DIN = 7696
EPS = 1e-6
NCORES = 8
SLOT = 1408


class Res:
    __slots__ = ("w", "r", "x")

    def __init__(self):
        self.w = None
        self.r = {}
        self.x = False


def RL(n):
    return [Res() for _ in range(n)]


class EngS:
    def __init__(self, name, sem):
        self.name = name
        self.sem = sem
        self.count = 0
        self.ops = []
        self.waited = {}


class Prog:
    def __init__(self, sems, dsems):
        self.E = {n: EngS(n, sems[n]) for n in ("pe", "act", "dve", "pool", "sp")}
        self.dsems = dsems
        self.dcount = [0] * len(dsems)
        self.dnext = 0

    def _deps(self, reads, writes, own=None):
        need = {}

        def add(k, v):
            if need.get(k, 0) < v:
                need[k] = v

        for r in reads:
            if r.w:
                add(*r.w)
            if r.x:
                for k, v in r.r.items():
                    if k != own:
                        add(k, v)
        for w in writes:
            if w.w:
                add(*w.w)
            for k, v in w.r.items():
                add(k, v)
        return need

    def _waits(self, e, need, skip_own=False):
        for k, v in need.items():
            if skip_own and k == e.name:
                continue
            if e.waited.get(k, 0) < v:
                e.waited[k] = v
                e.ops.append(("wait", k, v))

    def op(self, eng, fn, reads=(), writes=(), inc=True):
        e = self.E[eng]
        need = self._deps(reads, writes, own=e.name)
        self._waits(e, need, skip_own=(eng == "pe"))
        e.ops.append(("op", fn, inc))
        idx = e.count + 1
        if inc:
            e.count = idx
        for r in reads:
            if r.r.get(e.name, 0) < idx:
                r.r[e.name] = idx
        for w in writes:
            w.w = (e.name, idx)
            w.r = {}

    def dma(self, eng, fn, reads=(), writes=()):
        e = self.E[eng]
        need = self._deps(reads, writes)
        j = self.dnext
        self.dnext = (j + 1) % len(self.dsems)
        key = ("d", j)
        if self.dcount[j] > 0:
            v = 16 * self.dcount[j]
            if need.get(key, 0) < v:
                need[key] = v
        self._waits(e, need)
        self.dcount[j] += 1
        val = 16 * self.dcount[j]
        e.ops.append(("dma", fn, key))
        for r in reads:
            if r.r.get(key, 0) < val:
                r.r[key] = val
        for w in writes:
            w.w = (key, val)
            w.r = {}

    def semof(self, key):
        if isinstance(key, tuple):
            return self.dsems[key[1]]
        return self.E[key].sem

    def replay(self, eng, name):
        e = self.E[name]
        for item in e.ops:
            if item[0] == "wait":
                eng.wait_ge(self.semof(item[1]), item[2])
            elif item[0] == "op":
                ins = item[1](eng)
                if item[2]:
                    ins.then_inc(e.sem, 1)
            else:
                ins = item[1](eng)
                ins.then_inc(self.semof(item[2]), 16)

    def finish(self):
        e = self.E["sp"]
        for j, c in enumerate(self.dcount):
            if c > 0:
                e.ops.append(("wait", ("d", j), 16 * c))
        for n in ("pe", "act", "dve", "pool"):
            if self.E[n].count > 0:
                e.ops.append(("wait", n, self.E[n].count))


OQ, OK_, OV, OZ, OA, OB, OQS, OKS, OVS, OGA, OGB = 0, 1024, 2048, 3072, 4096, 4104, 4112, 5136, 5392, 5648, 6672
BIG = 30000.0
M_TRI, M_M1, M_M2, M_BLK = 0, 1, 2, 3
SM_LASTP, SM_LASTS, SM_ANYP, SM_ANYS, SM_SEQ, SM_MC = 0, 2, 18, 19, 20, 36


def build(LP, LPRE, NSQ=16, TS=8, stages=99, MIX=0xFFFF, GW=2):
    NS_TOK = NSQ * TS
    nc = bass.Bass("TRN2", target_bir_lowering=False)

    def din(name, shape, dt=F32):
        return nc.dram_tensor(name, list(shape), dt, kind="ExternalInput").ap()

    def dout(name, shape, dt=F32):
        return nc.dram_tensor(name, list(shape), dt, kind="ExternalOutput").ap()

    xp = din("xp", [LP, D])
    xpre = din("xpre", [LPRE, D])
    cosX = din("cosX", [128, LPRE])
    sinX = din("sinX", [128, LPRE])
    mprev0_d = din("mprev0", [128, 128])
    xs = din("xs", [NS_TOK, D])
    w1i = din("w1i", [D, 2 * DFF])
    w1o = din("w1o", [DFF, D])
    w2i = din("w2i", [D, 2 * DFF])
    w2o = din("w2o", [DFF, D])
    nrm = din("nrm", [128, 4, DC])
    ident_d = din("ident", [128, 128])
    win = din("win", [D, DIN])
    wks = din("wks", [D, 512])
    wout = din("wout", [D, D])
    cw_d = din("cw", [128, 24, 4])
    gv_d = din("gv", [1, 160])
    cosP = din("cosP", [128, LP])
    sinP = din("sinP", [128, LP])
    cosS = din("cosS", [128, NS_TOK])
    sinS = din("sinS", [128, NS_TOK])
    rmT_d = din("rmT", [128, 128])
    mk_d = din("mk", [128, 8, 128])
    mkb_d = din("mkb", [128, 3, 128])
    mkS_d = din("mkS", [128, 512])
    sm_d = din("sm", [128, 64])
    sconv = din("sconv", [NSQ * 3, 3072])
    sgdn = din("sgdn", [NSQ, 8, 128, 128])
    ck = din("ck", [NSQ, 128, 256])
    cv = din("cv", [NSQ, 128, 256])
    yp = dout("yp", [LP, D])
    ys = dout("ys", [NS_TOK, D])
    convp = dout("convp", [3, 3072])
    convs = dout("convs", [NSQ * 3, 3072])
    gdnp = dout("gdnp", [8, 128, 128])
    gdns = dout("gdns", [NSQ, 8, 128, 128])
    kp = dout("kp", [128, 256])
    ksd = dout("ksd", [NSQ, 128, 256])
    vp = dout("vp", [128, 256])
    vsd = dout("vsd", [NSQ, 128, 256])

    es = ExitStack()
    with es:
        es.enter_context(nc.allow_low_precision("bf16 matmul operands, fp32 accumulate"))
        sems = {n: es.enter_context(nc.semaphore("sem_" + n)) for n in ("pe", "act", "dve", "pool", "sp")}
        dsems = [es.enter_context(nc.semaphore("dsem%d" % i)) for i in range(24)]
        P = Prog(sems, dsems)

        def sb(name, shape, dt=F32):
            return es.enter_context(nc.sbuf_tensor(name, list(shape), dt))

        def A(fn, reads, writes):
            P.op("act", fn, reads, writes)

        def V(fn, reads, writes):
            P.op("dve", fn, reads, writes)

        def M(fn, reads, writes, inc=True):
            P.op("pe", fn, reads, writes, inc)

        def G_(fn, reads, writes):
            P.op("pool", fn, reads, writes)

        xin = sb("xin", [128, 4, D])
        xin_r = RL(4)
        xT = sb("xT", [128, DC, 512])
        xT_r = RL(DC)
        xn = sb("xn", [128, DC, 512], BF16)
        xn_r = RL(DC)
        hT = sb("hT", [128, FC, 512], BF16)
        hT_r = RL(FC)
        NSLOT = 2
        NSTG = 2
        wslots = [sb("wslot%d" % i, [128, SLOT], BF16) for i in range(NSLOT)]
        wslot_r = RL(NSLOT)
        wnext = [0]
        wstg = [sb("wstg%d" % i, [128, SLOT]) for i in range(NSTG)]
        wstg_r = RL(NSTG)
        snext = [0]
        sq = [sb("sq%d" % i, [128, 512], BF16) for i in range(2)]
        sq_r = RL(2)
        TF = [sb("tf%d" % i, [128, 520]) for i in range(6)]
        TF_r = RL(6)
        tfn = [0]

        def tf():
            i = tfn[0]
            tfn[0] = (i + 1) % 6
            return TF[i], TF_r[i]
        ident = sb("ident_sb", [128, 128])
        ident_bf = sb("ident_bf", [128, 128], BF16)
        ones_bf = sb("ones_bf", [128, 128], BF16)
        ones_f = sb("ones_f", [128, 128])
        nrm_sb = sb("nrm_sb", [128, 4, DC])
        qT = hT[:, 0:8, :]
        qT_r = hT_r[0:8]
        kT = hT[:, 8:16, :]
        kT_r = hT_r[8:16]
        ksT = hT[:, 16:22, :].rearrange("p j n -> p (j n)")[:, 0:2560].rearrange("p (k n) -> p k n", k=4)
        ksT_r = Res()
        for _j in range(16, 22):
            hT_r[_j] = ksT_r
        mixT = xn
        mixT_r = xn_r
        k_tok = sb("k_tok", [128, 4, 8, 128], BF16)
        k_tok_r = RL(8)
        v_tok = sb("v_tok", [128, 4, 8, 128], BF16)
        v_tok_r = RL(8)
        zg = xin[:, 0:2, :].rearrange("p b d -> p (b d)").bitcast(BF16).rearrange("p (h n) -> p h n", h=8)
        zg_r = [xin_r[h // 4] for h in range(8)]
        vcall = xin[:, 2:4, :].rearrange("p b d -> p (b d)").bitcast(BF16).rearrange("p (s k d) -> p s k d", s=16, k=4)
        vcall_r = [xin_r[2 + s // 8] for s in range(16)]
        gbT = sb("gbT", [128, 8, 512], BF16)
        gbT_r = RL(8)
        qsT = sb("qsT", [128, 8, 512], BF16)
        qsT_r = RL(8)
        vs_tok = sb("vs_tok", [128, 5, 4, 128], BF16)
        vs_tok_r = Res()
        kcar = sb("kcar", [128, 4, 128], BF16)
        kcar_r = Res()
        csn = sb("csn", [128, 2, 512])
        csn_r = Res()
        ks32 = sb("ks32", [128, 4, 128])
        ks32_r = Res()
        vs32 = sb("vs32", [128, 256])
        vs32_r = Res()
        cst = sb("cst", [128, 24, 48])
        cst_r = Res()
        cw = sb("cw_sb", [128, 24, 4])
        gv = sb("gv_sb", [128, 160])
        gnw = gv[:, 16:144]
        negA = sb("negA", [128, 8])
        esink = sb("esink", [1, 4, 512], BF16)
        esinkS = sb("esinkS", [1, 4, 512], BF16)
        mkS = sb("mkS_sb", [128, 512], BF16)
        esink_f = sb("esink_f", [1, 16])
        rmT = sb("rmT_sb", [128, 128])
        mk = sb("mk_sb", [128, 8, 128])
        mkb = sb("mkb_sb", [128, 4, 128], BF16)
        sm = sb("sm_sb", [128, 64])
        gch = [sb("gch%d" % i, [8, 128]) for i in range(4)]
        gch_r = RL(4)
        gates_f = sb("gates_f", [128, 9, 32])
        gts_r = Res()
        gcT = sb("gcT", [8, 512])
        gcT_r = Res()
        gl = sb("gl", [128, 4, 128])
        gl_r = Res()
        x2pool = xin[:, 2:4, :].rearrange("p b d -> p (b d)")
        x2off = [0]
        s2res = []
        x3off = [0]
        s3res = []

        def pool3(nel):
            k, o = divmod(x3off[0], 512)
            assert o + nel <= 512 and k < 4
            x3off[0] += nel
            return TF[k][:, o:o + nel]

        def wt(name, dt=F32, n=1, s2=True):
            tiles = [sb("%s%d" % (name, i), [128, 128], dt) for i in range(2 * n)]
            res = RL(2 * n)
            if s2:
                for i in range(n):
                    if dt == F32:
                        v = x2pool[:, x2off[0]:x2off[0] + 128]
                        x2off[0] += 128
                    else:
                        v = x2pool[:, x2off[0]:x2off[0] + 64].bitcast(BF16)
                        x2off[0] += 64
                    r_ = Res()
                    tiles.append(v)
                    res.append(r_)
                    s2res.append(r_)
                for i in range(n):
                    v = pool3(128) if dt == F32 else pool3(64).bitcast(BF16)
                    r_ = Res()
                    tiles.append(v)
                    res.append(r_)
                    s3res.append(r_)
            return tiles, res
        e1, e1_r = wt("e1", F32, 2)
        Dm, Dm_r = wt("Dm", F32, 1)
        DTm, DTm_r = wt("DTm", F32, 1)
        EG, EG_r = wt("EG", F32, 1)
        qg, qg_r = wt("qg", BF16, 1)
        Bm, Bm_r = wt("Bm", BF16, 1)
        Pm, Pm_r = wt("Pm", BF16, 3)
        PTm, PTm_r = wt("PTm", BF16, 3)
        TT, TT_r = wt("TT", BF16, 2)
        QKD, QKD_r = wt("QKD", BF16, 1)
        vb, vb_r = wt("vb", BF16, 1)
        kbg, kbg_r = wt("kbg", BF16, 1)
        kgt, kgt_r = wt("kgt", BF16, 1)
        ub, ub_r = wt("ub", F32, 1)
        wTb, wTb_r = wt("wTb", BF16, 1)
        vnew, vnew_r = wt("vnew", BF16, 1)
        vnf, vnf_r = wt("vnf", F32, 1, s2=False)
        ogn, ogn_r = wt("ogn", BF16, 1)
        oq, oq_r = wt("oq", F32, 1, s2=False)
        otmp, otmp_r = wt("otmp", F32, 1)
        assert x2off[0] <= 2048, x2off[0]
        dummy = sb("dummy_sb", [128, 2])
        dummy_r = Res()
        ctr = {}

        def nxt(name, n):
            i = ctr.get(name, 0)
            ctr[name] = (i + 1) % n
            return i
        small = sb("small", [128, 16])
        small_r = RL(4)
        Sf = sb("Sf", [128, 16, 128])
        Sf_r = RL(16)
        Sbf = sb("Sbf", [128, 16, 128], BF16)
        Sbf_r = RL(16)
        pT = [sb("pT%d" % i, [128, 512], BF16) for i in range(2)]
        pT_r = RL(2)
        kcs = sb("kcs", [128, 4, 2, 64])
        kcs_r = Res()
        kcT = [sb("kcT%d" % i, [128, 4, 128], BF16) for i in range(2)]
        kcT_r = RL(2)
        pTc = Sbf[:, :, :].rearrange("p (k a) n -> p k (a n)", k=4)
        pTc_r = [Sbf_r[4 * k:4 * k + 4] for k in range(4)]
        knew = sb("knew", [128, 4, 64])
        knew_r = Res()
        NB_ = 8
        banks = [es.enter_context(nc.psum_tensor("bank%d" % i, [128, 512], F32)) for i in range(NB_)]
        bank_r = RL(NB_)
        for _r in bank_r:
            _r.x = True
        bnext = [0]
        banks_bf = [banks[i][:, :].bitcast(BF16) for i in range(NB_)]
        pbf = [banks_bf[6], banks_bf[7]]
        pbf_r = [bank_r[6], bank_r[7]]
        pbn = [0]

        def psum():
            i = bnext[0]
            bnext[0] = (i + 1) % NB_
            return banks[i], bank_r[i]

        def psumbf():
            i = pbn[0]
            pbn[0] = 1 - i
            return pbf[i], pbf_r[i]

        ident_r, nrm_r, ones_r, c_r = Res(), Res(), Res(), Res()
        P.dma("sp", lambda e: e.dma_start(out=ident[:, :], in_=ident_d[:, :]), writes=[ident_r])
        P.dma("sp", lambda e: e.dma_start(out=nrm_sb[:, :, :], in_=nrm[:, :, :]), writes=[nrm_r])
        V(lambda e: e.memset(ones_bf[:, :], 1.0), [], [ones_r])
        cl = [Res() for _ in range(8)]
        P.dma("sp", lambda e: e.dma_start(out=cw[:, :, :], in_=cw_d[:, :, :]), writes=[cl[0]])
        P.dma("sp", lambda e: e.dma_start(out=gv[:, :], in_=gv_d[0:1, :].broadcast_to([128, 160])), writes=[cl[1]])
        P.dma("sp", lambda e: e.dma_start(out=rmT[:, :], in_=rmT_d[:, :]), writes=[cl[2]])
        P.dma("sp", lambda e: e.dma_start(out=mk[:, :, :], in_=mk_d[:, :, :]), writes=[cl[3]])
        P.dma("sp", lambda e: e.dma_start(out=sm[:, :], in_=sm_d[:, :]), writes=[cl[4]])
        _t, _tr = tf()
        P.dma("sp", lambda e: e.dma_start(out=_t[:, 0:384], in_=mkb_d[:, :, :].rearrange("p a b -> p (a b)")), writes=[_tr])
        V(lambda e: e.tensor_copy(mkb[:, 0:3, :].rearrange("p a b -> p (a b)"), _t[:, 0:384]), [_tr], [cl[5]])
        _t3, _t3r = tf()
        P.dma("sp", lambda e: e.dma_start(out=_t3[:, 0:128], in_=mprev0_d[:, :]), writes=[_t3r])
        V(lambda e: e.tensor_copy(mkb[:, 3, :], _t3[:, 0:128]), [_t3r], [cl[7]])
        V(lambda e: e.memset(kcar[:, :, :], 0.0), [], [kcar_r])
        V(lambda e: e.memset(vs_tok[:, 0, :, :], 0.0), [], [vs_tok_r])
        V(lambda e: e.tensor_copy(ident_bf[:, :], ident[:, :]), [ident_r], [cl[6]])
        V(lambda e: e.memset(ones_f[:, :], 1.0), [], [cl[6]])
        A(lambda e: e.activation(negA[:, :], gv[:, 0:8], AF.Exp), [cl[1]], [c_r])
        V(lambda e: e.tensor_scalar(negA[:, :], negA[:, :], -1.0, None, ALU.mult), [c_r], [c_r])
        A(lambda e: e.activation(esink_f[:, :], gv[0:1, 144:160], AF.Exp), [cl[1]], [c_r])
        for par in range(2):
            src = esink_f[0:1, :].rearrange("o (k i par) -> o par k i", k=4, i=2)[:, par]
            dstv = esink[0:1, :, :].rearrange("o k (par i q) -> o par k i q", par=2, i=2)[:, par]
            V(lambda e, src=src, dstv=dstv: e.tensor_copy(dstv, src.unsqueeze(3).broadcast_to([1, 4, 2, 128])), [c_r], [c_r])
            for kvh in range(4):
                src2 = esink_f[0:1, 4 * kvh:4 * kvh + 4].rearrange("o (i par) -> o par i", i=2)[:, par]
                dst2 = esinkS[0:1, kvh, :].rearrange("o (par s i t) -> o par s i t", par=2, s=NSQ, i=2)[:, par]
                V(lambda e, src2=src2, dst2=dst2: e.tensor_copy(dst2, src2.unsqueeze(1).unsqueeze(3).broadcast_to([1, NSQ, 2, TS])), [c_r], [c_r])
        _t2, _t2r = tf()
        P.dma("sp", lambda e: e.dma_start(out=_t2[:, 0:512], in_=mkS_d[:, :]), writes=[_t2r])
        V(lambda e: e.tensor_copy(mkS[:, :], _t2[:, 0:512]), [_t2r], [c_r])
        V(lambda e: e.memset(cst[:, :, :], 0.0), [], [cst_r])
        for _i in range(2):
            V(lambda e, _i=_i: e.memset(vnew[_i][:, :], 0.0), [], [vnew_r[_i]])
        V(lambda e: e.memset(Sf[:, 0:8, :], 0.0), [], Sf_r[0:8])
        V(lambda e: e.memset(Sbf[:, 0:8, :], 0.0), [], Sbf_r[0:8])
        CALL = [ident_r, ones_r, c_r] + cl

        wscr = nc.dram_tensor("wscr", [200, 128, SLOT], BF16).ap()
        wreg = {}
        half_r = [[Res(), Res()] for _k in range(NSTG)]
        _hv = [wstg[_k][:, :].bitcast(BF16) for _k in range(NSTG)]
        fslots = [(wslots[0], wslot_r[0], None),
                  (_hv[0][:, 0:SLOT], half_r[0][0], wstg_r[0]),
                  (_hv[1][:, 0:SLOT], half_r[1][0], wstg_r[1]),
                  (wslots[1], wslot_r[1], None),
                  (_hv[0][:, SLOT:2 * SLOT], half_r[0][1], wstg_r[0]),
                  (_hv[1][:, SLOT:2 * SLOT], half_r[1][1], wstg_r[1])]
        fnext = [0]
        cnext = [0]

        def _wload(key, src, n1, ncol):
            if key in wreg:
                tid, tres = wreg[key]
                j = fnext[0]
                fnext[0] = (j + 1) % len(fslots)
                buf, bres, extra = fslots[j]
                wr = [bres]
                if extra is not None:
                    wr.append(extra)
                P.dma("sp", lambda e: e.dma_start(out=buf[:, 0:n1 * ncol], in_=wscr[tid, :, 0:n1 * ncol]), reads=[tres], writes=wr)
                return buf[:, 0:n1 * ncol].rearrange("p (c n) -> p c n", c=n1), bres
            k = snext[0]
            snext[0] = (k + 1) % NSTG
            i = wnext[0]
            wnext[0] = (i + 1) % NSLOT
            sview = wstg[k][:, 0:n1 * ncol].rearrange("p (c n) -> p c n", c=n1)
            view = wslots[i][:, 0:n1 * ncol].rearrange("p (c n) -> p c n", c=n1)
            P.dma("sp", lambda e: e.dma_start(out=sview, in_=src), writes=[wstg_r[k], half_r[k][0], half_r[k][1]])
            ce = ("pool", "act", "dve")[cnext[0] % 3]
            cnext[0] += 1
            if ce == "act":
                P.op("act", lambda e: e.copy(wslots[i][:, 0:n1 * ncol], wstg[k][:, 0:n1 * ncol]),
                     reads=[wstg_r[k]], writes=[wslot_r[i]])
            else:
                P.op(ce, lambda e: e.tensor_copy(wslots[i][:, 0:n1 * ncol], wstg[k][:, 0:n1 * ncol]),
                     reads=[wstg_r[k]], writes=[wslot_r[i]])
            tid = len(wreg)
            tres = Res()
            wreg[key] = (tid, tres)
            P.dma("sp", lambda e: e.dma_start(out=wscr[tid, :, 0:n1 * ncol], in_=wslots[i][:, 0:n1 * ncol]),
                  reads=[wslot_r[i]], writes=[tres])
            return view, wslot_r[i]

        def wload_in(W, col0, ncol):
            src = W.rearrange("(c p) n -> p c n", p=128)[:, :, col0:col0 + ncol]
            return _wload((id(W), "i", col0, ncol), src, DC, ncol)

        def wload_out(W, nj, col0, ncol, r0=0):
            src = W[r0:r0 + nj * 128, :].rearrange("(j p) n -> p j n", p=128)[:, :, col0:col0 + ncol]
            return _wload((id(W), "o", col0, ncol, r0, nj), src, nj, ncol)

        def load_x(xd, tok0, ntok):
            nb = ntok // 128
            src = xd[tok0:tok0 + ntok, :].rearrange("(b p) d -> p b d", p=128)
            P.dma("sp", lambda e: e.dma_start(out=xin[:, 0:nb, :], in_=src), writes=xin_r[0:nb])
            for c in range(DC):
                ps, pr = psum()
                for b in range(nb):
                    P.op("pe", lambda e, b=b, c=c, ps=ps: e.transpose(
                        ps[:, b * 128:(b + 1) * 128], xin[:, b, c * 128:(c + 1) * 128], ident[:, :]),
                        reads=[xin_r[b], ident_r], writes=[pr], inc=(b == nb - 1))
                eng = "act" if c % 2 == 0 else "dve"
                if eng == "act":
                    P.op("act", lambda e, c=c, ps=ps: e.copy(xT[:, c, 0:ntok], ps[:, 0:ntok]),
                         reads=[pr], writes=[xT_r[c]])
                else:
                    P.op("dve", lambda e, c=c, ps=ps: e.tensor_copy(xT[:, c, 0:ntok], ps[:, 0:ntok]),
                         reads=[pr], writes=[xT_r[c]])

        def rmsnorm(which, out, out_r, ntok):
            ps, pr = psum()
            for c in range(DC):
                s = c % 2
                P.op("act", lambda e, c=c, s=s: e.activation(sq[s][:, 0:ntok], xT[:, c, 0:ntok], AF.Square),
                     reads=[xT_r[c]], writes=[sq_r[s]])
                P.op("pe", lambda e, c=c, s=s, ps=ps: e.matmul(ps[:, 0:ntok], ones_bf[:, :], sq[s][:, 0:ntok],
                                                               start=(c == 0), stop=(c == DC - 1)),
                     reads=[sq_r[s], ones_r], writes=[pr], inc=True)
            sd, sd_r = tf()
            rstd, rstd_r = tf()
            P.op("act", lambda e, ps=ps: e.activation(sd[:, 0:ntok], ps[:, 0:ntok], AF.Ln, bias=EPS, scale=1.0 / D),
                 reads=[pr], writes=[sd_r])
            P.op("act", lambda e: e.activation(rstd[:, 0:ntok], sd[:, 0:ntok], AF.Exp, scale=-0.5), reads=[sd_r], writes=[rstd_r])
            for c in range(DC):
                P.op("dve", lambda e, c=c: e.scalar_tensor_tensor(
                    out[:, c, 0:ntok], xT[:, c, 0:ntok], nrm_sb[:, which, c:c + 1], rstd[:, 0:ntok],
                    ALU.mult, ALU.mult),
                    reads=[xT_r[c], rstd_r, nrm_r], writes=[out_r[c]])

        def ffn(Wi, Wo, ntok):
            for j in range(FC):
                wg, wgr = wload_in(Wi, 128 * j, 128)
                wu, wur = wload_in(Wi, DFF + 128 * j, 128)
                pg, pgr = psum()
                for c in range(DC):
                    P.op("pe", lambda e, c=c, pg=pg, wg=wg: e.matmul(
                        pg[:, 0:ntok], wg[:, c, :], xn[:, c, 0:ntok], start=(c == 0), stop=(c == DC - 1)),
                        reads=[wgr, xn_r[c]], writes=[pgr], inc=(c == DC - 1))
                pu, pur = psum()
                for c in range(DC):
                    P.op("pe", lambda e, c=c, pu=pu, wu=wu: e.matmul(
                        pu[:, 0:ntok], wu[:, c, :], xn[:, c, 0:ntok], start=(c == 0), stop=(c == DC - 1)),
                        reads=[wur, xn_r[c]], writes=[pur], inc=(c == DC - 1))
                sgb, sgr = tf()
                P.op("act", lambda e, sgb=sgb, pg=pg: e.activation(sgb[:, 0:ntok], pg[:, 0:ntok], AF.Silu),
                     reads=[pgr], writes=[sgr])
                P.op("dve", lambda e, sgb=sgb, pu=pu, j=j: e.tensor_tensor(
                    hT[:, j, 0:ntok], sgb[:, 0:ntok], pu[:, 0:ntok], ALU.mult),
                    reads=[sgr, pur], writes=[hT_r[j]])
            for c in range(DC):
                po, por = psum()
                for half in range(2):
                    wo, wor = wload_out(Wo, 11, 128 * c, 128, r0=half * 11 * 128)
                    for jj in range(11):
                        j = half * 11 + jj
                        P.op("pe", lambda e, j=j, jj=jj, po=po, wo=wo: e.matmul(
                            po[:, 0:ntok], wo[:, jj, :], hT[:, j, 0:ntok], start=(j == 0), stop=(j == FC - 1)),
                            reads=[wor, hT_r[j]], writes=[por], inc=(j == FC - 1 or jj == 10))
                P.op("dve", lambda e, c=c, po=po: e.scalar_tensor_tensor(
                    xT[:, c, 0:ntok], po[:, 0:ntok], 0.5, xT[:, c, 0:ntok], ALU.mult, ALU.add),
                    reads=[por, xT_r[c]], writes=[xT_r[c]])

        def store_y(yd, tok0, ntok):
            nb = ntok // 128
            for b in range(nb):
                for cg in range(2):
                    ps, pr = psum()
                    for ci in range(4):
                        c = cg * 4 + ci
                        P.op("pe", lambda e, b=b, c=c, ci=ci, ps=ps: e.transpose(
                            ps[:, ci * 128:(ci + 1) * 128], xT[:, c, b * 128:(b + 1) * 128], ident[:, :]),
                            reads=[xT_r[c], ident_r], writes=[pr], inc=(ci == 3))
                    if cg == 0:
                        P.op("act", lambda e, b=b, cg=cg, ps=ps: e.copy(xin[:, b, cg * 512:(cg + 1) * 512], ps[:, :]),
                             reads=[pr], writes=[xin_r[b]])
                    else:
                        P.op("dve", lambda e, b=b, cg=cg, ps=ps: e.tensor_copy(xin[:, b, cg * 512:(cg + 1) * 512], ps[:, :]),
                             reads=[pr], writes=[xin_r[b]])
            dst = yd[tok0:tok0 + ntok, :].rearrange("(b p) d -> p b d", p=128)
            P.dma("sp", lambda e: e.dma_start(out=dst, in_=xin[:, 0:nb, :]), reads=xin_r[0:nb])


        class UC:
            def __init__(self, u):
                self.u = u
                self.c = {}
                self.pn = 0

            def n(self, name, n):
                i = self.c.get(name, 0)
                self.c[name] = (i + 1) % n
                return self.u * n + i

            def ps(self):
                i = 2 * self.u + self.pn
                self.pn = 1 - self.pn
                self.last = i
                return banks[i], bank_r[i]

            def pb(self):
                i = 2 * self.u + self.pn
                self.pn = 1 - self.pn
                self.last = i
                return banks_bf[i], bank_r[i]

            def pb_of(self, i):
                return banks_bf[i], bank_r[i]

            def tf(self, k):
                return TF[2 * self.u + k], TF_r[2 * self.u + k]

            def sq(self):
                return sqx[self.u], sqx_r[self.u]
        UCS = [UC(0), UC(1), UC(2), UC(3)]
        sqx = [sq[0], sq[1], pT[0], pT[1]]
        sqx_r = [sq_r[0], sq_r[1], pT_r[0], pT_r[1]]
        pTx = [pT[0], pT[1], sq[0], sq[1]]
        pTx_r = [pT_r[0], pT_r[1], sq_r[0], sq_r[1]]

        def run_interleaved(gens, W):
            active = []
            it = iter(gens)
            slots = list(range(W))
            pending = True
            while True:
                while len(active) < W and pending:
                    try:
                        mk_ = next(it)
                    except StopIteration:
                        pending = False
                        break
                    u = slots.pop(0)
                    active.append((u, mk_(UCS[u])))
                if not active:
                    break
                for ent in list(active):
                    u, g = ent
                    try:
                        next(g)
                    except StopIteration:
                        active.remove(ent)
                        slots.append(u)

        def chunk_gen(U, W, off, ch, ntok, fn):
            w, wr = wload_in(W, off + 128 * ch, 128)
            ps, pr = U.ps()
            for c in range(DC):
                M(lambda e, c=c: e.matmul(ps[:, 0:ntok], w[:, c, :], xn[:, c, 0:ntok],
                                          start=(c == 0), stop=(c == DC - 1)), [wr, xn_r[c]], [pr], inc=(c == DC - 1))
            yield
            yield from fn(U, ch, ps, pr)

        def l2norm(U, y, yr, tmp, tmpr, out, outr, ntok, scale):
            sqb, sqr = U.sq()
            A(lambda e: e.activation(sqb[:, 0:ntok], y[:, 0:ntok], AF.Square), [yr], [sqr])
            ps, pr = U.ps()
            M(lambda e: e.matmul(ps[:, 0:ntok], ones_bf[:, :], sqb[:, 0:ntok], start=True, stop=True),
              [sqr, ones_r], [pr])
            yield
            A(lambda e: e.activation(tmp[:, 0:ntok], ps[:, 0:ntok], AF.Ln, bias=EPS, scale=1.0), [pr], [tmpr])
            A(lambda e: e.activation(tmp[:, 0:ntok], tmp[:, 0:ntok], AF.Exp, scale=-0.5), [tmpr], [tmpr])
            yield
            V(lambda e: e.scalar_tensor_tensor(out, y[:, 0:ntok], scale, tmp[:, 0:ntok], ALU.mult, ALU.mult),
              [yr, tmpr], [outr])
            yield

        def to_tok(U, src, srcr, dst, dstr, nb):
            pb, pbr = U.pb()
            for tb in range(nb):
                M(lambda e, tb=tb: e.transpose(pb[:, tb * 128:(tb + 1) * 128], src[:, tb * 128:(tb + 1) * 128], ident_bf[:, :]),
                  [srcr] + CALL, [pbr], inc=(tb == nb - 1))
            yield
            A(lambda e: e.copy(dst, pb[:, 0:nb * 128].rearrange("p (b d) -> p b d", b=nb)), [pbr], [dstr])

        def inproj(mode, tok0, ntok, lastx=False):
            nb = ntok // 128
            isP = mode in ("P", "X")
            isX = mode == "X"
            NSEQ, T = (1, ntok) if isP else (NSQ, TS)
            cs_, sn_ = (cosP, sinP) if mode == "P" else ((cosX, sinX) if isX else (cosS, sinS))
            c0 = tok0 if isP else 0
            P.dma("sp", lambda e: e.dma_start(out=csn[:, 0, 0:ntok], in_=cs_[:, c0:c0 + ntok]), writes=[csn_r])
            P.dma("sp", lambda e: e.dma_start(out=csn[:, 1, 0:ntok], in_=sn_[:, c0:c0 + ntok]), writes=[csn_r])
            wab, wabr = wload_in(win, OA, 16)
            psab, psabr = psum()
            for tb in range(nb):
                for c in range(DC):
                    M(lambda e, tb=tb, c=c: e.matmul(psab[:, tb * 16:(tb + 1) * 16], xn[:, c, tb * 128:(tb + 1) * 128],
                                                     wab[:, c, 0:16], start=(c == 0), stop=(c == DC - 1)),
                      [wabr, xn_r[c]], [psabr], inc=(c == DC - 1))
            gates("P" if isX else mode, nb, psab, psabr)
            gens = []

            def add(W, off, nch, fn):
                for ch in range(nch):
                    gens.append(lambda U, ch=ch: chunk_gen(U, W, off, ch, ntok, fn))

            def conv_ep(grp):
                def f(U, ch, ps, pr):
                    cidx = grp * 8 + ch
                    rawb, rawr = U.tf(0)
                    accb, accr = U.tf(1)
                    rb = rawb[:, 0:NSEQ * (3 + T)].rearrange("p (s t) -> p s t", s=NSEQ)
                    psv = ps[:, 0:ntok].rearrange("p (s t) -> p s t", s=NSEQ)
                    A(lambda e: e.copy(rb[:, :, 3:3 + T], psv), [pr], [rawr])
                    cv_ = cst[:, cidx, 0:NSEQ * 3].rearrange("p (s j) -> p s j", s=NSEQ)
                    V(lambda e: e.tensor_copy(rb[:, :, 0:3], cv_), [cst_r], [rawr])
                    av = accb[:, 0:ntok].rearrange("p (s t) -> p s t", s=NSEQ)
                    if not (isX and grp == 0):
                        A(lambda e: e.mul(av, psv, cw[:, cidx, 3:4]), [pr] + CALL, [accr])
                    yield
                    V(lambda e: e.tensor_copy(cv_, rb[:, :, T:T + 3]), [rawr], [cst_r])
                    if isX and grp == 0:
                        return
                    for j in range(0, 3):
                        V(lambda e, j=j: e.scalar_tensor_tensor(av, rb[:, :, j:j + T], cw[:, cidx, j:j + 1], av,
                                                                ALU.mult, ALU.add), [rawr, accr], [accr])
                    yield
                    if grp == 2:
                        vt_, vtr_ = U.sq()
                        A(lambda e: e.activation(vt_[:, 0:ntok], accb[:, 0:ntok], AF.Silu), [accr], [vtr_])
                        yield
                        yield from to_tok(U, vt_, vtr_, v_tok[:, 0:nb, ch, :], v_tok_r[ch], nb)
                    else:
                        A(lambda e: e.activation(rawb[:, 0:ntok], accb[:, 0:ntok], AF.Silu), [accr], [rawr])
                        yield
                        if grp == 0:
                            yield from l2norm(U, rawb, rawr, accb, accr, qT[:, ch, 0:ntok], qT_r[ch], ntok, 128.0 ** -0.5)
                        else:
                            yield from l2norm(U, rawb, rawr, accb, accr, kT[:, ch, 0:ntok], kT_r[ch], ntok, 1.0)
                            yield from to_tok(U, kT[:, ch, :], kT_r[ch], k_tok[:, 0:nb, ch, :], k_tok_r[ch], nb)
                return f
            if (not isX) or lastx:
                add(win, OQ, 8, conv_ep(0))
            add(win, OK_, 8, conv_ep(1))
            add(win, OV, 8, conv_ep(2))

            def z_ep(U, ch, ps, pr):
                A(lambda e: e.activation(zg[:, ch, 0:ntok], ps[:, 0:ntok], AF.Silu), [pr], [zg_r[ch]])
                yield
            if not isX:
                add(win, OZ, 8, z_ep)

            def ga_ep(U, ch, ps, pr):
                t1b, t1_r = U.tf(0)
                A(lambda e: e.activation(t1b[:, 0:ntok], ps[:, 0:ntok], AF.Sigmoid), [pr], [t1_r])
                yield
                V(lambda e: e.tensor_tensor(zg[:, ch, 0:ntok], zg[:, ch, 0:ntok], t1b[:, 0:ntok], ALU.mult),
                  [t1_r, zg_r[ch]], [zg_r[ch]])
            if not isX:
                add(win, OGA, 8, ga_ep)

            def gb_ep(U, ch, ps, pr):
                A(lambda e: e.activation(gbT[:, ch, 0:ntok], ps[:, 0:ntok], AF.Sigmoid), [pr], [gbT_r[ch]])
                yield
            if not isX:
                add(win, OGB, 8, gb_ep)

            def rope_ep(kind):
                def f(U, ch, ps, pr):
                    yb, ybr = U.tf(0)
                    t2b, t2_r = U.tf(1)
                    t1b, t1_r = yb, ybr
                    A(lambda e: e.copy(yb[:, 0:ntok], ps[:, 0:ntok]), [pr], [ybr])
                    yield
                    p2, p2r = U.ps()
                    M(lambda e: e.matmul(p2[:, 0:ntok], rmT[:, :], yb[:, 0:ntok], start=True, stop=True),
                      [ybr] + CALL, [p2r])
                    V(lambda e: e.tensor_tensor(t2b[:, 0:ntok], yb[:, 0:ntok], csn[:, 0, 0:ntok], ALU.mult),
                      [ybr, csn_r], [t2_r])
                    yield
                    V(lambda e: e.tensor_tensor(t1b[:, 0:ntok], p2[:, 0:ntok], csn[:, 1, 0:ntok], ALU.mult),
                      [p2r, csn_r], [t1_r])
                    if kind == "q":
                        V(lambda e: e.tensor_tensor(qsT[:, ch, 0:ntok], t1b[:, 0:ntok], t2b[:, 0:ntok], ALU.add),
                          [t1_r, t2_r], [qsT_r[ch]])
                    else:
                        V(lambda e: e.tensor_tensor(ksT[:, ch, 128:128 + ntok], t1b[:, 0:ntok], t2b[:, 0:ntok], ALU.add),
                          [t1_r, t2_r], [ksT_r])
                        V(lambda e: e.tensor_tensor(ks32[:, ch, :], t1b[:, ntok - 128:ntok], t2b[:, ntok - 128:ntok], ALU.add),
                          [t1_r, t2_r], [ks32_r])
                    yield
                return f
            if not isX:
                add(win, OQS, 8, rope_ep("q"))
            if (not isX) or lastx:
                add(wks, 0, 4, rope_ep("k"))

            def vs_ep(U, ch, ps, pr):
                yb, ybr = U.tf(0)
                A(lambda e: e.copy(yb[:, 0:ntok], ps[:, 0:ntok]), [pr], [ybr])
                yield
                p2, p2r = U.ps()
                for tb in range(nb):
                    M(lambda e, tb=tb: e.transpose(p2[:, tb * 128:(tb + 1) * 128], yb[:, tb * 128:(tb + 1) * 128], ident[:, :]),
                      [ybr, ident_r], [p2r], inc=(tb == nb - 1))
                yield
                src = p2[:, 0:nb * 128].rearrange("p (b h d) -> p b h d", b=nb, h=2)
                for dup in range(2):
                    V(lambda e, dup=dup: e.tensor_copy(vs_tok[:, 1:1 + nb, 2 * ch:2 * ch + 2, dup * 64:(dup + 1) * 64], src),
                      [p2r], [vs_tok_r])
                V(lambda e: e.tensor_copy(vs32[:, ch * 128:(ch + 1) * 128], p2[:, (nb - 1) * 128:nb * 128]), [p2r], [vs32_r])
            if (not isX) or lastx:
                add(win, OVS, 2, vs_ep)
            run_interleaved(gens, 3)

        def gates(mode, nb, psab, psabr):
            isS = 0 if mode == "P" else 1
            NBK = 2 if mode == "P" else 16
            G = gates_f
            ab = psab[:, 0:nb * 16].rearrange("p (b x) -> p b x", b=nb)

            def g3(i):
                return G[:, i, 0:nb * 8].rearrange("p (b h) -> p b h", b=nb)
            V(lambda e: e.tensor_tensor(g3(0), ab[:, :, 0:8], gv[:, 8:16].unsqueeze(1).broadcast_to([128, nb, 8]), ALU.add),
              [psabr] + CALL, [gts_r])
            A(lambda e: e.activation(g3(0), g3(0), AF.Exp), [gts_r], [gts_r])
            A(lambda e: e.activation(g3(0), g3(0), AF.Ln, bias=1.0), [gts_r], [gts_r])
            V(lambda e: e.tensor_tensor(g3(1), g3(0), negA[:, :].unsqueeze(1).broadcast_to([128, nb, 8]), ALU.mult),
              [gts_r] + CALL, [gts_r])
            A(lambda e: e.activation(g3(2), ab[:, :, 8:16], AF.Sigmoid), [psabr], [gts_r])
            V(lambda e: e.tensor_scalar(g3(3), g3(2), -1.0, None, ALU.mult), [gts_r], [gts_r])
            pg, pgr = psum()
            pgT, pgTr = psum()
            for tb in range(nb):
                M(lambda e, tb=tb: e.matmul(pg[:, tb * 8:(tb + 1) * 8], mk[:, M_TRI + 4 * isS, :], G[:, 1, tb * 8:(tb + 1) * 8],
                                            start=True, stop=True), [gts_r] + CALL, [pgr])
                M(lambda e, tb=tb: e.matmul(pgT[0:8, tb * 128:(tb + 1) * 128], G[:, 1, tb * 8:(tb + 1) * 8], mk[:, M_TRI + 4 * isS, :],
                                            start=True, stop=True), [gts_r] + CALL, [pgTr])
            V(lambda e: e.tensor_copy(G[:, 4, 0:nb * 8], pg[:, 0:nb * 8]), [pgr], [gts_r])
            A(lambda e: e.copy(gcT[:, 0:nb * 128], pgT[0:8, 0:nb * 128]), [pgTr], [gcT_r])
            A(lambda e: e.activation(G[:, 5, 0:nb * 8], G[:, 4, 0:nb * 8], AF.Exp), [gts_r], [gts_r])
            V(lambda e: e.tensor_tensor(G[:, 6, 0:nb * 8], G[:, 2, 0:nb * 8], G[:, 5, 0:nb * 8], ALU.mult), [gts_r], [gts_r])
            lm = sm[:, SM_LASTP:SM_LASTP + 2] if not isS else sm[:, SM_LASTS:SM_LASTS + 16]
            pl, plr = psum()
            rlt, rl_r = tf()
            rl = rlt[:, 0:512].rearrange("p (b x) -> p b x", b=4)
            for tb in range(nb):
                V(lambda e, tb=tb: e.tensor_tensor(
                    rl[:, tb, 0:NBK * 8].rearrange("p (k h) -> p k h", k=NBK),
                    G[:, 4, tb * 8:(tb + 1) * 8].unsqueeze(1).broadcast_to([128, NBK, 8]),
                    lm.unsqueeze(2).broadcast_to([128, NBK, 8]), ALU.mult), [gts_r] + CALL, [rl_r])
                M(lambda e, tb=tb: e.matmul(pl[:, tb * 128:tb * 128 + NBK * 8], ones_f[:, :], rl[:, tb, 0:NBK * 8],
                                            start=True, stop=True), [rl_r] + CALL, [plr])
            for tb in range(nb):
                A(lambda e, tb=tb: e.activation(gl[:, tb, 0:NBK * 8], pl[:, tb * 128:tb * 128 + NBK * 8], AF.Exp), [plr], [gl_r])
            anyc = sm[:, SM_ANYP + isS:SM_ANYP + isS + 1]
            V(lambda e: e.tensor_scalar(G[:, 8, 0:nb * 8], G[:, 4, 0:nb * 8], anyc, None, ALU.mult), [gts_r] + CALL, [gts_r])
            po, por = psum()
            for tb in range(nb):
                M(lambda e, tb=tb: e.matmul(po[:, tb * 8:(tb + 1) * 8], mk[:, M_BLK + 4 * isS, :], G[:, 8, tb * 8:(tb + 1) * 8],
                                            start=True, stop=True), [gts_r] + CALL, [por])
            V(lambda e: e.tensor_tensor(G[:, 7, 0:nb * 8], po[:, 0:nb * 8], G[:, 4, 0:nb * 8], ALU.subtract), [por, gts_r], [gts_r])
            A(lambda e: e.activation(G[:, 7, 0:nb * 8], G[:, 7, 0:nb * 8], AF.Exp), [gts_r], [gts_r])

        def gdn_prep(U, mode, tb, h, d, so=False):
            isS = 0 if mode == "P" else 1
            NL = 6 if mode == "P" else 3
            G = gates_f
            u = U.u
            cols = slice(tb * 128, (tb + 1) * 128)
            gcol = G[:, 4, tb * 8 + h:tb * 8 + h + 1]
            i = u
            d["i"] = i
            psG, psGr = U.ps()
            M(lambda e: e.matmul(psG[:, 0:128], kT[:, h, cols], kT[:, h, cols], start=True, stop=True), [kT_r[h]], [psGr])
            if not so:
                M(lambda e: e.matmul(psG[:, 128:256], kT[:, h, cols], qT[:, h, cols], start=True, stop=True), [kT_r[h], qT_r[h]], [psGr])
            V(lambda e: e.tensor_scalar(gch[u][:, :], gcT[:, cols], ident[0:8, h:h + 1], None, ALU.mult), [gcT_r, ident_r], [gch_r[u]])
            psR, psRr = U.ps()
            M(lambda e: e.matmul(psR[:, 0:128], ones_f[0:8, :], gch[u][:, :], start=True, stop=True), [gch_r[u]] + CALL, [psRr])
            V(lambda e: e.tensor_scalar(vb[i][:, :], v_tok[:, tb, h, :], G[:, 2, tb * 8 + h:tb * 8 + h + 1], None, ALU.mult),
               [v_tok_r[h], gts_r], [vb_r[i]])
            V(lambda e: e.tensor_scalar(kbg[i][:, :], k_tok[:, tb, h, :], G[:, 6, tb * 8 + h:tb * 8 + h + 1], None, ALU.mult),
               [k_tok_r[h], gts_r], [kbg_r[i]])
            yield
            ia = U.n("e1", 2)
            ib = U.n("e1", 2)
            V(lambda e: e.scalar_tensor_tensor(e1[ia][:, :], psR[:, 0:128], gcol, mk[:, M_M1 + 4 * isS, :], ALU.subtract, ALU.add),
              [psRr, gts_r] + CALL, [e1_r[ia]])
            if not so:
                V(lambda e: e.scalar_tensor_tensor(e1[ib][:, :], psR[:, 0:128], gcol, mk[:, M_M2 + 4 * isS, :], ALU.subtract, ALU.add),
                  [psRr, gts_r] + CALL, [e1_r[ib]])
                A(lambda e: e.activation(EG[i][:, :], psR[:, 0:128], AF.Exp), [psRr], [EG_r[i]])
            yield
            A(lambda e: e.activation(Dm[i][:, :], e1[ia][:, :], AF.Exp, scale=-1.0), [e1_r[ia]], [Dm_r[i]])
            if not so:
                A(lambda e: e.activation(DTm[i][:, :], e1[ib][:, :], AF.Exp), [e1_r[ib]], [DTm_r[i]])
                V(lambda e: e.tensor_tensor(qg[i][:, :], qT[:, h, cols], EG[i][:, :], ALU.mult), [qT_r[h], EG_r[i]], [qg_r[i]])
            yield
            V(lambda e: e.scalar_tensor_tensor(Bm[i][:, :], psG[:, 0:128], G[:, 3, tb * 8 + h:tb * 8 + h + 1], Dm[i][:, :],
                                               ALU.mult, ALU.mult), [psGr, gts_r, Dm_r[i]], [Bm_r[i]])
            if not so:
                V(lambda e: e.tensor_tensor(QKD[i][:, :], psG[:, 128:256], DTm[i][:, :], ALU.mult), [psGr, DTm_r[i]], [QKD_r[i]])
            yield
            pb, pbr = U.pb()
            M(lambda e: e.transpose(pb[:, 0:128], Bm[i][:, :], ident_bf[:, :]), [Bm_r[i]] + CALL, [pbr])
            yield
            p0 = U.n("Pm", 3)
            A(lambda e: e.copy(PTm[p0][:, :], pb[:, 0:128]), [pbr], [PTm_r[p0]])
            t0 = U.n("TT", 2)
            V(lambda e, t0=t0: e.tensor_tensor(TT[t0][:, :], pb[:, 0:128], ident_bf[:, :], ALU.add), [pbr] + CALL, [TT_r[t0]])
            yield
            Pc, Pcr = Bm[i], Bm_r[i]
            PTc, PTcr = PTm[p0], PTm_r[p0]
            for k in range(NL - 1):
                ps, psr = U.ps()
                M(lambda e, PTc=PTc, Pc=Pc, ps=ps: e.matmul(ps[:, 0:128], PTc[:, :], Pc[:, :], start=True, stop=True),
                  [PTcr, Pcr], [psr])
                if k < NL - 2:
                    M(lambda e, PTc=PTc, Pc=Pc, ps=ps: e.matmul(ps[:, 128:256], Pc[:, :], PTc[:, :], start=True, stop=True),
                      [PTcr, Pcr], [psr])
                yield
                pn = U.n("Pm", 3)
                A(lambda e, pn=pn, ps=ps: e.copy(Pm[pn][:, :], ps[:, 0:128]), [psr], [Pm_r[pn]])
                if k < NL - 2:
                    A(lambda e, pn=pn, ps=ps: e.copy(PTm[pn][:, :], ps[:, 128:256]), [psr], [PTm_r[pn]])
                Pc, Pcr = Pm[pn], Pm_r[pn]
                PTc, PTcr = PTm[pn], PTm_r[pn]
                yield
                ps2, ps2r = U.ps()
                M(lambda e, Pc=Pc, t0=t0, ps2=ps2: e.matmul(ps2[:, 0:128], Pc[:, :], TT[t0][:, :], start=True, stop=True),
                  [Pcr, TT_r[t0]], [ps2r])
                yield
                t1 = U.n("TT", 2)
                V(lambda e, t0=t0, t1=t1, ps2=ps2: e.tensor_tensor(TT[t1][:, :], ps2[:, 0:128], TT[t0][:, :], ALU.add),
                  [ps2r, TT_r[t0]], [TT_r[t1]])
                t0 = t1
                yield
            d["TT"] = TT[t0]
            d["TTr"] = TT_r[t0]

        def o_finish(U, pso, psor, tb, h, pbsel=None):
            cols = slice(tb * 128, (tb + 1) * 128)
            j = U.u
            sm_ = small[:, 4 * j:4 * j + 4]
            smr = small_r[j]
            A(lambda e: e.activation(otmp[j][:, :], pso, AF.Square, accum_out=sm_[:, 0:1]), [psor], [otmp_r[j], smr])
            A(lambda e: e.activation(sm_[:, 1:2], sm_[:, 0:1], AF.Ln, bias=EPS, scale=1.0 / 128), [smr], [smr])
            A(lambda e: e.activation(sm_[:, 2:3], sm_[:, 1:2], AF.Exp, scale=-0.5), [smr], [smr])
            yield
            V(lambda e: e.scalar_tensor_tensor(ogn[j][:, :], pso, sm_[:, 2:3], gnw, ALU.mult, ALU.mult),
              [psor, smr] + CALL, [ogn_r[j]])
            yield
            pb, pbr = pbsel if pbsel is not None else U.pb()
            M(lambda e: e.transpose(pb[:, 0:128], ogn[j][:, :], ident_bf[:, :]), [ogn_r[j]] + CALL, [pbr])
            yield
            V(lambda e: e.tensor_tensor(otmp[j][:, :], pb[:, 0:128], zg[:, h, cols], ALU.mult), [pbr, zg_r[h]], [otmp_r[j]])
            V(lambda e: e.tensor_tensor(mixT[:, h, cols], mixT[:, h, cols], otmp[j][:, :], ALU.add), [otmp_r[j], mixT_r[h]], [mixT_r[h]])

        def gdn_prompt(nb, so=False):
            G = gates_f

            def unit(U, tb, h):
                d = {}
                yield from gdn_prep(U, "P", tb, h, d, so)
                i, TTc, TTr = d["i"], d["TT"], d["TTr"]
                V(lambda e: e.tensor_scalar(kgt[i][:, :], k_tok[:, tb, h, :], G[:, 7, tb * 8 + h:tb * 8 + h + 1], None, ALU.mult),
                   [k_tok_r[h], gts_r], [kgt_r[i]])
                pu, pur = U.ps()
                M(lambda e: e.matmul(pu[:, 0:128], TTc[:, :], vb[i][:, :], start=True, stop=True), [TTr, vb_r[i]], [pur])
                M(lambda e: e.matmul(pu[:, 128:256], kbg[i][:, :], TTc[:, :], start=True, stop=True), [TTr, kbg_r[i]], [pur])
                yield
                A(lambda e: e.copy(ub[i][:, :], pu[:, 0:128]), [pur], [ub_r[i]])
                A(lambda e: e.copy(wTb[i][:, :], pu[:, 128:256]), [pur], [wTb_r[i]])
                yield
                pw, pwr = U.ps()
                pS, pSr = U.ps()
                pS_i = U.last
                for cc in range(2):
                    r = slice(64 * cc, 64 * cc + 64)
                    M(lambda e, r=r: e.matmul(pw[r, 0:128], wTb[i][:, r], Sbf[:, h, :], start=True, stop=True),
                      [wTb_r[i], Sbf_r[h]], [pwr])
                    yield
                    V(lambda e, r=r: e.tensor_tensor(vnew[i][r, :], ub[i][r, :], pw[r, 0:128], ALU.subtract),
                      [ub_r[i], pwr], [vnew_r[i]])
                    yield
                    if not so:
                        M(lambda e, r=r: e.matmul(pw[r, 128:256], qg[i][:, r], Sbf[:, h, :], start=True, stop=False),
                          [qg_r[i], Sbf_r[h]], [pwr], inc=False)
                        M(lambda e, r=r: e.matmul(pw[r, 128:256], QKD[i][:, r], vnew[i][:, :], start=False, stop=True),
                          [QKD_r[i], vnew_r[i]], [pwr])
                    M(lambda e, r=r, cc=cc, pS=pS: e.matmul(pS[:, 0:128], kgt[i][r, :], vnew[i][r, :], start=True, stop=True),
                      [kgt_r[i], vnew_r[i]], [pSr])
                    yield
                    V(lambda e, cc=cc, pS=pS: e.scalar_tensor_tensor(Sf[:, h, :], Sf[:, h, :], gl[:, tb, cc * 8 + h:cc * 8 + h + 1],
                                                                     pS[:, 0:128], ALU.mult, ALU.add),
                      [Sf_r[h], gl_r, pSr], [Sf_r[h]])
                    A(lambda e: e.copy(Sbf[:, h, :], Sf[:, h, :]), [Sf_r[h]], [Sbf_r[h]])
                    yield
                if not so:
                    yield from o_finish(U, pw[:, 128:256], pwr, tb, h, U.pb_of(pS_i))

            gens = []
            for tb in range(nb):
                for h in range(8):
                    gens.append(lambda U, tb=tb, h=h: unit(U, tb, h))
            V(lambda e: e.memset(vnew[2][:, :], 0.0), [], [xin_r[2], xin_r[3]] + s2res)
            V(lambda e: e.memset(vnew[3][:, :], 0.0), [], TF_r[0:4] + s3res)
            run_interleaved(gens, 4)
            V(lambda e: e.memset(dummy[:, 0:1], 0.0), [], [xin_r[2], xin_r[3], dummy_r] + s2res)
            V(lambda e: e.memset(dummy[:, 1:2], 0.0), [], TF_r[0:4] + [dummy_r] + s3res)

        def gdn_sample():
            G = gates_f
            tb = 0

            def unit(h):
                P.dma("sp", lambda e, h=h: e.dma_start(out=Sf[:, 0:NSQ, :], in_=sgdn[:, h].rearrange("s k v -> k s v")), writes=Sf_r)
                A(lambda e: e.copy(Sbf[:, 0:NSQ, :], Sf[:, 0:NSQ, :]), Sf_r, Sbf_r)
                d = {}
                for _ in gdn_prep(UCS[0], "S", tb, h, d):
                    pass
                i, TTc, TTr = d["i"], d["TT"], d["TTr"]
                pu, pur = psum()
                M(lambda e: e.matmul(pu[:, 0:128], vb[i][:, :], TTc[:, :], start=True, stop=True), [TTr, vb_r[i]], [pur])
                M(lambda e: e.matmul(pu[:, 128:256], kbg[i][:, :], TTc[:, :], start=True, stop=True), [TTr, kbg_r[i]], [pur])
                A(lambda e: e.copy(ub[i][:, :], pu[:, 0:128]), [pur], [ub_r[i]])
                A(lambda e: e.copy(wTb[i][:, :], pu[:, 128:256]), [pur], [wTb_r[i]])
                pw, pwr = psum()
                for s in range(NSQ):
                    c8 = slice(8 * s, 8 * s + 8)
                    M(lambda e, s=s, c8=c8: e.matmul(pw[:, 8 * s:8 * s + 8], Sbf[:, s, :], wTb[i][:, c8], start=True, stop=True),
                      [wTb_r[i], Sbf_r[s]], [pwr], inc=False)
                    M(lambda e, s=s, c8=c8: e.matmul(pw[:, 128 + 8 * s:128 + 8 * s + 8], Sbf[:, s, :], qg[i][:, c8], start=True, stop=True),
                      [qg_r[i], Sbf_r[s]], [pwr], inc=(s == NSQ - 1))
                V(lambda e: e.tensor_tensor(vnf[i][:, :], ub[i][:, :], pw[:, 0:128], ALU.subtract), [ub_r[i], pwr], [vnf_r[i]])
                A(lambda e: e.copy(oq[i][:, :], pw[:, 128:256]), [pwr], [oq_r[i]])
                p2, p2r = psum()
                M(lambda e: e.transpose(p2[:, 0:128], vnf[i][:, :], ident[:, :]), [vnf_r[i], ident_r], [p2r])
                V(lambda e: e.tensor_copy(vnew[i][:, :], p2[:, 0:128]), [p2r], [vnew_r[i]])
                M(lambda e: e.matmul(p2[:, 128:256], vnew[i][:, :], QKD[i][:, :], start=True, stop=True), [vnew_r[i], QKD_r[i]], [p2r])
                V(lambda e: e.tensor_tensor(oq[i][:, :], oq[i][:, :], p2[:, 128:256], ALU.add), [oq_r[i], p2r], [oq_r[i]])
                M(lambda e: e.transpose(p2[:, 256:384], oq[i][:, :], ident[:, :]), [oq_r[i], ident_r], [p2r])
                for _ in o_finish(UCS[0], p2[:, 256:384], p2r, tb, h):
                    pass
                for s in range(NSQ):
                    kk = nxt("kgt", 2)
                    V(lambda e, s=s, kk=kk: e.tensor_scalar(kgt[kk][:, :], k_tok[:, tb, h, :], G[:, 7, h:h + 1],
                                                            sm[:, SM_SEQ + s:SM_SEQ + s + 1], ALU.mult, ALU.mult),
                      [k_tok_r[h], gts_r] + CALL, [kgt_r[kk]])
                    pS, pSr = psum()
                    M(lambda e, kk=kk: e.matmul(pS[:, 0:128], kgt[kk][:, :], vnew[i][:, :], start=True, stop=True),
                      [kgt_r[kk], vnew_r[i]], [pSr])
                    V(lambda e, s=s: e.scalar_tensor_tensor(Sf[:, s, :], Sf[:, s, :], gl[:, 0, s * 8 + h:s * 8 + h + 1], pS[:, 0:128],
                                                            ALU.mult, ALU.add), [Sf_r[s], gl_r, pSr], [Sf_r[s]])
                P.dma("sp", lambda e, h=h: e.dma_start(out=gdns[:, h].rearrange("s k v -> k s v"), in_=Sf[:, 0:NSQ, :]), reads=Sf_r)
            for h in range(8):
                unit(h)

        def swa_block(U, mode, tb, kvh, first_tile, cache_fn=None):
            cols = slice(tb * 128, (tb + 1) * 128)
            isS = mode == "S"
            kbs = []
            if not isS:
                kbs.append((tb, 3 if (first_tile and tb == 0) else 2))
                kbs.append((tb + 1, 0))
            else:
                kbs.append((1, 1))
            u = U.u
            psO, psOr = banks[2 * u], bank_r[2 * u]
            psD, psDr = banks[2 * u + 1], bank_r[2 * u + 1]
            npt = len(kbs)
            for n_, (blk, mi) in enumerate(kbs):
                psS2 = [(banks[2 * u + 4], bank_r[2 * u + 4]), (banks[2 * u + 5], bank_r[2 * u + 5])]
                pi = 2 * u + n_
                for par in range(2):
                    r = slice(64 * par, 64 * par + 64)
                    psS, psSr = psS2[par]
                    out = psS[:, 0:256]
                    if isS:
                        rhs = qsT[r, 2 * kvh:2 * kvh + 2, 0:128].rearrange("p i (s t) -> p s i t", s=NSQ)
                    else:
                        rhs = qsT[r, 2 * kvh:2 * kvh + 2, cols]
                    M(lambda e, r=r, out=out, blk=blk, rhs=rhs: e.matmul(out, ksT[r, kvh, blk * 128:(blk + 1) * 128], rhs,
                                                                          start=True, stop=True),
                      [ksT_r, qsT_r[2 * kvh], qsT_r[2 * kvh + 1]], [psSr])
                    yield
                    A(lambda e, pi=pi, psS=psS, par=par: e.activation(pTx[pi][:, par * 256:(par + 1) * 256], psS[:, 0:256], AF.Exp, scale=0.125),
                      [psSr], [pTx_r[pi]])
                yield
                if isS:
                    V(lambda e, pi=pi: e.tensor_tensor(pTx[pi][:, :], pTx[pi][:, :], mkS[:, :], ALU.mult),
                      [pTx_r[pi]] + CALL, [pTx_r[pi]])
                else:
                    pv = pTx[pi][:, :].rearrange("p (g q) -> p g q", g=4)
                    V(lambda e, pv=pv, mi=mi: e.tensor_tensor(pv, pv, mkb[:, mi, :].unsqueeze(1).broadcast_to([128, 4, 128]), ALU.mult),
                       [pTx_r[pi]] + CALL, [pTx_r[pi]])
                yield
                last = (n_ == npt - 1)
                M(lambda e, pi=pi, blk=blk, n_=n_, last=last: e.matmul(psO[:, :], vs_tok[:, blk, kvh, :], pTx[pi][:, :],
                                                                       start=(n_ == 0), stop=last),
                  [vs_tok_r, pTx_r[pi]], [psOr])
                M(lambda e, pi=pi, n_=n_: e.matmul(psD[:, :], ones_bf[:, :], pTx[pi][:, :], start=(n_ == 0), stop=False),
                  [pTx_r[pi], ones_r], [psDr])
            if cache_fn is not None:
                cache_fn(kvh, psO, psOr, psD, psDr)
            esk = esinkS if isS else esink
            M(lambda e: e.matmul(psD[:, :], ones_bf[0:1, :], esk[0:1, kvh, :], start=False, stop=True), [ones_r] + CALL, [psDr])
            yield
            rden, rden_r = U.tf(0)
            osw, osw_r = U.tf(1)
            V(lambda e: e.reciprocal(rden[:, 0:512], psD[:, :]), [psDr], [rden_r])
            V(lambda e: e.tensor_tensor(osw[:, 0:512], psO[:, :], rden[:, 0:512], ALU.mult), [psOr, rden_r], [osw_r])
            yield
            for par in range(2):
                r = slice(64 * par, 64 * par + 64)
                if isS:
                    src = osw[r, par * 256:(par + 1) * 256].rearrange("p (s i t) -> p i s t", s=NSQ, i=2)
                    dst = mixT[r, 2 * kvh:2 * kvh + 2, 0:128].rearrange("p i (s t) -> p i s t", s=NSQ)
                    gb_ = gbT[r, 2 * kvh:2 * kvh + 2, 0:128].rearrange("p i (s t) -> p i s t", s=NSQ)
                else:
                    src = osw[r, par * 256:(par + 1) * 256].rearrange("p (i q) -> p i q", i=2)
                    dst = mixT[r, 2 * kvh:2 * kvh + 2, cols]
                    gb_ = gbT[r, 2 * kvh:2 * kvh + 2, cols]
                V(lambda e, src=src, dst=dst, gb_=gb_: e.tensor_tensor(dst, src, gb_, ALU.mult),
                  [osw_r, gbT_r[2 * kvh], gbT_r[2 * kvh + 1]], [mixT_r[2 * kvh], mixT_r[2 * kvh + 1]])

        def swa_prompt(nb, first_tile, ntok, carry_only=False):
            V(lambda e: e.tensor_copy(ksT[:, :, 0:128], kcar[:, :, :]), [kcar_r], [ksT_r])
            gens = []
            for tb in range(0 if carry_only else nb):
                for kvh in range(4):
                    gens.append(lambda U, tb=tb, kvh=kvh: swa_block(U, "P", tb, kvh, first_tile))
            run_interleaved(gens, 2)
            V(lambda e: e.tensor_copy(kcar[:, :, :], ksT[:, :, ntok:ntok + 128]), [ksT_r], [kcar_r])
            V(lambda e: e.tensor_copy(vs_tok[:, 0, :, :], vs_tok[:, nb, :, :]), [vs_tok_r], [vs_tok_r])

        mcb = sb("mcb", [128, TS], BF16)
        mcb_r = Res()

        def swa_sample():
            V(lambda e: e.tensor_copy(mcb[:, :], sm[:, SM_MC:SM_MC + TS]), CALL, [mcb_r])
            for s in range(NSQ):
                for dup in range(2):
                    P.dma("sp", lambda e, s=s, dup=dup: e.dma_start(out=kcs[:, :, dup, :], in_=ck[s].rearrange("r (k d) -> r k d", k=4)), writes=[kcs_r])
                p2, p2r = psum()
                for kvh in range(4):
                    M(lambda e, kvh=kvh, p2=p2: e.transpose(p2[:, kvh * 128:(kvh + 1) * 128], kcs[:, kvh, :, :].rearrange("p a d -> p (a d)"), ident[:, :]),
                      [kcs_r, ident_r], [p2r], inc=(kvh == 3))
                ki = nxt("kcT", 2)
                A(lambda e, ki=ki, p2=p2: e.copy(kcT[ki][:, :, :], p2[:, :].rearrange("p (k r) -> p k r", k=4)), [p2r], [kcT_r[ki]])
                psS2 = [psum(), psum()]
                for par in range(2):
                    psS, psSr = psS2[par]
                    r = slice(64 * par, 64 * par + 64)
                    for kvh in range(4):
                        out = psS[:, kvh * 16:kvh * 16 + 16]
                        M(lambda e, r=r, out=out, s=s, kvh=kvh, ki=ki: e.matmul(out, kcT[ki][r, kvh, :],
                                                                            qsT[r, 2 * kvh:2 * kvh + 2, 8 * s:8 * s + 8], start=True, stop=True),
                          [kcT_r[ki], qsT_r[2 * kvh], qsT_r[2 * kvh + 1]], [psSr], inc=(kvh == 3))
                    dst = pTc[:, :, :].rearrange("p k (par s x) -> p par s k x", par=2, s=NSQ)[:, par, s]
                    A(lambda e, dst=dst, psS=psS: e.activation(dst, psS[:, 0:64].rearrange("p (k x) -> p k x", k=4), AF.Exp, scale=0.125),
                      [psSr], Sbf_r)
            for kvh in range(4):
                pv = pTc[:, kvh, :].rearrange("p (x t) -> p x t", t=TS)
                V(lambda e, pv=pv: e.tensor_tensor(pv, pv, mcb[:, :].unsqueeze(1).broadcast_to([128, 64, TS]), ALU.mult),
                  pTc_r[kvh] + [mcb_r], pTc_r[kvh])

            def cache_fn(kvh, psO, psOr, psD, psDr):
                M(lambda e: e.matmul(psD[:, :], ones_bf[:, :], pTc[:, kvh, :], start=False, stop=False), pTc_r[kvh] + [ones_r], [psDr])
                for s in range(NSQ):
                    for par in range(2):
                        c16 = slice(par * 256 + s * 16, par * 256 + s * 16 + 16)
                        for half in range(2):
                            lastm = (s == NSQ - 1 and par == 1 and half == 1)
                            M(lambda e, c16=c16, s=s, half=half, lastm=lastm: e.matmul(
                                psO[64 * half:64 * half + 64, c16], vcall[:, s, kvh, :], pTc[:, kvh, c16], start=False, stop=True, skip_group_check=True),
                              [vcall_r[s]] + pTc_r[kvh], [psOr], inc=lastm)
            for kvh in range(4):
                for _ in swa_block(UCS[0], "S", 0, kvh, False, cache_fn):
                    pass

        def load_cache():
            for s in range(NSQ):
                vt, vtr = tf()
                P.dma("sp", lambda e, s=s, vt=vt: e.dma_start(out=vt[:, 0:256], in_=cv[s]), writes=[vtr])
                V(lambda e, s=s, vt=vt: e.tensor_copy(vcall[:, s, :, :].rearrange("p k d -> p (k d)"), vt[:, 0:256]), [vtr], [vcall_r[s]])

        def load_sconv():
            P.dma("sp", lambda e: e.dma_start(out=xin[0:NSQ * 3, 0, :], in_=sconv[:, 0:1024]), writes=[xin_r[0]])
            P.dma("sp", lambda e: e.dma_start(out=xin[0:NSQ * 3, 1, :], in_=sconv[:, 1024:2048]), writes=[xin_r[1]])
            P.dma("sp", lambda e: e.dma_start(out=xin[0:NSQ * 3, 2, :], in_=sconv[:, 2048:3072]), writes=[xin_r[2]])
            for g in range(6):
                p2, p2r = psum()
                for ci in range(4):
                    cidx = g * 4 + ci
                    M(lambda e, cidx=cidx, ci=ci, p2=p2: e.transpose(p2[:, ci * 128:ci * 128 + NSQ * 3],
                                                              xin[0:NSQ * 3, cidx // 8, (cidx % 8) * 128:(cidx % 8 + 1) * 128], ident[0:NSQ * 3, 0:NSQ * 3]),
                      [xin_r[cidx // 8], ident_r], [p2r], inc=(ci == 3))
                V(lambda e, g=g, p2=p2: e.tensor_copy(cst[:, g * 4:(g + 1) * 4, 0:NSQ * 3],
                                               p2[:, :].rearrange("p (c x) -> p c x", c=4)[:, :, 0:NSQ * 3]), [p2r], [cst_r])

        def store_conv(dst, nrow):
            for g in range(6):
                p2, p2r = psum()
                for ci in range(4):
                    cidx = g * 4 + ci
                    M(lambda e, cidx=cidx, ci=ci, p2=p2: e.transpose(p2[0:nrow, ci * 128:(ci + 1) * 128], cst[:, cidx, 0:nrow], ident[:, :]),
                      [cst_r, ident_r], [p2r], inc=(ci == 3))
                A(lambda e, g=g, p2=p2: e.copy(xin[0:nrow, g // 2, (g % 2) * 512:(g % 2 + 1) * 512], p2[0:nrow, :]), [p2r], [xin_r[g // 2]])
            for q in range(3):
                P.dma("sp", lambda e, q=q: e.dma_start(out=dst[:, q * 1024:(q + 1) * 1024], in_=xin[0:nrow, q, :]), reads=[xin_r[q]])

        def store_kv(kdst, vdst):
            p2, p2r = psum()
            for kvh in range(4):
                M(lambda e, kvh=kvh: e.transpose(p2[:, kvh * 128:(kvh + 1) * 128], ks32[:, kvh, :], ident[:, :]),
                  [ks32_r, ident_r], [p2r], inc=(kvh == 3))
            A(lambda e: e.copy(knew[:, :, :], p2[:, :].rearrange("p (k x) -> p k x", k=4)[:, :, 0:64]), [p2r], [knew_r])
            P.dma("sp", lambda e: e.dma_start(out=kdst, in_=knew[:, :, :]), reads=[knew_r])
            P.dma("sp", lambda e: e.dma_start(out=vdst, in_=vs32[:, :].rearrange("p (k d) -> p k d", k=4)), reads=[vs32_r])

        def outproj(ntok):
            for c in range(DC):
                wo, wor = wload_out(wout, DC, 128 * c, 128)
                po, por = psum()
                for j in range(DC):
                    M(lambda e, j=j, po=po, wo=wo: e.matmul(po[:, 0:ntok], wo[:, j, :], mixT[:, j, 0:ntok],
                                                            start=(j == 0), stop=(j == DC - 1)),
                      [wor, mixT_r[j]], [por], inc=(j == DC - 1))
                V(lambda e, c=c, po=po: e.tensor_tensor(xT[:, c, 0:ntok], po[:, 0:ntok], xT[:, c, 0:ntok], ALU.add),
                  [por, xT_r[c]], [xT_r[c]])

        def tile_pass(mode, xd, yd, tok0, ntok, first, last):
            nb = ntok // 128
            load_x(xd, tok0, ntok)
            rmsnorm(0, xn, xn_r, ntok)
            ffn(w1i, w1o, ntok)
            if stages >= 2:
                rmsnorm(1, xn, xn_r, ntok)
                if mode == "X":
                    inproj("X", tok0, ntok, lastx=last)
                    if last:
                        swa_prompt(nb, False, ntok, carry_only=True)
                    gdn_prompt(nb, so=True)
                    return
                if MIX & 1:
                    inproj(mode, tok0, ntok)
                if mode == "P":
                    if MIX & 2:
                        swa_prompt(nb, first, ntok)
                    if MIX & 4:
                        gdn_prompt(nb)
                else:
                    if MIX & 8:
                        swa_sample()
                    if MIX & 16:
                        gdn_sample()
                if MIX & 32:
                    outproj(ntok)
            if stages >= 3:
                rmsnorm(2, xn, xn_r, ntok)
                ffn(w2i, w2o, ntok)
            rmsnorm(3, xT, xT_r, ntok)
            store_y(yd, tok0, ntok)

        NX = LPRE // 512
        for t in range(NX):
            tile_pass("X", xpre, None, t * 512, 512, False, t == NX - 1)
        NT = LP // 512
        for t in range(NT):
            tile_pass("P", xp, yp, t * 512, 512, t == 0, t == NT - 1)
        if stages >= 2:
            if MIX & 64:
                store_conv(convp, 3)
            if MIX & 128:
                store_kv(kp.rearrange("r (k d) -> r k d", k=4), vp.rearrange("r (k d) -> r k d", k=4))
            if MIX & 256:
                P.dma("sp", lambda e: e.dma_start(out=gdnp.rearrange("h k v -> k h v"), in_=Sf[:, 0:8, :]), reads=Sf_r[0:8])
            if MIX & 512:
                load_sconv()
            if MIX & 1024:
                load_cache()
            if MIX & 2048:
                P.dma("sp", lambda e: e.dma_start(out=ksd[:, 0:120, :], in_=ck[:, 8:128, :]))
                P.dma("sp", lambda e: e.dma_start(out=vsd[:, 0:120, :], in_=cv[:, 8:128, :]))
        tile_pass("S", xs, ys, 0, NS_TOK, False, True)
        if stages >= 2:
            if MIX & 4096:
                store_conv(convs, NSQ * 3)
            if MIX & 8192:
                store_kv(ksd[:, 120:128, :].rearrange("s t (k d) -> s t k d", k=4), vsd[:, 120:128, :].rearrange("s t (k d) -> s t k d", k=4))
        P.finish()

        with nc.Block() as block:
            @block.tensor
            def _(eng):
                P.replay(eng, "pe")

            @block.scalar
            def _(eng):
                P.replay(eng, "act")

            @block.vector
            def _(eng):
                P.replay(eng, "dve")

            @block.gpsimd
            def _(eng):
                P.replay(eng, "pool")

            @block.sync
            def _(eng):
                P.replay(eng, "sp")
    return nc


def host_consts(NSQ, TS):
    f = np.float32
    ident = np.eye(128, dtype=f)
    p = np.arange(128)
    rm = np.zeros((128, 128), f)
    for m in range(128):
        d = m % 64
        if d < 8:
            rm[m, m + 8] = -1.0
        elif d < 16:
            rm[m, m - 8] = 1.0
    rmT = np.ascontiguousarray(rm.T)
    mk = np.zeros((128, 8, 128), f)
    for isS, C in ((0, 64), (1, TS)):
        same = (p[:, None] // C) == (p[None, :] // C)
        mk[:, 0 + 4 * isS, :] = (same & (p[:, None] <= p[None, :])).astype(f)
        mk[:, 1 + 4 * isS, :] = np.where(same & (p[:, None] > p[None, :]), 0.0, BIG)
        mk[:, 2 + 4 * isS, :] = np.where(same & (p[None, :] >= p[:, None]), 0.0, -BIG)
        mk[:, 3 + 4 * isS, :] = same.astype(f)
    mkb = np.zeros((128, 3, 128), f)
    mprev = (p[:, None] > p[None, :]).astype(f)
    mkb[:, 0, :] = (p[:, None] <= p[None, :]).astype(f)
    mkb[:, 1, :] = (((p[:, None] // TS) == (p[None, :] // TS)) & (p[:, None] <= p[None, :])).astype(f)
    mkb[:, 2, :] = (p[:, None] > p[None, :]).astype(f)
    mkS = np.zeros((128, 2, NSQ, 2, TS), f)
    for k in range(128):
        ks_, kt_ = k // TS, k % TS
        mkS[k, :, ks_, :, kt_:] = 1.0
    mkS = np.ascontiguousarray(mkS.reshape(128, 512))
    sm = np.zeros((128, 64), f)
    for b in range(2):
        sm[64 * b + 63, SM_LASTP + b] = 1.0
    for s in range(NSQ):
        sm[TS * s + TS - 1, SM_LASTS + s] = 1.0
        sm[TS * s:TS * (s + 1), SM_SEQ + s] = 1.0
    sm[:, SM_ANYP] = (p % 64 == 63).astype(f)
    sm[:, SM_ANYS] = (p % TS == TS - 1).astype(f)
    for t in range(TS):
        sm[:, SM_MC + t] = (p > t).astype(f)
    inv = (np.float32(500000.0) ** (-(np.arange(0, 16, 2, dtype=f)) / np.float32(16))).astype(f)

    def tables(pos):
        ang = (pos.astype(f)[:, None] * inv[None, :]).astype(f)
        c = np.ones((128, len(pos)), f)
        s = np.zeros((128, len(pos)), f)
        for pp in range(128):
            d = pp % 64
            if d < 16:
                c[pp] = np.cos(ang[:, d % 8])
                s[pp] = np.sin(ang[:, d % 8])
        return np.ascontiguousarray(c), np.ascontiguousarray(s)
    cosS, sinS = tables(np.tile(16384 + np.arange(TS), NSQ))
    return dict(ident=ident, rmT=rmT, mk=mk, mkb=mkb, mkS=mkS, sm=sm, cosS=cosS, sinS=sinS), tables, mprev


def pvec(v):
    return np.ascontiguousarray(v.reshape(DC, 128).T)


_CACHE = {}
STAGES = 99
MIXF = 0xFFFF


def kernel(x_prompt, x_sample, state_conv, state_gdn, cache_swa_k, cache_swa_v,
           norm_ffn1, w_ffn1_in, w_ffn1_out, norm_mix, w_in, conv_w, gdn_a_log, gdn_dt_bias,
           gdn_norm, swa_sinks, w_out, norm_ffn2, w_ffn2_in, w_ffn2_out, norm_final):
    f = np.float32
    A_ = lambda a: np.asarray(a, f)
    x_prompt = A_(x_prompt)
    x_sample = A_(x_sample)
    B, L, _ = x_prompt.shape
    NSQ = x_sample.shape[0] // NCORES
    T = x_sample.shape[1]
    LP = L // 2
    key = (L, STAGES, MIXF)
    if key not in _CACHE:
        _CACHE[key] = build(LP, LP, NSQ, T, stages=STAGES, MIX=MIXF)
    nc = _CACHE[key]
    nrm = np.ascontiguousarray(np.stack([pvec(A_(norm_ffn1)[0]), pvec(A_(norm_mix)[0]),
                                         pvec(A_(norm_ffn2)[0]), pvec(A_(norm_final))], axis=1))
    hc, tables, mprev = host_consts(NSQ, T)
    tabs = [tables(np.arange(hf * LP, (hf + 1) * LP)) for hf in range(2)]
    zx = np.zeros((LP, D), f)
    zm = np.zeros((128, 128), f)
    win = np.ascontiguousarray(A_(w_in)[0])
    wks = np.zeros((D, 512), f)
    for kvh in range(4):
        for dup in range(2):
            wks[:, kvh * 128 + dup * 64:kvh * 128 + (dup + 1) * 64] = win[:, OKS + kvh * 64:OKS + (kvh + 1) * 64]
    cw = np.ascontiguousarray(A_(conv_w)[0].reshape(4, 24, 128).transpose(2, 1, 0))
    gv = np.concatenate([A_(gdn_a_log)[0], A_(gdn_dt_bias)[0], A_(gdn_norm)[0], A_(swa_sinks)[0]])[None, :].astype(f)
    shared = {
        "w1i": np.ascontiguousarray(A_(w_ffn1_in)[0]), "w1o": np.ascontiguousarray(A_(w_ffn1_out)[0]),
        "w2i": np.ascontiguousarray(A_(w_ffn2_in)[0]), "w2o": np.ascontiguousarray(A_(w_ffn2_out)[0]),
        "nrm": nrm, "win": win, "wks": wks, "wout": np.ascontiguousarray(A_(w_out)[0]),
        "cw": cw, "gv": np.ascontiguousarray(gv),
    }
    shared.update(hc)
    sc, sg_, ck_, cv_ = A_(state_conv)[0], A_(state_gdn)[0], A_(cache_swa_k)[0], A_(cache_swa_v)[0]
    in_maps = []
    for c in range(NCORES):
        m = dict(shared)
        sl = slice(c * NSQ, (c + 1) * NSQ)
        sq_, hf = c // 2, c % 2
        m["xp"] = np.ascontiguousarray(x_prompt[sq_, hf * LP:(hf + 1) * LP])
        m["xpre"] = np.ascontiguousarray(x_prompt[sq_, 0:LP]) if hf == 1 else zx
        m["cosP"], m["sinP"] = tabs[hf]
        m["cosX"], m["sinX"] = tabs[0]
        m["mprev0"] = mprev if hf == 1 else zm
        m["xs"] = np.ascontiguousarray(x_sample[sl].reshape(NSQ * T, D))
        m["sconv"] = np.ascontiguousarray(sc[sl].reshape(NSQ * 3, 3072))
        m["sgdn"] = np.ascontiguousarray(sg_[sl])
        m["ck"] = np.ascontiguousarray(ck_[sl].reshape(NSQ, 128, 256))
        m["cv"] = np.ascontiguousarray(cv_[sl].reshape(NSQ, 128, 256))
        in_maps.append(m)
    res = run_bass_kernel_spmd(nc, in_maps, core_ids=list(range(NCORES)))
    R = res.results
    cat = lambda k, shp: np.concatenate([np.asarray(R[c][k], f).reshape(shp) for c in range(NCORES)], axis=0)
    stk = lambda k, shp: np.stack([np.asarray(R[2 * b + 1][k], f).reshape(shp) for b in range(B)], axis=0)
    y_prompt = np.stack([np.concatenate([np.asarray(R[2 * b]["yp"], f), np.asarray(R[2 * b + 1]["yp"], f)], axis=0) for b in range(B)], axis=0)
    y_sample = cat("ys", (NSQ, T, D))
    conv_p = stk("convp", (3, 3072))[None]
    conv_s = cat("convs", (NSQ, 3, 3072))[None]
    gdn_p = stk("gdnp", (8, 128, 128))[None]
    gdn_s = cat("gdns", (NSQ, 8, 128, 128))[None]
    k_p = stk("kp", (128, 4, 64))[None]
    k_s = cat("ksd", (NSQ, 128, 4, 64))[None]
    v_p = stk("vp", (128, 4, 64))[None]
    v_s = cat("vsd", (NSQ, 128, 4, 64))[None]
    return (y_prompt, y_sample, conv_p, conv_s, gdn_p, gdn_s, k_p, k_s, v_p, v_s)
```
